# Optimizing a Trainium2 kernel written in Bass

```python
import math
import jax, jax.numpy as jnp
from jax import lax
import numpy as np

D_MODEL = 1024
BATCH = 16
SEQ = 2048
DEPTH = 4

GRID_W = 64
CTX_LEN = 256
N_MIXERS = 3
EPS = 1e-6
N_HEADS = 16
QK_NOPE = 64
QK_ROPE = 32
V_DIM = 64
Q_LORA = 256
KV_LORA = 128
ROPE_THETA = 10000.0
Q_BLOCK = 128
CONV_W = 31
SHORT_W = 3
POS_EMB = 33
FILTER_FO = 64
DECAY_FAST = 0.3
DECAY_SLOW = 1.5
DECAY_TARGET = 1e-2
D_FF = 2816
FFN_CONV_W = 3
N_A = (DEPTH + 2) // 3
N_B = (DEPTH + 1) // 3
N_C = DEPTH // 3

kernel_name = 'hybrid_mla_conformer_hyena_dit'

F32 = jnp.float32


def rmsnorm(x, g):
    xf = x.astype(F32)
    y = xf * lax.rsqrt(jnp.mean(xf * xf, axis=-1, keepdims=True) + EPS)
    return (y * g.astype(F32)).astype(x.dtype)


def layernorm(x, g, b):
    xf = x.astype(F32)
    mu = jnp.mean(xf, axis=-1, keepdims=True)
    var = jnp.mean(jnp.square(xf - mu), axis=-1, keepdims=True)
    return ((xf - mu) * lax.rsqrt(var + EPS) * g.astype(F32) + b.astype(F32)).astype(x.dtype)


def modulate(x, g, shift, scale):
    return rmsnorm(x, g) * (1 + scale) + shift


def dwconv(x, w, b):
    pad = (w.shape[0] - 1) // 2
    y = lax.conv_general_dilated(x, w[:, None, :].astype(x.dtype), window_strides=(1,),
                                 padding=[(pad, pad)], dimension_numbers=('NWC', 'WIO', 'NWC'),
                                 feature_group_count=x.shape[-1])
    return y + b


def axial_angles(L):
    rows = L // GRID_W
    row = jnp.repeat(jnp.arange(rows), GRID_W).astype(F32)
    col = jnp.tile(jnp.arange(GRID_W), rows).astype(F32)
    half = QK_ROPE // 2
    inv = ROPE_THETA ** (-(jnp.arange(0, half, 2, dtype=F32) / half))
    return jnp.concatenate([row[:, None] * inv, col[:, None] * inv], axis=-1)


def apply_axial_rope(x, ang):
    q = QK_ROPE // 4
    xs = x.astype(F32).reshape(x.shape[:-1] + (2, 2, q))
    a = ang.reshape(ang.shape[0], 2, q)
    if x.ndim == 4:
        a = a[:, None]
    cos, sin = jnp.cos(a), jnp.sin(a)
    x1, x2 = xs[..., 0, :], xs[..., 1, :]
    out = jnp.stack([x1 * cos - x2 * sin, x1 * sin + x2 * cos], axis=-2)
    return out.reshape(x.shape).astype(x.dtype)


def mla_q(h, w_dq, q_norm, w_uq, qk_gain, ang):
    B, L, _ = h.shape
    cq = rmsnorm(h @ w_dq, q_norm)
    q = (cq @ w_uq).reshape(B, L, N_HEADS, QK_NOPE + QK_ROPE)
    q_nope = rmsnorm(q[..., :QK_NOPE], qk_gain[0, :QK_NOPE])
    q_pe = rmsnorm(q[..., QK_NOPE:], qk_gain[0, QK_NOPE:])
    if ang is not None:
        q_pe = apply_axial_rope(q_pe, ang)
    return q_nope, q_pe


def mla_kv(h, w_dkv, kv_norm, w_ukv, qk_gain, ang):
    B, L, _ = h.shape
    kv = h @ w_dkv
    c_kv = rmsnorm(kv[..., :KV_LORA], kv_norm)
    k_pe = rmsnorm(kv[..., KV_LORA:], qk_gain[1, QK_NOPE:])
    if ang is not None:
        k_pe = apply_axial_rope(k_pe, ang)
    kvu = (c_kv @ w_ukv).reshape(B, L, N_HEADS, QK_NOPE + V_DIM)
    k_nope = rmsnorm(kvu[..., :QK_NOPE], qk_gain[1, :QK_NOPE])
    return k_nope, k_pe, kvu[..., QK_NOPE:]


def mla_attend(q_nope, q_pe, k_nope, k_pe, v):
    scale = (QK_NOPE + QK_ROPE) ** -0.5
    s = jnp.einsum('bqhd,bkhd->bhqk', q_nope, k_nope) + jnp.einsum('bqhr,bkr->bhqk', q_pe, k_pe)
    p = jax.nn.softmax(s.astype(F32) * scale, axis=-1).astype(v.dtype)
    return jnp.einsum('bhqk,bkhd->bqhd', p, v)


def mla_latent_attention(q_nope, q_pe, k_nope, k_pe, v):
    B, L = q_nope.shape[:2]
    nblk = L // Q_BLOCK

    def blocks(t):
        return t.reshape((B, nblk, Q_BLOCK) + t.shape[2:]).swapaxes(0, 1)

    o = lax.map(lambda qb: mla_attend(qb[0], qb[1], k_nope, k_pe, v), (blocks(q_nope), blocks(q_pe)))
    return o.swapaxes(0, 1).reshape(B, L, N_HEADS * V_DIM)


def conformer_conv(h, w1, b1, wdw, bdw, lng, lnb, w2, b2):
    a = h @ w1 + b1
    u = a[..., :D_MODEL] * jax.nn.sigmoid(a[..., D_MODEL:])
    u = dwconv(u, wdw, bdw)
    u = jax.nn.silu(layernorm(u, lng, lnb))
    return u @ w2 + b2


def hyena_filter(L, f_w1, f_b1, f_w2, f_b2, f_w3, sin_freq):
    t = jnp.linspace(0.0, 1.0, L, dtype=F32)[:, None]
    bands = (POS_EMB - 1) // 2
    w = 2.0 * math.pi * jnp.arange(L, dtype=F32) / L
    f = jnp.linspace(1e-4, bands - 1, bands, dtype=F32)
    fw = w[:, None] * f[None, :]
    z = jnp.concatenate([t, jnp.cos(fw), -jnp.sin(fw)], axis=-1)
    fr = sin_freq.astype(F32)
    hdn = jnp.sin(fr * (z @ f_w1.astype(F32) + f_b1.astype(F32)))
    hdn = jnp.sin(fr * (hdn @ f_w2.astype(F32) + f_b2.astype(F32)))
    hf = hdn @ f_w3.astype(F32)
    deltas = jnp.linspace(math.log(DECAY_TARGET) / DECAY_FAST, math.log(DECAY_TARGET) / DECAY_SLOW,
                          D_MODEL, dtype=F32)
    decay = jnp.exp(-t * jnp.abs(deltas))
    h_fwd = hf[:, :D_MODEL] * decay
    h_bwd = hf[:, D_MODEL:] * decay
    k = jnp.concatenate([h_fwd, jnp.zeros((1, D_MODEL), F32), h_bwd[1:][::-1]], axis=0)
    return k / (jnp.sum(jnp.abs(k), axis=0, keepdims=True) + EPS)


def fft_longconv(u, k):
    L = u.shape[1]
    U = jnp.fft.rfft(u.astype(F32), n=2 * L, axis=1)
    K = jnp.fft.rfft(k, n=2 * L, axis=0)
    return jnp.fft.irfft(U * K[None], n=2 * L, axis=1)[:, :L].astype(u.dtype)


def hyena(h, w_in, b_in, w_short, b_short, f_w1, f_b1, f_w2, f_b2, f_w3, sin_freq, skip, w_out, b_out):
    L = h.shape[1]
    u = dwconv(h @ w_in + b_in, w_short, b_short)
    x0, x1, v = jnp.split(u, 3, axis=-1)
    k = hyena_filter(L, f_w1, f_b1, f_w2, f_b2, f_w3, sin_freq)
    v = v * x1
    v = fft_longconv(v, k) + skip * v
    return (v * x0) @ w_out + b_out


def conv_ffn(h, w_up, w_dw, b_dw, w_down):
    a = h @ w_up
    g = dwconv(a[..., :D_FF], w_dw, b_dw)
    return (jax.nn.silu(g) * a[..., D_FF:]) @ w_down


def setup_inputs(seed: int = 0) -> dict:
    key = jax.random.key(seed)
    ks = iter(jax.random.split(key, 41))
    D, H = D_MODEL, N_HEADS

    def nrm(shape, fan_in, s=1.0):
        return jax.random.normal(next(ks), shape, F32) * (s * fan_in ** -0.5)

    def gain(shape):
        return 1.0 + 0.02 * jax.random.normal(next(ks), shape, F32)

    def bias(shape, s=0.02):
        return s * jax.random.normal(next(ks), shape, F32)

    return {
        'x': jax.random.normal(next(ks), (BATCH, SEQ, D), F32),
        'c': jax.random.normal(next(ks), (BATCH, D), F32),
        'ctx': jax.random.normal(next(ks), (BATCH, CTX_LEN, D), F32),
        'c_ctx': jax.random.normal(next(ks), (D,), F32),
        'ada_w': nrm((DEPTH, D, 6 * D), D, 0.5),
        'ada_b': bias((DEPTH, 6 * D)),
        'norm_mix': gain((DEPTH, D)),
        'norm_ffn': gain((DEPTH, D)),
        'mla_w_dq': nrm((N_A, D, Q_LORA), D),
        'mla_q_norm': gain((N_A, Q_LORA)),
        'mla_w_uq': nrm((N_A, Q_LORA, H * (QK_NOPE + QK_ROPE)), Q_LORA),
        'mla_w_dkv': nrm((N_A, D, KV_LORA + QK_ROPE), D),
        'mla_kv_norm': gain((N_A, KV_LORA)),
        'mla_w_ukv': nrm((N_A, KV_LORA, H * (QK_NOPE + V_DIM)), KV_LORA),
        'mla_qk_gain': gain((N_A, 2, QK_NOPE + QK_ROPE)),
        'mla_w_o': nrm((N_A, H * V_DIM, D), H * V_DIM),
        'cf_w_pw1': nrm((N_B, D, 2 * D), D),
        'cf_b_pw1': bias((N_B, 2 * D)),
        'cf_w_dw': nrm((N_B, CONV_W, D), CONV_W),
        'cf_b_dw': bias((N_B, D)),
        'cf_ln_g': gain((N_B, D)),
        'cf_ln_b': bias((N_B, D)),
        'cf_w_pw2': nrm((N_B, D, D), D),
        'cf_b_pw2': bias((N_B, D)),
        'hy_w_in': nrm((N_C, D, 3 * D), D),
        'hy_b_in': bias((N_C, 3 * D)),
        'hy_w_short': nrm((N_C, SHORT_W, 3 * D), SHORT_W),
        'hy_b_short': bias((N_C, 3 * D)),
        'hy_f_w1': nrm((N_C, POS_EMB, FILTER_FO), POS_EMB),
        'hy_f_b1': bias((N_C, FILTER_FO), 0.1),
        'hy_f_w2': nrm((N_C, FILTER_FO, FILTER_FO), FILTER_FO),
        'hy_f_b2': bias((N_C, FILTER_FO), 0.1),
        'hy_f_w3': nrm((N_C, FILTER_FO, 2 * D), FILTER_FO),
        'hy_sin_freq': gain((N_C, FILTER_FO)),
        'hy_skip': bias((N_C, D), 0.5),
        'hy_w_out': nrm((N_C, D, D), D),
        'hy_b_out': bias((N_C, D)),
        'ffn_w_up': nrm((DEPTH, D, 2 * D_FF), D),
        'ffn_w_dw': nrm((DEPTH, FFN_CONV_W, D_FF), FFN_CONV_W),
        'ffn_b_dw': bias((DEPTH, D_FF)),
        'ffn_w_down': nrm((DEPTH, D_FF, D), D_FF),
    }


def reference(x, c, ctx, c_ctx, ada_w, ada_b, norm_mix, norm_ffn,
              mla_w_dq, mla_q_norm, mla_w_uq, mla_w_dkv, mla_kv_norm, mla_w_ukv, mla_qk_gain, mla_w_o,
              cf_w_pw1, cf_b_pw1, cf_w_dw, cf_b_dw, cf_ln_g, cf_ln_b, cf_w_pw2, cf_b_pw2,
              hy_w_in, hy_b_in, hy_w_short, hy_b_short, hy_f_w1, hy_f_b1, hy_f_w2, hy_f_b2, hy_f_w3,
              hy_sin_freq, hy_skip, hy_w_out, hy_b_out,
              ffn_w_up, ffn_w_dw, ffn_b_dw, ffn_w_down):
    B, L, _ = x.shape
    Lc = ctx.shape[1]
    ang = axial_angles(L)
    s_lat = jax.nn.silu(c)[:, None, :]
    s_ctx = jax.nn.silu(c_ctx)
    for i in range(DEPTH):
        kind = i % N_MIXERS
        j = i // N_MIXERS
        need_ctx_out = i < DEPTH - 1
        sh1, sc1, g1, sh2, sc2, g2 = jnp.split(s_lat @ ada_w[i] + ada_b[i], 6, axis=-1)
        csh1, csc1, cg1, csh2, csc2, cg2 = jnp.split(s_ctx @ ada_w[i] + ada_b[i], 6, axis=-1)
        hl = modulate(x, norm_mix[i], sh1, sc1)
        yc = None
        if kind == 0:
            hc = modulate(ctx, norm_mix[i], csh1, csc1)
            q_nope, q_pe = mla_q(hl, mla_w_dq[j], mla_q_norm[j], mla_w_uq[j], mla_qk_gain[j], ang)
            kl = mla_kv(hl, mla_w_dkv[j], mla_kv_norm[j], mla_w_ukv[j], mla_qk_gain[j], ang)
            kc = mla_kv(hc, mla_w_dkv[j], mla_kv_norm[j], mla_w_ukv[j], mla_qk_gain[j], None)
            k_nope = jnp.concatenate([kc[0], kl[0]], axis=1)
            k_pe = jnp.concatenate([kc[1], kl[1]], axis=1)
            v = jnp.concatenate([kc[2], kl[2]], axis=1)
            yl = mla_latent_attention(q_nope, q_pe, k_nope, k_pe, v) @ mla_w_o[j]
            if need_ctx_out:
                qc_nope, qc_pe = mla_q(hc, mla_w_dq[j], mla_q_norm[j], mla_w_uq[j], mla_qk_gain[j], None)
                yc = mla_attend(qc_nope, qc_pe, *kc).reshape(B, Lc, N_HEADS * V_DIM) @ mla_w_o[j]
        elif kind == 1:
            cf = (cf_w_pw1[j], cf_b_pw1[j], cf_w_dw[j], cf_b_dw[j], cf_ln_g[j], cf_ln_b[j], cf_w_pw2[j], cf_b_pw2[j])
            yl = conformer_conv(hl, *cf)
            if need_ctx_out:
                yc = conformer_conv(modulate(ctx, norm_mix[i], csh1, csc1), *cf)
        else:
            hy = (hy_w_in[j], hy_b_in[j], hy_w_short[j], hy_b_short[j], hy_f_w1[j], hy_f_b1[j], hy_f_w2[j],
                  hy_f_b2[j], hy_f_w3[j], hy_sin_freq[j], hy_skip[j], hy_w_out[j], hy_b_out[j])
            yl = hyena(hl, *hy)
            if need_ctx_out:
                yc = hyena(modulate(ctx, norm_mix[i], csh1, csc1), *hy)
        x = x + g1 * yl
        ffn = (ffn_w_up[i], ffn_w_dw[i], ffn_b_dw[i], ffn_w_down[i])
        x = x + g2 * conv_ffn(modulate(x, norm_ffn[i], sh2, sc2), *ffn)
        if need_ctx_out:
            ctx = ctx + cg1 * yc
            ctx = ctx + cg2 * conv_ffn(modulate(ctx, norm_ffn[i], csh2, csc2), *ffn)
    return x
```

```python
import contextlib
import os
import numpy as np
import concourse.bass as bass
import concourse.mybir as mybir
from concourse.bass_utils import run_bass_kernel_spmd

F32 = mybir.dt.float32
BF16 = mybir.dt.bfloat16
ALU = mybir.AluOpType
AF = mybir.ActivationFunctionType

D = 1024
KC = 8
L = 2048
LC = 256
T = 16 + LC + 16 + L + 16
C0 = 16
L0 = 16 + LC + 16
TILES = [(C0, LC)] + [(L0 + 512 * j, 512) for j in range(4)]
D_FF = 2816
FC = 22
DEPTH = 4
EPS = 1e-6
NB = 2
DBG_STOP = int(os.environ.get('MK_STOP', '0'))
DBG_HEADS = (int(os.environ['MK_HEADS']) if 'MK_HEADS' in os.environ else None)


class Res:
    __slots__ = ("w", "r")

    def __init__(self):
        self.w = None
        self.r = {}


class Sched:
    ENG = ("pe", "act", "dve", "pool", "sp")
    NDMA = 8
    EPOCH = 6000

    def __init__(self):
        self.streams = {e: [] for e in self.ENG}
        self.cnt = {e: 0 for e in self.ENG}
        self.waited = {e: {} for e in self.ENG}
        self.dma_n = {e: 0 for e in self.ENG}
        self.last = {}

    def op(self, eng, fn, reads=(), writes=(), dma=False):
        deps = {}

        def add(k, v):
            if deps.get(k, 0) < v:
                deps[k] = v

        for r in reads:
            if r.w is not None:
                add(*r.w)
        for r in writes:
            if r.w is not None:
                add(*r.w)
            for k, v in r.r.items():
                add(k, v)
        if dma:
            j = self.dma_n[eng]
            self.dma_n[eng] += 1
            key = ("dma", eng, j % self.NDMA)
            val = 16 * (j // self.NDMA + 1)
            if j >= self.NDMA:
                add(key, val - 16)
            inc = 16
        else:
            key = ("eng", eng, self.cnt[eng] // self.EPOCH)
            val = self.cnt[eng] % self.EPOCH + 1
            self.cnt[eng] += 1
            inc = 1
        self.last[key] = val
        wd = self.waited[eng]
        waits = []
        for k, v in deps.items():
            if eng == "pe" and k[0] == "eng" and k[1] == "pe":
                continue
            if wd.get(k, 0) < v:
                wd[k] = v
                waits.append((k, v))
        self.streams[eng].append((fn, waits, key, inc))
        for r in reads:
            if r.r.get(key, 0) < val:
                r.r[key] = val
        for r in writes:
            r.w = (key, val)
            r.r = {}

    def barrier(self, engs=None):
        for e in (engs or self.ENG):
            wd = self.waited[e]
            waits = []
            for k, v in self.last.items():
                if wd.get(k, 0) < v:
                    wd[k] = v
                    waits.append((k, v))
            if waits:
                self.streams[e].append((None, waits, None, 0))

    def emit(self, nc, stack):
        sems = {}
        for e in self.ENG:
            for (fn, waits, key, inc) in self.streams[e]:
                for k in [w[0] for w in waits] + ([key] if key is not None else []):
                    if k not in sems:
                        sems[k] = stack.enter_context(nc.semaphore("s_" + "_".join(str(x) for x in k)))
        block = stack.enter_context(nc.Block())

        def run(ename, eng):
            for (fn, waits, key, inc) in self.streams[ename]:
                for k, v in waits:
                    eng.wait_ge(sems[k], v)
                if fn is not None:
                    fn(eng).then_inc(sems[key], inc)

        @block.tensor
        def _(e):
            run("pe", e)

        @block.scalar
        def _(e):
            run("act", e)

        @block.vector
        def _(e):
            run("dve", e)

        @block.gpsimd
        def _(e):
            run("pool", e)

        @block.sync
        def _(e):
            run("sp", e)


def fm(v):
    v = np.asarray(v, np.float32)
    c = v.shape[-1] // 128
    lead = v.shape[:-1]
    return np.ascontiguousarray(np.moveaxis(v.reshape(lead + (c, 128)), -1, 0)).reshape(128, -1)


def pack_layer_vec(inp, i):
    kind, j = i % 3, i // 3
    parts = [("ada_b", fm(inp["ada_b"][i])),
             ("norm_mix", fm(inp["norm_mix"][i])),
             ("norm_ffn", fm(inp["norm_ffn"][i])),
             ("ffn_b_dw", fm(inp["ffn_b_dw"][i])),
             ("ffn_w_dw", fm(inp["ffn_w_dw"][i]))]
    if kind == 0:
        g = np.asarray(inp["mla_qk_gain"][j], np.float32)
        def col96(v):
            o = np.zeros((128, 1), np.float32)
            o[:96, 0] = v
            return o
        pe = np.arange(32)
        perm = np.where((pe % 16) < 8, pe + 8, pe - 8)
        gqp = np.zeros(96, np.float32); gqp[64:] = g[0, 64 + perm]
        gkp = np.zeros(96, np.float32); gkp[64:] = g[1, 64 + perm]
        invn = np.concatenate([np.full(64, 1 / 64.0), np.full(32, 1 / 32.0)]).astype(np.float32)
        parts += [("mla_qn", fm(inp["mla_q_norm"][j])), ("mla_kvn", fm(inp["mla_kv_norm"][j])),
                  ("mla_gq", col96(g[0])), ("mla_gqp", col96(gqp)), ("mla_gk", col96(g[1])), ("mla_gkp", col96(gkp)),
                  ("mla_invn", col96(invn))]
    if kind == 2:
        def col64(v):
            o = np.zeros((128, 1), np.float32)
            o[:64, 0] = v
            return o
        parts += [("hy_b_in", fm(inp["hy_b_in"][j])), ("hy_w_short", fm(inp["hy_w_short"][j])),
                  ("hy_b_short", fm(inp["hy_b_short"][j])), ("hy_b_out", fm(inp["hy_b_out"][j])),
                  ("hy_fb1", col64(inp["hy_f_b1"][j])), ("hy_fb2", col64(inp["hy_f_b2"][j])), ("hy_fr", col64(inp["hy_sin_freq"][j]))]
    if kind == 1:
        parts += [("cf_b_pw1", fm(inp["cf_b_pw1"][j])),
                  ("cf_w_dw", fm(inp["cf_w_dw"][j])),
                  ("cf_b_dw", fm(inp["cf_b_dw"][j])),
                  ("cf_ln_g", fm(inp["cf_ln_g"][j])),
                  ("cf_ln_b", fm(inp["cf_ln_b"][j])),
                  ("cf_b_pw2", fm(inp["cf_b_pw2"][j]))]
    off = {}
    o = 0
    for n, a in parts:
        off[n] = o
        o += a.shape[1]
    return np.concatenate([a for _, a in parts], axis=1), off


def vec_offsets(i):
    kind = i % 3
    sizes = [("ada_b", 48), ("norm_mix", 8), ("norm_ffn", 8), ("ffn_b_dw", 22), ("ffn_w_dw", 66)]
    if kind == 0:
        sizes += [("mla_qn", 2), ("mla_kvn", 1), ("mla_gq", 1), ("mla_gqp", 1), ("mla_gk", 1), ("mla_gkp", 1), ("mla_invn", 1)]
    if kind == 2:
        sizes += [("hy_b_in", 24), ("hy_w_short", 72), ("hy_b_short", 24), ("hy_b_out", 8), ("hy_fb1", 1), ("hy_fb2", 1), ("hy_fr", 1)]
    if kind == 1:
        sizes += [("cf_b_pw1", 16), ("cf_w_dw", 248), ("cf_b_dw", 8), ("cf_ln_g", 8), ("cf_ln_b", 8), ("cf_b_pw2", 8)]
    off = {}
    o = 0
    for n, s in sizes:
        off[n] = o
        o += s
    return off, o


NPV = 512
ARENA = 42000


class Prog:
    def __init__(self, layers, nb=NB):
        self.layers = layers
        self.nb = nb
        self.nc = bass.Bass("TRN2", target_bir_lowering=False)
        self.S = Sched()
        self.dram = {}

    def din(self, name, shape, dt=F32):
        t = self.nc.dram_tensor(name, list(shape), dt, kind="ExternalInput").ap()
        self.dram[name] = t
        return t

    def dout(self, name, shape, dt=F32):
        t = self.nc.dram_tensor(name, list(shape), dt, kind="ExternalOutput").ap()
        self.dram[name] = t
        return t

    def build(self):
        nc, S = self.nc, self.S
        nb = self.nb
        xT = self.din("xT", [nb, D, T])
        cT = self.din("cT", [nb, 128, KC * 2])
        ident_d = self.din("ident", [128, 128])
        bd_d = self.din("bd96", [128, 128])
        yT = self.dout("yT", [nb, D, L])
        cOut = self.dout("cOut", [nb, D, LC])
        W = {}
        for i in self.layers:
            kind, j = i % 3, i // 3
            W[i] = dict(
                ada_w=self.din(f"ada_w{i}", [D, 6 * D]),
                vec=self.din(f"vec{i}", [128, NPV]),
                ffn_up=self.din(f"ffn_up{i}", [D, 2 * D_FF]),
                ffn_down=self.din(f"ffn_down{i}", [D_FF, D]),
            )
            if kind == 0:
                if "rope_cs" not in self.dram:
                    self.din("rope_cs", [128, 2, T])
                W[i].update(dq=self.din(f"mla_dq{i}", [D, 256]), kv=self.din(f"mla_kv{i}", [D, 320]),
                            uq=self.din(f"mla_uq{i}", [256, 16 * 192]), ukv=self.din(f"mla_ukv{i}", [128, 2048]),
                            wo=self.din(f"mla_o{i}", [D, D]), rope=self.dram["rope_cs"])
            if kind == 2:
                W[i].update(hy_in=self.din(f"hy_in{i}", [D, 3 * D]), hy_out=self.din(f"hy_out{i}", [D, D]),
                            fw1=self.din(f"hy_fw1_{i}", [33, 64]), fw2=self.din(f"hy_fw2_{i}", [64, 64]),
                            fw3=self.din(f"hy_fw3_{i}", [64, 2 * D]), skipb=self.din(f"hy_skipb{i}", [128, D]))
                for tag, Lx in (("l", L), ("c", LC)):
                    for nm in ("dfc", "dfs", "idc", "ids"):
                        W[i][f"{nm}_{tag}"] = self.din(f"{nm}_{tag}", [Lx, Lx], BF16)
                    W[i][f"z_{tag}"] = self.din(f"z_{tag}", [33, Lx])
                    W[i][f"negt_{tag}"] = self.din(f"negt_{tag}", [128, Lx // 128])
                    W[i][f"kspec_{tag}"] = self.nc.dram_tensor(f"kspec_{tag}", [2, Lx, D], F32).ap()
                W[i]["absd"] = self.din("absd", [128, D])
                W[i]["x0d"] = self.nc.dram_tensor("x0d", [D, T], BF16).ap()
                self.rX0D = [Res() for _ in range(KC)]
                self.rKS = {"l": Res(), "c": Res()}
            if kind == 1:
                W[i].update(cf_w1=self.din(f"cf_w1_{i}", [D, 2 * D]), cf_w2=self.din(f"cf_w2_{i}", [D, D]))
        with contextlib.ExitStack() as st:
            self.st = st

            def sb(name, shape, dt):
                return st.enter_context(nc.sbuf_tensor(name, list(shape), dt))

            self.X = sb("X", [128, KC, T], F32)
            self.rX = [[Res() for _ in TILES] for _ in range(KC)]
            self.H = sb("H", [128, KC, T], BF16)
            self.rH = [[Res() for _ in TILES] for _ in range(KC)]
            self.SCR = sb("SCR", [128, ARENA], BF16)
            self.carve_ffn()
            self.VEC = sb("VEC", [128, NPV], F32)
            self.rVEC = Res()
            self.MOD = sb("MOD", [128, 48, 2], F32)
            self.rMOD = Res()
            self.AM = sb("AM", [128, 2, KC, 2], F32)
            self.rAM = Res()
            self.SV = sb("SV", [128, KC, 2], F32)
            self.SVB = sb("SVB", [128, KC, 2], BF16)
            self.rSV = Res()
            self.ONES = sb("ONES", [128, 128], BF16)
            self.rONES = Res()
            self.EPSB = sb("EPSB", [128, 1], F32)
            self.RSTD = sb("RSTD", [128, 512], F32)
            self.rRSTD = Res()
            self.MU = sb("MU", [128, 512], F32)
            self.rMU = Res()
            self.TMP = [sb(f"TMP{k}", [128, 512], F32) for k in range(2)]
            self.rTMP = [Res(), Res()]
            self.tmp_i = 0
            self.BD = sb("BD", [128, 128], BF16)
            self.PS = [st.enter_context(nc.psum_tensor(f"PS{k}", [128, 512], F32)) for k in range(8)]
            self.acc_i = 0
            self.rPS = [Res() for _ in range(8)]
            self.ps_i = 0
            self.wa_i = 0
            self.dq = 0

            S.op("pool", lambda e: e.memset(self.ONES[:, :], 1.0), writes=[self.rONES])
            S.op("pool", lambda e: e.memset(self.EPSB[:, :], EPS), writes=[self.rONES])
            self.zero_pads()
            self.IDENT = sb("IDENT", [128, 128], BF16)
            self.rID = Res()
            S.op("pool", lambda e: e.dma_start(out=self.IDENT[:, :], in_=ident_d[:, :]), writes=[self.rID], dma=True)
            S.op("pool", lambda e: e.dma_start(out=self.BD[:, :], in_=bd_d[:, :]), writes=[self.rID], dma=True)

            routs = []
            for b in range(nb):
                self.load_x(xT, cT, b)
                for i in self.layers:
                    self.layer(i, W[i], b, last=(i == DEPTH - 1))
                routs += self.store_x(yT, cOut, b)
            S.barrier(["sp"])
            S.emit(nc, st)
        return nc

    def psum(self):
        k = self.ps_i
        self.ps_i = (k + 1) % 6
        return self.PS[k], self.rPS[k]

    def psum_acc(self):
        k = 6 + self.acc_i
        self.acc_i = 1 - self.acc_i
        return self.PS[k], self.rPS[k]

    def arena_reset(self, mark=0):
        self.a_off = mark

    def alloc(self, n_el, dt=BF16, shape=None):
        nb16 = n_el * (2 if dt == F32 else 1)
        nb16 = (nb16 + 15) // 16 * 16
        a = self.a_off
        assert a + nb16 <= ARENA, (a, nb16)
        self.a_off = a + nb16
        ap = self.SCR[:, a:a + n_el * (2 if dt == F32 else 1)]
        if dt == F32:
            ap = ap.bitcast(F32)
        return ap

    def carve_ffn(self):
        self.arena_reset()
        self.G = self.alloc(6 * T).rearrange("p (c t) -> p c t", t=T)
        self.rG = [Res() for _ in range(6)]
        self.AROW = self.alloc(T)
        self.CROW = self.alloc(T, F32)
        self.VROW = self.alloc(T)
        self.rA, self.rC, self.rV = Res(), Res(), Res()
        self.WA = [self.alloc(KC * 512).rearrange("p (c n) -> p c n", n=512) for _ in range(2)]
        self.rWA = [[Res(), Res()], [Res(), Res()]]
        self.WB = self.alloc(6 * 1024)
        self.rWB = Res()
        self.SQ = self.alloc(KC * 512).rearrange("p (c n) -> p c n", n=512)
        self.rSQ = Res()

    def zero_pads(self):
        S = self.S
        S.op("pool", lambda e: e.memset(self.AROW[:, :], 0.0), writes=[self.rA])
        S.op("pool", lambda e: e.memset(self.VROW[:, :], 0.0), writes=[self.rV])
        for (a, bnd) in ((0, 16), (C0 + LC, L0), (L0 + L, T)):
            S.op("pool", lambda e, a=a, bnd=bnd: e.memset(self.G[:, :, a:bnd], 0.0), writes=self.rG)

    def tmp(self):
        k = self.tmp_i
        self.tmp_i = 1 - k
        return self.TMP[k], self.rTMP[k]

    def dma_eng(self):
        self.dq += 1
        return "sp"

    def load_w(self, dst_ap, src_ap, res):
        self.S.op("pool", lambda e: e.dma_start(out=dst_ap, in_=src_ap), writes=(res if isinstance(res, list) else [res]), dma=True)

    def load_x(self, xT, cT, b):
        S = self.S
        for c in range(KC):
            S.op("sp", lambda e, c=c: e.dma_start(out=self.X[:, c, :], in_=xT[b, c * 128:(c + 1) * 128, :]),
                 writes=self.rX[c], dma=True)
        S.op("sp", lambda e: e.dma_start(out=self.SV[:, :, :], in_=cT[b].rearrange("p (c t) -> p c t", t=2)),
             writes=[self.rSV], dma=True)
        S.op("act", lambda e: e.activation(out=self.SVB[:, :, :], in_=self.SV[:, :, :], func=AF.Silu),
             reads=[], writes=[self.rSV])

    def store_x(self, yT, cOut, b):
        S = self.S
        rs = []
        for c in range(KC):
            r = Res()
            S.op("sp", lambda e, c=c: e.dma_start(out=yT[b, c * 128:(c + 1) * 128, :], in_=self.X[:, c, L0:L0 + L]),
                 reads=self.rX[c], writes=[r], dma=True)
            r2 = Res()
            S.op("sp", lambda e, c=c: e.dma_start(out=cOut[b, c * 128:(c + 1) * 128, :], in_=self.X[:, c, C0:C0 + LC]),
                 reads=self.rX[c], writes=[r2], dma=True)
            rs += [r, r2]
        return rs

    def adaln(self, i, Wd):
        S = self.S
        off, _ = vec_offsets(i)
        S.op("sp", lambda e: e.dma_start(out=self.VEC[:, :], in_=Wd["vec"][:, :]), writes=[self.rVEC], dma=True)
        ada_w = Wd["ada_w"].rearrange("(c p) n -> p c n", p=128)
        for blk in range(12):
            k = self.wa_i
            self.wa_i = 1 - k
            wa, rwa = self.WA[k], self.rWA[k]
            self.S.op("pool", lambda e, wa=wa, blk=blk: e.dma_start(out=wa[:, :, :], in_=ada_w[:, :, blk * 512:(blk + 1) * 512]), writes=rwa, dma=True)
            ps, rps = self.psum()

            def mm(e, wa=wa, ps=ps):
                ins = None
                for f in range(4):
                    for kc in range(KC):
                        ins = e.matmul(ps[:, f * 2:f * 2 + 2], lhsT=wa[:, kc, f * 128:(f + 1) * 128],
                                       rhs=self.SVB[:, kc, :], start=(kc == 0), stop=(kc == KC - 1))
                return ins
            S.op("pe", mm, reads=rwa + [self.rSV], writes=[rps])
            ab = off["ada_b"] + blk * 4
            S.op("dve", lambda e, ps=ps, blk=blk, ab=ab: e.tensor_tensor(
                out=self.MOD[:, blk * 4:blk * 4 + 4, :], in0=ps[:, 0:8].rearrange("p (f t) -> p f t", t=2),
                in1=self.VEC[:, ab:ab + 4].unsqueeze(2).to_broadcast([128, 4, 2]), op=ALU.add),
                reads=[self.rVEC], writes=[rps, self.rMOD])
        for m, (nname, g) in enumerate((("norm_mix", 1), ("norm_ffn", 4))):
            no = off[nname]
            S.op("dve", lambda e, m=m, g=g: e.tensor_scalar_add(out=self.AM[:, m, :, :], in0=self.MOD[:, g * 8:(g + 1) * 8, :], scalar1=1.0),
                 reads=[self.rMOD], writes=[self.rAM])
            S.op("dve", lambda e, m=m, no=no: e.tensor_tensor(
                out=self.AM[:, m, :, :], in0=self.AM[:, m, :, :],
                in1=self.VEC[:, no:no + 8].unsqueeze(2).to_broadcast([128, 8, 2]), op=ALU.mult),
                reads=[self.rVEC, self.rAM], writes=[self.rAM])

    def modulate(self, m, tiles):
        S = self.S
        shg = 0 if m == 0 else 3
        for ti in tiles:
            c0, w = TILES[ti]
            col = 1 if ti == 0 else 0
            for c in range(KC):
                S.op("act", lambda e, c=c, c0=c0, w=w: e.activation(out=self.SQ[:, c, 0:w], in_=self.X[:, c, c0:c0 + w], func=AF.Square),
                     reads=[self.rX[c][ti]], writes=[self.rSQ])
            ps, rps = self.psum()

            def mm(e, ps=ps, w=w):
                ins = None
                for c in range(KC):
                    ins = e.matmul(ps[:, 0:w], lhsT=self.ONES[:, :], rhs=self.SQ[:, c, 0:w], start=(c == 0), stop=(c == KC - 1))
                return ins
            S.op("pe", mm, reads=[self.rSQ, self.rONES], writes=[rps])
            S.op("act", lambda e, ps=ps, w=w: e.activation(out=self.RSTD[:, 0:w], in_=ps[:, 0:w], func=AF.Sqrt, scale=1.0 / D, bias=self.EPSB[:, 0:1]),
                 reads=[self.rONES], writes=[rps, self.rRSTD])
            S.op("dve", lambda e, w=w: e.reciprocal(out=self.RSTD[:, 0:w], in_=self.RSTD[:, 0:w]), reads=[], writes=[self.rRSTD])
            for c in range(KC):
                tp, rtp = self.tmp()
                S.op("dve", lambda e, tp=tp, c=c, c0=c0, w=w: e.tensor_tensor(out=tp[:, 0:w], in0=self.X[:, c, c0:c0 + w], in1=self.RSTD[:, 0:w], op=ALU.mult),
                     reads=[self.rX[c][ti], self.rRSTD], writes=[rtp])
                S.op("act", lambda e, tp=tp, c=c, c0=c0, w=w, col=col: e.activation(
                    out=self.H[:, c, c0:c0 + w], in_=tp[:, 0:w], func=AF.Identity,
                    scale=self.AM[:, m, c, col:col + 1], bias=self.MOD[:, shg * 8 + c, col:col + 1]),
                    reads=[rtp, self.rAM, self.rMOD], writes=[self.rH[c][ti]])

    def ffn(self, i, Wd, tiles):
        S = self.S
        off, _ = vec_offsets(i)
        up = Wd["ffn_up"].rearrange("(c p) n -> p c n", p=128)
        down = Wd["ffn_down"]
        bo, wo = off["ffn_b_dw"], off["ffn_w_dw"]
        lo = TILES[tiles[0]][0]
        hi = TILES[tiles[-1]][0] + TILES[tiles[-1]][1]
        for (g0, gn) in ((0, 6), (6, 6), (12, 5), (17, 5)):
            wbv = self.WB[:, 0:gn * 1024].rearrange("p (c n) -> p c n", n=1024)
            self.load_w(wbv, down[g0 * 128:(g0 + gn) * 128, :].rearrange("(c p) n -> p c n", p=128), self.rWB)
            for hc in range(g0, g0 + gn):
                k = self.wa_i
                self.wa_i = 1 - k
                wa, rwa = self.WA[k], self.rWA[k]
                self.load_w(wa[:, :, 0:128], up[:, :, hc * 128:(hc + 1) * 128], rwa[0])
                self.load_w(wa[:, :, 128:256], up[:, :, D_FF + hc * 128:D_FF + (hc + 1) * 128], rwa[1])
                for ti in tiles:
                    c0, w = TILES[ti]
                    for half, (dst, rdst) in enumerate(((self.AROW, self.rA), (self.VROW, self.rV))):
                        ps, rps = self.psum()

                        def mm(e, ps=ps, wa=wa, half=half, c0=c0, w=w):
                            ins = None
                            for kc in range(KC):
                                ins = e.matmul(ps[:, 0:w], lhsT=wa[:, kc, half * 128:(half + 1) * 128],
                                               rhs=self.H[:, kc, c0:c0 + w], start=(kc == 0), stop=(kc == KC - 1))
                            return ins
                        S.op("pe", mm, reads=[rwa[half]] + [self.rH[kc][ti] for kc in range(KC)], writes=[rps])
                        S.op("act", lambda e, ps=ps, dst=dst, c0=c0, w=w: e.copy(out=dst[:, c0:c0 + w], in_=ps[:, 0:w]),
                             reads=[], writes=[rps, rdst])
                S.op("dve", lambda e, hc=hc: e.tensor_scalar(
                    out=self.CROW[:, lo:hi], in0=self.AROW[:, lo:hi], scalar1=self.VEC[:, wo + 22 + hc:wo + 22 + hc + 1],
                    scalar2=self.VEC[:, bo + hc:bo + hc + 1], op0=ALU.mult, op1=ALU.add),
                    reads=[self.rA, self.rVEC], writes=[self.rC])
                S.op("dve", lambda e, hc=hc: e.scalar_tensor_tensor(
                    out=self.CROW[:, lo:hi], in0=self.AROW[:, lo - 1:hi - 1], scalar=self.VEC[:, wo + hc:wo + hc + 1],
                    in1=self.CROW[:, lo:hi], op0=ALU.mult, op1=ALU.add),
                    reads=[self.rA, self.rVEC], writes=[self.rC])
                S.op("dve", lambda e, hc=hc: e.scalar_tensor_tensor(
                    out=self.CROW[:, lo:hi], in0=self.AROW[:, lo + 1:hi + 1], scalar=self.VEC[:, wo + 44 + hc:wo + 44 + hc + 1],
                    in1=self.CROW[:, lo:hi], op0=ALU.mult, op1=ALU.add),
                    reads=[self.rA, self.rVEC], writes=[self.rC])
                S.op("act", lambda e: e.activation(out=self.CROW[:, lo:hi], in_=self.CROW[:, lo:hi], func=AF.Silu),
                     reads=[], writes=[self.rC])
                gi = hc - g0
                S.op("dve", lambda e, gi=gi: e.tensor_tensor(out=self.G[:, gi, lo:hi], in0=self.CROW[:, lo:hi], in1=self.VROW[:, lo:hi], op=ALU.mult),
                     reads=[self.rC, self.rV], writes=[self.rG[gi]])
            for dc in range(KC):
                for ti in tiles:
                    c0, w = TILES[ti]
                    col = 1 if ti == 0 else 0
                    ps, rps = self.psum()

                    def mm(e, ps=ps, wbv=wbv, dc=dc, c0=c0, w=w, gn=gn):
                        ins = None
                        for gi in range(gn):
                            ins = e.matmul(ps[:, 0:w], lhsT=wbv[:, gi, dc * 128:(dc + 1) * 128], rhs=self.G[:, gi, c0:c0 + w],
                                           start=(gi == 0), stop=(gi == gn - 1))
                        return ins
                    S.op("pe", mm, reads=[self.rWB] + [self.rG[gi] for gi in range(gn)], writes=[rps])
                    S.op("dve", lambda e, ps=ps, dc=dc, c0=c0, w=w, col=col: e.scalar_tensor_tensor(
                        out=self.X[:, dc, c0:c0 + w], in0=ps[:, 0:w], scalar=self.MOD[:, 40 + dc, col:col + 1],
                        in1=self.X[:, dc, c0:c0 + w], op0=ALU.mult, op1=ALU.add),
                        reads=[self.rMOD], writes=[rps, self.rX[dc][ti]])

    def mla(self, i, Wd, need_ctx):
        S = self.S
        off, _ = vec_offsets(i)
        V = lambda name, k=0: self.VEC[:, off[name] + k:off[name] + k + 1]
        V96 = lambda name: self.VEC[0:96, off[name]:off[name] + 1]
        all_tiles = [0, 1, 2, 3, 4]
        qtiles = all_tiles if need_ctx else [1, 2, 3, 4]
        att_scale = float(96 ** -0.5)
        self.arena_reset()
        CQ = self.alloc(2 * T).rearrange("p (c t) -> p c t", t=T)
        CKV = self.alloc(T)
        KPE = self.alloc(T)
        ROPE = self.alloc(2 * T, F32).rearrange("p (c t) -> p c t", t=T)
        rCQ = [Res() for _ in TILES]
        rCKV = [Res() for _ in TILES]
        rKPE = [Res() for _ in TILES]
        rROPE = Res()
        SQh = self.alloc(512)
        rSQh = Res()
        mark = self.a_off
        S.op("sp", lambda e: e.dma_start(out=ROPE[:, :, :], in_=Wd["rope"][:, :, :]), writes=[rROPE], dma=True)
        WDQ = self.alloc(KC * 256).rearrange("p (c n) -> p c n", n=256)
        WKV = self.alloc(KC * 320).rearrange("p (c n) -> p c n", n=320)
        CQF = self.alloc(2 * 512, F32).rearrange("p (c n) -> p c n", n=512)
        SQb = self.alloc(2 * 512).rearrange("p (c n) -> p c n", n=512)
        rWDQ, rWKV, rCQF, rSQb = Res(), Res(), Res(), Res()
        self.load_w(WDQ, Wd["dq"].rearrange("(c p) n -> p c n", p=128), rWDQ)
        self.load_w(WKV, Wd["kv"].rearrange("(c p) n -> p c n", p=128), rWKV)

        def rope_norm(ps_raw, rps_raw, ps_perm, rps_perm, g, gp, out_ap, rout, lo, hi, c0, w):
            S.op("act", lambda e: e.activation(out=SQh[0:96, 0:w], in_=ps_raw[0:96, 0:w], func=AF.Square),
                 reads=[], writes=[rps_raw, rSQh])
            pss, rpss = self.psum()
            S.op("pe", lambda e: e.matmul(pss[0:96, 0:w], lhsT=self.BD[0:96, 0:96], rhs=SQh[0:96, 0:w], start=True, stop=True),
                 reads=[rSQh, self.rID], writes=[rpss])
            S.op("act", lambda e: e.activation(out=self.RSTD[0:96, 0:w], in_=pss[0:96, 0:w], func=AF.Sqrt,
                                               scale=V96("mla_invn"), bias=self.EPSB[0:96, 0:1]),
                 reads=[self.rVEC, self.rONES], writes=[rpss, self.rRSTD])
            S.op("dve", lambda e: e.reciprocal(out=self.RSTD[0:96, 0:w], in_=self.RSTD[0:96, 0:w]), reads=[], writes=[self.rRSTD])
            t1, rt1 = self.TMP[0], self.rTMP[0]
            t2, rt2 = self.TMP[1], self.rTMP[1]
            S.op("dve", lambda e: e.scalar_tensor_tensor(out=t1[0:96, 0:w], in0=ps_raw[0:96, 0:w], scalar=g, in1=ROPE[0:96, 0, c0:c0 + w],
                                                         op0=ALU.mult, op1=ALU.mult),
                 reads=[self.rVEC, rROPE], writes=[rps_raw, rt1])
            S.op("dve", lambda e: e.scalar_tensor_tensor(out=t2[0:96, 0:w], in0=ps_perm[0:96, 0:w], scalar=gp, in1=ROPE[0:96, 1, c0:c0 + w],
                                                         op0=ALU.mult, op1=ALU.mult),
                 reads=[self.rVEC, rROPE], writes=[rps_perm, rt2])
            S.op("pool", lambda e: e.tensor_tensor(out=t1[0:96, 0:w], in0=t1[0:96, 0:w], in1=t2[0:96, 0:w], op=ALU.add),
                 reads=[rt2], writes=[rt1])
            S.op("dve", lambda e: e.tensor_tensor(out=out_ap, in0=t1[lo:hi, 0:w], in1=self.RSTD[lo:hi, 0:w], op=ALU.mult),
                 reads=[rt1, self.rRSTD], writes=[rout])

        def p1_tile(ti):
            c0, w = TILES[ti]
            rh = [self.rH[kc][ti] for kc in range(KC)]
            for oc in range(2):
                ps, rps = self.psum()

                def mm(e, ps=ps, oc=oc):
                    ins = None
                    for kc in range(KC):
                        ins = e.matmul(ps[:, 0:w], lhsT=WDQ[:, kc, oc * 128:(oc + 1) * 128], rhs=self.H[:, kc, c0:c0 + w],
                                       start=(kc == 0), stop=(kc == KC - 1))
                    return ins
                S.op("pe", mm, reads=[rWDQ] + rh, writes=[rps])
                S.op("act", lambda e, ps=ps, oc=oc: e.copy(out=CQF[:, oc, 0:w], in_=ps[:, 0:w]), reads=[], writes=[rps, rCQF])
            S.op("act", lambda e: e.activation(out=SQb[:, :, 0:w], in_=CQF[:, :, 0:w], func=AF.Square), reads=[rCQF], writes=[rSQb])
            ps, rps = self.psum()

            def mm(e, ps=ps):
                e.matmul(ps[:, 0:w], lhsT=self.ONES[:, :], rhs=SQb[:, 0, 0:w], start=True, stop=False)
                return e.matmul(ps[:, 0:w], lhsT=self.ONES[:, :], rhs=SQb[:, 1, 0:w], start=False, stop=True)
            S.op("pe", mm, reads=[rSQb, self.rONES], writes=[rps])
            S.op("act", lambda e, ps=ps: e.activation(out=self.RSTD[:, 0:w], in_=ps[:, 0:w], func=AF.Sqrt, scale=1.0 / 256, bias=self.EPSB[:, 0:1]),
                 reads=[self.rONES], writes=[rps, self.rRSTD])
            S.op("dve", lambda e: e.reciprocal(out=self.RSTD[:, 0:w], in_=self.RSTD[:, 0:w]), reads=[], writes=[self.rRSTD])
            for oc in range(2):
                S.op("dve", lambda e, oc=oc: e.scalar_tensor_tensor(out=CQ[:, oc, c0:c0 + w], in0=CQF[:, oc, 0:w], scalar=V("mla_qn", oc),
                                                                 in1=self.RSTD[:, 0:w], op0=ALU.mult, op1=ALU.mult),
                     reads=[rCQF, self.rVEC, self.rRSTD], writes=[rCQ[ti]])
            ps, rps = self.psum()

            def mm(e, ps=ps):
                ins = None
                for kc in range(KC):
                    ins = e.matmul(ps[:, 0:w], lhsT=WKV[:, kc, 0:128], rhs=self.H[:, kc, c0:c0 + w], start=(kc == 0), stop=(kc == KC - 1))
                return ins
            S.op("pe", mm, reads=[rWKV] + rh, writes=[rps])
            S.op("act", lambda e, ps=ps: e.copy(out=CQF[:, 0, 0:w], in_=ps[:, 0:w]), reads=[], writes=[rps, rCQF])
            S.op("act", lambda e: e.activation(out=SQb[:, 0, 0:w], in_=CQF[:, 0, 0:w], func=AF.Square), reads=[rCQF], writes=[rSQb])
            ps, rps = self.psum()
            S.op("pe", lambda e, ps=ps: e.matmul(ps[:, 0:w], lhsT=self.ONES[:, :], rhs=SQb[:, 0, 0:w], start=True, stop=True),
                 reads=[rSQb, self.rONES], writes=[rps])
            S.op("act", lambda e, ps=ps: e.activation(out=self.RSTD[:, 0:w], in_=ps[:, 0:w], func=AF.Sqrt, scale=1.0 / 128, bias=self.EPSB[:, 0:1]),
                 reads=[self.rONES], writes=[rps, self.rRSTD])
            S.op("dve", lambda e: e.reciprocal(out=self.RSTD[:, 0:w], in_=self.RSTD[:, 0:w]), reads=[], writes=[self.rRSTD])
            S.op("dve", lambda e: e.scalar_tensor_tensor(out=CKV[:, c0:c0 + w], in0=CQF[:, 0, 0:w], scalar=V("mla_kvn"),
                                                         in1=self.RSTD[:, 0:w], op0=ALU.mult, op1=ALU.mult),
                 reads=[rCQF, self.rVEC, self.rRSTD], writes=[rCKV[ti]])
            pss = []
            for var in range(2):
                ps, rps = self.psum()
                pss.append((ps, rps))

                def mm(e, ps=ps, var=var):
                    ins = None
                    for kc in range(KC):
                        ins = e.matmul(ps[0:96, 0:w], lhsT=WKV[:, kc, 128 + var * 96:224 + var * 96], rhs=self.H[:, kc, c0:c0 + w],
                                       start=(kc == 0), stop=(kc == KC - 1))
                    return ins
                S.op("pe", mm, reads=[rWKV] + rh, writes=[rps])
            rope_norm(pss[0][0], pss[0][1], pss[1][0], pss[1][1], V96("mla_gk"), V96("mla_gkp"),
                      KPE[64:96, c0:c0 + w], rKPE[ti], 64, 96, c0, w)

        for ti in all_tiles:
            p1_tile(ti)
        if DBG_STOP == 1:
            S.barrier()
            return

        S.barrier()
        self.arena_reset(mark)
        WUQ = self.alloc(2 * 3072).rearrange("p (c n) -> p c n", n=3072)
        WUKV = self.alloc(2048)
        rWUQ, rWUKV = Res(), Res()
        self.load_w(WUQ, Wd["uq"].rearrange("(c p) n -> p c n", p=128), rWUQ)
        self.load_w(WUKV, Wd["ukv"][:, :], rWUKV)
        QH = self.alloc(T)
        KH = self.alloc(T)
        rQH = [Res() for _ in TILES]
        rKH = [Res() for _ in TILES]
        rKHpe = Res()
        VH = [self.alloc(18 * 128).rearrange("p (j n) -> p j n", n=128) for _ in range(2)]
        rVH = [Res(), Res()]
        PT = [self.alloc(512) for _ in range(4)]
        rPT = [Res() for _ in range(4)]
        RD = self.MU
        rRD = self.rMU
        pt_i = 0
        S.op("pool", lambda e: e.memset(VH[0][:, :, 64:128], 1.0), writes=[rVH[0]])
        S.op("pool", lambda e: e.memset(VH[1][:, :, 0:64], 1.0), writes=[rVH[1]])

        def kcols(j):
            return (C0 + 128 * j) if j < 2 else (L0 + 128 * (j - 2))

        def ktile(j):
            return 0 if j < 2 else 1 + (j - 2) // 4

        def head(h):
            nonlocal pt_i
            par = h % 2
            voff = 0 if par == 0 else 64
            for ti in qtiles:
                c0, w = TILES[ti]
                pss = []
                for var in range(2):
                    ps, rps = self.psum()
                    pss.append((ps, rps))

                    def mm(e, ps=ps, var=var, c0=c0, w=w):
                        e.matmul(ps[0:96, 0:w], lhsT=WUQ[:, 0, h * 192 + var * 96:h * 192 + var * 96 + 96], rhs=CQ[:, 0, c0:c0 + w], start=True, stop=False)
                        return e.matmul(ps[0:96, 0:w], lhsT=WUQ[:, 1, h * 192 + var * 96:h * 192 + var * 96 + 96], rhs=CQ[:, 1, c0:c0 + w], start=False, stop=True)
                    S.op("pe", mm, reads=[rWUQ, rCQ[ti]], writes=[rps])
                rope_norm(pss[0][0], pss[0][1], pss[1][0], pss[1][1], V96("mla_gq"), V96("mla_gqp"),
                          QH[0:96, c0:c0 + w], rQH[ti], 0, 96, c0, w)
            S.op("pool", lambda e: e.tensor_copy(out=KH[64:96, :], in_=KPE[64:96, :]), reads=rKPE, writes=[rKHpe])
            for ti in all_tiles:
                c0, w = TILES[ti]
                ps, rps = self.psum()
                S.op("pe", lambda e, ps=ps, c0=c0, w=w: e.matmul(ps[0:64, 0:w], lhsT=WUKV[:, h * 128:h * 128 + 64], rhs=CKV[:, c0:c0 + w], start=True, stop=True),
                     reads=[rWUKV, rCKV[ti]], writes=[rps])
                S.op("act", lambda e, ps=ps, w=w: e.activation(out=SQh[0:64, 0:w], in_=ps[0:64, 0:w], func=AF.Square), reads=[], writes=[rps, rSQh])
                ps2, rps2 = self.psum()
                S.op("pe", lambda e, ps2=ps2, w=w: e.matmul(ps2[0:64, 0:w], lhsT=self.BD[0:64, 0:64], rhs=SQh[0:64, 0:w], start=True, stop=True),
                     reads=[rSQh, self.rID], writes=[rps2])
                S.op("act", lambda e, ps2=ps2, w=w: e.activation(out=self.RSTD[0:64, 0:w], in_=ps2[0:64, 0:w], func=AF.Sqrt, scale=1.0 / 64, bias=self.EPSB[0:64, 0:1]),
                     reads=[self.rONES], writes=[rps2, self.rRSTD])
                S.op("dve", lambda e, w=w: e.reciprocal(out=self.RSTD[0:64, 0:w], in_=self.RSTD[0:64, 0:w]), reads=[], writes=[self.rRSTD])
                S.op("dve", lambda e, ps=ps, c0=c0, w=w: e.scalar_tensor_tensor(out=KH[0:64, c0:c0 + w], in0=ps[0:64, 0:w], scalar=self.VEC[0:64, off["mla_gk"]:off["mla_gk"] + 1],
                                                                             in1=self.RSTD[0:64, 0:w], op0=ALU.mult, op1=ALU.mult),
                     reads=[self.rVEC, self.rRSTD], writes=[rps, rKH[ti]])
            for j0 in range(0, 18, 8):
                n = min(8, 18 - j0)
                ps, rps = self.psum()

                def mm(e, ps=ps, j0=j0, n=n):
                    ins = None
                    for jj in range(n):
                        kc0 = kcols(j0 + jj)
                        ins = e.matmul(ps[:, jj * 64:(jj + 1) * 64], lhsT=CKV[:, kc0:kc0 + 128], rhs=WUKV[:, h * 128 + 64:(h + 1) * 128], start=True, stop=True)
                    return ins
                S.op("pe", mm, reads=[rWUKV] + rCKV, writes=[rps])
                S.op("act", lambda e, ps=ps, j0=j0, n=n: e.copy(out=VH[par][:, j0:j0 + n, voff:voff + 64], in_=ps[:, 0:n * 64].rearrange("p (j d) -> p j d", d=64)),
                     reads=[], writes=[rps, rVH[par]])
            for ti in qtiles:
                c0, w = TILES[ti]
                keys = list(range(18)) if ti > 0 else [0, 1]
                pso, rpso = self.psum_acc()
                for j in keys:
                    kc0 = kcols(j)
                    ps, rps = self.psum()
                    S.op("pe", lambda e, ps=ps, kc0=kc0, c0=c0, w=w: e.matmul(ps[:, 0:w], lhsT=KH[0:96, kc0:kc0 + 128], rhs=QH[0:96, c0:c0 + w], start=True, stop=True),
                         reads=[rKH[ktile(j)], rKHpe, rQH[ti]], writes=[rps])
                    pt, rpt = PT[pt_i], rPT[pt_i]
                    pt_i = (pt_i + 1) % 4
                    S.op("act", lambda e, ps=ps, pt=pt, w=w: e.activation(out=pt[:, 0:w], in_=ps[:, 0:w], func=AF.Exp, scale=att_scale),
                         reads=[], writes=[rps, rpt])
                    S.op("pe", lambda e, pt=pt, j=j, w=w, pso=pso, first=(j == keys[0]), lastk=(j == keys[-1]): e.matmul(
                        pso[:, 0:w], lhsT=VH[par][:, j, :], rhs=pt[:, 0:w], start=first, stop=lastk),
                        reads=[rpt, rVH[par]], writes=[rpso])
                ch = h // 2
                if par == 0:
                    S.op("dve", lambda e, pso=pso, w=w: e.reciprocal(out=RD[64:128, 0:w], in_=pso[64:128, 0:w]), reads=[], writes=[rpso, rRD])
                    S.op("dve", lambda e, pso=pso, ch=ch, c0=c0, w=w: e.tensor_tensor(out=self.H[0:64, ch, c0:c0 + w], in0=pso[0:64, 0:w], in1=RD[64:128, 0:w], op=ALU.mult),
                         reads=[rRD], writes=[rpso, self.rH[ch][ti]])
                else:
                    S.op("dve", lambda e, pso=pso, w=w: e.reciprocal(out=RD[0:64, 0:w], in_=pso[0:64, 0:w]), reads=[], writes=[rpso, rRD])
                    S.op("dve", lambda e, pso=pso, ch=ch, c0=c0, w=w: e.tensor_tensor(out=self.H[64:128, ch, c0:c0 + w], in0=pso[64:128, 0:w], in1=RD[0:64, 0:w], op=ALU.mult),
                         reads=[rRD], writes=[rpso, self.rH[ch][ti]])

        for h in range(16 if DBG_HEADS is None else DBG_HEADS):
            head(h)
        if DBG_STOP == 2:
            S.barrier()
            return

        S.barrier()
        self.arena_reset()
        WO = self.alloc(KC * 1024).rearrange("p (c n) -> p c n", n=1024)
        rWO = Res()
        wo_v = Wd["wo"].rearrange("(c p) n -> p c n", p=128)
        rWOh = [Res(), Res()]
        for hh in range(2):
            self.load_w(WO[:, hh * 4:(hh + 1) * 4, :], wo_v[:, hh * 4:(hh + 1) * 4, :], rWOh[hh])
        for ti in qtiles:
            c0, w = TILES[ti]
            col = 1 if ti == 0 else 0
            for dc in range(KC):
                ps, rps = self.psum()

                def mm(e, ps=ps, dc=dc, c0=c0, w=w):
                    ins = None
                    for c in range(KC):
                        ins = e.matmul(ps[:, 0:w], lhsT=WO[:, c, dc * 128:(dc + 1) * 128], rhs=self.H[:, c, c0:c0 + w], start=(c == 0), stop=(c == KC - 1))
                    return ins
                S.op("pe", mm, reads=rWOh + [self.rH[c][ti] for c in range(KC)], writes=[rps])
                S.op("dve", lambda e, ps=ps, dc=dc, c0=c0, w=w, col=col: e.scalar_tensor_tensor(
                    out=self.X[:, dc, c0:c0 + w], in0=ps[:, 0:w], scalar=self.MOD[:, 16 + dc, col:col + 1],
                    in1=self.X[:, dc, c0:c0 + w], op0=ALU.mult, op1=ALU.add),
                    reads=[self.rMOD], writes=[rps, self.rX[dc][ti]])

    def hyena(self, i, Wd, b):
        S = self.S
        off, _ = vec_offsets(i)
        V = lambda name, k=0: self.VEC[:, off[name] + k:off[name] + k + 1]
        V64 = lambda name: self.VEC[0:64, off[name]:off[name] + 1]
        tiles = [0, 1, 2, 3, 4]
        lo, hi = C0, L0 + L
        PI = float(np.pi)
        seqs = [("l", L, 16, L0, 0), ("c", LC, 2, C0, 16)]
        self.arena_reset()
        VX = self.alloc(KC * T).rearrange("p (c t) -> p c t", t=T)
        rVX = [Res() for _ in range(KC)]
        AROW = self.alloc(T)
        CROW = self.alloc(T, F32)
        X1ROW = self.alloc(T, F32)
        X0ROW = self.alloc(T)
        rA, rC, rX1, rX0R = Res(), Res(), Res(), Res()
        WA = [self.alloc(KC * 128).rearrange("p (c n) -> p c n", n=128) for _ in range(2)]
        rWA = [Res(), Res()]
        wa_i = [0]
        S.op("pool", lambda e: e.memset(AROW[:, :], 0.0), writes=[rA])
        S.op("pool", lambda e: e.memset(X0ROW[:, :], 0.0), writes=[rX0R])
        w_in = Wd["hy_in"].rearrange("(c p) n -> p c n", p=128)
        bi, wsh, bsh = off["hy_b_in"], off["hy_w_short"], off["hy_b_short"]

        def proj_conv(oc, dst, rdst):
            k = wa_i[0]
            wa_i[0] = 1 - k
            wa, rwa = WA[k], rWA[k]
            self.load_w(wa, w_in[:, :, oc * 128:(oc + 1) * 128], rwa)
            for ti in tiles:
                c0, w = TILES[ti]
                ps, rps = self.psum()

                def mm(e, ps=ps, c0=c0, w=w):
                    ins = None
                    for kc in range(KC):
                        ins = e.matmul(ps[:, 0:w], lhsT=wa[:, kc, :], rhs=self.H[:, kc, c0:c0 + w], start=(kc == 0), stop=(kc == KC - 1))
                    return ins
                S.op("pe", mm, reads=[rwa] + [self.rH[kc][ti] for kc in range(KC)], writes=[rps])
                S.op("act", lambda e, ps=ps, c0=c0, w=w: e.activation(out=AROW[:, c0:c0 + w], in_=ps[:, 0:w], func=AF.Identity,
                                                                   bias=self.VEC[:, bi + oc:bi + oc + 1], scale=1.0),
                     reads=[self.rVEC], writes=[rps, rA])
            S.op("dve", lambda e: e.tensor_scalar(out=CROW[:, lo:hi], in0=AROW[:, lo:hi], scalar1=self.VEC[:, wsh + 24 + oc:wsh + 25 + oc],
                                                  scalar2=self.VEC[:, bsh + oc:bsh + oc + 1], op0=ALU.mult, op1=ALU.add),
                 reads=[rA, self.rVEC], writes=[rC])
            S.op("dve", lambda e: e.scalar_tensor_tensor(out=CROW[:, lo:hi], in0=AROW[:, lo - 1:hi - 1], scalar=self.VEC[:, wsh + oc:wsh + oc + 1],
                                                         in1=CROW[:, lo:hi], op0=ALU.mult, op1=ALU.add),
                 reads=[rA, self.rVEC], writes=[rC])
            S.op("dve", lambda e: e.scalar_tensor_tensor(out=dst[:, lo:hi], in0=AROW[:, lo + 1:hi + 1], scalar=self.VEC[:, wsh + 48 + oc:wsh + 49 + oc],
                                                         in1=CROW[:, lo:hi], op0=ALU.mult, op1=ALU.add),
                 reads=[rA, self.rVEC, rC], writes=[rdst])

        x0d = Wd["x0d"]
        for c in range(KC):
            proj_conv(c, X0ROW, rX0R)
            S.op("sp", lambda e, c=c: e.dma_start(out=x0d[c * 128:(c + 1) * 128, :], in_=X0ROW[:, :]),
                 reads=[rX0R], writes=[self.rX0D[c]], dma=True)
            proj_conv(KC + c, X1ROW, rX1)
            proj_conv(2 * KC + c, CROW, rC)
            S.op("dve", lambda e, c=c: e.tensor_tensor(out=VX[:, c, lo:hi], in0=CROW[:, lo:hi], in1=X1ROW[:, lo:hi], op=ALU.mult),
                 reads=[rC, rX1], writes=[rVX[c]])
        VT = self.H[:, :, :].rearrange("p c t -> p (c t)")[:, 0:18 * D].rearrange("p (j n) -> p j n", n=D)
        rVT = Res()
        allH = [r for row in self.rH for r in row]
        for (tag, Lx, nt, col0, vt0) in seqs:
            for jt in range(nt):
                ps, rps = self.psum()
                psb = ps[:, :].bitcast(BF16)

                def tr(e, psb=psb, jt=jt, col0=col0):
                    ins = None
                    for c in range(KC):
                        ins = e.transpose(out=psb[:, c * 128:(c + 1) * 128], in_=VX[:, c, col0 + jt * 128:col0 + (jt + 1) * 128], identity=self.IDENT[:, :])
                    return ins
                S.op("pe", tr, reads=rVX + [self.rID], writes=[rps])
                S.op("act", lambda e, psb=psb, jt=jt, vt0=vt0: e.copy(out=VT[:, vt0 + jt, :], in_=psb[:, 0:D]),
                     reads=[], writes=[rps, rVT] + (allH if (jt == 0 and tag == "l") else []))
        S.barrier()
        if b == 0:
            for (tag, Lx, nt, col0, vt0) in seqs:
                self.hy_filter(i, Wd, tag, Lx, nt)
                S.barrier()
        w_out = Wd["hy_out"]
        for q in range(4):
            self.arena_reset()
            X0q = self.alloc(2 * T).rearrange("p (c t) -> p c t", t=T)
            rX0q = Res()
            WOq = self.alloc(2 * D).rearrange("p (c n) -> p c n", n=D)
            rWOq = Res()
            S.op("sp", lambda e, q=q: e.dma_start(out=X0q[:, :, :], in_=x0d[q * 256:(q + 1) * 256, :].rearrange("(c p) t -> p c t", p=128)),
                 reads=self.rX0D, writes=[rX0q], dma=True)
            self.load_w(WOq, w_out[q * 256:(q + 1) * 256, :].rearrange("(c p) n -> p c n", p=128), rWOq)
            Z = self.alloc(2 * 256).rearrange("p (c n) -> p c n", n=256)
            rZ = Res()
            KSb = [self.alloc(2 * 256, F32).rearrange("p (c n) -> p c n", n=256) for _ in range(2)]
            rKSb = [Res(), Res()]
            SC = [self.alloc(256, F32) for _ in range(2)]
            rSC = [Res(), Res()]
            mark = self.a_off
            for (tag, Lx, nt, col0, vt0) in seqs:
                S.barrier()
                self.arena_reset(mark)
                self.hy_conv(i, Wd, q, tag, Lx, nt, col0, vt0, VT, rVT, X0q, rX0q, WOq, rWOq, Z, rZ, KSb, rKSb, SC, rSC)
            S.barrier()

    def hy_filter(self, i, Wd, tag, Lx, nt):
        S = self.S
        off, _ = vec_offsets(i)
        V64 = lambda name: self.VEC[0:64, off[name]:off[name] + 1]
        PI = float(np.pi)
        self.arena_reset()
        ZT = self.alloc(Lx, F32)
        HD1 = self.alloc(Lx, F32)
        HD2 = self.alloc(Lx, F32)
        FW1 = self.alloc(64, F32)
        FW2 = self.alloc(64, F32)
        NEGT = self.alloc(16, F32)
        CST = self.alloc(8, F32)
        ONESF = self.alloc(128, F32)
        rZT, rHD1, rHD2, rFW, rNEGT, rCST, rONESF = (Res() for _ in range(7))
        S.op("sp", lambda e: e.dma_start(out=ZT[0:33, :], in_=Wd[f"z_{tag}"][:, :]), writes=[rZT], dma=True)
        S.op("sp", lambda e: e.dma_start(out=FW1[0:33, :], in_=Wd["fw1"][:, :]), writes=[rFW], dma=True)
        S.op("sp", lambda e: e.dma_start(out=FW2[0:64, :], in_=Wd["fw2"][:, :]), writes=[rFW], dma=True)
        S.op("sp", lambda e: e.dma_start(out=NEGT[:, 0:nt], in_=Wd[f"negt_{tag}"][:, :]), writes=[rNEGT], dma=True)
        S.op("pool", lambda e: e.memset(ONESF[:, :], 1.0), writes=[rONESF])
        S.op("pool", lambda e: e.memset(CST[:, 2:3], -PI), writes=[rCST])
        S.op("dve", lambda e: e.tensor_tensor(out=CST[0:64, 0:1], in0=V64("hy_fb1"), in1=V64("hy_fr"), op=ALU.mult), reads=[self.rVEC], writes=[rCST])
        S.op("dve", lambda e: e.tensor_tensor(out=CST[0:64, 1:2], in0=V64("hy_fb2"), in1=V64("hy_fr"), op=ALU.mult), reads=[self.rVEC], writes=[rCST])

        RR, rRR = self.MU, self.rMU

        def sin_layer(src, rsrc, kdim, wmat, cstcol, dst, rdst):
            for t0 in range(0, Lx, 512):
                w = min(512, Lx - t0)
                ps, rps = self.psum()
                S.op("pe", lambda e, ps=ps, t0=t0, w=w: e.matmul(ps[0:64, 0:w], lhsT=wmat[0:kdim, 0:64], rhs=src[0:kdim, t0:t0 + w], start=True, stop=True),
                     reads=[rFW, rsrc], writes=[rps])
                S.op("act", lambda e, ps=ps, t0=t0, w=w: e.activation(out=dst[0:64, t0:t0 + w], in_=ps[0:64, 0:w], func=AF.Identity,
                                                                   scale=V64("hy_fr"), bias=CST[0:64, cstcol:cstcol + 1]),
                     reads=[self.rVEC, rCST], writes=[rps, rdst])
                MAGIC = 12582912.0
                S.op("dve", lambda e, t0=t0, w=w: e.tensor_scalar(out=RR[0:64, 0:w], in0=dst[0:64, t0:t0 + w], scalar1=1.0 / (2.0 * PI), scalar2=MAGIC,
                                                                op0=ALU.mult, op1=ALU.add),
                     reads=[rdst], writes=[rRR])
                S.op("dve", lambda e, w=w: e.tensor_scalar(out=RR[0:64, 0:w], in0=RR[0:64, 0:w], scalar1=MAGIC, scalar2=-2.0 * PI,
                                                         op0=ALU.subtract, op1=ALU.mult),
                     reads=[], writes=[rRR])
                S.op("dve", lambda e, t0=t0, w=w: e.tensor_tensor(out=dst[0:64, t0:t0 + w], in0=dst[0:64, t0:t0 + w], in1=RR[0:64, 0:w], op=ALU.add),
                     reads=[rRR], writes=[rdst])
                S.op("dve", lambda e, t0=t0, w=w: e.tensor_scalar(out=dst[0:64, t0:t0 + w], in0=dst[0:64, t0:t0 + w], scalar1=-PI, scalar2=PI,
                                                                op0=ALU.max, op1=ALU.min),
                     reads=[], writes=[rdst])
                S.op("act", lambda e, t0=t0, w=w: e.activation(out=dst[0:64, t0:t0 + w], in_=dst[0:64, t0:t0 + w], func=AF.Sin, scale=1.0),
                     reads=[], writes=[rdst])
        sin_layer(ZT, rZT, 33, FW1, 0, HD1, rHD1)
        sin_layer(HD1, rHD1, 64, FW2, 1, HD2, rHD2)
        mark = self.a_off
        kspec = Wd[f"kspec_{tag}"]
        dfm = {0: Wd[f"dfc_{tag}"].rearrange("(j p) f -> p j f", p=128), 1: Wd[f"dfs_{tag}"].rearrange("(j p) f -> p j f", p=128)}
        nf = nt
        for q in range(4):
            S.barrier()
            self.arena_reset(mark)
            A = self.alloc(nt * 256).rearrange("p (j n) -> p j n", n=256)
            Bm = self.alloc(nt * 256).rearrange("p (j n) -> p j n", n=256)
            rAB = Res()
            FW3 = self.alloc(512, F32)
            ABSD = self.alloc(256, F32)
            SKq = self.alloc(256, F32)
            RN = self.alloc(256, F32)
            rFW3, rABSD, rSK, rRN = Res(), Res(), Res(), Res()
            DEC = self.alloc(256, F32)
            KF = self.alloc(256, F32)
            KB = self.alloc(256, F32)
            AF_ = self.alloc(256, F32)
            AB_ = self.alloc(256, F32)
            rDEC, rKF, rKB, rAF, rABb = (Res() for _ in range(5))
            DF = [self.alloc(nt * 128).rearrange("p (j n) -> p j n", n=128) for _ in range(2)]
            rDF = [Res(), Res()]
            KS = [self.alloc(256, F32) for _ in range(2)]
            rKSo = [Res(), Res()]
            S.op("sp", lambda e, q=q: e.dma_start(out=FW3[0:64, 0:256], in_=Wd["fw3"][:, q * 256:(q + 1) * 256]), writes=[rFW3], dma=True)
            S.op("sp", lambda e, q=q: e.dma_start(out=FW3[0:64, 256:512], in_=Wd["fw3"][:, D + q * 256:D + (q + 1) * 256]), writes=[rFW3], dma=True)
            S.op("sp", lambda e, q=q: e.dma_start(out=ABSD[:, :], in_=Wd["absd"][:, q * 256:(q + 1) * 256]), writes=[rABSD], dma=True)
            S.op("sp", lambda e, q=q: e.dma_start(out=SKq[:, :], in_=Wd["skipb"][:, q * 256:(q + 1) * 256]), writes=[rSK], dma=True)
            psS, rpsS = self.psum_acc()
            for j in range(nt):
                ps, rps = self.psum()
                S.op("pe", lambda e, ps=ps, j=j: e.matmul(ps[:, 0:512], lhsT=HD2[0:64, j * 128:(j + 1) * 128], rhs=FW3[0:64, 0:512], start=True, stop=True),
                     reads=[rHD2, rFW3], writes=[rps])
                S.op("act", lambda e, j=j: e.activation(out=DEC[:, :], in_=ABSD[:, :], func=AF.Exp, scale=NEGT[:, j:j + 1]),
                     reads=[rABSD, rNEGT], writes=[rDEC])
                S.op("dve", lambda e, ps=ps: e.tensor_tensor(out=KF[:, :], in0=ps[:, 0:256], in1=DEC[:, :], op=ALU.mult), reads=[rDEC], writes=[rps, rKF])
                S.op("dve", lambda e, ps=ps: e.tensor_tensor(out=KB[:, :], in0=ps[:, 256:512], in1=DEC[:, :], op=ALU.mult), reads=[rDEC], writes=[rps, rKB])
                if j == 0:
                    S.op("pool", lambda e: e.memset(KB[0:1, :], 0.0), writes=[rKB])
                S.op("pool", lambda e, j=j: e.tensor_tensor(out=A[:, j, :], in0=KF[:, :], in1=KB[:, :], op=ALU.add), reads=[rKF, rKB], writes=[rAB])
                S.op("pool", lambda e, j=j: e.tensor_tensor(out=Bm[:, j, :], in0=KF[:, :], in1=KB[:, :], op=ALU.subtract), reads=[rKF, rKB], writes=[rAB])
                S.op("act", lambda e: e.activation(out=AF_[:, :], in_=KF[:, :], func=AF.Abs), reads=[rKF], writes=[rAF])
                S.op("act", lambda e: e.activation(out=AB_[:, :], in_=KB[:, :], func=AF.Abs), reads=[rKB], writes=[rABb])
                S.op("pe", lambda e, j=j: e.matmul(psS[:, 0:256], lhsT=ONESF[:, :], rhs=AF_[:, :], start=(j == 0), stop=False), reads=[rONESF, rAF], writes=[rpsS])
                S.op("pe", lambda e, j=j: e.matmul(psS[:, 0:256], lhsT=ONESF[:, :], rhs=AB_[:, :], start=False, stop=(j == nt - 1)), reads=[rONESF, rABb], writes=[rpsS])
            S.op("dve", lambda e: e.tensor_scalar_add(out=RN[:, :], in0=psS[:, 0:256], scalar1=EPS), reads=[], writes=[rpsS, rRN])
            S.op("dve", lambda e: e.reciprocal(out=RN[:, :], in_=RN[:, :]), reads=[], writes=[rRN])
            for fb in range(nf):
                for typ in range(2):
                    S.op("sp", lambda e, fb=fb, typ=typ: e.dma_start(out=DF[typ][:, :, :], in_=dfm[typ][:, :, fb * 128:(fb + 1) * 128]), writes=[rDF[typ]], dma=True)
                    ps, rps = self.psum()
                    src = A if typ == 0 else Bm

                    def mm(e, ps=ps, typ=typ, src=src):
                        ins = None
                        for j in range(nt):
                            ins = e.matmul(ps[:, 0:256], lhsT=DF[typ][:, j, :], rhs=src[:, j, :], start=(j == 0), stop=(j == nt - 1))
                        return ins
                    S.op("pe", mm, reads=[rDF[typ], rAB], writes=[rps])
                    S.op("dve", lambda e, ps=ps, typ=typ: e.tensor_tensor(out=KS[typ][:, :], in0=ps[:, 0:256], in1=RN[:, :], op=ALU.mult), reads=[rRN], writes=[rps, rKSo[typ]])
                    if typ == 0:
                        S.op("pool", lambda e: e.tensor_tensor(out=KS[0][:, :], in0=KS[0][:, :], in1=SKq[:, :], op=ALU.add), reads=[rSK], writes=[rKSo[0]])
                    S.op("sp", lambda e, fb=fb, typ=typ, q=q: e.dma_start(out=kspec[typ, fb * 128:(fb + 1) * 128, q * 256:(q + 1) * 256], in_=KS[typ][:, :]),
                         reads=[rKSo[typ]], writes=[self.rKS[tag]], dma=True)

    def hy_conv(self, i, Wd, q, tag, Lx, nt, col0, vt0, VT, rVT, X0q, rX0q, WOq, rWOq, Z, rZ, KSb, rKSb, SC, rSC):
        S = self.S
        off, _ = vec_offsets(i)
        nf = nt
        kspec = Wd[f"kspec_{tag}"]
        dfm = {0: Wd[f"dfc_{tag}"].rearrange("(j p) f -> p j f", p=128), 1: Wd[f"dfs_{tag}"].rearrange("(j p) f -> p j f", p=128)}
        idm = {0: Wd[f"idc_{tag}"].rearrange("(j p) t -> p j t", p=128), 1: Wd[f"ids_{tag}"].rearrange("(j p) t -> p j t", p=128)}
        Y = self.alloc(nf * 512).rearrange("p (f y n) -> p f y n", y=2, n=256)
        rY = Res()
        DF = [self.alloc(nt * 128).rearrange("p (j n) -> p j n", n=128) for _ in range(2)]
        rDF = [Res(), Res()]
        IB = [self.alloc(2 * nf * 256).rearrange("p (y f n) -> p y f n", y=2, n=256) for _ in range(2)]
        rIB = [Res(), Res()]
        U = [self.TMP[0], self.TMP[1]]
        rU = [self.rTMP[0], self.rTMP[1]]
        col = 1 if tag == "c" else 0
        bo = off["hy_b_out"]
        for fb in range(nf):
            kb = fb % 2
            S.op("sp", lambda e, fb=fb, kb=kb: e.dma_start(out=KSb[kb][:, :, :], in_=kspec[:, fb * 128:(fb + 1) * 128, q * 256:(q + 1) * 256].rearrange("y p n -> p y n")),
                 reads=[self.rKS[tag]], writes=[rKSb[kb]], dma=True)
            pss = []
            for typ in range(2):
                S.op("sp", lambda e, fb=fb, typ=typ: e.dma_start(out=DF[typ][:, :, :], in_=dfm[typ][:, :, fb * 128:(fb + 1) * 128]), writes=[rDF[typ]], dma=True)
                ps, rps = self.psum()
                pss.append((ps, rps))

                def mm(e, ps=ps, typ=typ):
                    ins = None
                    for j in range(nt):
                        ins = e.matmul(ps[:, 0:256], lhsT=DF[typ][:, j, :], rhs=VT[:, vt0 + j, q * 256:(q + 1) * 256], start=(j == 0), stop=(j == nt - 1))
                    return ins
                S.op("pe", mm, reads=[rDF[typ], rVT], writes=[rps])
                S.op("act", lambda e, ps=ps, typ=typ: e.copy(out=U[typ][:, 0:256], in_=ps[:, 0:256]), reads=[], writes=[rps, rU[typ]])
            Kc, Ks = KSb[kb][:, 0, :], KSb[kb][:, 1, :]
            S.op("dve", lambda e, Kc=Kc: e.tensor_tensor(out=SC[0][:, :], in0=U[0][:, 0:256], in1=Kc, op=ALU.mult), reads=[rU[0], rKSb[kb]], writes=[rSC[0]])
            S.op("pool", lambda e, Ks=Ks: e.tensor_tensor(out=SC[1][:, :], in0=U[1][:, 0:256], in1=Ks, op=ALU.mult), reads=[rU[1], rKSb[kb]], writes=[rSC[1]])
            S.op("dve", lambda e, fb=fb: e.tensor_tensor(out=Y[:, fb, 0, :], in0=SC[0][:, :], in1=SC[1][:, :], op=ALU.subtract), reads=[rSC[0], rSC[1]], writes=[rY])
            S.op("dve", lambda e, Ks=Ks: e.tensor_tensor(out=SC[0][:, :], in0=U[0][:, 0:256], in1=Ks, op=ALU.mult), reads=[rU[0], rKSb[kb]], writes=[rSC[0]])
            S.op("pool", lambda e, Kc=Kc: e.tensor_tensor(out=SC[1][:, :], in0=U[1][:, 0:256], in1=Kc, op=ALU.mult), reads=[rU[1], rKSb[kb]], writes=[rSC[1]])
            S.op("dve", lambda e, fb=fb: e.tensor_tensor(out=Y[:, fb, 1, :], in0=SC[0][:, :], in1=SC[1][:, :], op=ALU.add), reads=[rSC[0], rSC[1]], writes=[rY])
        for ts in range(Lx // 256):
            ib, rib = IB[ts % 2], rIB[ts % 2]
            for typ in range(2):
                S.op("sp", lambda e, ib=ib, typ=typ, ts=ts: e.dma_start(out=ib[:, typ, :, :], in_=idm[typ][:, :, ts * 256:(ts + 1) * 256]), writes=[rib], dma=True)
            cs = col0 + ts * 256
            ti = 0 if tag == "c" else 1 + (ts // 2)
            for cc in range(2):
                ps, rps = self.psum()

                def mm(e, ps=ps, ib=ib, cc=cc):
                    ins = None
                    for fb in range(nf):
                        for typ in range(2):
                            ins = e.matmul(ps[:, 0:256], lhsT=Y[:, fb, typ, cc * 128:(cc + 1) * 128], rhs=ib[:, typ, fb, :],
                                           start=(fb == 0 and typ == 0), stop=(fb == nf - 1 and typ == 1))
                    return ins
                S.op("pe", mm, reads=[rY, rib], writes=[rps])
                S.op("dve", lambda e, ps=ps, cc=cc, cs=cs: e.tensor_tensor(out=Z[:, cc, :], in0=ps[:, 0:256], in1=X0q[:, cc, cs:cs + 256], op=ALU.mult),
                     reads=[rX0q], writes=[rps, rZ])
            for o in range(KC):
                ps, rps = self.psum()

                def mm2(e, ps=ps, o=o):
                    e.matmul(ps[:, 0:256], lhsT=WOq[:, 0, o * 128:(o + 1) * 128], rhs=Z[:, 0, :], start=True, stop=False)
                    return e.matmul(ps[:, 0:256], lhsT=WOq[:, 1, o * 128:(o + 1) * 128], rhs=Z[:, 1, :], start=False, stop=True)
                S.op("pe", mm2, reads=[rWOq, rZ], writes=[rps])
                if q == 0:
                    S.op("dve", lambda e, ps=ps, o=o: e.tensor_scalar(out=self.RSTD[:, 0:256], in0=ps[:, 0:256], scalar1=self.VEC[:, bo + o:bo + o + 1],
                                                                   scalar2=self.MOD[:, 16 + o, col:col + 1], op0=ALU.add, op1=ALU.mult),
                         reads=[self.rVEC, self.rMOD], writes=[rps, self.rRSTD])
                    S.op("dve", lambda e, o=o, cs=cs: e.tensor_tensor(out=self.X[:, o, cs:cs + 256], in0=self.X[:, o, cs:cs + 256], in1=self.RSTD[:, 0:256], op=ALU.add),
                         reads=[self.rRSTD], writes=[self.rX[o][ti]])
                else:
                    S.op("dve", lambda e, ps=ps, o=o, cs=cs: e.scalar_tensor_tensor(out=self.X[:, o, cs:cs + 256], in0=ps[:, 0:256], scalar=self.MOD[:, 16 + o, col:col + 1],
                                                                                  in1=self.X[:, o, cs:cs + 256], op0=ALU.mult, op1=ALU.add),
                         reads=[self.rMOD], writes=[rps, self.rX[o][ti]])

    def layer(self, i, Wd, b, last):
        kind = i % 3
        tiles_all = [0, 1, 2, 3, 4]
        tiles_out = [1, 2, 3, 4] if last else tiles_all
        self.adaln(i, Wd)
        if kind == 0:
            self.modulate(0, tiles_all)
            self.S.barrier()
            self.mla(i, Wd, need_ctx=not last)
            self.S.barrier()
            self.carve_ffn()
            self.zero_pads()
        if kind == 2:
            self.modulate(0, tiles_all)
            self.S.barrier()
            self.hyena(i, Wd, b)
            self.S.barrier()
            self.carve_ffn()
            self.zero_pads()
        if kind == 1:
            self.modulate(0, tiles_out)
            self.conformer(i, Wd, tiles_out)
        self.modulate(1, tiles_out)
        self.ffn(i, Wd, tiles_out)

    def ubuf(self, c):
        if c < 6:
            return self.G[:, c, :], self.rG[c]
        return (self.AROW, self.rA) if c == 6 else (self.VROW, self.rV)

    def conformer(self, i, Wd, tiles):
        S = self.S
        off, _ = vec_offsets(i)
        w1 = Wd["cf_w1"].rearrange("(c p) n -> p c n", p=128)
        w2 = Wd["cf_w2"].rearrange("(c p) n -> p c n", p=128)
        b1o, wdo, bdo, lgo, lbo, b2o = (off[k] for k in ("cf_b_pw1", "cf_w_dw", "cf_b_dw", "cf_ln_g", "cf_ln_b", "cf_b_pw2"))
        for (a, bnd) in ((0, 16), (C0 + LC, L0), (L0 + L, T)):
            S.op("pool", lambda e, a=a, bnd=bnd: e.memset(self.G[:, :, a:bnd], 0.0), writes=self.rG)
        for c in range(KC):
            k = self.wa_i
            self.wa_i = 1 - k
            wa, rwa = self.WA[k], self.rWA[k]
            self.load_w(wa[:, :, 0:128], w1[:, :, c * 128:(c + 1) * 128], rwa[0])
            self.load_w(wa[:, :, 128:256], w1[:, :, D + c * 128:D + (c + 1) * 128], rwa[1])
            ub, rub = self.ubuf(c)
            for ti in tiles:
                c0, w = TILES[ti]
                pss = []
                for half in range(2):
                    ps, rps = self.psum()
                    pss.append((ps, rps))

                    def mm(e, ps=ps, wa=wa, half=half, c0=c0, w=w):
                        ins = None
                        for kc in range(KC):
                            ins = e.matmul(ps[:, 0:w], lhsT=wa[:, kc, half * 128:(half + 1) * 128],
                                           rhs=self.H[:, kc, c0:c0 + w], start=(kc == 0), stop=(kc == KC - 1))
                        return ins
                    S.op("pe", mm, reads=[rwa[half]] + [self.rH[kc][ti] for kc in range(KC)], writes=[rps])
                tp, rtp = self.tmp()
                S.op("act", lambda e, ps=pss[1][0], tp=tp, w=w, c=c: e.activation(
                    out=tp[:, 0:w], in_=ps[:, 0:w], func=AF.Sigmoid, bias=self.VEC[:, b1o + 8 + c:b1o + 9 + c], scale=1.0),
                    reads=[self.rVEC], writes=[pss[1][1], rtp])
                S.op("dve", lambda e, ps=pss[0][0], tp=tp, ub=ub, c0=c0, w=w, c=c: e.scalar_tensor_tensor(
                    out=ub[:, c0:c0 + w], in0=ps[:, 0:w], scalar=self.VEC[:, b1o + c:b1o + c + 1], in1=tp[:, 0:w],
                    op0=ALU.add, op1=ALU.mult),
                    reads=[self.rVEC, rtp], writes=[pss[0][1], rub])
        dg = self.WB[:, 0:31 * 128].rearrange("p (k n) -> p k n", n=128)
        for c in range(KC):
            ub, rub = self.ubuf(c)
            for kk in range(31):
                S.op("pool" if kk % 2 else "dve", lambda e, kk=kk, c=c: e.tensor_scalar_mul(
                    out=dg[:, kk, :], in0=self.IDENT[:, :], scalar1=self.VEC[:, wdo + kk * 8 + c:wdo + kk * 8 + c + 1]),
                    reads=[self.rID, self.rVEC], writes=[self.rWB])
            for ti in tiles:
                c0, w = TILES[ti]
                ps, rps = self.psum()

                def mm(e, ps=ps, ub=ub, c0=c0, w=w):
                    ins = None
                    for kk in range(31):
                        ins = e.matmul(ps[:, 0:w], lhsT=dg[:, kk, :], rhs=ub[:, c0 + kk - 15:c0 + kk - 15 + w],
                                       start=(kk == 0), stop=(kk == 30))
                    return ins
                S.op("pe", mm, reads=[self.rWB, rub], writes=[rps])
                S.op("act", lambda e, ps=ps, c=c, c0=c0, w=w: e.activation(
                    out=self.H[:, c, c0:c0 + w], in_=ps[:, 0:w], func=AF.Identity, bias=self.VEC[:, bdo + c:bdo + c + 1], scale=1.0),
                    reads=[self.rVEC], writes=[rps, self.rH[c][ti]])
        for k in range(2):
            self.load_w(self.WA[k][:, :, :], w2[:, :, k * 512:(k + 1) * 512], self.rWA[k])
        for ti in tiles:
            c0, w = TILES[ti]
            col = 1 if ti == 0 else 0
            for c in range(KC):
                S.op("act", lambda e, c=c, c0=c0, w=w: e.activation(out=self.SQ[:, c, 0:w], in_=self.H[:, c, c0:c0 + w], func=AF.Square),
                     reads=[self.rH[c][ti]], writes=[self.rSQ])
            ps1, rps1 = self.psum()
            ps2, rps2 = self.psum()

            def mm1(e, ps=ps1, c0=c0, w=w):
                ins = None
                for c in range(KC):
                    ins = e.matmul(ps[:, 0:w], lhsT=self.ONES[:, :], rhs=self.H[:, c, c0:c0 + w], start=(c == 0), stop=(c == KC - 1))
                return ins
            S.op("pe", mm1, reads=[self.rONES] + [self.rH[c][ti] for c in range(KC)], writes=[rps1])

            def mm2(e, ps=ps2, w=w):
                ins = None
                for c in range(KC):
                    ins = e.matmul(ps[:, 0:w], lhsT=self.ONES[:, :], rhs=self.SQ[:, c, 0:w], start=(c == 0), stop=(c == KC - 1))
                return ins
            S.op("pe", mm2, reads=[self.rONES, self.rSQ], writes=[rps2])
            S.op("act", lambda e, ps=ps1, w=w: e.activation(out=self.MU[:, 0:w], in_=ps[:, 0:w], func=AF.Identity, scale=1.0 / D),
                 reads=[], writes=[rps1, self.rMU])
            tp, rtp = self.tmp()
            S.op("dve", lambda e, tp=tp, w=w: e.tensor_tensor(out=tp[:, 0:w], in0=self.MU[:, 0:w], in1=self.MU[:, 0:w], op=ALU.mult),
                 reads=[self.rMU], writes=[rtp])
            S.op("dve", lambda e, ps=ps2, tp=tp, w=w: e.scalar_tensor_tensor(
                out=self.RSTD[:, 0:w], in0=ps[:, 0:w], scalar=1.0 / D, in1=tp[:, 0:w], op0=ALU.mult, op1=ALU.subtract),
                reads=[rtp], writes=[rps2, self.rRSTD])
            S.op("act", lambda e, w=w: e.activation(out=self.RSTD[:, 0:w], in_=self.RSTD[:, 0:w], func=AF.Sqrt, scale=1.0, bias=self.EPSB[:, 0:1]),
                 reads=[self.rONES], writes=[self.rRSTD])
            S.op("dve", lambda e, w=w: e.reciprocal(out=self.RSTD[:, 0:w], in_=self.RSTD[:, 0:w]), reads=[], writes=[self.rRSTD])
            for c in range(KC):
                ub, rub = self.ubuf(c)
                tp, rtp = self.tmp()
                S.op("dve", lambda e, tp=tp, c=c, c0=c0, w=w: e.tensor_tensor(out=tp[:, 0:w], in0=self.H[:, c, c0:c0 + w], in1=self.MU[:, 0:w], op=ALU.subtract),
                     reads=[self.rH[c][ti], self.rMU], writes=[rtp])
                S.op("dve", lambda e, tp=tp, w=w: e.tensor_tensor(out=tp[:, 0:w], in0=tp[:, 0:w], in1=self.RSTD[:, 0:w], op=ALU.mult),
                     reads=[self.rRSTD], writes=[rtp])
                S.op("act", lambda e, tp=tp, ub=ub, c=c, c0=c0, w=w: e.activation(
                    out=ub[:, c0:c0 + w], in_=tp[:, 0:w], func=AF.Silu, scale=self.VEC[:, lgo + c:lgo + c + 1], bias=self.VEC[:, lbo + c:lbo + c + 1]),
                    reads=[rtp, self.rVEC], writes=[rub])
            for dc in range(KC):
                ps, rps = self.psum()
                wa, rwa = self.WA[dc // 4], self.rWA[dc // 4]

                def mm(e, ps=ps, wa=wa, dc=dc, c0=c0, w=w):
                    ins = None
                    for c in range(KC):
                        ins = e.matmul(ps[:, 0:w], lhsT=wa[:, c, (dc % 4) * 128:(dc % 4 + 1) * 128], rhs=self.ubuf(c)[0][:, c0:c0 + w],
                                       start=(c == 0), stop=(c == KC - 1))
                    return ins
                S.op("pe", mm, reads=[rwa[0]] + [self.ubuf(c)[1] for c in range(KC)], writes=[rps])
                tp, rtp = self.tmp()
                S.op("dve", lambda e, ps=ps, tp=tp, dc=dc, w=w, col=col: e.tensor_scalar(
                    out=tp[:, 0:w], in0=ps[:, 0:w], scalar1=self.VEC[:, b2o + dc:b2o + dc + 1], scalar2=self.MOD[:, 16 + dc, col:col + 1],
                    op0=ALU.add, op1=ALU.mult),
                    reads=[self.rVEC, self.rMOD], writes=[rps, rtp])
                S.op("dve", lambda e, tp=tp, dc=dc, c0=c0, w=w: e.tensor_tensor(out=self.X[:, dc, c0:c0 + w], in0=self.X[:, dc, c0:c0 + w], in1=tp[:, 0:w], op=ALU.add),
                     reads=[rtp], writes=[self.rX[dc][ti]])


def make_inputs(inp, layers, core, nb=NB, x_override=None, ctx_override=None):
    m = {}
    xs = inp["x"] if x_override is None else x_override
    cs = inp["ctx"] if ctx_override is None else ctx_override
    xT = np.zeros((nb, D, T), np.float32)
    cT = np.zeros((nb, 128, KC * 2), np.float32)
    for bb in range(nb):
        b = core * nb + bb
        xT[bb, :, L0:L0 + L] = xs[b].T
        xT[bb, :, C0:C0 + LC] = cs[b].T
        sv = np.stack([inp["c"][b], inp["c_ctx"]], axis=-1)
        cT[bb] = sv.reshape(KC, 128, 2).transpose(1, 0, 2).reshape(128, KC * 2)
    m["ident"] = np.eye(128, dtype=np.float32)
    bd = np.zeros((128, 128), np.float32)
    bd[:64, :64] = 1.0
    bd[64:96, 64:96] = 1.0
    m["bd96"] = bd
    m["xT"] = xT
    m["cT"] = cT
    for i in layers:
        kind, j = i % 3, i // 3
        v, _ = pack_layer_vec(inp, i)
        vp = np.zeros((128, NPV), np.float32)
        vp[:, :v.shape[1]] = v
        m[f"vec{i}"] = vp
        m[f"ada_w{i}"] = np.ascontiguousarray(inp["ada_w"][i])
        m[f"ffn_up{i}"] = np.ascontiguousarray(inp["ffn_w_up"][i])
        m[f"ffn_down{i}"] = np.ascontiguousarray(inp["ffn_w_down"][i])
        if kind == 0:
            pe = np.arange(32)
            perm = np.where((pe % 16) < 8, pe + 8, pe - 8)
            wdkv = np.asarray(inp["mla_w_dkv"][j], np.float32)
            kv = np.zeros((D, 320), np.float32)
            kv[:, :128] = wdkv[:, :128]
            kv[:, 128 + 64:224] = wdkv[:, 128:160]
            kv[:, 224 + 64:320] = wdkv[:, 128 + perm]
            m[f"mla_kv{i}"] = kv
            wuq = np.asarray(inp["mla_w_uq"][j], np.float32).reshape(256, 16, 96)
            uq = np.zeros((256, 16, 192), np.float32)
            uq[:, :, :96] = wuq
            uq[:, :, 96 + 64:] = wuq[:, :, 64 + perm]
            m[f"mla_uq{i}"] = uq.reshape(256, 16 * 192)
            m[f"mla_dq{i}"] = np.ascontiguousarray(inp["mla_w_dq"][j])
            m[f"mla_ukv{i}"] = np.ascontiguousarray(inp["mla_w_ukv"][j])
            m[f"mla_o{i}"] = np.ascontiguousarray(inp["mla_w_o"][j])
            m["rope_cs"] = rope_tables()
        if kind == 2:
            m[f"hy_in{i}"] = np.ascontiguousarray(inp["hy_w_in"][j])
            m[f"hy_out{i}"] = np.ascontiguousarray(inp["hy_w_out"][j])
            m[f"hy_fw1_{i}"] = np.ascontiguousarray(inp["hy_f_w1"][j])
            m[f"hy_fw2_{i}"] = np.ascontiguousarray(inp["hy_f_w2"][j])
            m[f"hy_fw3_{i}"] = np.ascontiguousarray(inp["hy_f_w3"][j])
            m[f"hy_skipb{i}"] = np.ascontiguousarray(np.broadcast_to(np.asarray(inp["hy_skip"][j], np.float32)[None, :], (128, D)))
            m.update(hy_consts())
        if kind == 1:
            m[f"cf_w1_{i}"] = np.ascontiguousarray(inp["cf_w_pw1"][j])
            m[f"cf_w2_{i}"] = np.ascontiguousarray(inp["cf_w_pw2"][j])
    return m


_HYC = {}


def hy_consts():
    if _HYC:
        return _HYC
    import ml_dtypes
    bf = ml_dtypes.bfloat16
    out = {}
    for tag, Lx in (("l", L), ("c", LC)):
        N = 2 * Lx
        tau = np.arange(Lx, dtype=np.float64)[:, None]
        f = np.arange(Lx, dtype=np.float64)[None, :] + 0.5
        th = 2 * np.pi * tau * f / N
        out[f"dfc_{tag}"] = np.cos(th).astype(np.float32).astype(bf)
        out[f"dfs_{tag}"] = np.sin(th).astype(np.float32).astype(bf)
        out[f"idc_{tag}"] = np.ascontiguousarray(((2.0 / N) * np.cos(th).T).astype(np.float32)).astype(bf)
        out[f"ids_{tag}"] = np.ascontiguousarray(((2.0 / N) * np.sin(th).T).astype(np.float32)).astype(bf)
        t = np.linspace(0.0, 1.0, Lx, dtype=np.float32)
        w = (2.0 * np.pi * np.arange(Lx, dtype=np.float32) / Lx).astype(np.float32)
        fr = np.linspace(1e-4, 15.0, 16, dtype=np.float32)
        fw = w[:, None] * fr[None, :]
        z = np.concatenate([t[:, None], np.cos(fw), -np.sin(fw)], axis=-1).astype(np.float32)
        out[f"z_{tag}"] = np.ascontiguousarray(z.T)
        out[f"negt_{tag}"] = np.ascontiguousarray((-t).reshape(Lx // 128, 128).T)
    deltas = np.linspace(np.log(1e-2) / 0.3, np.log(1e-2) / 1.5, D, dtype=np.float32)
    out["absd"] = np.ascontiguousarray(np.broadcast_to(np.abs(deltas)[None, :], (128, D))).astype(np.float32)
    _HYC.update(out)
    return _HYC


def rope_tables():
    t = np.arange(L)
    row = (t // 64).astype(np.float32)
    colp = (t % 64).astype(np.float32)
    inv = (10000.0 ** (-(np.arange(0, 16, 2, dtype=np.float32) / 16.0))).astype(np.float32)
    ang = np.concatenate([row[:, None] * inv, colp[:, None] * inv], axis=-1).astype(np.float32)
    cs = np.zeros((128, 2, T), np.float32)
    cs[:, 0, :] = 1.0
    for d in range(32):
        a, b, jj = d // 16, (d % 16) // 8, d % 8
        cs[64 + d, 0, L0:L0 + L] = np.cos(ang[:, a * 8 + jj])
        cs[64 + d, 1, L0:L0 + L] = np.sin(ang[:, a * 8 + jj]) * (-1.0 if b == 0 else 1.0)
    return cs


def kernel(**inputs):
    inp = {k: np.asarray(v) for k, v in inputs.items()}
    layers = list(range(DEPTH))
    prog = Prog(layers)
    nc = prog.build()
    in_maps = [make_inputs(inp, layers, c) for c in range(8)]
    res = run_bass_kernel_spmd(nc, in_maps, core_ids=list(range(8)))
    out = np.empty((16, L, D), np.float32)
    for c in range(8):
        y = res.results[c]["yT"]
        for bb in range(NB):
            out[c * NB + bb] = y[bb].T
    return out
```

```python
import contextlib
import os
import numpy as np
import concourse.bass as bass
import concourse.mybir as mybir
from concourse.bass_utils import run_bass_kernel_spmd

F32 = mybir.dt.float32
BF16 = mybir.dt.bfloat16
ALU = mybir.AluOpType
AF = mybir.ActivationFunctionType

D = 1024
KC = 8
L = 2048
LC = 256
T = 16 + LC + 16 + L + 16
C0 = 16
L0 = 16 + LC + 16
TILES = [(C0, LC)] + [(L0 + 512 * j, 512) for j in range(4)]
D_FF = 2816
FC = 22
DEPTH = 4
EPS = 1e-6
NB = 2
DBG_STOP = int(os.environ.get('MK_STOP', '0'))
DBG_HEADS = (int(os.environ['MK_HEADS']) if 'MK_HEADS' in os.environ else None)


class Res:
    __slots__ = ("w", "r")

    def __init__(self):
        self.w = None
        self.r = {}


class Sched:
    ENG = ("pe", "act", "dve", "pool", "sp")
    NDMA = 8
    EPOCH = 6000

    def __init__(self):
        self.streams = {e: [] for e in self.ENG}
        self.cnt = {e: 0 for e in self.ENG}
        self.waited = {e: {} for e in self.ENG}
        self.dma_n = {e: 0 for e in self.ENG}
        self.last = {}

    def op(self, eng, fn, reads=(), writes=(), dma=False):
        deps = {}

        def add(k, v):
            if deps.get(k, 0) < v:
                deps[k] = v

        for r in reads:
            if r.w is not None:
                add(*r.w)
        for r in writes:
            if r.w is not None:
                add(*r.w)
            for k, v in r.r.items():
                add(k, v)
        if dma:
            j = self.dma_n[eng]
            self.dma_n[eng] += 1
            key = ("dma", eng, j % self.NDMA)
            val = 16 * (j // self.NDMA + 1)
            if j >= self.NDMA:
                add(key, val - 16)
            inc = 16
        else:
            key = ("eng", eng, self.cnt[eng] // self.EPOCH)
            val = self.cnt[eng] % self.EPOCH + 1
            self.cnt[eng] += 1
            inc = 1
        self.last[key] = val
        wd = self.waited[eng]
        waits = []
        for k, v in deps.items():
            if eng == "pe" and k[0] == "eng" and k[1] == "pe":
                continue
            if wd.get(k, 0) < v:
                wd[k] = v
                waits.append((k, v))
        self.streams[eng].append((fn, waits, key, inc))
        for r in reads:
            if r.r.get(key, 0) < val:
                r.r[key] = val
        for r in writes:
            r.w = (key, val)
            r.r = {}

    def barrier(self, engs=None):
        for e in (engs or self.ENG):
            wd = self.waited[e]
            waits = []
            for k, v in self.last.items():
                if wd.get(k, 0) < v:
                    wd[k] = v
                    waits.append((k, v))
            if waits:
                self.streams[e].append((None, waits, None, 0))

    def emit(self, nc, stack):
        sems = {}
        for e in self.ENG:
            for (fn, waits, key, inc) in self.streams[e]:
                for k in [w[0] for w in waits] + ([key] if key is not None else []):
                    if k not in sems:
                        sems[k] = stack.enter_context(nc.semaphore("s_" + "_".join(str(x) for x in k)))
        block = stack.enter_context(nc.Block())

        def run(ename, eng):
            for (fn, waits, key, inc) in self.streams[ename]:
                for k, v in waits:
                    eng.wait_ge(sems[k], v)
                if fn is not None:
                    fn(eng).then_inc(sems[key], inc)

        @block.tensor
        def _(e):
            run("pe", e)

        @block.scalar
        def _(e):
            run("act", e)

        @block.vector
        def _(e):
            run("dve", e)

        @block.gpsimd
        def _(e):
            run("pool", e)

        @block.sync
        def _(e):
            run("sp", e)


def fm(v):
    v = np.asarray(v, np.float32)
    c = v.shape[-1] // 128
    lead = v.shape[:-1]
    return np.ascontiguousarray(np.moveaxis(v.reshape(lead + (c, 128)), -1, 0)).reshape(128, -1)


def pack_layer_vec(inp, i):
    kind, j = i % 3, i // 3
    parts = [("ada_b", fm(inp["ada_b"][i])),
             ("norm_mix", fm(inp["norm_mix"][i])),
             ("norm_ffn", fm(inp["norm_ffn"][i])),
             ("ffn_b_dw", fm(inp["ffn_b_dw"][i])),
             ("ffn_w_dw", fm(inp["ffn_w_dw"][i]))]
    if kind == 0:
        g = np.asarray(inp["mla_qk_gain"][j], np.float32)
        def col96(v):
            o = np.zeros((128, 1), np.float32)
            o[:96, 0] = v
            return o
        pe = np.arange(32)
        perm = np.where((pe % 16) < 8, pe + 8, pe - 8)
        gqp = np.zeros(96, np.float32); gqp[64:] = g[0, 64 + perm]
        gkp = np.zeros(96, np.float32); gkp[64:] = g[1, 64 + perm]
        invn = np.concatenate([np.full(64, 1 / 64.0), np.full(32, 1 / 32.0)]).astype(np.float32)
        parts += [("mla_qn", fm(inp["mla_q_norm"][j])), ("mla_kvn", fm(inp["mla_kv_norm"][j])),
                  ("mla_gq", col96(g[0])), ("mla_gqp", col96(gqp)), ("mla_gk", col96(g[1])), ("mla_gkp", col96(gkp)),
                  ("mla_invn", col96(invn))]
    if kind == 2:
        def col64(v):
            o = np.zeros((128, 1), np.float32)
            o[:64, 0] = v
            return o
        parts += [("hy_b_in", fm(inp["hy_b_in"][j])), ("hy_w_short", fm(inp["hy_w_short"][j])),
                  ("hy_b_short", fm(inp["hy_b_short"][j])), ("hy_b_out", fm(inp["hy_b_out"][j])),
                  ("hy_fb1", col64(inp["hy_f_b1"][j])), ("hy_fb2", col64(inp["hy_f_b2"][j])), ("hy_fr", col64(inp["hy_sin_freq"][j]))]
    if kind == 1:
        parts += [("cf_b_pw1", fm(inp["cf_b_pw1"][j])),
                  ("cf_w_dw", fm(inp["cf_w_dw"][j])),
                  ("cf_b_dw", fm(inp["cf_b_dw"][j])),
                  ("cf_ln_g", fm(inp["cf_ln_g"][j])),
                  ("cf_ln_b", fm(inp["cf_ln_b"][j])),
                  ("cf_b_pw2", fm(inp["cf_b_pw2"][j]))]
    off = {}
    o = 0
    for n, a in parts:
        off[n] = o
        o += a.shape[1]
    return np.concatenate([a for _, a in parts], axis=1), off


def vec_offsets(i):
    kind = i % 3
    sizes = [("ada_b", 48), ("norm_mix", 8), ("norm_ffn", 8), ("ffn_b_dw", 22), ("ffn_w_dw", 66)]
    if kind == 0:
        sizes += [("mla_qn", 2), ("mla_kvn", 1), ("mla_gq", 1), ("mla_gqp", 1), ("mla_gk", 1), ("mla_gkp", 1), ("mla_invn", 1)]
    if kind == 2:
        sizes += [("hy_b_in", 24), ("hy_w_short", 72), ("hy_b_short", 24), ("hy_b_out", 8), ("hy_fb1", 1), ("hy_fb2", 1), ("hy_fr", 1)]
    if kind == 1:
        sizes += [("cf_b_pw1", 16), ("cf_w_dw", 248), ("cf_b_dw", 8), ("cf_ln_g", 8), ("cf_ln_b", 8), ("cf_b_pw2", 8)]
    off = {}
    o = 0
    for n, s in sizes:
        off[n] = o
        o += s
    return off, o


NPV = 512
ARENA = 42000


class Prog:
    def __init__(self, layers, nb=NB):
        self.layers = layers
        self.nb = nb
        self.nc = bass.Bass("TRN2", target_bir_lowering=False)
        self.S = Sched()
        self.dram = {}

    def din(self, name, shape, dt=F32):
        t = self.nc.dram_tensor(name, list(shape), dt, kind="ExternalInput").ap()
        self.dram[name] = t
        return t

    def dout(self, name, shape, dt=F32):
        t = self.nc.dram_tensor(name, list(shape), dt, kind="ExternalOutput").ap()
        self.dram[name] = t
        return t

    def build(self):
        nc, S = self.nc, self.S
        nb = self.nb
        xT = self.din("xT", [nb, D, T])
        cT = self.din("cT", [nb, 128, KC * 2])
        ident_d = self.din("ident", [128, 128])
        bd_d = self.din("bd96", [128, 128])
        yT = self.dout("yT", [nb, D, L])
        cOut = self.dout("cOut", [nb, D, LC])
        W = {}
        for i in self.layers:
            kind, j = i % 3, i // 3
            W[i] = dict(
                ada_w=self.din(f"ada_w{i}", [D, 6 * D]),
                vec=self.din(f"vec{i}", [128, NPV]),
                ffn_up=self.din(f"ffn_up{i}", [D, 2 * D_FF]),
                ffn_down=self.din(f"ffn_down{i}", [D_FF, D]),
            )
            if kind == 0:
                if "rope_cs" not in self.dram:
                    self.din("rope_cs", [128, 2, T])
                W[i].update(dq=self.din(f"mla_dq{i}", [D, 256]), kv=self.din(f"mla_kv{i}", [D, 320]),
                            uq=self.din(f"mla_uq{i}", [256, 16 * 192]), ukv=self.din(f"mla_ukv{i}", [128, 2048]),
                            wo=self.din(f"mla_o{i}", [D, D]), rope=self.dram["rope_cs"])
            if kind == 2:
                W[i].update(hy_in=self.din(f"hy_in{i}", [D, 3 * D]), hy_out=self.din(f"hy_out{i}", [D, D]),
                            fw1=self.din(f"hy_fw1_{i}", [33, 64]), fw2=self.din(f"hy_fw2_{i}", [64, 64]),
                            fw3=self.din(f"hy_fw3_{i}", [64, 2 * D]), skipb=self.din(f"hy_skipb{i}", [128, D]))
                for tag, Lx in (("l", L), ("c", LC)):
                    for nm in ("dfc", "dfs", "idc", "ids"):
                        W[i][f"{nm}_{tag}"] = self.din(f"{nm}_{tag}", [Lx, Lx], BF16)
                    W[i][f"z_{tag}"] = self.din(f"z_{tag}", [33, Lx])
                    W[i][f"negt_{tag}"] = self.din(f"negt_{tag}", [128, Lx // 128])
                    W[i][f"kspec_{tag}"] = self.nc.dram_tensor(f"kspec_{tag}", [2, Lx, D], F32).ap()
                W[i]["absd"] = self.din("absd", [128, D])
                W[i]["x0d"] = self.nc.dram_tensor("x0d", [D, T], BF16).ap()
                self.rX0D = [Res() for _ in range(KC)]
                self.rKS = {"l": Res(), "c": Res()}
            if kind == 1:
                W[i].update(cf_w1=self.din(f"cf_w1_{i}", [D, 2 * D]), cf_w2=self.din(f"cf_w2_{i}", [D, D]))
        with contextlib.ExitStack() as st:
            self.st = st

            def sb(name, shape, dt):
                return st.enter_context(nc.sbuf_tensor(name, list(shape), dt))

            self.X = sb("X", [128, KC, T], F32)
            self.rX = [[Res() for _ in TILES] for _ in range(KC)]
            self.H = sb("H", [128, KC, T], BF16)
            self.rH = [[Res() for _ in TILES] for _ in range(KC)]
            self.SCR = sb("SCR", [128, ARENA], BF16)
            self.carve_ffn()
            self.VEC = sb("VEC", [128, NPV], F32)
            self.rVEC = Res()
            self.MOD = sb("MOD", [128, 48, 2], F32)
            self.rMOD = Res()
            self.AM = sb("AM", [128, 2, KC, 2], F32)
            self.rAM = Res()
            self.SV = sb("SV", [128, KC, 2], F32)
            self.SVB = sb("SVB", [128, KC, 2], BF16)
            self.rSV = Res()
            self.ONES = sb("ONES", [128, 128], BF16)
            self.rONES = Res()
            self.EPSB = sb("EPSB", [128, 1], F32)
            self.RSTD = sb("RSTD", [128, 512], F32)
            self.rRSTD = Res()
            self.MU = sb("MU", [128, 512], F32)
            self.rMU = Res()
            self.TMP = [sb(f"TMP{k}", [128, 512], F32) for k in range(2)]
            self.rTMP = [Res(), Res()]
            self.tmp_i = 0
            self.BD = sb("BD", [128, 128], BF16)
            self.PS = [st.enter_context(nc.psum_tensor(f"PS{k}", [128, 512], F32)) for k in range(8)]
            self.acc_i = 0
            self.rPS = [Res() for _ in range(8)]
            self.ps_i = 0
            self.wa_i = 0
            self.dq = 0

            S.op("pool", lambda e: e.memset(self.ONES[:, :], 1.0), writes=[self.rONES])
            S.op("pool", lambda e: e.memset(self.EPSB[:, :], EPS), writes=[self.rONES])
            self.zero_pads()
            self.IDENT = sb("IDENT", [128, 128], BF16)
            self.rID = Res()
            S.op("pool", lambda e: e.dma_start(out=self.IDENT[:, :], in_=ident_d[:, :]), writes=[self.rID], dma=True)
            S.op("pool", lambda e: e.dma_start(out=self.BD[:, :], in_=bd_d[:, :]), writes=[self.rID], dma=True)

            routs = []
            for b in range(nb):
                self.load_x(xT, cT, b)
                for i in self.layers:
                    self.layer(i, W[i], b, last=(i == DEPTH - 1))
                routs += self.store_x(yT, cOut, b)
            S.barrier(["sp"])
            S.emit(nc, st)
        return nc

    def psum(self):
        k = self.ps_i
        self.ps_i = (k + 1) % 6
        return self.PS[k], self.rPS[k]

    def psum_acc(self):
        k = 6 + self.acc_i
        self.acc_i = 1 - self.acc_i
        return self.PS[k], self.rPS[k]

    def arena_reset(self, mark=0):
        self.a_off = mark

    def alloc(self, n_el, dt=BF16, shape=None):
        nb16 = n_el * (2 if dt == F32 else 1)
        nb16 = (nb16 + 15) // 16 * 16
        a = self.a_off
        assert a + nb16 <= ARENA, (a, nb16)
        self.a_off = a + nb16
        ap = self.SCR[:, a:a + n_el * (2 if dt == F32 else 1)]
        if dt == F32:
            ap = ap.bitcast(F32)
        return ap

    def carve_ffn(self):
        self.arena_reset()
        self.G = self.alloc(6 * T).rearrange("p (c t) -> p c t", t=T)
        self.rG = [Res() for _ in range(6)]
        self.AROW = self.alloc(T)
        self.CROW = self.alloc(T, F32)
        self.VROW = self.alloc(T)
        self.rA, self.rC, self.rV = Res(), Res(), Res()
        self.WA = [self.alloc(KC * 512).rearrange("p (c n) -> p c n", n=512) for _ in range(2)]
        self.rWA = [[Res(), Res()], [Res(), Res()]]
        self.WB = self.alloc(6 * 1024)
        self.rWB = Res()
        self.SQ = self.alloc(KC * 512).rearrange("p (c n) -> p c n", n=512)
        self.rSQ = Res()

    def zero_pads(self):
        S = self.S
        S.op("pool", lambda e: e.memset(self.AROW[:, :], 0.0), writes=[self.rA])
        S.op("pool", lambda e: e.memset(self.VROW[:, :], 0.0), writes=[self.rV])
        for (a, bnd) in ((0, 16), (C0 + LC, L0), (L0 + L, T)):
            S.op("pool", lambda e, a=a, bnd=bnd: e.memset(self.G[:, :, a:bnd], 0.0), writes=self.rG)

    def tmp(self):
        k = self.tmp_i
        self.tmp_i = 1 - k
        return self.TMP[k], self.rTMP[k]

    def dma_eng(self):
        self.dq += 1
        return "sp"

    def load_w(self, dst_ap, src_ap, res):
        self.S.op("pool", lambda e: e.dma_start(out=dst_ap, in_=src_ap), writes=(res if isinstance(res, list) else [res]), dma=True)

    def load_x(self, xT, cT, b):
        S = self.S
        for c in range(KC):
            S.op("sp", lambda e, c=c: e.dma_start(out=self.X[:, c, :], in_=xT[b, c * 128:(c + 1) * 128, :]),
                 writes=self.rX[c], dma=True)
        S.op("sp", lambda e: e.dma_start(out=self.SV[:, :, :], in_=cT[b].rearrange("p (c t) -> p c t", t=2)),
             writes=[self.rSV], dma=True)
        S.op("act", lambda e: e.activation(out=self.SVB[:, :, :], in_=self.SV[:, :, :], func=AF.Silu),
             reads=[], writes=[self.rSV])

    def store_x(self, yT, cOut, b):
        S = self.S
        rs = []
        for c in range(KC):
            r = Res()
            S.op("sp", lambda e, c=c: e.dma_start(out=yT[b, c * 128:(c + 1) * 128, :], in_=self.X[:, c, L0:L0 + L]),
                 reads=self.rX[c], writes=[r], dma=True)
            r2 = Res()
            S.op("sp", lambda e, c=c: e.dma_start(out=cOut[b, c * 128:(c + 1) * 128, :], in_=self.X[:, c, C0:C0 + LC]),
                 reads=self.rX[c], writes=[r2], dma=True)
            rs += [r, r2]
        return rs

    def adaln(self, i, Wd):
        S = self.S
        off, _ = vec_offsets(i)
        S.op("sp", lambda e: e.dma_start(out=self.VEC[:, :], in_=Wd["vec"][:, :]), writes=[self.rVEC], dma=True)
        ada_w = Wd["ada_w"].rearrange("(c p) n -> p c n", p=128)
        for blk in range(12):
            k = self.wa_i
            self.wa_i = 1 - k
            wa, rwa = self.WA[k], self.rWA[k]
            self.S.op("pool", lambda e, wa=wa, blk=blk: e.dma_start(out=wa[:, :, :], in_=ada_w[:, :, blk * 512:(blk + 1) * 512]), writes=rwa, dma=True)
            ps, rps = self.psum()

            def mm(e, wa=wa, ps=ps):
                ins = None
                for f in range(4):
                    for kc in range(KC):
                        ins = e.matmul(ps[:, f * 2:f * 2 + 2], lhsT=wa[:, kc, f * 128:(f + 1) * 128],
                                       rhs=self.SVB[:, kc, :], start=(kc == 0), stop=(kc == KC - 1))
                return ins
            S.op("pe", mm, reads=rwa + [self.rSV], writes=[rps])
            ab = off["ada_b"] + blk * 4
            S.op("dve", lambda e, ps=ps, blk=blk, ab=ab: e.tensor_tensor(
                out=self.MOD[:, blk * 4:blk * 4 + 4, :], in0=ps[:, 0:8].rearrange("p (f t) -> p f t", t=2),
                in1=self.VEC[:, ab:ab + 4].unsqueeze(2).to_broadcast([128, 4, 2]), op=ALU.add),
                reads=[self.rVEC], writes=[rps, self.rMOD])
        for m, (nname, g) in enumerate((("norm_mix", 1), ("norm_ffn", 4))):
            no = off[nname]
            S.op("dve", lambda e, m=m, g=g: e.tensor_scalar_add(out=self.AM[:, m, :, :], in0=self.MOD[:, g * 8:(g + 1) * 8, :], scalar1=1.0),
                 reads=[self.rMOD], writes=[self.rAM])
            S.op("dve", lambda e, m=m, no=no: e.tensor_tensor(
                out=self.AM[:, m, :, :], in0=self.AM[:, m, :, :],
                in1=self.VEC[:, no:no + 8].unsqueeze(2).to_broadcast([128, 8, 2]), op=ALU.mult),
                reads=[self.rVEC, self.rAM], writes=[self.rAM])

    def modulate(self, m, tiles):
        S = self.S
        shg = 0 if m == 0 else 3
        for ti in tiles:
            c0, w = TILES[ti]
            col = 1 if ti == 0 else 0
            for c in range(KC):
                S.op("act", lambda e, c=c, c0=c0, w=w: e.activation(out=self.SQ[:, c, 0:w], in_=self.X[:, c, c0:c0 + w], func=AF.Square),
                     reads=[self.rX[c][ti]], writes=[self.rSQ])
            ps, rps = self.psum()

            def mm(e, ps=ps, w=w):
                ins = None
                for c in range(KC):
                    ins = e.matmul(ps[:, 0:w], lhsT=self.ONES[:, :], rhs=self.SQ[:, c, 0:w], start=(c == 0), stop=(c == KC - 1))
                return ins
            S.op("pe", mm, reads=[self.rSQ, self.rONES], writes=[rps])
            S.op("act", lambda e, ps=ps, w=w: e.activation(out=self.RSTD[:, 0:w], in_=ps[:, 0:w], func=AF.Sqrt, scale=1.0 / D, bias=self.EPSB[:, 0:1]),
                 reads=[self.rONES], writes=[rps, self.rRSTD])
            S.op("dve", lambda e, w=w: e.reciprocal(out=self.RSTD[:, 0:w], in_=self.RSTD[:, 0:w]), reads=[], writes=[self.rRSTD])
            for c in range(KC):
                tp, rtp = self.tmp()
                S.op("dve", lambda e, tp=tp, c=c, c0=c0, w=w: e.tensor_tensor(out=tp[:, 0:w], in0=self.X[:, c, c0:c0 + w], in1=self.RSTD[:, 0:w], op=ALU.mult),
                     reads=[self.rX[c][ti], self.rRSTD], writes=[rtp])
                S.op("act", lambda e, tp=tp, c=c, c0=c0, w=w, col=col: e.activation(
                    out=self.H[:, c, c0:c0 + w], in_=tp[:, 0:w], func=AF.Identity,
                    scale=self.AM[:, m, c, col:col + 1], bias=self.MOD[:, shg * 8 + c, col:col + 1]),
                    reads=[rtp, self.rAM, self.rMOD], writes=[self.rH[c][ti]])

    def ffn(self, i, Wd, tiles):
        S = self.S
        off, _ = vec_offsets(i)
        up = Wd["ffn_up"].rearrange("(c p) n -> p c n", p=128)
        down = Wd["ffn_down"]
        bo, wo = off["ffn_b_dw"], off["ffn_w_dw"]
        lo = TILES[tiles[0]][0]
        hi = TILES[tiles[-1]][0] + TILES[tiles[-1]][1]
        for (g0, gn) in ((0, 6), (6, 6), (12, 5), (17, 5)):
            wbv = self.WB[:, 0:gn * 1024].rearrange("p (c n) -> p c n", n=1024)
            self.load_w(wbv, down[g0 * 128:(g0 + gn) * 128, :].rearrange("(c p) n -> p c n", p=128), self.rWB)
            for hc in range(g0, g0 + gn):
                k = self.wa_i
                self.wa_i = 1 - k
                wa, rwa = self.WA[k], self.rWA[k]
                self.load_w(wa[:, :, 0:128], up[:, :, hc * 128:(hc + 1) * 128], rwa[0])
                self.load_w(wa[:, :, 128:256], up[:, :, D_FF + hc * 128:D_FF + (hc + 1) * 128], rwa[1])
                for ti in tiles:
                    c0, w = TILES[ti]
                    for half, (dst, rdst) in enumerate(((self.AROW, self.rA), (self.VROW, self.rV))):
                        ps, rps = self.psum()

                        def mm(e, ps=ps, wa=wa, half=half, c0=c0, w=w):
                            ins = None
                            for kc in range(KC):
                                ins = e.matmul(ps[:, 0:w], lhsT=wa[:, kc, half * 128:(half + 1) * 128],
                                               rhs=self.H[:, kc, c0:c0 + w], start=(kc == 0), stop=(kc == KC - 1))
                            return ins
                        S.op("pe", mm, reads=[rwa[half]] + [self.rH[kc][ti] for kc in range(KC)], writes=[rps])
                        S.op("act", lambda e, ps=ps, dst=dst, c0=c0, w=w: e.copy(out=dst[:, c0:c0 + w], in_=ps[:, 0:w]),
                             reads=[], writes=[rps, rdst])
                S.op("dve", lambda e, hc=hc: e.tensor_scalar(
                    out=self.CROW[:, lo:hi], in0=self.AROW[:, lo:hi], scalar1=self.VEC[:, wo + 22 + hc:wo + 22 + hc + 1],
                    scalar2=self.VEC[:, bo + hc:bo + hc + 1], op0=ALU.mult, op1=ALU.add),
                    reads=[self.rA, self.rVEC], writes=[self.rC])
                S.op("dve", lambda e, hc=hc: e.scalar_tensor_tensor(
                    out=self.CROW[:, lo:hi], in0=self.AROW[:, lo - 1:hi - 1], scalar=self.VEC[:, wo + hc:wo + hc + 1],
                    in1=self.CROW[:, lo:hi], op0=ALU.mult, op1=ALU.add),
                    reads=[self.rA, self.rVEC], writes=[self.rC])
                S.op("dve", lambda e, hc=hc: e.scalar_tensor_tensor(
                    out=self.CROW[:, lo:hi], in0=self.AROW[:, lo + 1:hi + 1], scalar=self.VEC[:, wo + 44 + hc:wo + 44 + hc + 1],
                    in1=self.CROW[:, lo:hi], op0=ALU.mult, op1=ALU.add),
                    reads=[self.rA, self.rVEC], writes=[self.rC])
                S.op("act", lambda e: e.activation(out=self.CROW[:, lo:hi], in_=self.CROW[:, lo:hi], func=AF.Silu),
                     reads=[], writes=[self.rC])
                gi = hc - g0
                S.op("dve", lambda e, gi=gi: e.tensor_tensor(out=self.G[:, gi, lo:hi], in0=self.CROW[:, lo:hi], in1=self.VROW[:, lo:hi], op=ALU.mult),
                     reads=[self.rC, self.rV], writes=[self.rG[gi]])
            for dc in range(KC):
                for ti in tiles:
                    c0, w = TILES[ti]
                    col = 1 if ti == 0 else 0
                    ps, rps = self.psum()

                    def mm(e, ps=ps, wbv=wbv, dc=dc, c0=c0, w=w, gn=gn):
                        ins = None
                        for gi in range(gn):
                            ins = e.matmul(ps[:, 0:w], lhsT=wbv[:, gi, dc * 128:(dc + 1) * 128], rhs=self.G[:, gi, c0:c0 + w],
                                           start=(gi == 0), stop=(gi == gn - 1))
                        return ins
                    S.op("pe", mm, reads=[self.rWB] + [self.rG[gi] for gi in range(gn)], writes=[rps])
                    S.op("dve", lambda e, ps=ps, dc=dc, c0=c0, w=w, col=col: e.scalar_tensor_tensor(
                        out=self.X[:, dc, c0:c0 + w], in0=ps[:, 0:w], scalar=self.MOD[:, 40 + dc, col:col + 1],
                        in1=self.X[:, dc, c0:c0 + w], op0=ALU.mult, op1=ALU.add),
                        reads=[self.rMOD], writes=[rps, self.rX[dc][ti]])

    def mla(self, i, Wd, need_ctx):
        S = self.S
        off, _ = vec_offsets(i)
        V = lambda name, k=0: self.VEC[:, off[name] + k:off[name] + k + 1]
        V96 = lambda name: self.VEC[0:96, off[name]:off[name] + 1]
        all_tiles = [0, 1, 2, 3, 4]
        qtiles = all_tiles if need_ctx else [1, 2, 3, 4]
        att_scale = float(96 ** -0.5)
        self.arena_reset()
        CQ = self.alloc(2 * T).rearrange("p (c t) -> p c t", t=T)
        CKV = self.alloc(T)
        KPE = self.alloc(T)
        ROPE = self.alloc(2 * T, F32).rearrange("p (c t) -> p c t", t=T)
        rCQ = [Res() for _ in TILES]
        rCKV = [Res() for _ in TILES]
        rKPE = [Res() for _ in TILES]
        rROPE = Res()
        SQh = self.alloc(512)
        rSQh = Res()
        mark = self.a_off
        S.op("sp", lambda e: e.dma_start(out=ROPE[:, :, :], in_=Wd["rope"][:, :, :]), writes=[rROPE], dma=True)
        WDQ = self.alloc(KC * 256).rearrange("p (c n) -> p c n", n=256)
        WKV = self.alloc(KC * 320).rearrange("p (c n) -> p c n", n=320)
        CQF = self.alloc(2 * 512, F32).rearrange("p (c n) -> p c n", n=512)
        SQb = self.alloc(2 * 512).rearrange("p (c n) -> p c n", n=512)
        rWDQ, rWKV, rCQF, rSQb = Res(), Res(), Res(), Res()
        self.load_w(WDQ, Wd["dq"].rearrange("(c p) n -> p c n", p=128), rWDQ)
        self.load_w(WKV, Wd["kv"].rearrange("(c p) n -> p c n", p=128), rWKV)

        def rope_norm(ps_raw, rps_raw, ps_perm, rps_perm, g, gp, out_ap, rout, lo, hi, c0, w):
            S.op("act", lambda e: e.activation(out=SQh[0:96, 0:w], in_=ps_raw[0:96, 0:w], func=AF.Square),
                 reads=[], writes=[rps_raw, rSQh])
            pss, rpss = self.psum()
            S.op("pe", lambda e: e.matmul(pss[0:96, 0:w], lhsT=self.BD[0:96, 0:96], rhs=SQh[0:96, 0:w], start=True, stop=True),
                 reads=[rSQh, self.rID], writes=[rpss])
            S.op("act", lambda e: e.activation(out=self.RSTD[0:96, 0:w], in_=pss[0:96, 0:w], func=AF.Sqrt,
                                               scale=V96("mla_invn"), bias=self.EPSB[0:96, 0:1]),
                 reads=[self.rVEC, self.rONES], writes=[rpss, self.rRSTD])
            S.op("dve", lambda e: e.reciprocal(out=self.RSTD[0:96, 0:w], in_=self.RSTD[0:96, 0:w]), reads=[], writes=[self.rRSTD])
            t1, rt1 = self.TMP[0], self.rTMP[0]
            t2, rt2 = self.TMP[1], self.rTMP[1]
            S.op("dve", lambda e: e.scalar_tensor_tensor(out=t1[0:96, 0:w], in0=ps_raw[0:96, 0:w], scalar=g, in1=ROPE[0:96, 0, c0:c0 + w],
                                                         op0=ALU.mult, op1=ALU.mult),
                 reads=[self.rVEC, rROPE], writes=[rps_raw, rt1])
            S.op("dve", lambda e: e.scalar_tensor_tensor(out=t2[0:96, 0:w], in0=ps_perm[0:96, 0:w], scalar=gp, in1=ROPE[0:96, 1, c0:c0 + w],
                                                         op0=ALU.mult, op1=ALU.mult),
                 reads=[self.rVEC, rROPE], writes=[rps_perm, rt2])
            S.op("pool", lambda e: e.tensor_tensor(out=t1[0:96, 0:w], in0=t1[0:96, 0:w], in1=t2[0:96, 0:w], op=ALU.add),
                 reads=[rt2], writes=[rt1])
            S.op("dve", lambda e: e.tensor_tensor(out=out_ap, in0=t1[lo:hi, 0:w], in1=self.RSTD[lo:hi, 0:w], op=ALU.mult),
                 reads=[rt1, self.rRSTD], writes=[rout])

        def p1_tile(ti):
            c0, w = TILES[ti]
            rh = [self.rH[kc][ti] for kc in range(KC)]
            for oc in range(2):
                ps, rps = self.psum()

                def mm(e, ps=ps, oc=oc):
                    ins = None
                    for kc in range(KC):
                        ins = e.matmul(ps[:, 0:w], lhsT=WDQ[:, kc, oc * 128:(oc + 1) * 128], rhs=self.H[:, kc, c0:c0 + w],
                                       start=(kc == 0), stop=(kc == KC - 1))
                    return ins
                S.op("pe", mm, reads=[rWDQ] + rh, writes=[rps])
                S.op("act", lambda e, ps=ps, oc=oc: e.copy(out=CQF[:, oc, 0:w], in_=ps[:, 0:w]), reads=[], writes=[rps, rCQF])
            S.op("act", lambda e: e.activation(out=SQb[:, :, 0:w], in_=CQF[:, :, 0:w], func=AF.Square), reads=[rCQF], writes=[rSQb])
            ps, rps = self.psum()

            def mm(e, ps=ps):
                e.matmul(ps[:, 0:w], lhsT=self.ONES[:, :], rhs=SQb[:, 0, 0:w], start=True, stop=False)
                return e.matmul(ps[:, 0:w], lhsT=self.ONES[:, :], rhs=SQb[:, 1, 0:w], start=False, stop=True)
            S.op("pe", mm, reads=[rSQb, self.rONES], writes=[rps])
            S.op("act", lambda e, ps=ps: e.activation(out=self.RSTD[:, 0:w], in_=ps[:, 0:w], func=AF.Sqrt, scale=1.0 / 256, bias=self.EPSB[:, 0:1]),
                 reads=[self.rONES], writes=[rps, self.rRSTD])
            S.op("dve", lambda e: e.reciprocal(out=self.RSTD[:, 0:w], in_=self.RSTD[:, 0:w]), reads=[], writes=[self.rRSTD])
            for oc in range(2):
                S.op("dve", lambda e, oc=oc: e.scalar_tensor_tensor(out=CQ[:, oc, c0:c0 + w], in0=CQF[:, oc, 0:w], scalar=V("mla_qn", oc),
                                                                 in1=self.RSTD[:, 0:w], op0=ALU.mult, op1=ALU.mult),
                     reads=[rCQF, self.rVEC, self.rRSTD], writes=[rCQ[ti]])
            ps, rps = self.psum()

            def mm(e, ps=ps):
                ins = None
                for kc in range(KC):
                    ins = e.matmul(ps[:, 0:w], lhsT=WKV[:, kc, 0:128], rhs=self.H[:, kc, c0:c0 + w], start=(kc == 0), stop=(kc == KC - 1))
                return ins
            S.op("pe", mm, reads=[rWKV] + rh, writes=[rps])
            S.op("act", lambda e, ps=ps: e.copy(out=CQF[:, 0, 0:w], in_=ps[:, 0:w]), reads=[], writes=[rps, rCQF])
            S.op("act", lambda e: e.activation(out=SQb[:, 0, 0:w], in_=CQF[:, 0, 0:w], func=AF.Square), reads=[rCQF], writes=[rSQb])
            ps, rps = self.psum()
            S.op("pe", lambda e, ps=ps: e.matmul(ps[:, 0:w], lhsT=self.ONES[:, :], rhs=SQb[:, 0, 0:w], start=True, stop=True),
                 reads=[rSQb, self.rONES], writes=[rps])
            S.op("act", lambda e, ps=ps: e.activation(out=self.RSTD[:, 0:w], in_=ps[:, 0:w], func=AF.Sqrt, scale=1.0 / 128, bias=self.EPSB[:, 0:1]),
                 reads=[self.rONES], writes=[rps, self.rRSTD])
            S.op("dve", lambda e: e.reciprocal(out=self.RSTD[:, 0:w], in_=self.RSTD[:, 0:w]), reads=[], writes=[self.rRSTD])
            S.op("dve", lambda e: e.scalar_tensor_tensor(out=CKV[:, c0:c0 + w], in0=CQF[:, 0, 0:w], scalar=V("mla_kvn"),
                                                         in1=self.RSTD[:, 0:w], op0=ALU.mult, op1=ALU.mult),
                 reads=[rCQF, self.rVEC, self.rRSTD], writes=[rCKV[ti]])
            pss = []
            for var in range(2):
                ps, rps = self.psum()
                pss.append((ps, rps))

                def mm(e, ps=ps, var=var):
                    ins = None
                    for kc in range(KC):
                        ins = e.matmul(ps[0:96, 0:w], lhsT=WKV[:, kc, 128 + var * 96:224 + var * 96], rhs=self.H[:, kc, c0:c0 + w],
                                       start=(kc == 0), stop=(kc == KC - 1))
                    return ins
                S.op("pe", mm, reads=[rWKV] + rh, writes=[rps])
            rope_norm(pss[0][0], pss[0][1], pss[1][0], pss[1][1], V96("mla_gk"), V96("mla_gkp"),
                      KPE[64:96, c0:c0 + w], rKPE[ti], 64, 96, c0, w)

        for ti in all_tiles:
            p1_tile(ti)
        if DBG_STOP == 1:
            S.barrier()
            return

        S.barrier()
        self.arena_reset(mark)
        WUKV = self.alloc(2048)
        rWUKV = Res()
        self.load_w(WUKV, Wd["ukv"][:, :], rWUKV)
        uq_v = Wd["uq"].rearrange("(c p) n -> p c n", p=128)
        WUQh = [self.alloc(2 * 192).rearrange("p (c n) -> p c n", n=192) for _ in range(2)]
        rWUQh = [Res(), Res()]
        QH = [self.alloc(T) for _ in range(2)]
        KH = [self.alloc(T) for _ in range(2)]
        rQH = [[Res() for _ in TILES] for _ in range(2)]
        rKH = [[Res() for _ in TILES] for _ in range(2)]
        rKHpe = [Res(), Res()]
        VH = [self.alloc(18 * 128).rearrange("p (j n) -> p j n", n=128) for _ in range(2)]
        rVH = [Res(), Res()]
        NPT = 6
        PT = [self.alloc(512) for _ in range(NPT)]
        rPT = [Res() for _ in range(NPT)]
        RD = self.MU
        rRD = self.rMU
        pt_i = 0
        S.op("pool", lambda e: e.memset(VH[0][:, :, 64:128], 1.0), writes=[rVH[0]])
        S.op("pool", lambda e: e.memset(VH[1][:, :, 0:64], 1.0), writes=[rVH[1]])

        def kcols(j):
            return (C0 + 128 * j) if j < 2 else (L0 + 128 * (j - 2))

        def ktile(j):
            return 0 if j < 2 else 1 + (j - 2) // 4

        def prep(h):
            s = h % 2
            voff = 0 if s == 0 else 64
            wq, rwq = WUQh[s], rWUQh[s]
            self.load_w(wq, uq_v[:, :, h * 192:(h + 1) * 192], rwq)
            for ti in qtiles:
                c0, w = TILES[ti]
                pss = []
                for var in range(2):
                    ps, rps = self.psum()
                    pss.append((ps, rps))

                    def mm(e, ps=ps, var=var, c0=c0, w=w):
                        e.matmul(ps[0:96, 0:w], lhsT=wq[:, 0, var * 96:var * 96 + 96], rhs=CQ[:, 0, c0:c0 + w], start=True, stop=False)
                        return e.matmul(ps[0:96, 0:w], lhsT=wq[:, 1, var * 96:var * 96 + 96], rhs=CQ[:, 1, c0:c0 + w], start=False, stop=True)
                    S.op("pe", mm, reads=[rwq, rCQ[ti]], writes=[rps])
                rope_norm(pss[0][0], pss[0][1], pss[1][0], pss[1][1], V96("mla_gq"), V96("mla_gqp"),
                          QH[s][0:96, c0:c0 + w], rQH[s][ti], 0, 96, c0, w)
            S.op("pool", lambda e: e.tensor_copy(out=KH[s][64:96, :], in_=KPE[64:96, :]), reads=rKPE, writes=[rKHpe[s]])
            for ti in all_tiles:
                c0, w = TILES[ti]
                ps, rps = self.psum()
                S.op("pe", lambda e, ps=ps, c0=c0, w=w: e.matmul(ps[0:64, 0:w], lhsT=WUKV[:, h * 128:h * 128 + 64], rhs=CKV[:, c0:c0 + w], start=True, stop=True),
                     reads=[rWUKV, rCKV[ti]], writes=[rps])
                S.op("act", lambda e, ps=ps, w=w: e.activation(out=SQh[0:64, 0:w], in_=ps[0:64, 0:w], func=AF.Square), reads=[], writes=[rps, rSQh])
                ps2, rps2 = self.psum()
                S.op("pe", lambda e, ps2=ps2, w=w: e.matmul(ps2[0:64, 0:w], lhsT=self.BD[0:64, 0:64], rhs=SQh[0:64, 0:w], start=True, stop=True),
                     reads=[rSQh, self.rID], writes=[rps2])
                S.op("act", lambda e, ps2=ps2, w=w: e.activation(out=self.RSTD[0:64, 0:w], in_=ps2[0:64, 0:w], func=AF.Sqrt, scale=1.0 / 64, bias=self.EPSB[0:64, 0:1]),
                     reads=[self.rONES], writes=[rps2, self.rRSTD])
                S.op("dve", lambda e, w=w: e.reciprocal(out=self.RSTD[0:64, 0:w], in_=self.RSTD[0:64, 0:w]), reads=[], writes=[self.rRSTD])
                S.op("dve", lambda e, ps=ps, c0=c0, w=w: e.scalar_tensor_tensor(out=KH[s][0:64, c0:c0 + w], in0=ps[0:64, 0:w], scalar=self.VEC[0:64, off["mla_gk"]:off["mla_gk"] + 1],
                                                                             in1=self.RSTD[0:64, 0:w], op0=ALU.mult, op1=ALU.mult),
                     reads=[self.rVEC, self.rRSTD], writes=[rps, rKH[s][ti]])
            for j0 in range(0, 18, 8):
                n = min(8, 18 - j0)
                ps, rps = self.psum()

                def mm(e, ps=ps, j0=j0, n=n):
                    ins = None
                    for jj in range(n):
                        kc0 = kcols(j0 + jj)
                        ins = e.matmul(ps[:, jj * 64:(jj + 1) * 64], lhsT=CKV[:, kc0:kc0 + 128], rhs=WUKV[:, h * 128 + 64:(h + 1) * 128], start=True, stop=True)
                    return ins
                S.op("pe", mm, reads=[rWUKV] + rCKV, writes=[rps])
                S.op("dve", lambda e, ps=ps, j0=j0, n=n: e.tensor_copy(out=VH[s][:, j0:j0 + n, voff:voff + 64], in_=ps[:, 0:n * 64].rearrange("p (j d) -> p j d", d=64)),
                     reads=[], writes=[rps, rVH[s]])

        def attend(h):
            nonlocal pt_i
            s = h % 2
            ch = h // 2
            items = []
            for ti in qtiles:
                keys = list(range(18)) if ti > 0 else [0, 1]
                for j in keys:
                    items.append((ti, j, j == keys[0], j == keys[-1]))
            LA = 2
            qk = {}

            def issue_qk(idx):
                ti, j, _, _ = items[idx]
                c0, w = TILES[ti]
                kc0 = kcols(j)
                ps, rps = self.psum()
                S.op("pe", lambda e, ps=ps: e.matmul(ps[:, 0:w], lhsT=KH[s][0:96, kc0:kc0 + 128], rhs=QH[s][0:96, c0:c0 + w], start=True, stop=True),
                     reads=[rKH[s][ktile(j)], rKHpe[s], rQH[s][ti]], writes=[rps])
                qk[idx] = (ps, rps)
            for idx in range(min(LA, len(items))):
                issue_qk(idx)
            pso, rpso = None, None
            for idx, (ti, j, first, lastk) in enumerate(items):
                c0, w = TILES[ti]
                if idx + LA < len(items):
                    issue_qk(idx + LA)
                if first:
                    pso, rpso = self.psum_acc()
                ps, rps = qk.pop(idx)
                pt, rpt = PT[pt_i], rPT[pt_i]
                pt_i = (pt_i + 1) % NPT
                S.op("act", lambda e, ps=ps, pt=pt, w=w: e.activation(out=pt[:, 0:w], in_=ps[:, 0:w], func=AF.Exp, scale=att_scale),
                     reads=[], writes=[rps, rpt])
                S.op("pe", lambda e, pt=pt, j=j, w=w, pso=pso, first=first, lastk=lastk: e.matmul(
                    pso[:, 0:w], lhsT=VH[s][:, j, :], rhs=pt[:, 0:w], start=first, stop=lastk),
                    reads=[rpt, rVH[s]], writes=[rpso])
                if lastk:
                    if s == 0:
                        S.op("dve", lambda e, pso=pso, w=w: e.reciprocal(out=RD[64:128, 0:w], in_=pso[64:128, 0:w]), reads=[], writes=[rpso, rRD])
                        S.op("dve", lambda e, pso=pso, c0=c0, w=w: e.tensor_tensor(out=self.H[0:64, ch, c0:c0 + w], in0=pso[0:64, 0:w], in1=RD[64:128, 0:w], op=ALU.mult),
                             reads=[rRD], writes=[rpso, self.rH[ch][ti]])
                    else:
                        S.op("dve", lambda e, pso=pso, w=w: e.reciprocal(out=RD[0:64, 0:w], in_=pso[0:64, 0:w]), reads=[], writes=[rpso, rRD])
                        S.op("dve", lambda e, pso=pso, c0=c0, w=w: e.tensor_tensor(out=self.H[64:128, ch, c0:c0 + w], in0=pso[64:128, 0:w], in1=RD[0:64, 0:w], op=ALU.mult),
                             reads=[rRD], writes=[rpso, self.rH[ch][ti]])

        NH = 16 if DBG_HEADS is None else DBG_HEADS
        prep(0)
        for h in range(NH):
            if h + 1 < NH:
                prep(h + 1)
            attend(h)
        if DBG_STOP == 2:
            S.barrier()
            return

        S.barrier()
        self.arena_reset()
        WO = self.alloc(KC * 1024).rearrange("p (c n) -> p c n", n=1024)
        rWO = Res()
        wo_v = Wd["wo"].rearrange("(c p) n -> p c n", p=128)
        rWOh = [Res(), Res()]
        for hh in range(2):
            self.load_w(WO[:, hh * 4:(hh + 1) * 4, :], wo_v[:, hh * 4:(hh + 1) * 4, :], rWOh[hh])
        for ti in qtiles:
            c0, w = TILES[ti]
            col = 1 if ti == 0 else 0
            for dc in range(KC):
                ps, rps = self.psum()

                def mm(e, ps=ps, dc=dc, c0=c0, w=w):
                    ins = None
                    for c in range(KC):
                        ins = e.matmul(ps[:, 0:w], lhsT=WO[:, c, dc * 128:(dc + 1) * 128], rhs=self.H[:, c, c0:c0 + w], start=(c == 0), stop=(c == KC - 1))
                    return ins
                S.op("pe", mm, reads=rWOh + [self.rH[c][ti] for c in range(KC)], writes=[rps])
                S.op("dve", lambda e, ps=ps, dc=dc, c0=c0, w=w, col=col: e.scalar_tensor_tensor(
                    out=self.X[:, dc, c0:c0 + w], in0=ps[:, 0:w], scalar=self.MOD[:, 16 + dc, col:col + 1],
                    in1=self.X[:, dc, c0:c0 + w], op0=ALU.mult, op1=ALU.add),
                    reads=[self.rMOD], writes=[rps, self.rX[dc][ti]])

    def hyena(self, i, Wd, b):
        S = self.S
        off, _ = vec_offsets(i)
        V = lambda name, k=0: self.VEC[:, off[name] + k:off[name] + k + 1]
        V64 = lambda name: self.VEC[0:64, off[name]:off[name] + 1]
        tiles = [0, 1, 2, 3, 4]
        lo, hi = C0, L0 + L
        PI = float(np.pi)
        seqs = [("l", L, 16, L0, 0), ("c", LC, 2, C0, 16)]
        self.arena_reset()
        VX = self.alloc(KC * T).rearrange("p (c t) -> p c t", t=T)
        rVX = [Res() for _ in range(KC)]
        AROW = self.alloc(T)
        CROW = self.alloc(T, F32)
        X1ROW = self.alloc(T, F32)
        X0ROW = self.alloc(T)
        rA, rC, rX1, rX0R = Res(), Res(), Res(), Res()
        WA = [self.alloc(KC * 128).rearrange("p (c n) -> p c n", n=128) for _ in range(2)]
        rWA = [Res(), Res()]
        wa_i = [0]
        S.op("pool", lambda e: e.memset(AROW[:, :], 0.0), writes=[rA])
        S.op("pool", lambda e: e.memset(X0ROW[:, :], 0.0), writes=[rX0R])
        w_in = Wd["hy_in"].rearrange("(c p) n -> p c n", p=128)
        bi, wsh, bsh = off["hy_b_in"], off["hy_w_short"], off["hy_b_short"]

        def proj_conv(oc, dst, rdst):
            k = wa_i[0]
            wa_i[0] = 1 - k
            wa, rwa = WA[k], rWA[k]
            self.load_w(wa, w_in[:, :, oc * 128:(oc + 1) * 128], rwa)
            for ti in tiles:
                c0, w = TILES[ti]
                ps, rps = self.psum()

                def mm(e, ps=ps, c0=c0, w=w):
                    ins = None
                    for kc in range(KC):
                        ins = e.matmul(ps[:, 0:w], lhsT=wa[:, kc, :], rhs=self.H[:, kc, c0:c0 + w], start=(kc == 0), stop=(kc == KC - 1))
                    return ins
                S.op("pe", mm, reads=[rwa] + [self.rH[kc][ti] for kc in range(KC)], writes=[rps])
                S.op("act", lambda e, ps=ps, c0=c0, w=w: e.activation(out=AROW[:, c0:c0 + w], in_=ps[:, 0:w], func=AF.Identity,
                                                                   bias=self.VEC[:, bi + oc:bi + oc + 1], scale=1.0),
                     reads=[self.rVEC], writes=[rps, rA])
            S.op("dve", lambda e: e.tensor_scalar(out=CROW[:, lo:hi], in0=AROW[:, lo:hi], scalar1=self.VEC[:, wsh + 24 + oc:wsh + 25 + oc],
                                                  scalar2=self.VEC[:, bsh + oc:bsh + oc + 1], op0=ALU.mult, op1=ALU.add),
                 reads=[rA, self.rVEC], writes=[rC])
            S.op("dve", lambda e: e.scalar_tensor_tensor(out=CROW[:, lo:hi], in0=AROW[:, lo - 1:hi - 1], scalar=self.VEC[:, wsh + oc:wsh + oc + 1],
                                                         in1=CROW[:, lo:hi], op0=ALU.mult, op1=ALU.add),
                 reads=[rA, self.rVEC], writes=[rC])
            S.op("dve", lambda e: e.scalar_tensor_tensor(out=dst[:, lo:hi], in0=AROW[:, lo + 1:hi + 1], scalar=self.VEC[:, wsh + 48 + oc:wsh + 49 + oc],
                                                         in1=CROW[:, lo:hi], op0=ALU.mult, op1=ALU.add),
                 reads=[rA, self.rVEC, rC], writes=[rdst])

        x0d = Wd["x0d"]
        for c in range(KC):
            proj_conv(c, X0ROW, rX0R)
            S.op("sp", lambda e, c=c: e.dma_start(out=x0d[c * 128:(c + 1) * 128, :], in_=X0ROW[:, :]),
                 reads=[rX0R], writes=[self.rX0D[c]], dma=True)
            proj_conv(KC + c, X1ROW, rX1)
            proj_conv(2 * KC + c, CROW, rC)
            S.op("dve", lambda e, c=c: e.tensor_tensor(out=VX[:, c, lo:hi], in0=CROW[:, lo:hi], in1=X1ROW[:, lo:hi], op=ALU.mult),
                 reads=[rC, rX1], writes=[rVX[c]])
        VT = self.H[:, :, :].rearrange("p c t -> p (c t)")[:, 0:18 * D].rearrange("p (j n) -> p j n", n=D)
        rVT = Res()
        allH = [r for row in self.rH for r in row]
        for (tag, Lx, nt, col0, vt0) in seqs:
            for jt in range(nt):
                ps, rps = self.psum()
                psb = ps[:, :].bitcast(BF16)

                def tr(e, psb=psb, jt=jt, col0=col0):
                    ins = None
                    for c in range(KC):
                        ins = e.transpose(out=psb[:, c * 128:(c + 1) * 128], in_=VX[:, c, col0 + jt * 128:col0 + (jt + 1) * 128], identity=self.IDENT[:, :])
                    return ins
                S.op("pe", tr, reads=rVX + [self.rID], writes=[rps])
                S.op("act", lambda e, psb=psb, jt=jt, vt0=vt0: e.copy(out=VT[:, vt0 + jt, :], in_=psb[:, 0:D]),
                     reads=[], writes=[rps, rVT] + (allH if (jt == 0 and tag == "l") else []))
        S.barrier()
        if b == 0:
            for (tag, Lx, nt, col0, vt0) in seqs:
                self.hy_filter(i, Wd, tag, Lx, nt)
                S.barrier()
        w_out = Wd["hy_out"]
        for q in range(4):
            self.arena_reset()
            X0q = self.alloc(2 * T).rearrange("p (c t) -> p c t", t=T)
            rX0q = Res()
            WOq = self.alloc(2 * D).rearrange("p (c n) -> p c n", n=D)
            rWOq = Res()
            S.op("sp", lambda e, q=q: e.dma_start(out=X0q[:, :, :], in_=x0d[q * 256:(q + 1) * 256, :].rearrange("(c p) t -> p c t", p=128)),
                 reads=self.rX0D, writes=[rX0q], dma=True)
            self.load_w(WOq, w_out[q * 256:(q + 1) * 256, :].rearrange("(c p) n -> p c n", p=128), rWOq)
            Z = self.alloc(2 * 256).rearrange("p (c n) -> p c n", n=256)
            rZ = Res()
            KSb = [self.alloc(2 * 256, F32).rearrange("p (c n) -> p c n", n=256) for _ in range(2)]
            rKSb = [Res(), Res()]
            SC = [self.alloc(256, F32) for _ in range(2)]
            rSC = [Res(), Res()]
            mark = self.a_off
            for (tag, Lx, nt, col0, vt0) in seqs:
                S.barrier()
                self.arena_reset(mark)
                self.hy_conv(i, Wd, q, tag, Lx, nt, col0, vt0, VT, rVT, X0q, rX0q, WOq, rWOq, Z, rZ, KSb, rKSb, SC, rSC)
            S.barrier()

    def hy_filter(self, i, Wd, tag, Lx, nt):
        S = self.S
        off, _ = vec_offsets(i)
        V64 = lambda name: self.VEC[0:64, off[name]:off[name] + 1]
        PI = float(np.pi)
        self.arena_reset()
        ZT = self.alloc(Lx, F32)
        HD1 = self.alloc(Lx, F32)
        HD2 = self.alloc(Lx, F32)
        FW1 = self.alloc(64, F32)
        FW2 = self.alloc(64, F32)
        NEGT = self.alloc(16, F32)
        CST = self.alloc(8, F32)
        ONESF = self.alloc(128, F32)
        rZT, rHD1, rHD2, rFW, rNEGT, rCST, rONESF = (Res() for _ in range(7))
        S.op("sp", lambda e: e.dma_start(out=ZT[0:33, :], in_=Wd[f"z_{tag}"][:, :]), writes=[rZT], dma=True)
        S.op("sp", lambda e: e.dma_start(out=FW1[0:33, :], in_=Wd["fw1"][:, :]), writes=[rFW], dma=True)
        S.op("sp", lambda e: e.dma_start(out=FW2[0:64, :], in_=Wd["fw2"][:, :]), writes=[rFW], dma=True)
        S.op("sp", lambda e: e.dma_start(out=NEGT[:, 0:nt], in_=Wd[f"negt_{tag}"][:, :]), writes=[rNEGT], dma=True)
        S.op("pool", lambda e: e.memset(ONESF[:, :], 1.0), writes=[rONESF])
        S.op("pool", lambda e: e.memset(CST[:, 2:3], -PI), writes=[rCST])
        S.op("dve", lambda e: e.tensor_tensor(out=CST[0:64, 0:1], in0=V64("hy_fb1"), in1=V64("hy_fr"), op=ALU.mult), reads=[self.rVEC], writes=[rCST])
        S.op("dve", lambda e: e.tensor_tensor(out=CST[0:64, 1:2], in0=V64("hy_fb2"), in1=V64("hy_fr"), op=ALU.mult), reads=[self.rVEC], writes=[rCST])

        RR, rRR = self.MU, self.rMU

        def sin_layer(src, rsrc, kdim, wmat, cstcol, dst, rdst):
            for t0 in range(0, Lx, 512):
                w = min(512, Lx - t0)
                ps, rps = self.psum()
                S.op("pe", lambda e, ps=ps, t0=t0, w=w: e.matmul(ps[0:64, 0:w], lhsT=wmat[0:kdim, 0:64], rhs=src[0:kdim, t0:t0 + w], start=True, stop=True),
                     reads=[rFW, rsrc], writes=[rps])
                S.op("act", lambda e, ps=ps, t0=t0, w=w: e.activation(out=dst[0:64, t0:t0 + w], in_=ps[0:64, 0:w], func=AF.Identity,
                                                                   scale=V64("hy_fr"), bias=CST[0:64, cstcol:cstcol + 1]),
                     reads=[self.rVEC, rCST], writes=[rps, rdst])
                MAGIC = 12582912.0
                S.op("dve", lambda e, t0=t0, w=w: e.tensor_scalar(out=RR[0:64, 0:w], in0=dst[0:64, t0:t0 + w], scalar1=1.0 / (2.0 * PI), scalar2=MAGIC,
                                                                op0=ALU.mult, op1=ALU.add),
                     reads=[rdst], writes=[rRR])
                S.op("dve", lambda e, w=w: e.tensor_scalar(out=RR[0:64, 0:w], in0=RR[0:64, 0:w], scalar1=MAGIC, scalar2=-2.0 * PI,
                                                         op0=ALU.subtract, op1=ALU.mult),
                     reads=[], writes=[rRR])
                S.op("dve", lambda e, t0=t0, w=w: e.tensor_tensor(out=dst[0:64, t0:t0 + w], in0=dst[0:64, t0:t0 + w], in1=RR[0:64, 0:w], op=ALU.add),
                     reads=[rRR], writes=[rdst])
                S.op("dve", lambda e, t0=t0, w=w: e.tensor_scalar(out=dst[0:64, t0:t0 + w], in0=dst[0:64, t0:t0 + w], scalar1=-PI, scalar2=PI,
                                                                op0=ALU.max, op1=ALU.min),
                     reads=[], writes=[rdst])
                S.op("act", lambda e, t0=t0, w=w: e.activation(out=dst[0:64, t0:t0 + w], in_=dst[0:64, t0:t0 + w], func=AF.Sin, scale=1.0),
                     reads=[], writes=[rdst])
        sin_layer(ZT, rZT, 33, FW1, 0, HD1, rHD1)
        sin_layer(HD1, rHD1, 64, FW2, 1, HD2, rHD2)
        mark = self.a_off
        kspec = Wd[f"kspec_{tag}"]
        dfm = {0: Wd[f"dfc_{tag}"].rearrange("(j p) f -> p j f", p=128), 1: Wd[f"dfs_{tag}"].rearrange("(j p) f -> p j f", p=128)}
        nf = nt
        for q in range(4):
            S.barrier()
            self.arena_reset(mark)
            A = self.alloc(nt * 256).rearrange("p (j n) -> p j n", n=256)
            Bm = self.alloc(nt * 256).rearrange("p (j n) -> p j n", n=256)
            rAB = Res()
            FW3 = self.alloc(512, F32)
            ABSD = self.alloc(256, F32)
            SKq = self.alloc(256, F32)
            RN = self.alloc(256, F32)
            rFW3, rABSD, rSK, rRN = Res(), Res(), Res(), Res()
            DEC = self.alloc(256, F32)
            KF = self.alloc(256, F32)
            KB = self.alloc(256, F32)
            AF_ = self.alloc(256, F32)
            AB_ = self.alloc(256, F32)
            rDEC, rKF, rKB, rAF, rABb = (Res() for _ in range(5))
            DF = [self.alloc(nt * 128).rearrange("p (j n) -> p j n", n=128) for _ in range(2)]
            rDF = [Res(), Res()]
            KS = [self.alloc(256, F32) for _ in range(2)]
            rKSo = [Res(), Res()]
            S.op("sp", lambda e, q=q: e.dma_start(out=FW3[0:64, 0:256], in_=Wd["fw3"][:, q * 256:(q + 1) * 256]), writes=[rFW3], dma=True)
            S.op("sp", lambda e, q=q: e.dma_start(out=FW3[0:64, 256:512], in_=Wd["fw3"][:, D + q * 256:D + (q + 1) * 256]), writes=[rFW3], dma=True)
            S.op("sp", lambda e, q=q: e.dma_start(out=ABSD[:, :], in_=Wd["absd"][:, q * 256:(q + 1) * 256]), writes=[rABSD], dma=True)
            S.op("sp", lambda e, q=q: e.dma_start(out=SKq[:, :], in_=Wd["skipb"][:, q * 256:(q + 1) * 256]), writes=[rSK], dma=True)
            psS, rpsS = self.psum_acc()
            for j in range(nt):
                ps, rps = self.psum()
                S.op("pe", lambda e, ps=ps, j=j: e.matmul(ps[:, 0:512], lhsT=HD2[0:64, j * 128:(j + 1) * 128], rhs=FW3[0:64, 0:512], start=True, stop=True),
                     reads=[rHD2, rFW3], writes=[rps])
                S.op("act", lambda e, j=j: e.activation(out=DEC[:, :], in_=ABSD[:, :], func=AF.Exp, scale=NEGT[:, j:j + 1]),
                     reads=[rABSD, rNEGT], writes=[rDEC])
                S.op("dve", lambda e, ps=ps: e.tensor_tensor(out=KF[:, :], in0=ps[:, 0:256], in1=DEC[:, :], op=ALU.mult), reads=[rDEC], writes=[rps, rKF])
                S.op("dve", lambda e, ps=ps: e.tensor_tensor(out=KB[:, :], in0=ps[:, 256:512], in1=DEC[:, :], op=ALU.mult), reads=[rDEC], writes=[rps, rKB])
                if j == 0:
                    S.op("pool", lambda e: e.memset(KB[0:1, :], 0.0), writes=[rKB])
                S.op("pool", lambda e, j=j: e.tensor_tensor(out=A[:, j, :], in0=KF[:, :], in1=KB[:, :], op=ALU.add), reads=[rKF, rKB], writes=[rAB])
                S.op("pool", lambda e, j=j: e.tensor_tensor(out=Bm[:, j, :], in0=KF[:, :], in1=KB[:, :], op=ALU.subtract), reads=[rKF, rKB], writes=[rAB])
                S.op("act", lambda e: e.activation(out=AF_[:, :], in_=KF[:, :], func=AF.Abs), reads=[rKF], writes=[rAF])
                S.op("act", lambda e: e.activation(out=AB_[:, :], in_=KB[:, :], func=AF.Abs), reads=[rKB], writes=[rABb])
                S.op("pe", lambda e, j=j: e.matmul(psS[:, 0:256], lhsT=ONESF[:, :], rhs=AF_[:, :], start=(j == 0), stop=False), reads=[rONESF, rAF], writes=[rpsS])
                S.op("pe", lambda e, j=j: e.matmul(psS[:, 0:256], lhsT=ONESF[:, :], rhs=AB_[:, :], start=False, stop=(j == nt - 1)), reads=[rONESF, rABb], writes=[rpsS])
            S.op("dve", lambda e: e.tensor_scalar_add(out=RN[:, :], in0=psS[:, 0:256], scalar1=EPS), reads=[], writes=[rpsS, rRN])
            S.op("dve", lambda e: e.reciprocal(out=RN[:, :], in_=RN[:, :]), reads=[], writes=[rRN])
            for fb in range(nf):
                for typ in range(2):
                    S.op("sp", lambda e, fb=fb, typ=typ: e.dma_start(out=DF[typ][:, :, :], in_=dfm[typ][:, :, fb * 128:(fb + 1) * 128]), writes=[rDF[typ]], dma=True)
                    ps, rps = self.psum()
                    src = A if typ == 0 else Bm

                    def mm(e, ps=ps, typ=typ, src=src):
                        ins = None
                        for j in range(nt):
                            ins = e.matmul(ps[:, 0:256], lhsT=DF[typ][:, j, :], rhs=src[:, j, :], start=(j == 0), stop=(j == nt - 1))
                        return ins
                    S.op("pe", mm, reads=[rDF[typ], rAB], writes=[rps])
                    S.op("dve", lambda e, ps=ps, typ=typ: e.tensor_tensor(out=KS[typ][:, :], in0=ps[:, 0:256], in1=RN[:, :], op=ALU.mult), reads=[rRN], writes=[rps, rKSo[typ]])
                    if typ == 0:
                        S.op("pool", lambda e: e.tensor_tensor(out=KS[0][:, :], in0=KS[0][:, :], in1=SKq[:, :], op=ALU.add), reads=[rSK], writes=[rKSo[0]])
                    S.op("sp", lambda e, fb=fb, typ=typ, q=q: e.dma_start(out=kspec[typ, fb * 128:(fb + 1) * 128, q * 256:(q + 1) * 256], in_=KS[typ][:, :]),
                         reads=[rKSo[typ]], writes=[self.rKS[tag]], dma=True)

    def hy_conv(self, i, Wd, q, tag, Lx, nt, col0, vt0, VT, rVT, X0q, rX0q, WOq, rWOq, Z, rZ, KSb, rKSb, SC, rSC):
        S = self.S
        off, _ = vec_offsets(i)
        nf = nt
        kspec = Wd[f"kspec_{tag}"]
        dfm = {0: Wd[f"dfc_{tag}"].rearrange("(j p) f -> p j f", p=128), 1: Wd[f"dfs_{tag}"].rearrange("(j p) f -> p j f", p=128)}
        idm = {0: Wd[f"idc_{tag}"].rearrange("(j p) t -> p j t", p=128), 1: Wd[f"ids_{tag}"].rearrange("(j p) t -> p j t", p=128)}
        Y = self.alloc(nf * 512).rearrange("p (f y n) -> p f y n", y=2, n=256)
        rY = Res()
        DF = [self.alloc(nt * 128).rearrange("p (j n) -> p j n", n=128) for _ in range(2)]
        rDF = [Res(), Res()]
        IB = [self.alloc(2 * nf * 256).rearrange("p (y f n) -> p y f n", y=2, n=256) for _ in range(2)]
        rIB = [Res(), Res()]
        U = [self.TMP[0], self.TMP[1]]
        rU = [self.rTMP[0], self.rTMP[1]]
        col = 1 if tag == "c" else 0
        bo = off["hy_b_out"]
        for fb in range(nf):
            kb = fb % 2
            S.op("sp", lambda e, fb=fb, kb=kb: e.dma_start(out=KSb[kb][:, :, :], in_=kspec[:, fb * 128:(fb + 1) * 128, q * 256:(q + 1) * 256].rearrange("y p n -> p y n")),
                 reads=[self.rKS[tag]], writes=[rKSb[kb]], dma=True)
            pss = []
            for typ in range(2):
                S.op("sp", lambda e, fb=fb, typ=typ: e.dma_start(out=DF[typ][:, :, :], in_=dfm[typ][:, :, fb * 128:(fb + 1) * 128]), writes=[rDF[typ]], dma=True)
                ps, rps = self.psum()
                pss.append((ps, rps))

                def mm(e, ps=ps, typ=typ):
                    ins = None
                    for j in range(nt):
                        ins = e.matmul(ps[:, 0:256], lhsT=DF[typ][:, j, :], rhs=VT[:, vt0 + j, q * 256:(q + 1) * 256], start=(j == 0), stop=(j == nt - 1))
                    return ins
                S.op("pe", mm, reads=[rDF[typ], rVT], writes=[rps])
                S.op("act", lambda e, ps=ps, typ=typ: e.copy(out=U[typ][:, 0:256], in_=ps[:, 0:256]), reads=[], writes=[rps, rU[typ]])
            Kc, Ks = KSb[kb][:, 0, :], KSb[kb][:, 1, :]
            S.op("dve", lambda e, Kc=Kc: e.tensor_tensor(out=SC[0][:, :], in0=U[0][:, 0:256], in1=Kc, op=ALU.mult), reads=[rU[0], rKSb[kb]], writes=[rSC[0]])
            S.op("pool", lambda e, Ks=Ks: e.tensor_tensor(out=SC[1][:, :], in0=U[1][:, 0:256], in1=Ks, op=ALU.mult), reads=[rU[1], rKSb[kb]], writes=[rSC[1]])
            S.op("dve", lambda e, fb=fb: e.tensor_tensor(out=Y[:, fb, 0, :], in0=SC[0][:, :], in1=SC[1][:, :], op=ALU.subtract), reads=[rSC[0], rSC[1]], writes=[rY])
            S.op("dve", lambda e, Ks=Ks: e.tensor_tensor(out=SC[0][:, :], in0=U[0][:, 0:256], in1=Ks, op=ALU.mult), reads=[rU[0], rKSb[kb]], writes=[rSC[0]])
            S.op("pool", lambda e, Kc=Kc: e.tensor_tensor(out=SC[1][:, :], in0=U[1][:, 0:256], in1=Kc, op=ALU.mult), reads=[rU[1], rKSb[kb]], writes=[rSC[1]])
            S.op("dve", lambda e, fb=fb: e.tensor_tensor(out=Y[:, fb, 1, :], in0=SC[0][:, :], in1=SC[1][:, :], op=ALU.add), reads=[rSC[0], rSC[1]], writes=[rY])
        for ts in range(Lx // 256):
            ib, rib = IB[ts % 2], rIB[ts % 2]
            for typ in range(2):
                S.op("sp", lambda e, ib=ib, typ=typ, ts=ts: e.dma_start(out=ib[:, typ, :, :], in_=idm[typ][:, :, ts * 256:(ts + 1) * 256]), writes=[rib], dma=True)
            cs = col0 + ts * 256
            ti = 0 if tag == "c" else 1 + (ts // 2)
            for cc in range(2):
                ps, rps = self.psum()

                def mm(e, ps=ps, ib=ib, cc=cc):
                    ins = None
                    for fb in range(nf):
                        for typ in range(2):
                            ins = e.matmul(ps[:, 0:256], lhsT=Y[:, fb, typ, cc * 128:(cc + 1) * 128], rhs=ib[:, typ, fb, :],
                                           start=(fb == 0 and typ == 0), stop=(fb == nf - 1 and typ == 1))
                    return ins
                S.op("pe", mm, reads=[rY, rib], writes=[rps])
                S.op("dve", lambda e, ps=ps, cc=cc, cs=cs: e.tensor_tensor(out=Z[:, cc, :], in0=ps[:, 0:256], in1=X0q[:, cc, cs:cs + 256], op=ALU.mult),
                     reads=[rX0q], writes=[rps, rZ])
            for o in range(KC):
                ps, rps = self.psum()

                def mm2(e, ps=ps, o=o):
                    e.matmul(ps[:, 0:256], lhsT=WOq[:, 0, o * 128:(o + 1) * 128], rhs=Z[:, 0, :], start=True, stop=False)
                    return e.matmul(ps[:, 0:256], lhsT=WOq[:, 1, o * 128:(o + 1) * 128], rhs=Z[:, 1, :], start=False, stop=True)
                S.op("pe", mm2, reads=[rWOq, rZ], writes=[rps])
                if q == 0:
                    S.op("dve", lambda e, ps=ps, o=o: e.tensor_scalar(out=self.RSTD[:, 0:256], in0=ps[:, 0:256], scalar1=self.VEC[:, bo + o:bo + o + 1],
                                                                   scalar2=self.MOD[:, 16 + o, col:col + 1], op0=ALU.add, op1=ALU.mult),
                         reads=[self.rVEC, self.rMOD], writes=[rps, self.rRSTD])
                    S.op("dve", lambda e, o=o, cs=cs: e.tensor_tensor(out=self.X[:, o, cs:cs + 256], in0=self.X[:, o, cs:cs + 256], in1=self.RSTD[:, 0:256], op=ALU.add),
                         reads=[self.rRSTD], writes=[self.rX[o][ti]])
                else:
                    S.op("dve", lambda e, ps=ps, o=o, cs=cs: e.scalar_tensor_tensor(out=self.X[:, o, cs:cs + 256], in0=ps[:, 0:256], scalar=self.MOD[:, 16 + o, col:col + 1],
                                                                                  in1=self.X[:, o, cs:cs + 256], op0=ALU.mult, op1=ALU.add),
                         reads=[self.rMOD], writes=[rps, self.rX[o][ti]])

    def layer(self, i, Wd, b, last):
        kind = i % 3
        tiles_all = [0, 1, 2, 3, 4]
        tiles_out = [1, 2, 3, 4] if last else tiles_all
        self.adaln(i, Wd)
        if kind == 0:
            self.modulate(0, tiles_all)
            self.S.barrier()
            self.mla(i, Wd, need_ctx=not last)
            self.S.barrier()
            self.carve_ffn()
            self.zero_pads()
        if kind == 2:
            self.modulate(0, tiles_all)
            self.S.barrier()
            self.hyena(i, Wd, b)
            self.S.barrier()
            self.carve_ffn()
            self.zero_pads()
        if kind == 1:
            self.modulate(0, tiles_out)
            self.conformer(i, Wd, tiles_out)
        self.modulate(1, tiles_out)
        self.ffn(i, Wd, tiles_out)

    def ubuf(self, c):
        if c < 6:
            return self.G[:, c, :], self.rG[c]
        return (self.AROW, self.rA) if c == 6 else (self.VROW, self.rV)

    def conformer(self, i, Wd, tiles):
        S = self.S
        off, _ = vec_offsets(i)
        w1 = Wd["cf_w1"].rearrange("(c p) n -> p c n", p=128)
        w2 = Wd["cf_w2"].rearrange("(c p) n -> p c n", p=128)
        b1o, wdo, bdo, lgo, lbo, b2o = (off[k] for k in ("cf_b_pw1", "cf_w_dw", "cf_b_dw", "cf_ln_g", "cf_ln_b", "cf_b_pw2"))
        for (a, bnd) in ((0, 16), (C0 + LC, L0), (L0 + L, T)):
            S.op("pool", lambda e, a=a, bnd=bnd: e.memset(self.G[:, :, a:bnd], 0.0), writes=self.rG)
        for c in range(KC):
            k = self.wa_i
            self.wa_i = 1 - k
            wa, rwa = self.WA[k], self.rWA[k]
            self.load_w(wa[:, :, 0:128], w1[:, :, c * 128:(c + 1) * 128], rwa[0])
            self.load_w(wa[:, :, 128:256], w1[:, :, D + c * 128:D + (c + 1) * 128], rwa[1])
            ub, rub = self.ubuf(c)
            for ti in tiles:
                c0, w = TILES[ti]
                pss = []
                for half in range(2):
                    ps, rps = self.psum()
                    pss.append((ps, rps))

                    def mm(e, ps=ps, wa=wa, half=half, c0=c0, w=w):
                        ins = None
                        for kc in range(KC):
                            ins = e.matmul(ps[:, 0:w], lhsT=wa[:, kc, half * 128:(half + 1) * 128],
                                           rhs=self.H[:, kc, c0:c0 + w], start=(kc == 0), stop=(kc == KC - 1))
                        return ins
                    S.op("pe", mm, reads=[rwa[half]] + [self.rH[kc][ti] for kc in range(KC)], writes=[rps])
                tp, rtp = self.tmp()
                S.op("act", lambda e, ps=pss[1][0], tp=tp, w=w, c=c: e.activation(
                    out=tp[:, 0:w], in_=ps[:, 0:w], func=AF.Sigmoid, bias=self.VEC[:, b1o + 8 + c:b1o + 9 + c], scale=1.0),
                    reads=[self.rVEC], writes=[pss[1][1], rtp])
                S.op("dve", lambda e, ps=pss[0][0], tp=tp, ub=ub, c0=c0, w=w, c=c: e.scalar_tensor_tensor(
                    out=ub[:, c0:c0 + w], in0=ps[:, 0:w], scalar=self.VEC[:, b1o + c:b1o + c + 1], in1=tp[:, 0:w],
                    op0=ALU.add, op1=ALU.mult),
                    reads=[self.rVEC, rtp], writes=[pss[0][1], rub])
        dg = self.WB[:, 0:31 * 128].rearrange("p (k n) -> p k n", n=128)
        for c in range(KC):
            ub, rub = self.ubuf(c)
            for kk in range(31):
                S.op("pool" if kk % 2 else "dve", lambda e, kk=kk, c=c: e.tensor_scalar_mul(
                    out=dg[:, kk, :], in0=self.IDENT[:, :], scalar1=self.VEC[:, wdo + kk * 8 + c:wdo + kk * 8 + c + 1]),
                    reads=[self.rID, self.rVEC], writes=[self.rWB])
            for ti in tiles:
                c0, w = TILES[ti]
                ps, rps = self.psum()

                def mm(e, ps=ps, ub=ub, c0=c0, w=w):
                    ins = None
                    for kk in range(31):
                        ins = e.matmul(ps[:, 0:w], lhsT=dg[:, kk, :], rhs=ub[:, c0 + kk - 15:c0 + kk - 15 + w],
                                       start=(kk == 0), stop=(kk == 30))
                    return ins
                S.op("pe", mm, reads=[self.rWB, rub], writes=[rps])
                S.op("act", lambda e, ps=ps, c=c, c0=c0, w=w: e.activation(
                    out=self.H[:, c, c0:c0 + w], in_=ps[:, 0:w], func=AF.Identity, bias=self.VEC[:, bdo + c:bdo + c + 1], scale=1.0),
                    reads=[self.rVEC], writes=[rps, self.rH[c][ti]])
        for k in range(2):
            self.load_w(self.WA[k][:, :, :], w2[:, :, k * 512:(k + 1) * 512], self.rWA[k])
        for ti in tiles:
            c0, w = TILES[ti]
            col = 1 if ti == 0 else 0
            for c in range(KC):
                S.op("act", lambda e, c=c, c0=c0, w=w: e.activation(out=self.SQ[:, c, 0:w], in_=self.H[:, c, c0:c0 + w], func=AF.Square),
                     reads=[self.rH[c][ti]], writes=[self.rSQ])
            ps1, rps1 = self.psum()
            ps2, rps2 = self.psum()

            def mm1(e, ps=ps1, c0=c0, w=w):
                ins = None
                for c in range(KC):
                    ins = e.matmul(ps[:, 0:w], lhsT=self.ONES[:, :], rhs=self.H[:, c, c0:c0 + w], start=(c == 0), stop=(c == KC - 1))
                return ins
            S.op("pe", mm1, reads=[self.rONES] + [self.rH[c][ti] for c in range(KC)], writes=[rps1])

            def mm2(e, ps=ps2, w=w):
                ins = None
                for c in range(KC):
                    ins = e.matmul(ps[:, 0:w], lhsT=self.ONES[:, :], rhs=self.SQ[:, c, 0:w], start=(c == 0), stop=(c == KC - 1))
                return ins
            S.op("pe", mm2, reads=[self.rONES, self.rSQ], writes=[rps2])
            S.op("act", lambda e, ps=ps1, w=w: e.activation(out=self.MU[:, 0:w], in_=ps[:, 0:w], func=AF.Identity, scale=1.0 / D),
                 reads=[], writes=[rps1, self.rMU])
            tp, rtp = self.tmp()
            S.op("dve", lambda e, tp=tp, w=w: e.tensor_tensor(out=tp[:, 0:w], in0=self.MU[:, 0:w], in1=self.MU[:, 0:w], op=ALU.mult),
                 reads=[self.rMU], writes=[rtp])
            S.op("dve", lambda e, ps=ps2, tp=tp, w=w: e.scalar_tensor_tensor(
                out=self.RSTD[:, 0:w], in0=ps[:, 0:w], scalar=1.0 / D, in1=tp[:, 0:w], op0=ALU.mult, op1=ALU.subtract),
                reads=[rtp], writes=[rps2, self.rRSTD])
            S.op("act", lambda e, w=w: e.activation(out=self.RSTD[:, 0:w], in_=self.RSTD[:, 0:w], func=AF.Sqrt, scale=1.0, bias=self.EPSB[:, 0:1]),
                 reads=[self.rONES], writes=[self.rRSTD])
            S.op("dve", lambda e, w=w: e.reciprocal(out=self.RSTD[:, 0:w], in_=self.RSTD[:, 0:w]), reads=[], writes=[self.rRSTD])
            for c in range(KC):
                ub, rub = self.ubuf(c)
                tp, rtp = self.tmp()
                S.op("dve", lambda e, tp=tp, c=c, c0=c0, w=w: e.tensor_tensor(out=tp[:, 0:w], in0=self.H[:, c, c0:c0 + w], in1=self.MU[:, 0:w], op=ALU.subtract),
                     reads=[self.rH[c][ti], self.rMU], writes=[rtp])
                S.op("dve", lambda e, tp=tp, w=w: e.tensor_tensor(out=tp[:, 0:w], in0=tp[:, 0:w], in1=self.RSTD[:, 0:w], op=ALU.mult),
                     reads=[self.rRSTD], writes=[rtp])
                S.op("act", lambda e, tp=tp, ub=ub, c=c, c0=c0, w=w: e.activation(
                    out=ub[:, c0:c0 + w], in_=tp[:, 0:w], func=AF.Silu, scale=self.VEC[:, lgo + c:lgo + c + 1], bias=self.VEC[:, lbo + c:lbo + c + 1]),
                    reads=[rtp, self.rVEC], writes=[rub])
            for dc in range(KC):
                ps, rps = self.psum()
                wa, rwa = self.WA[dc // 4], self.rWA[dc // 4]

                def mm(e, ps=ps, wa=wa, dc=dc, c0=c0, w=w):
                    ins = None
                    for c in range(KC):
                        ins = e.matmul(ps[:, 0:w], lhsT=wa[:, c, (dc % 4) * 128:(dc % 4 + 1) * 128], rhs=self.ubuf(c)[0][:, c0:c0 + w],
                                       start=(c == 0), stop=(c == KC - 1))
                    return ins
                S.op("pe", mm, reads=[rwa[0]] + [self.ubuf(c)[1] for c in range(KC)], writes=[rps])
                tp, rtp = self.tmp()
                S.op("dve", lambda e, ps=ps, tp=tp, dc=dc, w=w, col=col: e.tensor_scalar(
                    out=tp[:, 0:w], in0=ps[:, 0:w], scalar1=self.VEC[:, b2o + dc:b2o + dc + 1], scalar2=self.MOD[:, 16 + dc, col:col + 1],
                    op0=ALU.add, op1=ALU.mult),
                    reads=[self.rVEC, self.rMOD], writes=[rps, rtp])
                S.op("dve", lambda e, tp=tp, dc=dc, c0=c0, w=w: e.tensor_tensor(out=self.X[:, dc, c0:c0 + w], in0=self.X[:, dc, c0:c0 + w], in1=tp[:, 0:w], op=ALU.add),
                     reads=[rtp], writes=[self.rX[dc][ti]])


def make_inputs(inp, layers, core, nb=NB, x_override=None, ctx_override=None):
    m = {}
    xs = inp["x"] if x_override is None else x_override
    cs = inp["ctx"] if ctx_override is None else ctx_override
    xT = np.zeros((nb, D, T), np.float32)
    cT = np.zeros((nb, 128, KC * 2), np.float32)
    for bb in range(nb):
        b = core * nb + bb
        xT[bb, :, L0:L0 + L] = xs[b].T
        xT[bb, :, C0:C0 + LC] = cs[b].T
        sv = np.stack([inp["c"][b], inp["c_ctx"]], axis=-1)
        cT[bb] = sv.reshape(KC, 128, 2).transpose(1, 0, 2).reshape(128, KC * 2)
    m["ident"] = np.eye(128, dtype=np.float32)
    bd = np.zeros((128, 128), np.float32)
    bd[:64, :64] = 1.0
    bd[64:96, 64:96] = 1.0
    m["bd96"] = bd
    m["xT"] = xT
    m["cT"] = cT
    for i in layers:
        kind, j = i % 3, i // 3
        v, _ = pack_layer_vec(inp, i)
        vp = np.zeros((128, NPV), np.float32)
        vp[:, :v.shape[1]] = v
        m[f"vec{i}"] = vp
        m[f"ada_w{i}"] = np.ascontiguousarray(inp["ada_w"][i])
        m[f"ffn_up{i}"] = np.ascontiguousarray(inp["ffn_w_up"][i])
        m[f"ffn_down{i}"] = np.ascontiguousarray(inp["ffn_w_down"][i])
        if kind == 0:
            pe = np.arange(32)
            perm = np.where((pe % 16) < 8, pe + 8, pe - 8)
            wdkv = np.asarray(inp["mla_w_dkv"][j], np.float32)
            kv = np.zeros((D, 320), np.float32)
            kv[:, :128] = wdkv[:, :128]
            kv[:, 128 + 64:224] = wdkv[:, 128:160]
            kv[:, 224 + 64:320] = wdkv[:, 128 + perm]
            m[f"mla_kv{i}"] = kv
            wuq = np.asarray(inp["mla_w_uq"][j], np.float32).reshape(256, 16, 96)
            uq = np.zeros((256, 16, 192), np.float32)
            uq[:, :, :96] = wuq
            uq[:, :, 96 + 64:] = wuq[:, :, 64 + perm]
            m[f"mla_uq{i}"] = uq.reshape(256, 16 * 192)
            m[f"mla_dq{i}"] = np.ascontiguousarray(inp["mla_w_dq"][j])
            m[f"mla_ukv{i}"] = np.ascontiguousarray(inp["mla_w_ukv"][j])
            m[f"mla_o{i}"] = np.ascontiguousarray(inp["mla_w_o"][j])
            m["rope_cs"] = rope_tables()
        if kind == 2:
            m[f"hy_in{i}"] = np.ascontiguousarray(inp["hy_w_in"][j])
            m[f"hy_out{i}"] = np.ascontiguousarray(inp["hy_w_out"][j])
            m[f"hy_fw1_{i}"] = np.ascontiguousarray(inp["hy_f_w1"][j])
            m[f"hy_fw2_{i}"] = np.ascontiguousarray(inp["hy_f_w2"][j])
            m[f"hy_fw3_{i}"] = np.ascontiguousarray(inp["hy_f_w3"][j])
            m[f"hy_skipb{i}"] = np.ascontiguousarray(np.broadcast_to(np.asarray(inp["hy_skip"][j], np.float32)[None, :], (128, D)))
            m.update(hy_consts())
        if kind == 1:
            m[f"cf_w1_{i}"] = np.ascontiguousarray(inp["cf_w_pw1"][j])
            m[f"cf_w2_{i}"] = np.ascontiguousarray(inp["cf_w_pw2"][j])
    return m


_HYC = {}


def hy_consts():
    if _HYC:
        return _HYC
    import ml_dtypes
    bf = ml_dtypes.bfloat16
    out = {}
    for tag, Lx in (("l", L), ("c", LC)):
        N = 2 * Lx
        tau = np.arange(Lx, dtype=np.float64)[:, None]
        f = np.arange(Lx, dtype=np.float64)[None, :] + 0.5
        th = 2 * np.pi * tau * f / N
        out[f"dfc_{tag}"] = np.cos(th).astype(np.float32).astype(bf)
        out[f"dfs_{tag}"] = np.sin(th).astype(np.float32).astype(bf)
        out[f"idc_{tag}"] = np.ascontiguousarray(((2.0 / N) * np.cos(th).T).astype(np.float32)).astype(bf)
        out[f"ids_{tag}"] = np.ascontiguousarray(((2.0 / N) * np.sin(th).T).astype(np.float32)).astype(bf)
        t = np.linspace(0.0, 1.0, Lx, dtype=np.float32)
        w = (2.0 * np.pi * np.arange(Lx, dtype=np.float32) / Lx).astype(np.float32)
        fr = np.linspace(1e-4, 15.0, 16, dtype=np.float32)
        fw = w[:, None] * fr[None, :]
        z = np.concatenate([t[:, None], np.cos(fw), -np.sin(fw)], axis=-1).astype(np.float32)
        out[f"z_{tag}"] = np.ascontiguousarray(z.T)
        out[f"negt_{tag}"] = np.ascontiguousarray((-t).reshape(Lx // 128, 128).T)
    deltas = np.linspace(np.log(1e-2) / 0.3, np.log(1e-2) / 1.5, D, dtype=np.float32)
    out["absd"] = np.ascontiguousarray(np.broadcast_to(np.abs(deltas)[None, :], (128, D))).astype(np.float32)
    _HYC.update(out)
    return _HYC


def rope_tables():
    t = np.arange(L)
    row = (t // 64).astype(np.float32)
    colp = (t % 64).astype(np.float32)
    inv = (10000.0 ** (-(np.arange(0, 16, 2, dtype=np.float32) / 16.0))).astype(np.float32)
    ang = np.concatenate([row[:, None] * inv, colp[:, None] * inv], axis=-1).astype(np.float32)
    cs = np.zeros((128, 2, T), np.float32)
    cs[:, 0, :] = 1.0
    for d in range(32):
        a, b, jj = d // 16, (d % 16) // 8, d % 8
        cs[64 + d, 0, L0:L0 + L] = np.cos(ang[:, a * 8 + jj])
        cs[64 + d, 1, L0:L0 + L] = np.sin(ang[:, a * 8 + jj]) * (-1.0 if b == 0 else 1.0)
    return cs


def kernel(**inputs):
    inp = {k: np.asarray(v) for k, v in inputs.items()}
    layers = list(range(DEPTH))
    prog = Prog(layers)
    nc = prog.build()
    in_maps = [make_inputs(inp, layers, c) for c in range(8)]
    res = run_bass_kernel_spmd(nc, in_maps, core_ids=list(range(8)))
    out = np.empty((16, L, D), np.float32)
    for c in range(8):
        y = res.results[c]["yT"]
        for bb in range(NB):
            out[c * NB + bb] = y[bb].T
    return out
```

```python
import contextlib
import os
import numpy as np
import concourse.bass as bass
import concourse.mybir as mybir
from concourse.bass_utils import run_bass_kernel_spmd

F32 = mybir.dt.float32
BF16 = mybir.dt.bfloat16
ALU = mybir.AluOpType
AF = mybir.ActivationFunctionType

D = 1024
KC = 8
L = 2048
LC = 256
T = 16 + LC + 16 + L + 16
C0 = 16
L0 = 16 + LC + 16
TILES = [(C0, LC)] + [(L0 + 512 * j, 512) for j in range(4)]
D_FF = 2816
FC = 22
DEPTH = 4
EPS = 1e-6
NB = 2
DBG_STOP = int(os.environ.get('MK_STOP', '0'))
DBG_HEADS = (int(os.environ['MK_HEADS']) if 'MK_HEADS' in os.environ else None)


class Res:
    __slots__ = ("w", "r")

    def __init__(self):
        self.w = None
        self.r = {}


class Sched:
    ENG = ("pe", "act", "dve", "pool", "sp")
    NDMA = 8
    EPOCH = 6000

    def __init__(self):
        self.streams = {e: [] for e in self.ENG}
        self.cnt = {e: 0 for e in self.ENG}
        self.waited = {e: {} for e in self.ENG}
        self.dma_n = {e: 0 for e in self.ENG}
        self.last = {}

    def op(self, eng, fn, reads=(), writes=(), dma=False):
        deps = {}

        def add(k, v):
            if deps.get(k, 0) < v:
                deps[k] = v

        for r in reads:
            if r.w is not None:
                add(*r.w)
        for r in writes:
            if r.w is not None:
                add(*r.w)
            for k, v in r.r.items():
                add(k, v)
        if dma:
            j = self.dma_n[eng]
            self.dma_n[eng] += 1
            key = ("dma", eng, j % self.NDMA)
            val = 16 * (j // self.NDMA + 1)
            if j >= self.NDMA:
                add(key, val - 16)
            inc = 16
        else:
            key = ("eng", eng, self.cnt[eng] // self.EPOCH)
            val = self.cnt[eng] % self.EPOCH + 1
            self.cnt[eng] += 1
            inc = 1
        self.last[key] = val
        wd = self.waited[eng]
        waits = []
        for k, v in deps.items():
            if eng == "pe" and k[0] == "eng" and k[1] == "pe":
                continue
            if wd.get(k, 0) < v:
                wd[k] = v
                waits.append((k, v))
        self.streams[eng].append((fn, waits, key, inc))
        for r in reads:
            if r.r.get(key, 0) < val:
                r.r[key] = val
        for r in writes:
            r.w = (key, val)
            r.r = {}

    def barrier(self, engs=None):
        for e in (engs or self.ENG):
            wd = self.waited[e]
            waits = []
            for k, v in self.last.items():
                if wd.get(k, 0) < v:
                    wd[k] = v
                    waits.append((k, v))
            if waits:
                self.streams[e].append((None, waits, None, 0))

    def emit(self, nc, stack):
        sems = {}
        for e in self.ENG:
            for (fn, waits, key, inc) in self.streams[e]:
                for k in [w[0] for w in waits] + ([key] if key is not None else []):
                    if k not in sems:
                        sems[k] = stack.enter_context(nc.semaphore("s_" + "_".join(str(x) for x in k)))
        block = stack.enter_context(nc.Block())

        def run(ename, eng):
            for (fn, waits, key, inc) in self.streams[ename]:
                for k, v in waits:
                    eng.wait_ge(sems[k], v)
                if fn is not None:
                    fn(eng).then_inc(sems[key], inc)

        @block.tensor
        def _(e):
            run("pe", e)

        @block.scalar
        def _(e):
            run("act", e)

        @block.vector
        def _(e):
            run("dve", e)

        @block.gpsimd
        def _(e):
            run("pool", e)

        @block.sync
        def _(e):
            run("sp", e)


def fm(v):
    v = np.asarray(v, np.float32)
    c = v.shape[-1] // 128
    lead = v.shape[:-1]
    return np.ascontiguousarray(np.moveaxis(v.reshape(lead + (c, 128)), -1, 0)).reshape(128, -1)


def pack_layer_vec(inp, i):
    kind, j = i % 3, i // 3
    parts = [("ada_b", fm(inp["ada_b"][i])),
             ("norm_mix", fm(inp["norm_mix"][i])),
             ("norm_ffn", fm(inp["norm_ffn"][i])),
             ("ffn_b_dw", fm(inp["ffn_b_dw"][i])),
             ("ffn_w_dw", fm(inp["ffn_w_dw"][i]))]
    if kind == 0:
        g = np.asarray(inp["mla_qk_gain"][j], np.float32)
        def col96(v):
            o = np.zeros((128, 1), np.float32)
            o[:96, 0] = v
            return o
        pe = np.arange(32)
        perm = np.where((pe % 16) < 8, pe + 8, pe - 8)
        gqp = np.zeros(96, np.float32); gqp[64:] = g[0, 64 + perm]
        gkp = np.zeros(96, np.float32); gkp[64:] = g[1, 64 + perm]
        invn = np.concatenate([np.full(64, 1 / 64.0), np.full(32, 1 / 32.0)]).astype(np.float32)
        parts += [("mla_qn", fm(inp["mla_q_norm"][j])), ("mla_kvn", fm(inp["mla_kv_norm"][j])),
                  ("mla_gq", col96(g[0])), ("mla_gqp", col96(gqp)), ("mla_gk", col96(g[1])), ("mla_gkp", col96(gkp)),
                  ("mla_invn", col96(invn))]
    if kind == 2:
        def col64(v):
            o = np.zeros((128, 1), np.float32)
            o[:64, 0] = v
            return o
        parts += [("hy_b_in", fm(inp["hy_b_in"][j])), ("hy_w_short", fm(inp["hy_w_short"][j])),
                  ("hy_b_short", fm(inp["hy_b_short"][j])), ("hy_b_out", fm(inp["hy_b_out"][j])),
                  ("hy_fb1", col64(inp["hy_f_b1"][j])), ("hy_fb2", col64(inp["hy_f_b2"][j])), ("hy_fr", col64(inp["hy_sin_freq"][j]))]
    if kind == 1:
        parts += [("cf_b_pw1", fm(inp["cf_b_pw1"][j])),
                  ("cf_w_dw", fm(inp["cf_w_dw"][j])),
                  ("cf_b_dw", fm(inp["cf_b_dw"][j])),
                  ("cf_ln_g", fm(inp["cf_ln_g"][j])),
                  ("cf_ln_b", fm(inp["cf_ln_b"][j])),
                  ("cf_b_pw2", fm(inp["cf_b_pw2"][j]))]
    off = {}
    o = 0
    for n, a in parts:
        off[n] = o
        o += a.shape[1]
    return np.concatenate([a for _, a in parts], axis=1), off


def vec_offsets(i):
    kind = i % 3
    sizes = [("ada_b", 48), ("norm_mix", 8), ("norm_ffn", 8), ("ffn_b_dw", 22), ("ffn_w_dw", 66)]
    if kind == 0:
        sizes += [("mla_qn", 2), ("mla_kvn", 1), ("mla_gq", 1), ("mla_gqp", 1), ("mla_gk", 1), ("mla_gkp", 1), ("mla_invn", 1)]
    if kind == 2:
        sizes += [("hy_b_in", 24), ("hy_w_short", 72), ("hy_b_short", 24), ("hy_b_out", 8), ("hy_fb1", 1), ("hy_fb2", 1), ("hy_fr", 1)]
    if kind == 1:
        sizes += [("cf_b_pw1", 16), ("cf_w_dw", 248), ("cf_b_dw", 8), ("cf_ln_g", 8), ("cf_ln_b", 8), ("cf_b_pw2", 8)]
    off = {}
    o = 0
    for n, s in sizes:
        off[n] = o
        o += s
    return off, o


NPV = 512
ARENA = 42000


class Prog:
    def __init__(self, layers, nb=NB):
        self.layers = layers
        self.nb = nb
        self.nc = bass.Bass("TRN2", target_bir_lowering=False)
        self.S = Sched()
        self.dram = {}

    def din(self, name, shape, dt=F32):
        t = self.nc.dram_tensor(name, list(shape), dt, kind="ExternalInput").ap()
        self.dram[name] = t
        return t

    def dout(self, name, shape, dt=F32):
        t = self.nc.dram_tensor(name, list(shape), dt, kind="ExternalOutput").ap()
        self.dram[name] = t
        return t

    def build(self):
        nc, S = self.nc, self.S
        nb = self.nb
        xT = self.din("xT", [nb, D, T])
        cT = self.din("cT", [nb, 128, KC * 2])
        ident_d = self.din("ident", [128, 128])
        bd_d = self.din("bd96", [128, 128])
        yT = self.dout("yT", [nb, D, L])
        cOut = self.dout("cOut", [nb, D, LC])
        W = {}
        for i in self.layers:
            kind, j = i % 3, i // 3
            W[i] = dict(
                ada_w=self.din(f"ada_w{i}", [D, 6 * D]),
                vec=self.din(f"vec{i}", [128, NPV]),
                ffn_up=self.din(f"ffn_up{i}", [D, 2 * D_FF]),
                ffn_down=self.din(f"ffn_down{i}", [D_FF, D]),
            )
            if kind == 0:
                if "rope_cs" not in self.dram:
                    self.din("rope_cs", [128, 2, T])
                W[i].update(dq=self.din(f"mla_dq{i}", [D, 256]), kv=self.din(f"mla_kv{i}", [D, 320]),
                            uq=self.din(f"mla_uq{i}", [256, 16 * 192]), ukv=self.din(f"mla_ukv{i}", [128, 2048]),
                            wo=self.din(f"mla_o{i}", [D, D]), rope=self.dram["rope_cs"])
            if kind == 2:
                W[i].update(hy_in=self.din(f"hy_in{i}", [D, 3 * D]), hy_out=self.din(f"hy_out{i}", [D, D]),
                            fw1=self.din(f"hy_fw1_{i}", [33, 64]), fw2=self.din(f"hy_fw2_{i}", [64, 64]),
                            fw3=self.din(f"hy_fw3_{i}", [64, 2 * D]), skipb=self.din(f"hy_skipb{i}", [128, D]))
                for tag, Lx in (("l", L), ("c", LC)):
                    for nm in ("dfc", "dfs", "idc", "ids"):
                        W[i][f"{nm}_{tag}"] = self.din(f"{nm}_{tag}", [Lx, Lx], BF16)
                    W[i][f"z_{tag}"] = self.din(f"z_{tag}", [33, Lx])
                    W[i][f"negt_{tag}"] = self.din(f"negt_{tag}", [128, Lx // 128])
                    W[i][f"kspec_{tag}"] = self.nc.dram_tensor(f"kspec_{tag}", [2, Lx, D], F32).ap()
                W[i]["absd"] = self.din("absd", [128, D])
                W[i]["x0d"] = self.nc.dram_tensor("x0d", [D, T], BF16).ap()
                self.rX0D = [Res() for _ in range(KC)]
                self.rKS = {"l": Res(), "c": Res()}
            if kind == 1:
                W[i].update(cf_w1=self.din(f"cf_w1_{i}", [D, 2 * D]), cf_w2=self.din(f"cf_w2_{i}", [D, D]))
        with contextlib.ExitStack() as st:
            self.st = st

            def sb(name, shape, dt):
                return st.enter_context(nc.sbuf_tensor(name, list(shape), dt))

            self.X = sb("X", [128, KC, T], F32)
            self.rX = [[Res() for _ in TILES] for _ in range(KC)]
            self.H = sb("H", [128, KC, T], BF16)
            self.rH = [[Res() for _ in TILES] for _ in range(KC)]
            self.SCR = sb("SCR", [128, ARENA], BF16)
            self.carve_ffn()
            self.VEC = sb("VEC", [128, NPV], F32)
            self.rVEC = Res()
            self.MOD = sb("MOD", [128, 48, 2], F32)
            self.rMOD = Res()
            self.AM = sb("AM", [128, 2, KC, 2], F32)
            self.rAM = Res()
            self.SV = sb("SV", [128, KC, 2], F32)
            self.SVB = sb("SVB", [128, KC, 2], BF16)
            self.rSV = Res()
            self.ONES = sb("ONES", [128, 128], BF16)
            self.rONES = Res()
            self.EPSB = sb("EPSB", [128, 1], F32)
            self.RSTD = sb("RSTD", [128, 512], F32)
            self.rRSTD = Res()
            self.RSTD2 = sb("RSTD2", [128, 512], F32)
            self.rRSTD2 = Res()
            self.MU = sb("MU", [128, 512], F32)
            self.rMU = Res()
            self.TMP = [sb(f"TMP{k}", [128, 512], F32) for k in range(2)]
            self.rTMP = [Res(), Res()]
            self.tmp_i = 0
            self.BD = sb("BD", [128, 128], BF16)
            self.PS = [st.enter_context(nc.psum_tensor(f"PS{k}", [128, 512], F32)) for k in range(8)]
            self.acc_i = 0
            self.rPS = [Res() for _ in range(8)]
            self.ps_i = 0
            self.wa_i = 0
            self.dq = 0

            S.op("pool", lambda e: e.memset(self.ONES[:, :], 1.0), writes=[self.rONES])
            S.op("pool", lambda e: e.memset(self.EPSB[:, :], EPS), writes=[self.rONES])
            self.zero_pads()
            self.IDENT = sb("IDENT", [128, 128], BF16)
            self.rID = Res()
            S.op("pool", lambda e: e.dma_start(out=self.IDENT[:, :], in_=ident_d[:, :]), writes=[self.rID], dma=True)
            S.op("pool", lambda e: e.dma_start(out=self.BD[:, :], in_=bd_d[:, :]), writes=[self.rID], dma=True)

            routs = []
            for b in range(nb):
                self.load_x(xT, cT, b)
                for i in self.layers:
                    self.layer(i, W[i], b, last=(i == DEPTH - 1))
                routs += self.store_x(yT, cOut, b)
            S.barrier(["sp"])
            S.emit(nc, st)
        return nc

    def psum(self):
        k = self.ps_i
        self.ps_i = (k + 1) % 6
        return self.PS[k], self.rPS[k]

    def psum_acc(self):
        k = 6 + self.acc_i
        self.acc_i = 1 - self.acc_i
        return self.PS[k], self.rPS[k]

    def arena_reset(self, mark=0):
        self.a_off = mark

    def alloc(self, n_el, dt=BF16, shape=None):
        nb16 = n_el * (2 if dt == F32 else 1)
        nb16 = (nb16 + 15) // 16 * 16
        a = self.a_off
        assert a + nb16 <= ARENA, (a, nb16)
        self.a_off = a + nb16
        ap = self.SCR[:, a:a + n_el * (2 if dt == F32 else 1)]
        if dt == F32:
            ap = ap.bitcast(F32)
        return ap

    def carve_ffn(self):
        self.arena_reset()
        self.G = self.alloc(6 * T).rearrange("p (c t) -> p c t", t=T)
        self.rG = [Res() for _ in range(6)]
        self.AROW = self.alloc(T)
        self.CROW = self.alloc(T, F32)
        self.VROW = self.alloc(T)
        self.rA, self.rC, self.rV = Res(), Res(), Res()
        self.WA = [self.alloc(KC * 512).rearrange("p (c n) -> p c n", n=512) for _ in range(2)]
        self.rWA = [[Res(), Res()], [Res(), Res()]]
        self.WB = self.alloc(6 * 1024)
        self.rWB = Res()
        self.SQ = self.alloc(KC * 512).rearrange("p (c n) -> p c n", n=512)
        self.rSQ = Res()

    def zero_pads(self):
        S = self.S
        S.op("pool", lambda e: e.memset(self.AROW[:, :], 0.0), writes=[self.rA])
        S.op("pool", lambda e: e.memset(self.VROW[:, :], 0.0), writes=[self.rV])
        for (a, bnd) in ((0, 16), (C0 + LC, L0), (L0 + L, T)):
            S.op("pool", lambda e, a=a, bnd=bnd: e.memset(self.G[:, :, a:bnd], 0.0), writes=self.rG)

    def tmp(self):
        k = self.tmp_i
        self.tmp_i = 1 - k
        return self.TMP[k], self.rTMP[k]

    def dma_eng(self):
        self.dq += 1
        return "sp"

    def load_w(self, dst_ap, src_ap, res):
        self.S.op("pool", lambda e: e.dma_start(out=dst_ap, in_=src_ap), writes=(res if isinstance(res, list) else [res]), dma=True)

    def load_x(self, xT, cT, b):
        S = self.S
        for c in range(KC):
            S.op("sp", lambda e, c=c: e.dma_start(out=self.X[:, c, :], in_=xT[b, c * 128:(c + 1) * 128, :]),
                 writes=self.rX[c], dma=True)
        S.op("sp", lambda e: e.dma_start(out=self.SV[:, :, :], in_=cT[b].rearrange("p (c t) -> p c t", t=2)),
             writes=[self.rSV], dma=True)
        S.op("act", lambda e: e.activation(out=self.SVB[:, :, :], in_=self.SV[:, :, :], func=AF.Silu),
             reads=[], writes=[self.rSV])

    def store_x(self, yT, cOut, b):
        S = self.S
        rs = []
        for c in range(KC):
            r = Res()
            S.op("sp", lambda e, c=c: e.dma_start(out=yT[b, c * 128:(c + 1) * 128, :], in_=self.X[:, c, L0:L0 + L]),
                 reads=self.rX[c], writes=[r], dma=True)
            r2 = Res()
            S.op("sp", lambda e, c=c: e.dma_start(out=cOut[b, c * 128:(c + 1) * 128, :], in_=self.X[:, c, C0:C0 + LC]),
                 reads=self.rX[c], writes=[r2], dma=True)
            rs += [r, r2]
        return rs

    def adaln(self, i, Wd):
        S = self.S
        off, _ = vec_offsets(i)
        S.op("sp", lambda e: e.dma_start(out=self.VEC[:, :], in_=Wd["vec"][:, :]), writes=[self.rVEC], dma=True)
        ada_w = Wd["ada_w"].rearrange("(c p) n -> p c n", p=128)
        for blk in range(12):
            k = self.wa_i
            self.wa_i = 1 - k
            wa, rwa = self.WA[k], self.rWA[k]
            self.S.op("pool", lambda e, wa=wa, blk=blk: e.dma_start(out=wa[:, :, :], in_=ada_w[:, :, blk * 512:(blk + 1) * 512]), writes=rwa, dma=True)
            ps, rps = self.psum()

            def mm(e, wa=wa, ps=ps):
                ins = None
                for f in range(4):
                    for kc in range(KC):
                        ins = e.matmul(ps[:, f * 2:f * 2 + 2], lhsT=wa[:, kc, f * 128:(f + 1) * 128],
                                       rhs=self.SVB[:, kc, :], start=(kc == 0), stop=(kc == KC - 1))
                return ins
            S.op("pe", mm, reads=rwa + [self.rSV], writes=[rps])
            ab = off["ada_b"] + blk * 4
            S.op("dve", lambda e, ps=ps, blk=blk, ab=ab: e.tensor_tensor(
                out=self.MOD[:, blk * 4:blk * 4 + 4, :], in0=ps[:, 0:8].rearrange("p (f t) -> p f t", t=2),
                in1=self.VEC[:, ab:ab + 4].unsqueeze(2).to_broadcast([128, 4, 2]), op=ALU.add),
                reads=[self.rVEC], writes=[rps, self.rMOD])
        for m, (nname, g) in enumerate((("norm_mix", 1), ("norm_ffn", 4))):
            no = off[nname]
            S.op("dve", lambda e, m=m, g=g: e.tensor_scalar_add(out=self.AM[:, m, :, :], in0=self.MOD[:, g * 8:(g + 1) * 8, :], scalar1=1.0),
                 reads=[self.rMOD], writes=[self.rAM])
            S.op("dve", lambda e, m=m, no=no: e.tensor_tensor(
                out=self.AM[:, m, :, :], in0=self.AM[:, m, :, :],
                in1=self.VEC[:, no:no + 8].unsqueeze(2).to_broadcast([128, 8, 2]), op=ALU.mult),
                reads=[self.rVEC, self.rAM], writes=[self.rAM])

    def modulate(self, m, tiles):
        S = self.S
        shg = 0 if m == 0 else 3
        RS = [self.RSTD, self.RSTD2]
        rRS = [self.rRSTD, self.rRSTD2]

        def stage1(k):
            ti = tiles[k]
            c0, w = TILES[ti]
            rs, rrs = RS[k % 2], rRS[k % 2]
            for c in range(KC):
                S.op("act", lambda e, c=c: e.activation(out=self.SQ[:, c, 0:w], in_=self.X[:, c, c0:c0 + w], func=AF.Square),
                     reads=[self.rX[c][ti]], writes=[self.rSQ])
            ps, rps = self.psum()

            def mm(e):
                ins = None
                for c in range(KC):
                    ins = e.matmul(ps[:, 0:w], lhsT=self.ONES[:, :], rhs=self.SQ[:, c, 0:w], start=(c == 0), stop=(c == KC - 1))
                return ins
            S.op("pe", mm, reads=[self.rSQ, self.rONES], writes=[rps])
            S.op("act", lambda e: e.activation(out=rs[:, 0:w], in_=ps[:, 0:w], func=AF.Ln, scale=1.0 / D, bias=self.EPSB[:, 0:1]),
                 reads=[self.rONES], writes=[rps, rrs])
            S.op("act", lambda e: e.activation(out=rs[:, 0:w], in_=rs[:, 0:w], func=AF.Exp, scale=-0.5), reads=[], writes=[rrs])

        def stage2(k):
            ti = tiles[k]
            c0, w = TILES[ti]
            col = 1 if ti == 0 else 0
            rs, rrs = RS[k % 2], rRS[k % 2]
            for c in range(KC):
                tp, rtp = self.tmp()
                S.op("dve", lambda e, tp=tp, c=c: e.tensor_tensor(out=tp[:, 0:w], in0=self.X[:, c, c0:c0 + w], in1=rs[:, 0:w], op=ALU.mult),
                     reads=[self.rX[c][ti], rrs], writes=[rtp])
                S.op("pool", lambda e, tp=tp, c=c: e.tensor_scalar(
                    out=self.H[:, c, c0:c0 + w], in0=tp[:, 0:w], scalar1=self.AM[:, m, c, col:col + 1],
                    scalar2=self.MOD[:, shg * 8 + c, col:col + 1], op0=ALU.mult, op1=ALU.add),
                    reads=[rtp, self.rAM, self.rMOD], writes=[self.rH[c][ti]])
        stage1(0)
        for k in range(len(tiles)):
            if k + 1 < len(tiles):
                stage1(k + 1)
            stage2(k)

    def ffn(self, i, Wd, tiles):
        S = self.S
        off, _ = vec_offsets(i)
        up = Wd["ffn_up"].rearrange("(c p) n -> p c n", p=128)
        down = Wd["ffn_down"]
        bo, wo = off["ffn_b_dw"], off["ffn_w_dw"]
        lo = TILES[tiles[0]][0]
        hi = TILES[tiles[-1]][0] + TILES[tiles[-1]][1]
        for (g0, gn) in ((0, 6), (6, 6), (12, 5), (17, 5)):
            wbv = self.WB[:, 0:gn * 1024].rearrange("p (c n) -> p c n", n=1024)
            self.load_w(wbv, down[g0 * 128:(g0 + gn) * 128, :].rearrange("(c p) n -> p c n", p=128), self.rWB)
            for hc in range(g0, g0 + gn):
                k = self.wa_i
                self.wa_i = 1 - k
                wa, rwa = self.WA[k], self.rWA[k]
                self.load_w(wa[:, :, 0:128], up[:, :, hc * 128:(hc + 1) * 128], rwa[0])
                self.load_w(wa[:, :, 128:256], up[:, :, D_FF + hc * 128:D_FF + (hc + 1) * 128], rwa[1])
                for ti in tiles:
                    c0, w = TILES[ti]
                    for half, (dst, rdst) in enumerate(((self.AROW, self.rA), (self.VROW, self.rV))):
                        ps, rps = self.psum()

                        def mm(e, ps=ps, wa=wa, half=half, c0=c0, w=w):
                            ins = None
                            for kc in range(KC):
                                ins = e.matmul(ps[:, 0:w], lhsT=wa[:, kc, half * 128:(half + 1) * 128],
                                               rhs=self.H[:, kc, c0:c0 + w], start=(kc == 0), stop=(kc == KC - 1))
                            return ins
                        S.op("pe", mm, reads=[rwa[half]] + [self.rH[kc][ti] for kc in range(KC)], writes=[rps])
                        S.op("act", lambda e, ps=ps, dst=dst, c0=c0, w=w: e.copy(out=dst[:, c0:c0 + w], in_=ps[:, 0:w]),
                             reads=[], writes=[rps, rdst])
                S.op("dve", lambda e, hc=hc: e.tensor_scalar(
                    out=self.CROW[:, lo:hi], in0=self.AROW[:, lo:hi], scalar1=self.VEC[:, wo + 22 + hc:wo + 22 + hc + 1],
                    scalar2=self.VEC[:, bo + hc:bo + hc + 1], op0=ALU.mult, op1=ALU.add),
                    reads=[self.rA, self.rVEC], writes=[self.rC])
                S.op("dve", lambda e, hc=hc: e.scalar_tensor_tensor(
                    out=self.CROW[:, lo:hi], in0=self.AROW[:, lo - 1:hi - 1], scalar=self.VEC[:, wo + hc:wo + hc + 1],
                    in1=self.CROW[:, lo:hi], op0=ALU.mult, op1=ALU.add),
                    reads=[self.rA, self.rVEC], writes=[self.rC])
                S.op("dve", lambda e, hc=hc: e.scalar_tensor_tensor(
                    out=self.CROW[:, lo:hi], in0=self.AROW[:, lo + 1:hi + 1], scalar=self.VEC[:, wo + 44 + hc:wo + 44 + hc + 1],
                    in1=self.CROW[:, lo:hi], op0=ALU.mult, op1=ALU.add),
                    reads=[self.rA, self.rVEC], writes=[self.rC])
                S.op("act", lambda e: e.activation(out=self.CROW[:, lo:hi], in_=self.CROW[:, lo:hi], func=AF.Silu),
                     reads=[], writes=[self.rC])
                gi = hc - g0
                S.op("dve", lambda e, gi=gi: e.tensor_tensor(out=self.G[:, gi, lo:hi], in0=self.CROW[:, lo:hi], in1=self.VROW[:, lo:hi], op=ALU.mult),
                     reads=[self.rC, self.rV], writes=[self.rG[gi]])
            for dc in range(KC):
                for ti in tiles:
                    c0, w = TILES[ti]
                    col = 1 if ti == 0 else 0
                    ps, rps = self.psum()

                    def mm(e, ps=ps, wbv=wbv, dc=dc, c0=c0, w=w, gn=gn):
                        ins = None
                        for gi in range(gn):
                            ins = e.matmul(ps[:, 0:w], lhsT=wbv[:, gi, dc * 128:(dc + 1) * 128], rhs=self.G[:, gi, c0:c0 + w],
                                           start=(gi == 0), stop=(gi == gn - 1))
                        return ins
                    S.op("pe", mm, reads=[self.rWB] + [self.rG[gi] for gi in range(gn)], writes=[rps])
                    S.op("dve", lambda e, ps=ps, dc=dc, c0=c0, w=w, col=col: e.scalar_tensor_tensor(
                        out=self.X[:, dc, c0:c0 + w], in0=ps[:, 0:w], scalar=self.MOD[:, 40 + dc, col:col + 1],
                        in1=self.X[:, dc, c0:c0 + w], op0=ALU.mult, op1=ALU.add),
                        reads=[self.rMOD], writes=[rps, self.rX[dc][ti]])

    def mla(self, i, Wd, need_ctx):
        S = self.S
        off, _ = vec_offsets(i)
        V = lambda name, k=0: self.VEC[:, off[name] + k:off[name] + k + 1]
        V96 = lambda name: self.VEC[0:96, off[name]:off[name] + 1]
        all_tiles = [0, 1, 2, 3, 4]
        qtiles = all_tiles if need_ctx else [1, 2, 3, 4]
        att_scale = float(96 ** -0.5)
        self.arena_reset()
        CQ = self.alloc(2 * T).rearrange("p (c t) -> p c t", t=T)
        CKV = self.alloc(T)
        KPE = self.alloc(T)
        ROPE = self.alloc(2 * T, F32).rearrange("p (c t) -> p c t", t=T)
        rCQ = [Res() for _ in TILES]
        rCKV = [Res() for _ in TILES]
        rKPE = [Res() for _ in TILES]
        rROPE = Res()
        SQh = self.alloc(512)
        rSQh = Res()
        mark = self.a_off
        S.op("sp", lambda e: e.dma_start(out=ROPE[:, :, :], in_=Wd["rope"][:, :, :]), writes=[rROPE], dma=True)
        WDQ = self.alloc(KC * 256).rearrange("p (c n) -> p c n", n=256)
        WKV = self.alloc(KC * 320).rearrange("p (c n) -> p c n", n=320)
        CQF = self.alloc(2 * 512, F32).rearrange("p (c n) -> p c n", n=512)
        SQb = self.alloc(2 * 512).rearrange("p (c n) -> p c n", n=512)
        rWDQ, rWKV, rCQF, rSQb = Res(), Res(), Res(), Res()
        self.load_w(WDQ, Wd["dq"].rearrange("(c p) n -> p c n", p=128), rWDQ)
        self.load_w(WKV, Wd["kv"].rearrange("(c p) n -> p c n", p=128), rWKV)

        def rope_norm(ps_raw, rps_raw, ps_perm, rps_perm, g, gp, out_ap, rout, lo, hi, c0, w):
            S.op("act", lambda e: e.activation(out=SQh[0:96, 0:w], in_=ps_raw[0:96, 0:w], func=AF.Square),
                 reads=[], writes=[rps_raw, rSQh])
            pss, rpss = self.psum()
            S.op("pe", lambda e: e.matmul(pss[0:96, 0:w], lhsT=self.BD[0:96, 0:96], rhs=SQh[0:96, 0:w], start=True, stop=True),
                 reads=[rSQh, self.rID], writes=[rpss])
            S.op("act", lambda e: e.activation(out=self.RSTD[0:96, 0:w], in_=pss[0:96, 0:w], func=AF.Ln,
                                               scale=V96("mla_invn"), bias=self.EPSB[0:96, 0:1]),
                 reads=[self.rVEC, self.rONES], writes=[rpss, self.rRSTD])
            S.op("act", lambda e: e.activation(out=self.RSTD[0:96, 0:w], in_=self.RSTD[0:96, 0:w], func=AF.Exp, scale=-0.5), reads=[], writes=[self.rRSTD])
            t1, rt1 = self.TMP[0], self.rTMP[0]
            t2, rt2 = self.TMP[1], self.rTMP[1]
            S.op("dve", lambda e: e.scalar_tensor_tensor(out=t1[0:96, 0:w], in0=ps_raw[0:96, 0:w], scalar=g, in1=ROPE[0:96, 0, c0:c0 + w],
                                                         op0=ALU.mult, op1=ALU.mult),
                 reads=[self.rVEC, rROPE], writes=[rps_raw, rt1])
            S.op("dve", lambda e: e.scalar_tensor_tensor(out=t2[0:96, 0:w], in0=ps_perm[0:96, 0:w], scalar=gp, in1=ROPE[0:96, 1, c0:c0 + w],
                                                         op0=ALU.mult, op1=ALU.mult),
                 reads=[self.rVEC, rROPE], writes=[rps_perm, rt2])
            S.op("pool", lambda e: e.tensor_tensor(out=t1[0:96, 0:w], in0=t1[0:96, 0:w], in1=t2[0:96, 0:w], op=ALU.add),
                 reads=[rt2], writes=[rt1])
            S.op("dve", lambda e: e.tensor_tensor(out=out_ap, in0=t1[lo:hi, 0:w], in1=self.RSTD[lo:hi, 0:w], op=ALU.mult),
                 reads=[rt1, self.rRSTD], writes=[rout])

        def p1_tile(ti):
            c0, w = TILES[ti]
            rh = [self.rH[kc][ti] for kc in range(KC)]
            for oc in range(2):
                ps, rps = self.psum()

                def mm(e, ps=ps, oc=oc):
                    ins = None
                    for kc in range(KC):
                        ins = e.matmul(ps[:, 0:w], lhsT=WDQ[:, kc, oc * 128:(oc + 1) * 128], rhs=self.H[:, kc, c0:c0 + w],
                                       start=(kc == 0), stop=(kc == KC - 1))
                    return ins
                S.op("pe", mm, reads=[rWDQ] + rh, writes=[rps])
                S.op("act", lambda e, ps=ps, oc=oc: e.copy(out=CQF[:, oc, 0:w], in_=ps[:, 0:w]), reads=[], writes=[rps, rCQF])
            S.op("act", lambda e: e.activation(out=SQb[:, :, 0:w], in_=CQF[:, :, 0:w], func=AF.Square), reads=[rCQF], writes=[rSQb])
            ps, rps = self.psum()

            def mm(e, ps=ps):
                e.matmul(ps[:, 0:w], lhsT=self.ONES[:, :], rhs=SQb[:, 0, 0:w], start=True, stop=False)
                return e.matmul(ps[:, 0:w], lhsT=self.ONES[:, :], rhs=SQb[:, 1, 0:w], start=False, stop=True)
            S.op("pe", mm, reads=[rSQb, self.rONES], writes=[rps])
            S.op("act", lambda e, ps=ps: e.activation(out=self.RSTD[:, 0:w], in_=ps[:, 0:w], func=AF.Ln, scale=1.0 / 256, bias=self.EPSB[:, 0:1]),
                 reads=[self.rONES], writes=[rps, self.rRSTD])
            S.op("act", lambda e: e.activation(out=self.RSTD[:, 0:w], in_=self.RSTD[:, 0:w], func=AF.Exp, scale=-0.5), reads=[], writes=[self.rRSTD])
            for oc in range(2):
                S.op("dve", lambda e, oc=oc: e.scalar_tensor_tensor(out=CQ[:, oc, c0:c0 + w], in0=CQF[:, oc, 0:w], scalar=V("mla_qn", oc),
                                                                 in1=self.RSTD[:, 0:w], op0=ALU.mult, op1=ALU.mult),
                     reads=[rCQF, self.rVEC, self.rRSTD], writes=[rCQ[ti]])
            ps, rps = self.psum()

            def mm(e, ps=ps):
                ins = None
                for kc in range(KC):
                    ins = e.matmul(ps[:, 0:w], lhsT=WKV[:, kc, 0:128], rhs=self.H[:, kc, c0:c0 + w], start=(kc == 0), stop=(kc == KC - 1))
                return ins
            S.op("pe", mm, reads=[rWKV] + rh, writes=[rps])
            S.op("act", lambda e, ps=ps: e.copy(out=CQF[:, 0, 0:w], in_=ps[:, 0:w]), reads=[], writes=[rps, rCQF])
            S.op("act", lambda e: e.activation(out=SQb[:, 0, 0:w], in_=CQF[:, 0, 0:w], func=AF.Square), reads=[rCQF], writes=[rSQb])
            ps, rps = self.psum()
            S.op("pe", lambda e, ps=ps: e.matmul(ps[:, 0:w], lhsT=self.ONES[:, :], rhs=SQb[:, 0, 0:w], start=True, stop=True),
                 reads=[rSQb, self.rONES], writes=[rps])
            S.op("act", lambda e, ps=ps: e.activation(out=self.RSTD[:, 0:w], in_=ps[:, 0:w], func=AF.Ln, scale=1.0 / 128, bias=self.EPSB[:, 0:1]),
                 reads=[self.rONES], writes=[rps, self.rRSTD])
            S.op("act", lambda e: e.activation(out=self.RSTD[:, 0:w], in_=self.RSTD[:, 0:w], func=AF.Exp, scale=-0.5), reads=[], writes=[self.rRSTD])
            S.op("dve", lambda e: e.scalar_tensor_tensor(out=CKV[:, c0:c0 + w], in0=CQF[:, 0, 0:w], scalar=V("mla_kvn"),
                                                         in1=self.RSTD[:, 0:w], op0=ALU.mult, op1=ALU.mult),
                 reads=[rCQF, self.rVEC, self.rRSTD], writes=[rCKV[ti]])
            pss = []
            for var in range(2):
                ps, rps = self.psum()
                pss.append((ps, rps))

                def mm(e, ps=ps, var=var):
                    ins = None
                    for kc in range(KC):
                        ins = e.matmul(ps[0:96, 0:w], lhsT=WKV[:, kc, 128 + var * 96:224 + var * 96], rhs=self.H[:, kc, c0:c0 + w],
                                       start=(kc == 0), stop=(kc == KC - 1))
                    return ins
                S.op("pe", mm, reads=[rWKV] + rh, writes=[rps])
            rope_norm(pss[0][0], pss[0][1], pss[1][0], pss[1][1], V96("mla_gk"), V96("mla_gkp"),
                      KPE[64:96, c0:c0 + w], rKPE[ti], 64, 96, c0, w)

        for ti in all_tiles:
            p1_tile(ti)
        if DBG_STOP == 1:
            S.barrier()
            return

        S.barrier()
        self.arena_reset(mark)
        WUKV = self.alloc(2048)
        rWUKV = Res()
        self.load_w(WUKV, Wd["ukv"][:, :], rWUKV)
        uq_v = Wd["uq"].rearrange("(c p) n -> p c n", p=128)
        WUQh = [self.alloc(2 * 192).rearrange("p (c n) -> p c n", n=192) for _ in range(2)]
        rWUQh = [Res(), Res()]
        QH = [self.alloc(T) for _ in range(2)]
        KH = [self.alloc(T) for _ in range(2)]
        rQH = [[Res() for _ in TILES] for _ in range(2)]
        rKH = [[Res() for _ in TILES] for _ in range(2)]
        rKHpe = [Res(), Res()]
        VH = [self.alloc(18 * 128).rearrange("p (j n) -> p j n", n=128) for _ in range(2)]
        rVH = [Res(), Res()]
        NPT = 6
        PT = [self.alloc(512) for _ in range(NPT)]
        rPT = [Res() for _ in range(NPT)]
        RD = self.MU
        rRD = self.rMU
        pt_i = 0
        S.op("pool", lambda e: e.memset(VH[0][:, :, 64:128], 1.0), writes=[rVH[0]])
        S.op("pool", lambda e: e.memset(VH[1][:, :, 0:64], 1.0), writes=[rVH[1]])

        def kcols(j):
            return (C0 + 128 * j) if j < 2 else (L0 + 128 * (j - 2))

        def ktile(j):
            return 0 if j < 2 else 1 + (j - 2) // 4

        def prep(h):
            s = h % 2
            voff = 0 if s == 0 else 64
            wq, rwq = WUQh[s], rWUQh[s]
            self.load_w(wq, uq_v[:, :, h * 192:(h + 1) * 192], rwq)
            for ti in qtiles:
                c0, w = TILES[ti]
                pss = []
                for var in range(2):
                    ps, rps = self.psum()
                    pss.append((ps, rps))

                    def mm(e, ps=ps, var=var, c0=c0, w=w):
                        e.matmul(ps[0:96, 0:w], lhsT=wq[:, 0, var * 96:var * 96 + 96], rhs=CQ[:, 0, c0:c0 + w], start=True, stop=False)
                        return e.matmul(ps[0:96, 0:w], lhsT=wq[:, 1, var * 96:var * 96 + 96], rhs=CQ[:, 1, c0:c0 + w], start=False, stop=True)
                    S.op("pe", mm, reads=[rwq, rCQ[ti]], writes=[rps])
                rope_norm(pss[0][0], pss[0][1], pss[1][0], pss[1][1], V96("mla_gq"), V96("mla_gqp"),
                          QH[s][0:96, c0:c0 + w], rQH[s][ti], 0, 96, c0, w)
            S.op("pool", lambda e: e.tensor_copy(out=KH[s][64:96, :], in_=KPE[64:96, :]), reads=rKPE, writes=[rKHpe[s]])
            for ti in all_tiles:
                c0, w = TILES[ti]
                ps, rps = self.psum()
                S.op("pe", lambda e, ps=ps, c0=c0, w=w: e.matmul(ps[0:64, 0:w], lhsT=WUKV[:, h * 128:h * 128 + 64], rhs=CKV[:, c0:c0 + w], start=True, stop=True),
                     reads=[rWUKV, rCKV[ti]], writes=[rps])
                S.op("act", lambda e, ps=ps, w=w: e.activation(out=SQh[0:64, 0:w], in_=ps[0:64, 0:w], func=AF.Square), reads=[], writes=[rps, rSQh])
                ps2, rps2 = self.psum()
                S.op("pe", lambda e, ps2=ps2, w=w: e.matmul(ps2[0:64, 0:w], lhsT=self.BD[0:64, 0:64], rhs=SQh[0:64, 0:w], start=True, stop=True),
                     reads=[rSQh, self.rID], writes=[rps2])
                S.op("act", lambda e, ps2=ps2, w=w: e.activation(out=self.RSTD[0:64, 0:w], in_=ps2[0:64, 0:w], func=AF.Ln, scale=1.0 / 64, bias=self.EPSB[0:64, 0:1]),
                     reads=[self.rONES], writes=[rps2, self.rRSTD])
                S.op("act", lambda e, w=w: e.activation(out=self.RSTD[0:64, 0:w], in_=self.RSTD[0:64, 0:w], func=AF.Exp, scale=-0.5), reads=[], writes=[self.rRSTD])
                S.op("dve", lambda e, ps=ps, c0=c0, w=w: e.scalar_tensor_tensor(out=KH[s][0:64, c0:c0 + w], in0=ps[0:64, 0:w], scalar=self.VEC[0:64, off["mla_gk"]:off["mla_gk"] + 1],
                                                                             in1=self.RSTD[0:64, 0:w], op0=ALU.mult, op1=ALU.mult),
                     reads=[self.rVEC, self.rRSTD], writes=[rps, rKH[s][ti]])
            for j0 in range(0, 18, 8):
                n = min(8, 18 - j0)
                ps, rps = self.psum()

                def mm(e, ps=ps, j0=j0, n=n):
                    ins = None
                    for jj in range(n):
                        kc0 = kcols(j0 + jj)
                        ins = e.matmul(ps[:, jj * 64:(jj + 1) * 64], lhsT=CKV[:, kc0:kc0 + 128], rhs=WUKV[:, h * 128 + 64:(h + 1) * 128], start=True, stop=True)
                    return ins
                S.op("pe", mm, reads=[rWUKV] + rCKV, writes=[rps])
                S.op("dve", lambda e, ps=ps, j0=j0, n=n: e.tensor_copy(out=VH[s][:, j0:j0 + n, voff:voff + 64], in_=ps[:, 0:n * 64].rearrange("p (j d) -> p j d", d=64)),
                     reads=[], writes=[rps, rVH[s]])

        def attend(h):
            nonlocal pt_i
            s = h % 2
            ch = h // 2
            items = []
            for ti in qtiles:
                keys = list(range(18)) if ti > 0 else [0, 1]
                for j in keys:
                    items.append((ti, j, j == keys[0], j == keys[-1]))
            LA = 2
            qk = {}

            def issue_qk(idx):
                ti, j, _, _ = items[idx]
                c0, w = TILES[ti]
                kc0 = kcols(j)
                ps, rps = self.psum()
                S.op("pe", lambda e, ps=ps: e.matmul(ps[:, 0:w], lhsT=KH[s][0:96, kc0:kc0 + 128], rhs=QH[s][0:96, c0:c0 + w], start=True, stop=True),
                     reads=[rKH[s][ktile(j)], rKHpe[s], rQH[s][ti]], writes=[rps])
                qk[idx] = (ps, rps)
            for idx in range(min(LA, len(items))):
                issue_qk(idx)
            pso, rpso = None, None
            for idx, (ti, j, first, lastk) in enumerate(items):
                c0, w = TILES[ti]
                if idx + LA < len(items):
                    issue_qk(idx + LA)
                if first:
                    pso, rpso = self.psum_acc()
                ps, rps = qk.pop(idx)
                pt, rpt = PT[pt_i], rPT[pt_i]
                pt_i = (pt_i + 1) % NPT
                S.op("act", lambda e, ps=ps, pt=pt, w=w: e.activation(out=pt[:, 0:w], in_=ps[:, 0:w], func=AF.Exp, scale=att_scale),
                     reads=[], writes=[rps, rpt])
                S.op("pe", lambda e, pt=pt, j=j, w=w, pso=pso, first=first, lastk=lastk: e.matmul(
                    pso[:, 0:w], lhsT=VH[s][:, j, :], rhs=pt[:, 0:w], start=first, stop=lastk),
                    reads=[rpt, rVH[s]], writes=[rpso])
                if lastk:
                    if s == 0:
                        S.op("dve", lambda e, pso=pso, w=w: e.reciprocal(out=RD[64:128, 0:w], in_=pso[64:128, 0:w]), reads=[], writes=[rpso, rRD])
                        S.op("dve", lambda e, pso=pso, c0=c0, w=w: e.tensor_tensor(out=self.H[0:64, ch, c0:c0 + w], in0=pso[0:64, 0:w], in1=RD[64:128, 0:w], op=ALU.mult),
                             reads=[rRD], writes=[rpso, self.rH[ch][ti]])
                    else:
                        S.op("dve", lambda e, pso=pso, w=w: e.reciprocal(out=RD[0:64, 0:w], in_=pso[0:64, 0:w]), reads=[], writes=[rpso, rRD])
                        S.op("dve", lambda e, pso=pso, c0=c0, w=w: e.tensor_tensor(out=self.H[64:128, ch, c0:c0 + w], in0=pso[64:128, 0:w], in1=RD[0:64, 0:w], op=ALU.mult),
                             reads=[rRD], writes=[rpso, self.rH[ch][ti]])

        NH = 16 if DBG_HEADS is None else DBG_HEADS
        prep(0)
        for h in range(NH):
            if h + 1 < NH:
                prep(h + 1)
            attend(h)
        if DBG_STOP == 2:
            S.barrier()
            return

        S.barrier()
        self.arena_reset()
        WO = self.alloc(KC * 1024).rearrange("p (c n) -> p c n", n=1024)
        rWO = Res()
        wo_v = Wd["wo"].rearrange("(c p) n -> p c n", p=128)
        rWOh = [Res(), Res()]
        for hh in range(2):
            self.load_w(WO[:, hh * 4:(hh + 1) * 4, :], wo_v[:, hh * 4:(hh + 1) * 4, :], rWOh[hh])
        for ti in qtiles:
            c0, w = TILES[ti]
            col = 1 if ti == 0 else 0
            for dc in range(KC):
                ps, rps = self.psum()

                def mm(e, ps=ps, dc=dc, c0=c0, w=w):
                    ins = None
                    for c in range(KC):
                        ins = e.matmul(ps[:, 0:w], lhsT=WO[:, c, dc * 128:(dc + 1) * 128], rhs=self.H[:, c, c0:c0 + w], start=(c == 0), stop=(c == KC - 1))
                    return ins
                S.op("pe", mm, reads=rWOh + [self.rH[c][ti] for c in range(KC)], writes=[rps])
                S.op("dve", lambda e, ps=ps, dc=dc, c0=c0, w=w, col=col: e.scalar_tensor_tensor(
                    out=self.X[:, dc, c0:c0 + w], in0=ps[:, 0:w], scalar=self.MOD[:, 16 + dc, col:col + 1],
                    in1=self.X[:, dc, c0:c0 + w], op0=ALU.mult, op1=ALU.add),
                    reads=[self.rMOD], writes=[rps, self.rX[dc][ti]])

    def hyena(self, i, Wd, b):
        S = self.S
        off, _ = vec_offsets(i)
        V = lambda name, k=0: self.VEC[:, off[name] + k:off[name] + k + 1]
        V64 = lambda name: self.VEC[0:64, off[name]:off[name] + 1]
        tiles = [0, 1, 2, 3, 4]
        lo, hi = C0, L0 + L
        PI = float(np.pi)
        seqs = [("l", L, 16, L0, 0), ("c", LC, 2, C0, 16)]
        self.arena_reset()
        VX = self.alloc(KC * T).rearrange("p (c t) -> p c t", t=T)
        rVX = [Res() for _ in range(KC)]
        AROW = self.alloc(T)
        CROW = self.alloc(T, F32)
        X1ROW = self.alloc(T, F32)
        X0ROW = self.alloc(T)
        rA, rC, rX1, rX0R = Res(), Res(), Res(), Res()
        WA = [self.alloc(KC * 128).rearrange("p (c n) -> p c n", n=128) for _ in range(2)]
        rWA = [Res(), Res()]
        wa_i = [0]
        S.op("pool", lambda e: e.memset(AROW[:, :], 0.0), writes=[rA])
        S.op("pool", lambda e: e.memset(X0ROW[:, :], 0.0), writes=[rX0R])
        w_in = Wd["hy_in"].rearrange("(c p) n -> p c n", p=128)
        bi, wsh, bsh = off["hy_b_in"], off["hy_w_short"], off["hy_b_short"]

        def proj_conv(oc, dst, rdst):
            k = wa_i[0]
            wa_i[0] = 1 - k
            wa, rwa = WA[k], rWA[k]
            self.load_w(wa, w_in[:, :, oc * 128:(oc + 1) * 128], rwa)
            for ti in tiles:
                c0, w = TILES[ti]
                ps, rps = self.psum()

                def mm(e, ps=ps, c0=c0, w=w):
                    ins = None
                    for kc in range(KC):
                        ins = e.matmul(ps[:, 0:w], lhsT=wa[:, kc, :], rhs=self.H[:, kc, c0:c0 + w], start=(kc == 0), stop=(kc == KC - 1))
                    return ins
                S.op("pe", mm, reads=[rwa] + [self.rH[kc][ti] for kc in range(KC)], writes=[rps])
                S.op("act", lambda e, ps=ps, c0=c0, w=w: e.activation(out=AROW[:, c0:c0 + w], in_=ps[:, 0:w], func=AF.Identity,
                                                                   bias=self.VEC[:, bi + oc:bi + oc + 1], scale=1.0),
                     reads=[self.rVEC], writes=[rps, rA])
            S.op("dve", lambda e: e.tensor_scalar(out=CROW[:, lo:hi], in0=AROW[:, lo:hi], scalar1=self.VEC[:, wsh + 24 + oc:wsh + 25 + oc],
                                                  scalar2=self.VEC[:, bsh + oc:bsh + oc + 1], op0=ALU.mult, op1=ALU.add),
                 reads=[rA, self.rVEC], writes=[rC])
            S.op("dve", lambda e: e.scalar_tensor_tensor(out=CROW[:, lo:hi], in0=AROW[:, lo - 1:hi - 1], scalar=self.VEC[:, wsh + oc:wsh + oc + 1],
                                                         in1=CROW[:, lo:hi], op0=ALU.mult, op1=ALU.add),
                 reads=[rA, self.rVEC], writes=[rC])
            S.op("dve", lambda e: e.scalar_tensor_tensor(out=dst[:, lo:hi], in0=AROW[:, lo + 1:hi + 1], scalar=self.VEC[:, wsh + 48 + oc:wsh + 49 + oc],
                                                         in1=CROW[:, lo:hi], op0=ALU.mult, op1=ALU.add),
                 reads=[rA, self.rVEC, rC], writes=[rdst])

        x0d = Wd["x0d"]
        for c in range(KC):
            proj_conv(c, X0ROW, rX0R)
            S.op("sp", lambda e, c=c: e.dma_start(out=x0d[c * 128:(c + 1) * 128, :], in_=X0ROW[:, :]),
                 reads=[rX0R], writes=[self.rX0D[c]], dma=True)
            proj_conv(KC + c, X1ROW, rX1)
            proj_conv(2 * KC + c, CROW, rC)
            S.op("dve", lambda e, c=c: e.tensor_tensor(out=VX[:, c, lo:hi], in0=CROW[:, lo:hi], in1=X1ROW[:, lo:hi], op=ALU.mult),
                 reads=[rC, rX1], writes=[rVX[c]])
        VT = self.H[:, :, :].rearrange("p c t -> p (c t)")[:, 0:18 * D].rearrange("p (j n) -> p j n", n=D)
        rVT = Res()
        allH = [r for row in self.rH for r in row]
        for (tag, Lx, nt, col0, vt0) in seqs:
            for jt in range(nt):
                ps, rps = self.psum()
                psb = ps[:, :].bitcast(BF16)

                def tr(e, psb=psb, jt=jt, col0=col0):
                    ins = None
                    for c in range(KC):
                        ins = e.transpose(out=psb[:, c * 128:(c + 1) * 128], in_=VX[:, c, col0 + jt * 128:col0 + (jt + 1) * 128], identity=self.IDENT[:, :])
                    return ins
                S.op("pe", tr, reads=rVX + [self.rID], writes=[rps])
                S.op("act", lambda e, psb=psb, jt=jt, vt0=vt0: e.copy(out=VT[:, vt0 + jt, :], in_=psb[:, 0:D]),
                     reads=[], writes=[rps, rVT] + (allH if (jt == 0 and tag == "l") else []))
        S.barrier()
        if b == 0:
            for (tag, Lx, nt, col0, vt0) in seqs:
                self.hy_filter(i, Wd, tag, Lx, nt)
                S.barrier()
        w_out = Wd["hy_out"]
        for q in range(4):
            self.arena_reset()
            X0q = self.alloc(2 * T).rearrange("p (c t) -> p c t", t=T)
            rX0q = Res()
            WOq = self.alloc(2 * D).rearrange("p (c n) -> p c n", n=D)
            rWOq = Res()
            S.op("sp", lambda e, q=q: e.dma_start(out=X0q[:, :, :], in_=x0d[q * 256:(q + 1) * 256, :].rearrange("(c p) t -> p c t", p=128)),
                 reads=self.rX0D, writes=[rX0q], dma=True)
            self.load_w(WOq, w_out[q * 256:(q + 1) * 256, :].rearrange("(c p) n -> p c n", p=128), rWOq)
            Z = self.alloc(2 * 256).rearrange("p (c n) -> p c n", n=256)
            rZ = Res()
            KSb = [self.alloc(2 * 256, F32).rearrange("p (c n) -> p c n", n=256) for _ in range(2)]
            rKSb = [Res(), Res()]
            SC = [self.alloc(256, F32) for _ in range(2)]
            rSC = [Res(), Res()]
            mark = self.a_off
            for (tag, Lx, nt, col0, vt0) in seqs:
                S.barrier()
                self.arena_reset(mark)
                self.hy_conv(i, Wd, q, tag, Lx, nt, col0, vt0, VT, rVT, X0q, rX0q, WOq, rWOq, Z, rZ, KSb, rKSb, SC, rSC)
            S.barrier()

    def hy_filter(self, i, Wd, tag, Lx, nt):
        S = self.S
        off, _ = vec_offsets(i)
        V64 = lambda name: self.VEC[0:64, off[name]:off[name] + 1]
        PI = float(np.pi)
        self.arena_reset()
        ZT = self.alloc(Lx, F32)
        HD1 = self.alloc(Lx, F32)
        HD2 = self.alloc(Lx, F32)
        FW1 = self.alloc(64, F32)
        FW2 = self.alloc(64, F32)
        NEGT = self.alloc(16, F32)
        CST = self.alloc(8, F32)
        ONESF = self.alloc(128, F32)
        rZT, rHD1, rHD2, rFW, rNEGT, rCST, rONESF = (Res() for _ in range(7))
        S.op("sp", lambda e: e.dma_start(out=ZT[0:33, :], in_=Wd[f"z_{tag}"][:, :]), writes=[rZT], dma=True)
        S.op("sp", lambda e: e.dma_start(out=FW1[0:33, :], in_=Wd["fw1"][:, :]), writes=[rFW], dma=True)
        S.op("sp", lambda e: e.dma_start(out=FW2[0:64, :], in_=Wd["fw2"][:, :]), writes=[rFW], dma=True)
        S.op("sp", lambda e: e.dma_start(out=NEGT[:, 0:nt], in_=Wd[f"negt_{tag}"][:, :]), writes=[rNEGT], dma=True)
        S.op("pool", lambda e: e.memset(ONESF[:, :], 1.0), writes=[rONESF])
        S.op("pool", lambda e: e.memset(CST[:, 2:3], -PI), writes=[rCST])
        S.op("dve", lambda e: e.tensor_tensor(out=CST[0:64, 0:1], in0=V64("hy_fb1"), in1=V64("hy_fr"), op=ALU.mult), reads=[self.rVEC], writes=[rCST])
        S.op("dve", lambda e: e.tensor_tensor(out=CST[0:64, 1:2], in0=V64("hy_fb2"), in1=V64("hy_fr"), op=ALU.mult), reads=[self.rVEC], writes=[rCST])

        RR, rRR = self.MU, self.rMU

        def sin_layer(src, rsrc, kdim, wmat, cstcol, dst, rdst):
            for t0 in range(0, Lx, 512):
                w = min(512, Lx - t0)
                ps, rps = self.psum()
                S.op("pe", lambda e, ps=ps, t0=t0, w=w: e.matmul(ps[0:64, 0:w], lhsT=wmat[0:kdim, 0:64], rhs=src[0:kdim, t0:t0 + w], start=True, stop=True),
                     reads=[rFW, rsrc], writes=[rps])
                S.op("act", lambda e, ps=ps, t0=t0, w=w: e.activation(out=dst[0:64, t0:t0 + w], in_=ps[0:64, 0:w], func=AF.Identity,
                                                                   scale=V64("hy_fr"), bias=CST[0:64, cstcol:cstcol + 1]),
                     reads=[self.rVEC, rCST], writes=[rps, rdst])
                MAGIC = 12582912.0
                S.op("dve", lambda e, t0=t0, w=w: e.tensor_scalar(out=RR[0:64, 0:w], in0=dst[0:64, t0:t0 + w], scalar1=1.0 / (2.0 * PI), scalar2=MAGIC,
                                                                op0=ALU.mult, op1=ALU.add),
                     reads=[rdst], writes=[rRR])
                S.op("dve", lambda e, w=w: e.tensor_scalar(out=RR[0:64, 0:w], in0=RR[0:64, 0:w], scalar1=MAGIC, scalar2=-2.0 * PI,
                                                         op0=ALU.subtract, op1=ALU.mult),
                     reads=[], writes=[rRR])
                S.op("dve", lambda e, t0=t0, w=w: e.tensor_tensor(out=dst[0:64, t0:t0 + w], in0=dst[0:64, t0:t0 + w], in1=RR[0:64, 0:w], op=ALU.add),
                     reads=[rRR], writes=[rdst])
                S.op("dve", lambda e, t0=t0, w=w: e.tensor_scalar(out=dst[0:64, t0:t0 + w], in0=dst[0:64, t0:t0 + w], scalar1=-PI, scalar2=PI,
                                                                op0=ALU.max, op1=ALU.min),
                     reads=[], writes=[rdst])
                S.op("act", lambda e, t0=t0, w=w: e.activation(out=dst[0:64, t0:t0 + w], in_=dst[0:64, t0:t0 + w], func=AF.Sin, scale=1.0),
                     reads=[], writes=[rdst])
        sin_layer(ZT, rZT, 33, FW1, 0, HD1, rHD1)
        sin_layer(HD1, rHD1, 64, FW2, 1, HD2, rHD2)
        mark = self.a_off
        kspec = Wd[f"kspec_{tag}"]
        dfm = {0: Wd[f"dfc_{tag}"].rearrange("(j p) f -> p j f", p=128), 1: Wd[f"dfs_{tag}"].rearrange("(j p) f -> p j f", p=128)}
        nf = nt
        for q in range(4):
            S.barrier()
            self.arena_reset(mark)
            A = self.alloc(nt * 256).rearrange("p (j n) -> p j n", n=256)
            Bm = self.alloc(nt * 256).rearrange("p (j n) -> p j n", n=256)
            rAB = Res()
            FW3 = self.alloc(512, F32)
            ABSD = self.alloc(256, F32)
            SKq = self.alloc(256, F32)
            RN = self.alloc(256, F32)
            rFW3, rABSD, rSK, rRN = Res(), Res(), Res(), Res()
            DEC = self.alloc(256, F32)
            KF = self.alloc(256, F32)
            KB = self.alloc(256, F32)
            AF_ = self.alloc(256, F32)
            AB_ = self.alloc(256, F32)
            rDEC, rKF, rKB, rAF, rABb = (Res() for _ in range(5))
            DF = [self.alloc(nt * 128).rearrange("p (j n) -> p j n", n=128) for _ in range(2)]
            rDF = [Res(), Res()]
            KS = [self.alloc(256, F32) for _ in range(2)]
            rKSo = [Res(), Res()]
            S.op("sp", lambda e, q=q: e.dma_start(out=FW3[0:64, 0:256], in_=Wd["fw3"][:, q * 256:(q + 1) * 256]), writes=[rFW3], dma=True)
            S.op("sp", lambda e, q=q: e.dma_start(out=FW3[0:64, 256:512], in_=Wd["fw3"][:, D + q * 256:D + (q + 1) * 256]), writes=[rFW3], dma=True)
            S.op("sp", lambda e, q=q: e.dma_start(out=ABSD[:, :], in_=Wd["absd"][:, q * 256:(q + 1) * 256]), writes=[rABSD], dma=True)
            S.op("sp", lambda e, q=q: e.dma_start(out=SKq[:, :], in_=Wd["skipb"][:, q * 256:(q + 1) * 256]), writes=[rSK], dma=True)
            psS, rpsS = self.psum_acc()
            for j in range(nt):
                ps, rps = self.psum()
                S.op("pe", lambda e, ps=ps, j=j: e.matmul(ps[:, 0:512], lhsT=HD2[0:64, j * 128:(j + 1) * 128], rhs=FW3[0:64, 0:512], start=True, stop=True),
                     reads=[rHD2, rFW3], writes=[rps])
                S.op("act", lambda e, j=j: e.activation(out=DEC[:, :], in_=ABSD[:, :], func=AF.Exp, scale=NEGT[:, j:j + 1]),
                     reads=[rABSD, rNEGT], writes=[rDEC])
                S.op("dve", lambda e, ps=ps: e.tensor_tensor(out=KF[:, :], in0=ps[:, 0:256], in1=DEC[:, :], op=ALU.mult), reads=[rDEC], writes=[rps, rKF])
                S.op("dve", lambda e, ps=ps: e.tensor_tensor(out=KB[:, :], in0=ps[:, 256:512], in1=DEC[:, :], op=ALU.mult), reads=[rDEC], writes=[rps, rKB])
                if j == 0:
                    S.op("pool", lambda e: e.memset(KB[0:1, :], 0.0), writes=[rKB])
                S.op("pool", lambda e, j=j: e.tensor_tensor(out=A[:, j, :], in0=KF[:, :], in1=KB[:, :], op=ALU.add), reads=[rKF, rKB], writes=[rAB])
                S.op("pool", lambda e, j=j: e.tensor_tensor(out=Bm[:, j, :], in0=KF[:, :], in1=KB[:, :], op=ALU.subtract), reads=[rKF, rKB], writes=[rAB])
                S.op("act", lambda e: e.activation(out=AF_[:, :], in_=KF[:, :], func=AF.Abs), reads=[rKF], writes=[rAF])
                S.op("act", lambda e: e.activation(out=AB_[:, :], in_=KB[:, :], func=AF.Abs), reads=[rKB], writes=[rABb])
                S.op("pe", lambda e, j=j: e.matmul(psS[:, 0:256], lhsT=ONESF[:, :], rhs=AF_[:, :], start=(j == 0), stop=False), reads=[rONESF, rAF], writes=[rpsS])
                S.op("pe", lambda e, j=j: e.matmul(psS[:, 0:256], lhsT=ONESF[:, :], rhs=AB_[:, :], start=False, stop=(j == nt - 1)), reads=[rONESF, rABb], writes=[rpsS])
            S.op("dve", lambda e: e.tensor_scalar_add(out=RN[:, :], in0=psS[:, 0:256], scalar1=EPS), reads=[], writes=[rpsS, rRN])
            S.op("dve", lambda e: e.reciprocal(out=RN[:, :], in_=RN[:, :]), reads=[], writes=[rRN])
            for fb in range(nf):
                for typ in range(2):
                    S.op("sp", lambda e, fb=fb, typ=typ: e.dma_start(out=DF[typ][:, :, :], in_=dfm[typ][:, :, fb * 128:(fb + 1) * 128]), writes=[rDF[typ]], dma=True)
                    ps, rps = self.psum()
                    src = A if typ == 0 else Bm

                    def mm(e, ps=ps, typ=typ, src=src):
                        ins = None
                        for j in range(nt):
                            ins = e.matmul(ps[:, 0:256], lhsT=DF[typ][:, j, :], rhs=src[:, j, :], start=(j == 0), stop=(j == nt - 1))
                        return ins
                    S.op("pe", mm, reads=[rDF[typ], rAB], writes=[rps])
                    S.op("dve", lambda e, ps=ps, typ=typ: e.tensor_tensor(out=KS[typ][:, :], in0=ps[:, 0:256], in1=RN[:, :], op=ALU.mult), reads=[rRN], writes=[rps, rKSo[typ]])
                    if typ == 0:
                        S.op("pool", lambda e: e.tensor_tensor(out=KS[0][:, :], in0=KS[0][:, :], in1=SKq[:, :], op=ALU.add), reads=[rSK], writes=[rKSo[0]])
                    S.op("sp", lambda e, fb=fb, typ=typ, q=q: e.dma_start(out=kspec[typ, fb * 128:(fb + 1) * 128, q * 256:(q + 1) * 256], in_=KS[typ][:, :]),
                         reads=[rKSo[typ]], writes=[self.rKS[tag]], dma=True)

    def hy_conv(self, i, Wd, q, tag, Lx, nt, col0, vt0, VT, rVT, X0q, rX0q, WOq, rWOq, Z, rZ, KSb, rKSb, SC, rSC):
        S = self.S
        off, _ = vec_offsets(i)
        nf = nt
        kspec = Wd[f"kspec_{tag}"]
        dfm = {0: Wd[f"dfc_{tag}"].rearrange("(j p) f -> p j f", p=128), 1: Wd[f"dfs_{tag}"].rearrange("(j p) f -> p j f", p=128)}
        idm = {0: Wd[f"idc_{tag}"].rearrange("(j p) t -> p j t", p=128), 1: Wd[f"ids_{tag}"].rearrange("(j p) t -> p j t", p=128)}
        Y = self.alloc(nf * 512).rearrange("p (f y n) -> p f y n", y=2, n=256)
        rY = Res()
        DF = [self.alloc(nt * 128).rearrange("p (j n) -> p j n", n=128) for _ in range(2)]
        rDF = [Res(), Res()]
        IB = [self.alloc(2 * nf * 256).rearrange("p (y f n) -> p y f n", y=2, n=256) for _ in range(2)]
        rIB = [Res(), Res()]
        U = [self.TMP[0], self.TMP[1]]
        rU = [self.rTMP[0], self.rTMP[1]]
        col = 1 if tag == "c" else 0
        bo = off["hy_b_out"]
        for fb in range(nf):
            kb = fb % 2
            S.op("sp", lambda e, fb=fb, kb=kb: e.dma_start(out=KSb[kb][:, :, :], in_=kspec[:, fb * 128:(fb + 1) * 128, q * 256:(q + 1) * 256].rearrange("y p n -> p y n")),
                 reads=[self.rKS[tag]], writes=[rKSb[kb]], dma=True)
            pss = []
            for typ in range(2):
                S.op("sp", lambda e, fb=fb, typ=typ: e.dma_start(out=DF[typ][:, :, :], in_=dfm[typ][:, :, fb * 128:(fb + 1) * 128]), writes=[rDF[typ]], dma=True)
                ps, rps = self.psum()
                pss.append((ps, rps))

                def mm(e, ps=ps, typ=typ):
                    ins = None
                    for j in range(nt):
                        ins = e.matmul(ps[:, 0:256], lhsT=DF[typ][:, j, :], rhs=VT[:, vt0 + j, q * 256:(q + 1) * 256], start=(j == 0), stop=(j == nt - 1))
                    return ins
                S.op("pe", mm, reads=[rDF[typ], rVT], writes=[rps])
                S.op("act", lambda e, ps=ps, typ=typ: e.copy(out=U[typ][:, 0:256], in_=ps[:, 0:256]), reads=[], writes=[rps, rU[typ]])
            Kc, Ks = KSb[kb][:, 0, :], KSb[kb][:, 1, :]
            S.op("dve", lambda e, Kc=Kc: e.tensor_tensor(out=SC[0][:, :], in0=U[0][:, 0:256], in1=Kc, op=ALU.mult), reads=[rU[0], rKSb[kb]], writes=[rSC[0]])
            S.op("pool", lambda e, Ks=Ks: e.tensor_tensor(out=SC[1][:, :], in0=U[1][:, 0:256], in1=Ks, op=ALU.mult), reads=[rU[1], rKSb[kb]], writes=[rSC[1]])
            S.op("dve", lambda e, fb=fb: e.tensor_tensor(out=Y[:, fb, 0, :], in0=SC[0][:, :], in1=SC[1][:, :], op=ALU.subtract), reads=[rSC[0], rSC[1]], writes=[rY])
            S.op("dve", lambda e, Ks=Ks: e.tensor_tensor(out=SC[0][:, :], in0=U[0][:, 0:256], in1=Ks, op=ALU.mult), reads=[rU[0], rKSb[kb]], writes=[rSC[0]])
            S.op("pool", lambda e, Kc=Kc: e.tensor_tensor(out=SC[1][:, :], in0=U[1][:, 0:256], in1=Kc, op=ALU.mult), reads=[rU[1], rKSb[kb]], writes=[rSC[1]])
            S.op("dve", lambda e, fb=fb: e.tensor_tensor(out=Y[:, fb, 1, :], in0=SC[0][:, :], in1=SC[1][:, :], op=ALU.add), reads=[rSC[0], rSC[1]], writes=[rY])
        for ts in range(Lx // 256):
            ib, rib = IB[ts % 2], rIB[ts % 2]
            for typ in range(2):
                S.op("sp", lambda e, ib=ib, typ=typ, ts=ts: e.dma_start(out=ib[:, typ, :, :], in_=idm[typ][:, :, ts * 256:(ts + 1) * 256]), writes=[rib], dma=True)
            cs = col0 + ts * 256
            ti = 0 if tag == "c" else 1 + (ts // 2)
            for cc in range(2):
                ps, rps = self.psum()

                def mm(e, ps=ps, ib=ib, cc=cc):
                    ins = None
                    for fb in range(nf):
                        for typ in range(2):
                            ins = e.matmul(ps[:, 0:256], lhsT=Y[:, fb, typ, cc * 128:(cc + 1) * 128], rhs=ib[:, typ, fb, :],
                                           start=(fb == 0 and typ == 0), stop=(fb == nf - 1 and typ == 1))
                    return ins
                S.op("pe", mm, reads=[rY, rib], writes=[rps])
                S.op("dve", lambda e, ps=ps, cc=cc, cs=cs: e.tensor_tensor(out=Z[:, cc, :], in0=ps[:, 0:256], in1=X0q[:, cc, cs:cs + 256], op=ALU.mult),
                     reads=[rX0q], writes=[rps, rZ])
            for o in range(KC):
                ps, rps = self.psum()

                def mm2(e, ps=ps, o=o):
                    e.matmul(ps[:, 0:256], lhsT=WOq[:, 0, o * 128:(o + 1) * 128], rhs=Z[:, 0, :], start=True, stop=False)
                    return e.matmul(ps[:, 0:256], lhsT=WOq[:, 1, o * 128:(o + 1) * 128], rhs=Z[:, 1, :], start=False, stop=True)
                S.op("pe", mm2, reads=[rWOq, rZ], writes=[rps])
                if q == 0:
                    S.op("dve", lambda e, ps=ps, o=o: e.tensor_scalar(out=self.RSTD[:, 0:256], in0=ps[:, 0:256], scalar1=self.VEC[:, bo + o:bo + o + 1],
                                                                   scalar2=self.MOD[:, 16 + o, col:col + 1], op0=ALU.add, op1=ALU.mult),
                         reads=[self.rVEC, self.rMOD], writes=[rps, self.rRSTD])
                    S.op("dve", lambda e, o=o, cs=cs: e.tensor_tensor(out=self.X[:, o, cs:cs + 256], in0=self.X[:, o, cs:cs + 256], in1=self.RSTD[:, 0:256], op=ALU.add),
                         reads=[self.rRSTD], writes=[self.rX[o][ti]])
                else:
                    S.op("dve", lambda e, ps=ps, o=o, cs=cs: e.scalar_tensor_tensor(out=self.X[:, o, cs:cs + 256], in0=ps[:, 0:256], scalar=self.MOD[:, 16 + o, col:col + 1],
                                                                                  in1=self.X[:, o, cs:cs + 256], op0=ALU.mult, op1=ALU.add),
                         reads=[self.rMOD], writes=[rps, self.rX[o][ti]])

    def layer(self, i, Wd, b, last):
        kind = i % 3
        tiles_all = [0, 1, 2, 3, 4]
        tiles_out = [1, 2, 3, 4] if last else tiles_all
        self.adaln(i, Wd)
        if kind == 0:
            self.modulate(0, tiles_all)
            self.S.barrier()
            self.mla(i, Wd, need_ctx=not last)
            self.S.barrier()
            self.carve_ffn()
            self.zero_pads()
        if kind == 2:
            self.modulate(0, tiles_all)
            self.S.barrier()
            self.hyena(i, Wd, b)
            self.S.barrier()
            self.carve_ffn()
            self.zero_pads()
        if kind == 1:
            self.modulate(0, tiles_out)
            self.conformer(i, Wd, tiles_out)
        self.modulate(1, tiles_out)
        self.ffn(i, Wd, tiles_out)

    def ubuf(self, c):
        if c < 6:
            return self.G[:, c, :], self.rG[c]
        return (self.AROW, self.rA) if c == 6 else (self.VROW, self.rV)

    def conformer(self, i, Wd, tiles):
        S = self.S
        off, _ = vec_offsets(i)
        w1 = Wd["cf_w1"].rearrange("(c p) n -> p c n", p=128)
        w2 = Wd["cf_w2"].rearrange("(c p) n -> p c n", p=128)
        b1o, wdo, bdo, lgo, lbo, b2o = (off[k] for k in ("cf_b_pw1", "cf_w_dw", "cf_b_dw", "cf_ln_g", "cf_ln_b", "cf_b_pw2"))
        for (a, bnd) in ((0, 16), (C0 + LC, L0), (L0 + L, T)):
            S.op("pool", lambda e, a=a, bnd=bnd: e.memset(self.G[:, :, a:bnd], 0.0), writes=self.rG)
        for c in range(KC):
            k = self.wa_i
            self.wa_i = 1 - k
            wa, rwa = self.WA[k], self.rWA[k]
            self.load_w(wa[:, :, 0:128], w1[:, :, c * 128:(c + 1) * 128], rwa[0])
            self.load_w(wa[:, :, 128:256], w1[:, :, D + c * 128:D + (c + 1) * 128], rwa[1])
            ub, rub = self.ubuf(c)
            for ti in tiles:
                c0, w = TILES[ti]
                pss = []
                for half in range(2):
                    ps, rps = self.psum()
                    pss.append((ps, rps))

                    def mm(e, ps=ps, wa=wa, half=half, c0=c0, w=w):
                        ins = None
                        for kc in range(KC):
                            ins = e.matmul(ps[:, 0:w], lhsT=wa[:, kc, half * 128:(half + 1) * 128],
                                           rhs=self.H[:, kc, c0:c0 + w], start=(kc == 0), stop=(kc == KC - 1))
                        return ins
                    S.op("pe", mm, reads=[rwa[half]] + [self.rH[kc][ti] for kc in range(KC)], writes=[rps])
                tp, rtp = self.tmp()
                S.op("act", lambda e, ps=pss[1][0], tp=tp, w=w, c=c: e.activation(
                    out=tp[:, 0:w], in_=ps[:, 0:w], func=AF.Sigmoid, bias=self.VEC[:, b1o + 8 + c:b1o + 9 + c], scale=1.0),
                    reads=[self.rVEC], writes=[pss[1][1], rtp])
                S.op("dve", lambda e, ps=pss[0][0], tp=tp, ub=ub, c0=c0, w=w, c=c: e.scalar_tensor_tensor(
                    out=ub[:, c0:c0 + w], in0=ps[:, 0:w], scalar=self.VEC[:, b1o + c:b1o + c + 1], in1=tp[:, 0:w],
                    op0=ALU.add, op1=ALU.mult),
                    reads=[self.rVEC, rtp], writes=[pss[0][1], rub])
        dg = self.WB[:, 0:31 * 128].rearrange("p (k n) -> p k n", n=128)
        for c in range(KC):
            ub, rub = self.ubuf(c)
            for kk in range(31):
                S.op("pool" if kk % 2 else "dve", lambda e, kk=kk, c=c: e.tensor_scalar_mul(
                    out=dg[:, kk, :], in0=self.IDENT[:, :], scalar1=self.VEC[:, wdo + kk * 8 + c:wdo + kk * 8 + c + 1]),
                    reads=[self.rID, self.rVEC], writes=[self.rWB])
            for ti in tiles:
                c0, w = TILES[ti]
                ps, rps = self.psum()

                def mm(e, ps=ps, ub=ub, c0=c0, w=w):
                    ins = None
                    for kk in range(31):
                        ins = e.matmul(ps[:, 0:w], lhsT=dg[:, kk, :], rhs=ub[:, c0 + kk - 15:c0 + kk - 15 + w],
                                       start=(kk == 0), stop=(kk == 30))
                    return ins
                S.op("pe", mm, reads=[self.rWB, rub], writes=[rps])
                S.op("act", lambda e, ps=ps, c=c, c0=c0, w=w: e.activation(
                    out=self.H[:, c, c0:c0 + w], in_=ps[:, 0:w], func=AF.Identity, bias=self.VEC[:, bdo + c:bdo + c + 1], scale=1.0),
                    reads=[self.rVEC], writes=[rps, self.rH[c][ti]])
        for k in range(2):
            self.load_w(self.WA[k][:, :, :], w2[:, :, k * 512:(k + 1) * 512], self.rWA[k])
        for ti in tiles:
            c0, w = TILES[ti]
            col = 1 if ti == 0 else 0
            for c in range(KC):
                S.op("act", lambda e, c=c, c0=c0, w=w: e.activation(out=self.SQ[:, c, 0:w], in_=self.H[:, c, c0:c0 + w], func=AF.Square),
                     reads=[self.rH[c][ti]], writes=[self.rSQ])
            ps1, rps1 = self.psum()
            ps2, rps2 = self.psum()

            def mm1(e, ps=ps1, c0=c0, w=w):
                ins = None
                for c in range(KC):
                    ins = e.matmul(ps[:, 0:w], lhsT=self.ONES[:, :], rhs=self.H[:, c, c0:c0 + w], start=(c == 0), stop=(c == KC - 1))
                return ins
            S.op("pe", mm1, reads=[self.rONES] + [self.rH[c][ti] for c in range(KC)], writes=[rps1])

            def mm2(e, ps=ps2, w=w):
                ins = None
                for c in range(KC):
                    ins = e.matmul(ps[:, 0:w], lhsT=self.ONES[:, :], rhs=self.SQ[:, c, 0:w], start=(c == 0), stop=(c == KC - 1))
                return ins
            S.op("pe", mm2, reads=[self.rONES, self.rSQ], writes=[rps2])
            S.op("act", lambda e, ps=ps1, w=w: e.activation(out=self.MU[:, 0:w], in_=ps[:, 0:w], func=AF.Identity, scale=1.0 / D),
                 reads=[], writes=[rps1, self.rMU])
            tp, rtp = self.tmp()
            S.op("dve", lambda e, tp=tp, w=w: e.tensor_tensor(out=tp[:, 0:w], in0=self.MU[:, 0:w], in1=self.MU[:, 0:w], op=ALU.mult),
                 reads=[self.rMU], writes=[rtp])
            S.op("dve", lambda e, ps=ps2, tp=tp, w=w: e.scalar_tensor_tensor(
                out=self.RSTD[:, 0:w], in0=ps[:, 0:w], scalar=1.0 / D, in1=tp[:, 0:w], op0=ALU.mult, op1=ALU.subtract),
                reads=[rtp], writes=[rps2, self.rRSTD])
            S.op("act", lambda e, w=w: e.activation(out=self.RSTD[:, 0:w], in_=self.RSTD[:, 0:w], func=AF.Ln, scale=1.0, bias=self.EPSB[:, 0:1]),
                 reads=[self.rONES], writes=[self.rRSTD])
            S.op("act", lambda e, w=w: e.activation(out=self.RSTD[:, 0:w], in_=self.RSTD[:, 0:w], func=AF.Exp, scale=-0.5), reads=[], writes=[self.rRSTD])
            for c in range(KC):
                ub, rub = self.ubuf(c)
                tp, rtp = self.tmp()
                S.op("dve", lambda e, tp=tp, c=c, c0=c0, w=w: e.tensor_tensor(out=tp[:, 0:w], in0=self.H[:, c, c0:c0 + w], in1=self.MU[:, 0:w], op=ALU.subtract),
                     reads=[self.rH[c][ti], self.rMU], writes=[rtp])
                S.op("dve", lambda e, tp=tp, w=w: e.tensor_tensor(out=tp[:, 0:w], in0=tp[:, 0:w], in1=self.RSTD[:, 0:w], op=ALU.mult),
                     reads=[self.rRSTD], writes=[rtp])
                S.op("act", lambda e, tp=tp, ub=ub, c=c, c0=c0, w=w: e.activation(
                    out=ub[:, c0:c0 + w], in_=tp[:, 0:w], func=AF.Silu, scale=self.VEC[:, lgo + c:lgo + c + 1], bias=self.VEC[:, lbo + c:lbo + c + 1]),
                    reads=[rtp, self.rVEC], writes=[rub])
            for dc in range(KC):
                ps, rps = self.psum()
                wa, rwa = self.WA[dc // 4], self.rWA[dc // 4]

                def mm(e, ps=ps, wa=wa, dc=dc, c0=c0, w=w):
                    ins = None
                    for c in range(KC):
                        ins = e.matmul(ps[:, 0:w], lhsT=wa[:, c, (dc % 4) * 128:(dc % 4 + 1) * 128], rhs=self.ubuf(c)[0][:, c0:c0 + w],
                                       start=(c == 0), stop=(c == KC - 1))
                    return ins
                S.op("pe", mm, reads=[rwa[0]] + [self.ubuf(c)[1] for c in range(KC)], writes=[rps])
                tp, rtp = self.tmp()
                S.op("dve", lambda e, ps=ps, tp=tp, dc=dc, w=w, col=col: e.tensor_scalar(
                    out=tp[:, 0:w], in0=ps[:, 0:w], scalar1=self.VEC[:, b2o + dc:b2o + dc + 1], scalar2=self.MOD[:, 16 + dc, col:col + 1],
                    op0=ALU.add, op1=ALU.mult),
                    reads=[self.rVEC, self.rMOD], writes=[rps, rtp])
                S.op("dve", lambda e, tp=tp, dc=dc, c0=c0, w=w: e.tensor_tensor(out=self.X[:, dc, c0:c0 + w], in0=self.X[:, dc, c0:c0 + w], in1=tp[:, 0:w], op=ALU.add),
                     reads=[rtp], writes=[self.rX[dc][ti]])


def make_inputs(inp, layers, core, nb=NB, x_override=None, ctx_override=None):
    m = {}
    xs = inp["x"] if x_override is None else x_override
    cs = inp["ctx"] if ctx_override is None else ctx_override
    xT = np.zeros((nb, D, T), np.float32)
    cT = np.zeros((nb, 128, KC * 2), np.float32)
    for bb in range(nb):
        b = core * nb + bb
        xT[bb, :, L0:L0 + L] = xs[b].T
        xT[bb, :, C0:C0 + LC] = cs[b].T
        sv = np.stack([inp["c"][b], inp["c_ctx"]], axis=-1)
        cT[bb] = sv.reshape(KC, 128, 2).transpose(1, 0, 2).reshape(128, KC * 2)
    m["ident"] = np.eye(128, dtype=np.float32)
    bd = np.zeros((128, 128), np.float32)
    bd[:64, :64] = 1.0
    bd[64:96, 64:96] = 1.0
    m["bd96"] = bd
    m["xT"] = xT
    m["cT"] = cT
    for i in layers:
        kind, j = i % 3, i // 3
        v, _ = pack_layer_vec(inp, i)
        vp = np.zeros((128, NPV), np.float32)
        vp[:, :v.shape[1]] = v
        m[f"vec{i}"] = vp
        m[f"ada_w{i}"] = np.ascontiguousarray(inp["ada_w"][i])
        m[f"ffn_up{i}"] = np.ascontiguousarray(inp["ffn_w_up"][i])
        m[f"ffn_down{i}"] = np.ascontiguousarray(inp["ffn_w_down"][i])
        if kind == 0:
            pe = np.arange(32)
            perm = np.where((pe % 16) < 8, pe + 8, pe - 8)
            wdkv = np.asarray(inp["mla_w_dkv"][j], np.float32)
            kv = np.zeros((D, 320), np.float32)
            kv[:, :128] = wdkv[:, :128]
            kv[:, 128 + 64:224] = wdkv[:, 128:160]
            kv[:, 224 + 64:320] = wdkv[:, 128 + perm]
            m[f"mla_kv{i}"] = kv
            wuq = np.asarray(inp["mla_w_uq"][j], np.float32).reshape(256, 16, 96)
            uq = np.zeros((256, 16, 192), np.float32)
            uq[:, :, :96] = wuq
            uq[:, :, 96 + 64:] = wuq[:, :, 64 + perm]
            m[f"mla_uq{i}"] = uq.reshape(256, 16 * 192)
            m[f"mla_dq{i}"] = np.ascontiguousarray(inp["mla_w_dq"][j])
            m[f"mla_ukv{i}"] = np.ascontiguousarray(inp["mla_w_ukv"][j])
            m[f"mla_o{i}"] = np.ascontiguousarray(inp["mla_w_o"][j])
            m["rope_cs"] = rope_tables()
        if kind == 2:
            m[f"hy_in{i}"] = np.ascontiguousarray(inp["hy_w_in"][j])
            m[f"hy_out{i}"] = np.ascontiguousarray(inp["hy_w_out"][j])
            m[f"hy_fw1_{i}"] = np.ascontiguousarray(inp["hy_f_w1"][j])
            m[f"hy_fw2_{i}"] = np.ascontiguousarray(inp["hy_f_w2"][j])
            m[f"hy_fw3_{i}"] = np.ascontiguousarray(inp["hy_f_w3"][j])
            m[f"hy_skipb{i}"] = np.ascontiguousarray(np.broadcast_to(np.asarray(inp["hy_skip"][j], np.float32)[None, :], (128, D)))
            m.update(hy_consts())
        if kind == 1:
            m[f"cf_w1_{i}"] = np.ascontiguousarray(inp["cf_w_pw1"][j])
            m[f"cf_w2_{i}"] = np.ascontiguousarray(inp["cf_w_pw2"][j])
    return m


_HYC = {}


def hy_consts():
    if _HYC:
        return _HYC
    import ml_dtypes
    bf = ml_dtypes.bfloat16
    out = {}
    for tag, Lx in (("l", L), ("c", LC)):
        N = 2 * Lx
        tau = np.arange(Lx, dtype=np.float64)[:, None]
        f = np.arange(Lx, dtype=np.float64)[None, :] + 0.5
        th = 2 * np.pi * tau * f / N
        out[f"dfc_{tag}"] = np.cos(th).astype(np.float32).astype(bf)
        out[f"dfs_{tag}"] = np.sin(th).astype(np.float32).astype(bf)
        out[f"idc_{tag}"] = np.ascontiguousarray(((2.0 / N) * np.cos(th).T).astype(np.float32)).astype(bf)
        out[f"ids_{tag}"] = np.ascontiguousarray(((2.0 / N) * np.sin(th).T).astype(np.float32)).astype(bf)
        t = np.linspace(0.0, 1.0, Lx, dtype=np.float32)
        w = (2.0 * np.pi * np.arange(Lx, dtype=np.float32) / Lx).astype(np.float32)
        fr = np.linspace(1e-4, 15.0, 16, dtype=np.float32)
        fw = w[:, None] * fr[None, :]
        z = np.concatenate([t[:, None], np.cos(fw), -np.sin(fw)], axis=-1).astype(np.float32)
        out[f"z_{tag}"] = np.ascontiguousarray(z.T)
        out[f"negt_{tag}"] = np.ascontiguousarray((-t).reshape(Lx // 128, 128).T)
    deltas = np.linspace(np.log(1e-2) / 0.3, np.log(1e-2) / 1.5, D, dtype=np.float32)
    out["absd"] = np.ascontiguousarray(np.broadcast_to(np.abs(deltas)[None, :], (128, D))).astype(np.float32)
    _HYC.update(out)
    return _HYC


def rope_tables():
    t = np.arange(L)
    row = (t // 64).astype(np.float32)
    colp = (t % 64).astype(np.float32)
    inv = (10000.0 ** (-(np.arange(0, 16, 2, dtype=np.float32) / 16.0))).astype(np.float32)
    ang = np.concatenate([row[:, None] * inv, colp[:, None] * inv], axis=-1).astype(np.float32)
    cs = np.zeros((128, 2, T), np.float32)
    cs[:, 0, :] = 1.0
    for d in range(32):
        a, b, jj = d // 16, (d % 16) // 8, d % 8
        cs[64 + d, 0, L0:L0 + L] = np.cos(ang[:, a * 8 + jj])
        cs[64 + d, 1, L0:L0 + L] = np.sin(ang[:, a * 8 + jj]) * (-1.0 if b == 0 else 1.0)
    return cs


def kernel(**inputs):
    inp = {k: np.asarray(v) for k, v in inputs.items()}
    layers = list(range(DEPTH))
    prog = Prog(layers)
    nc = prog.build()
    in_maps = [make_inputs(inp, layers, c) for c in range(8)]
    res = run_bass_kernel_spmd(nc, in_maps, core_ids=list(range(8)))
    out = np.empty((16, L, D), np.float32)
    for c in range(8):
        y = res.results[c]["yT"]
        for bb in range(NB):
            out[c * NB + bb] = y[bb].T
    return out
```

```python
import contextlib
import os
import numpy as np
import concourse.bass as bass
import concourse.mybir as mybir
from concourse.bass_utils import run_bass_kernel_spmd

F32 = mybir.dt.float32
BF16 = mybir.dt.bfloat16
ALU = mybir.AluOpType
AF = mybir.ActivationFunctionType

D = 1024
KC = 8
L = 2048
LC = 256
T = 16 + LC + 16 + L + 16
C0 = 16
L0 = 16 + LC + 16
TILES = [(C0, LC)] + [(L0 + 512 * j, 512) for j in range(4)]
D_FF = 2816
FC = 22
DEPTH = 4
EPS = 1e-6
NB = 2
DBG_STOP = int(os.environ.get('MK_STOP', '0'))
DBG_HEADS = (int(os.environ['MK_HEADS']) if 'MK_HEADS' in os.environ else None)


class Res:
    __slots__ = ("w", "r")

    def __init__(self):
        self.w = None
        self.r = {}


class Sched:
    ENG = ("pe", "act", "dve", "pool", "sp")
    NDMA = 8
    EPOCH = 6000

    def __init__(self):
        self.streams = {e: [] for e in self.ENG}
        self.cnt = {e: 0 for e in self.ENG}
        self.waited = {e: {} for e in self.ENG}
        self.dma_n = {e: 0 for e in self.ENG}
        self.last = {}

    def op(self, eng, fn, reads=(), writes=(), dma=False):
        deps = {}

        def add(k, v):
            if deps.get(k, 0) < v:
                deps[k] = v

        for r in reads:
            if r.w is not None:
                add(*r.w)
        for r in writes:
            if r.w is not None:
                add(*r.w)
            for k, v in r.r.items():
                add(k, v)
        if dma:
            j = self.dma_n[eng]
            self.dma_n[eng] += 1
            key = ("dma", eng, j % self.NDMA)
            val = 16 * (j // self.NDMA + 1)
            if j >= self.NDMA:
                add(key, val - 16)
            inc = 16
        else:
            key = ("eng", eng, self.cnt[eng] // self.EPOCH)
            val = self.cnt[eng] % self.EPOCH + 1
            self.cnt[eng] += 1
            inc = 1
        self.last[key] = val
        wd = self.waited[eng]
        waits = []
        for k, v in deps.items():
            if eng == "pe" and k[0] == "eng" and k[1] == "pe":
                continue
            if wd.get(k, 0) < v:
                wd[k] = v
                waits.append((k, v))
        self.streams[eng].append((fn, waits, key, inc))
        for r in reads:
            if r.r.get(key, 0) < val:
                r.r[key] = val
        for r in writes:
            r.w = (key, val)
            r.r = {}

    def barrier(self, engs=None):
        for e in (engs or self.ENG):
            wd = self.waited[e]
            waits = []
            for k, v in self.last.items():
                if wd.get(k, 0) < v:
                    wd[k] = v
                    waits.append((k, v))
            if waits:
                self.streams[e].append((None, waits, None, 0))

    def emit(self, nc, stack):
        sems = {}
        for e in self.ENG:
            for (fn, waits, key, inc) in self.streams[e]:
                for k in [w[0] for w in waits] + ([key] if key is not None else []):
                    if k not in sems:
                        sems[k] = stack.enter_context(nc.semaphore("s_" + "_".join(str(x) for x in k)))
        block = stack.enter_context(nc.Block())

        def run(ename, eng):
            for (fn, waits, key, inc) in self.streams[ename]:
                for k, v in waits:
                    eng.wait_ge(sems[k], v)
                if fn is not None:
                    fn(eng).then_inc(sems[key], inc)

        @block.tensor
        def _(e):
            run("pe", e)

        @block.scalar
        def _(e):
            run("act", e)

        @block.vector
        def _(e):
            run("dve", e)

        @block.gpsimd
        def _(e):
            run("pool", e)

        @block.sync
        def _(e):
            run("sp", e)


def fm(v):
    v = np.asarray(v, np.float32)
    c = v.shape[-1] // 128
    lead = v.shape[:-1]
    return np.ascontiguousarray(np.moveaxis(v.reshape(lead + (c, 128)), -1, 0)).reshape(128, -1)


def pack_layer_vec(inp, i):
    kind, j = i % 3, i // 3
    parts = [("ada_b", fm(inp["ada_b"][i])),
             ("norm_mix", fm(inp["norm_mix"][i])),
             ("norm_ffn", fm(inp["norm_ffn"][i])),
             ("ffn_b_dw", fm(inp["ffn_b_dw"][i])),
             ("ffn_w_dw", fm(inp["ffn_w_dw"][i]))]
    if kind == 0:
        g = np.asarray(inp["mla_qk_gain"][j], np.float32)
        def col96(v):
            o = np.zeros((128, 1), np.float32)
            o[:96, 0] = v
            return o
        pe = np.arange(32)
        perm = np.where((pe % 16) < 8, pe + 8, pe - 8)
        gqp = np.zeros(96, np.float32); gqp[64:] = g[0, 64 + perm]
        gkp = np.zeros(96, np.float32); gkp[64:] = g[1, 64 + perm]
        invn = np.concatenate([np.full(64, 1 / 64.0), np.full(32, 1 / 32.0)]).astype(np.float32)
        parts += [("mla_qn", fm(inp["mla_q_norm"][j])), ("mla_kvn", fm(inp["mla_kv_norm"][j])),
                  ("mla_gq", col96(g[0])), ("mla_gqp", col96(gqp)), ("mla_gk", col96(g[1])), ("mla_gkp", col96(gkp)),
                  ("mla_invn", col96(invn))]
    if kind == 2:
        def col64(v):
            o = np.zeros((128, 1), np.float32)
            o[:64, 0] = v
            return o
        parts += [("hy_b_in", fm(inp["hy_b_in"][j])), ("hy_w_short", fm(inp["hy_w_short"][j])),
                  ("hy_b_short", fm(inp["hy_b_short"][j])), ("hy_b_out", fm(inp["hy_b_out"][j])),
                  ("hy_fb1", col64(inp["hy_f_b1"][j])), ("hy_fb2", col64(inp["hy_f_b2"][j])), ("hy_fr", col64(inp["hy_sin_freq"][j]))]
    if kind == 1:
        parts += [("cf_b_pw1", fm(inp["cf_b_pw1"][j])),
                  ("cf_w_dw", fm(inp["cf_w_dw"][j])),
                  ("cf_b_dw", fm(inp["cf_b_dw"][j])),
                  ("cf_ln_g", fm(inp["cf_ln_g"][j])),
                  ("cf_ln_b", fm(inp["cf_ln_b"][j])),
                  ("cf_b_pw2", fm(inp["cf_b_pw2"][j]))]
    off = {}
    o = 0
    for n, a in parts:
        off[n] = o
        o += a.shape[1]
    return np.concatenate([a for _, a in parts], axis=1), off


def vec_offsets(i):
    kind = i % 3
    sizes = [("ada_b", 48), ("norm_mix", 8), ("norm_ffn", 8), ("ffn_b_dw", 22), ("ffn_w_dw", 66)]
    if kind == 0:
        sizes += [("mla_qn", 2), ("mla_kvn", 1), ("mla_gq", 1), ("mla_gqp", 1), ("mla_gk", 1), ("mla_gkp", 1), ("mla_invn", 1)]
    if kind == 2:
        sizes += [("hy_b_in", 24), ("hy_w_short", 72), ("hy_b_short", 24), ("hy_b_out", 8), ("hy_fb1", 1), ("hy_fb2", 1), ("hy_fr", 1)]
    if kind == 1:
        sizes += [("cf_b_pw1", 16), ("cf_w_dw", 248), ("cf_b_dw", 8), ("cf_ln_g", 8), ("cf_ln_b", 8), ("cf_b_pw2", 8)]
    off = {}
    o = 0
    for n, s in sizes:
        off[n] = o
        o += s
    return off, o


NPV = 512
ARENA = 42112


class Prog:
    def __init__(self, layers, nb=NB):
        self.layers = layers
        self.nb = nb
        self.nc = bass.Bass("TRN2", target_bir_lowering=False)
        self.S = Sched()
        self.dram = {}

    def din(self, name, shape, dt=F32):
        t = self.nc.dram_tensor(name, list(shape), dt, kind="ExternalInput").ap()
        self.dram[name] = t
        return t

    def dout(self, name, shape, dt=F32):
        t = self.nc.dram_tensor(name, list(shape), dt, kind="ExternalOutput").ap()
        self.dram[name] = t
        return t

    def build(self):
        nc, S = self.nc, self.S
        nb = self.nb
        xT = self.din("xT", [nb, D, T])
        cT = self.din("cT", [nb, 128, KC * 2])
        ident_d = self.din("ident", [128, 128])
        bd_d = self.din("bd96", [128, 128])
        yT = self.dout("yT", [nb, D, L])
        cOut = self.dout("cOut", [nb, D, LC])
        W = {}
        for i in self.layers:
            kind, j = i % 3, i // 3
            W[i] = dict(
                ada_w=self.din(f"ada_w{i}", [D, 6 * D]),
                vec=self.din(f"vec{i}", [128, NPV]),
                ffn_up=self.din(f"ffn_up{i}", [D, 2 * D_FF]),
                ffn_down=self.din(f"ffn_down{i}", [D_FF, D]),
            )
            if kind == 0:
                if "rope_cs" not in self.dram:
                    self.din("rope_cs", [128, 2, T])
                W[i].update(dq=self.din(f"mla_dq{i}", [D, 256]), kv=self.din(f"mla_kv{i}", [D, 320]),
                            uq=self.din(f"mla_uq{i}", [256, 16 * 192]), ukv=self.din(f"mla_ukv{i}", [128, 2048]),
                            wo=self.din(f"mla_o{i}", [D, D]), rope=self.dram["rope_cs"])
            if kind == 2:
                W[i].update(hy_in=self.din(f"hy_in{i}", [D, 3 * D]), hy_out=self.din(f"hy_out{i}", [D, D]),
                            fw1=self.din(f"hy_fw1_{i}", [33, 64]), fw2=self.din(f"hy_fw2_{i}", [64, 64]),
                            fw3=self.din(f"hy_fw3_{i}", [64, 2 * D]), skipb=self.din(f"hy_skipb{i}", [128, D]))
                for tag, Lx in (("l", L), ("c", LC)):
                    for nm in ("dfc", "dfs", "idc", "ids"):
                        W[i][f"{nm}_{tag}"] = self.din(f"{nm}_{tag}", [Lx, Lx], BF16)
                    W[i][f"z_{tag}"] = self.din(f"z_{tag}", [33, Lx])
                    W[i][f"negt_{tag}"] = self.din(f"negt_{tag}", [128, Lx // 128])
                    W[i][f"kspec_{tag}"] = self.nc.dram_tensor(f"kspec_{tag}", [2, Lx, D], F32).ap()
                W[i]["absd"] = self.din("absd", [128, D])
                W[i]["x0d"] = self.nc.dram_tensor("x0d", [D, T], BF16).ap()
                self.rX0D = [Res() for _ in range(KC)]
                self.rKS = {"l": Res(), "c": Res()}
            if kind == 1:
                W[i].update(cf_w1=self.din(f"cf_w1_{i}", [D, 2 * D]), cf_w2=self.din(f"cf_w2_{i}", [D, D]))
        with contextlib.ExitStack() as st:
            self.st = st

            def sb(name, shape, dt):
                return st.enter_context(nc.sbuf_tensor(name, list(shape), dt))

            self.X = sb("X", [128, KC, T], F32)
            self.rX = [[Res() for _ in TILES] for _ in range(KC)]
            self.H = sb("H", [128, KC, T], BF16)
            self.rH = [[Res() for _ in TILES] for _ in range(KC)]
            self.SCR = sb("SCR", [128, ARENA], BF16)
            self.carve_ffn()
            self.VEC = sb("VEC", [128, NPV], F32)
            self.rVEC = Res()
            self.MOD = sb("MOD", [128, 48, 2], F32)
            self.rMOD = Res()
            self.AM = sb("AM", [128, 2, KC, 2], F32)
            self.rAM = Res()
            self.SV = sb("SV", [128, KC, 2], F32)
            self.SVB = sb("SVB", [128, KC, 2], BF16)
            self.rSV = Res()
            self.ONES = sb("ONES", [128, 128], BF16)
            self.rONES = Res()
            self.EPSB = sb("EPSB", [128, 1], F32)
            self.RSTD = sb("RSTD", [128, 512], F32)
            self.rRSTD = Res()
            self.RSTD2 = sb("RSTD2", [128, 512], F32)
            self.rRSTD2 = Res()
            self.MU = sb("MU", [128, 512], F32)
            self.rMU = Res()
            self.TMP = [sb(f"TMP{k}", [128, 512], F32) for k in range(2)]
            self.rTMP = [Res(), Res()]
            self.tmp_i = 0
            self.BD = sb("BD", [128, 128], BF16)
            self.PS = [st.enter_context(nc.psum_tensor(f"PS{k}", [128, 512], F32)) for k in range(8)]
            self.acc_i = 0
            self.rPS = [Res() for _ in range(8)]
            self.ps_i = 0
            self.wa_i = 0
            self.dq = 0

            S.op("pool", lambda e: e.memset(self.ONES[:, :], 1.0), writes=[self.rONES])
            S.op("pool", lambda e: e.memset(self.EPSB[:, :], EPS), writes=[self.rONES])
            self.zero_pads()
            self.IDENT = sb("IDENT", [128, 128], BF16)
            self.rID = Res()
            S.op("pool", lambda e: e.dma_start(out=self.IDENT[:, :], in_=ident_d[:, :]), writes=[self.rID], dma=True)
            S.op("pool", lambda e: e.dma_start(out=self.BD[:, :], in_=bd_d[:, :]), writes=[self.rID], dma=True)

            routs = []
            for b in range(nb):
                self.load_x(xT, cT, b)
                for i in self.layers:
                    self.layer(i, W[i], b, last=(i == DEPTH - 1))
                routs += self.store_x(yT, cOut, b)
            S.barrier(["sp"])
            S.emit(nc, st)
        return nc

    def psum(self):
        k = self.ps_i
        self.ps_i = (k + 1) % 6
        return self.PS[k], self.rPS[k]

    def psum_acc(self):
        k = 6 + self.acc_i
        self.acc_i = 1 - self.acc_i
        return self.PS[k], self.rPS[k]

    def arena_reset(self, mark=0):
        self.a_off = mark

    def alloc(self, n_el, dt=BF16, shape=None):
        nb16 = n_el * (2 if dt == F32 else 1)
        nb16 = (nb16 + 15) // 16 * 16
        a = self.a_off
        assert a + nb16 <= ARENA, (a, nb16)
        self.a_off = a + nb16
        ap = self.SCR[:, a:a + n_el * (2 if dt == F32 else 1)]
        if dt == F32:
            ap = ap.bitcast(F32)
        return ap

    def carve_ffn(self):
        self.arena_reset()
        self.G = self.alloc(6 * T).rearrange("p (c t) -> p c t", t=T)
        self.rG = [Res() for _ in range(6)]
        self.AROW = self.alloc(T)
        self.CROW = self.alloc(T, F32)
        self.VROW = self.alloc(T)
        self.rA, self.rC, self.rV = Res(), Res(), Res()
        self.WA = [self.alloc(KC * 512).rearrange("p (c n) -> p c n", n=512) for _ in range(2)]
        self.rWA = [[Res(), Res()], [Res(), Res()]]
        self.WB = self.alloc(6 * 1024)
        self.rWB = Res()
        self.SQ = self.alloc(KC * 512).rearrange("p (c n) -> p c n", n=512)
        self.rSQ = Res()

    def zero_pads(self):
        S = self.S
        S.op("pool", lambda e: e.memset(self.AROW[:, :], 0.0), writes=[self.rA])
        S.op("pool", lambda e: e.memset(self.VROW[:, :], 0.0), writes=[self.rV])
        for (a, bnd) in ((0, 16), (C0 + LC, L0), (L0 + L, T)):
            S.op("pool", lambda e, a=a, bnd=bnd: e.memset(self.G[:, :, a:bnd], 0.0), writes=self.rG)

    def tmp(self):
        k = self.tmp_i
        self.tmp_i = 1 - k
        return self.TMP[k], self.rTMP[k]

    def dma_eng(self):
        self.dq += 1
        return "sp"

    def load_w(self, dst_ap, src_ap, res):
        self.S.op("pool", lambda e: e.dma_start(out=dst_ap, in_=src_ap), writes=(res if isinstance(res, list) else [res]), dma=True)

    def load_x(self, xT, cT, b):
        S = self.S
        for c in range(KC):
            S.op("sp", lambda e, c=c: e.dma_start(out=self.X[:, c, :], in_=xT[b, c * 128:(c + 1) * 128, :]),
                 writes=self.rX[c], dma=True)
        S.op("sp", lambda e: e.dma_start(out=self.SV[:, :, :], in_=cT[b].rearrange("p (c t) -> p c t", t=2)),
             writes=[self.rSV], dma=True)
        S.op("act", lambda e: e.activation(out=self.SVB[:, :, :], in_=self.SV[:, :, :], func=AF.Silu),
             reads=[], writes=[self.rSV])

    def store_x(self, yT, cOut, b):
        S = self.S
        rs = []
        for c in range(KC):
            r = Res()
            S.op("sp", lambda e, c=c: e.dma_start(out=yT[b, c * 128:(c + 1) * 128, :], in_=self.X[:, c, L0:L0 + L]),
                 reads=self.rX[c], writes=[r], dma=True)
            r2 = Res()
            S.op("sp", lambda e, c=c: e.dma_start(out=cOut[b, c * 128:(c + 1) * 128, :], in_=self.X[:, c, C0:C0 + LC]),
                 reads=self.rX[c], writes=[r2], dma=True)
            rs += [r, r2]
        return rs

    def adaln(self, i, Wd):
        S = self.S
        off, _ = vec_offsets(i)
        S.op("sp", lambda e: e.dma_start(out=self.VEC[:, :], in_=Wd["vec"][:, :]), writes=[self.rVEC], dma=True)
        ada_w = Wd["ada_w"].rearrange("(c p) n -> p c n", p=128)
        for blk in range(12):
            k = self.wa_i
            self.wa_i = 1 - k
            wa, rwa = self.WA[k], self.rWA[k]
            self.S.op("pool", lambda e, wa=wa, blk=blk: e.dma_start(out=wa[:, :, :], in_=ada_w[:, :, blk * 512:(blk + 1) * 512]), writes=rwa, dma=True)
            ps, rps = self.psum()

            def mm(e, wa=wa, ps=ps):
                ins = None
                for f in range(4):
                    for kc in range(KC):
                        ins = e.matmul(ps[:, f * 2:f * 2 + 2], lhsT=wa[:, kc, f * 128:(f + 1) * 128],
                                       rhs=self.SVB[:, kc, :], start=(kc == 0), stop=(kc == KC - 1))
                return ins
            S.op("pe", mm, reads=rwa + [self.rSV], writes=[rps])
            ab = off["ada_b"] + blk * 4
            S.op("dve", lambda e, ps=ps, blk=blk, ab=ab: e.tensor_tensor(
                out=self.MOD[:, blk * 4:blk * 4 + 4, :], in0=ps[:, 0:8].rearrange("p (f t) -> p f t", t=2),
                in1=self.VEC[:, ab:ab + 4].unsqueeze(2).to_broadcast([128, 4, 2]), op=ALU.add),
                reads=[self.rVEC], writes=[rps, self.rMOD])
        for m, (nname, g) in enumerate((("norm_mix", 1), ("norm_ffn", 4))):
            no = off[nname]
            S.op("dve", lambda e, m=m, g=g: e.tensor_scalar_add(out=self.AM[:, m, :, :], in0=self.MOD[:, g * 8:(g + 1) * 8, :], scalar1=1.0),
                 reads=[self.rMOD], writes=[self.rAM])
            S.op("dve", lambda e, m=m, no=no: e.tensor_tensor(
                out=self.AM[:, m, :, :], in0=self.AM[:, m, :, :],
                in1=self.VEC[:, no:no + 8].unsqueeze(2).to_broadcast([128, 8, 2]), op=ALU.mult),
                reads=[self.rVEC, self.rAM], writes=[self.rAM])

    def modulate(self, m, tiles):
        S = self.S
        shg = 0 if m == 0 else 3
        RS = [self.RSTD, self.RSTD2]
        rRS = [self.rRSTD, self.rRSTD2]

        def stage1(k):
            ti = tiles[k]
            c0, w = TILES[ti]
            rs, rrs = RS[k % 2], rRS[k % 2]
            for c in range(KC):
                S.op("act", lambda e, c=c: e.activation(out=self.SQ[:, c, 0:w], in_=self.X[:, c, c0:c0 + w], func=AF.Square),
                     reads=[self.rX[c][ti]], writes=[self.rSQ])
            ps, rps = self.psum()

            def mm(e):
                ins = None
                for c in range(KC):
                    ins = e.matmul(ps[:, 0:w], lhsT=self.ONES[:, :], rhs=self.SQ[:, c, 0:w], start=(c == 0), stop=(c == KC - 1))
                return ins
            S.op("pe", mm, reads=[self.rSQ, self.rONES], writes=[rps])
            S.op("act", lambda e: e.activation(out=rs[:, 0:w], in_=ps[:, 0:w], func=AF.Ln, scale=1.0 / D, bias=self.EPSB[:, 0:1]),
                 reads=[self.rONES], writes=[rps, rrs])
            S.op("act", lambda e: e.activation(out=rs[:, 0:w], in_=rs[:, 0:w], func=AF.Exp, scale=-0.5), reads=[], writes=[rrs])

        def stage2(k):
            ti = tiles[k]
            c0, w = TILES[ti]
            col = 1 if ti == 0 else 0
            rs, rrs = RS[k % 2], rRS[k % 2]
            for c in range(KC):
                tp, rtp = self.tmp()
                S.op("dve", lambda e, tp=tp, c=c: e.tensor_tensor(out=tp[:, 0:w], in0=self.X[:, c, c0:c0 + w], in1=rs[:, 0:w], op=ALU.mult),
                     reads=[self.rX[c][ti], rrs], writes=[rtp])
                S.op("pool", lambda e, tp=tp, c=c: e.tensor_scalar(
                    out=self.H[:, c, c0:c0 + w], in0=tp[:, 0:w], scalar1=self.AM[:, m, c, col:col + 1],
                    scalar2=self.MOD[:, shg * 8 + c, col:col + 1], op0=ALU.mult, op1=ALU.add),
                    reads=[rtp, self.rAM, self.rMOD], writes=[self.rH[c][ti]])
        stage1(0)
        for k in range(len(tiles)):
            if k + 1 < len(tiles):
                stage1(k + 1)
            stage2(k)

    def ffn(self, i, Wd, tiles):
        S = self.S
        off, _ = vec_offsets(i)
        up = Wd["ffn_up"].rearrange("(c p) n -> p c n", p=128)
        down = Wd["ffn_down"]
        bo, wo = off["ffn_b_dw"], off["ffn_w_dw"]
        lo = TILES[tiles[0]][0]
        hi = TILES[tiles[-1]][0] + TILES[tiles[-1]][1]
        for (g0, gn) in ((0, 6), (6, 6), (12, 5), (17, 5)):
            wbv = self.WB[:, 0:gn * 1024].rearrange("p (c n) -> p c n", n=1024)
            self.load_w(wbv, down[g0 * 128:(g0 + gn) * 128, :].rearrange("(c p) n -> p c n", p=128), self.rWB)
            for hc in range(g0, g0 + gn):
                k = self.wa_i
                self.wa_i = 1 - k
                wa, rwa = self.WA[k], self.rWA[k]
                self.load_w(wa[:, :, 0:128], up[:, :, hc * 128:(hc + 1) * 128], rwa[0])
                self.load_w(wa[:, :, 128:256], up[:, :, D_FF + hc * 128:D_FF + (hc + 1) * 128], rwa[1])
                for ti in tiles:
                    c0, w = TILES[ti]
                    for half, (dst, rdst) in enumerate(((self.AROW, self.rA), (self.VROW, self.rV))):
                        ps, rps = self.psum()

                        def mm(e, ps=ps, wa=wa, half=half, c0=c0, w=w):
                            ins = None
                            for kc in range(KC):
                                ins = e.matmul(ps[:, 0:w], lhsT=wa[:, kc, half * 128:(half + 1) * 128],
                                               rhs=self.H[:, kc, c0:c0 + w], start=(kc == 0), stop=(kc == KC - 1))
                            return ins
                        S.op("pe", mm, reads=[rwa[half]] + [self.rH[kc][ti] for kc in range(KC)], writes=[rps])
                        S.op("act", lambda e, ps=ps, dst=dst, c0=c0, w=w: e.copy(out=dst[:, c0:c0 + w], in_=ps[:, 0:w]),
                             reads=[], writes=[rps, rdst])
                S.op("dve", lambda e, hc=hc: e.tensor_scalar(
                    out=self.CROW[:, lo:hi], in0=self.AROW[:, lo:hi], scalar1=self.VEC[:, wo + 22 + hc:wo + 22 + hc + 1],
                    scalar2=self.VEC[:, bo + hc:bo + hc + 1], op0=ALU.mult, op1=ALU.add),
                    reads=[self.rA, self.rVEC], writes=[self.rC])
                S.op("dve", lambda e, hc=hc: e.scalar_tensor_tensor(
                    out=self.CROW[:, lo:hi], in0=self.AROW[:, lo - 1:hi - 1], scalar=self.VEC[:, wo + hc:wo + hc + 1],
                    in1=self.CROW[:, lo:hi], op0=ALU.mult, op1=ALU.add),
                    reads=[self.rA, self.rVEC], writes=[self.rC])
                S.op("dve", lambda e, hc=hc: e.scalar_tensor_tensor(
                    out=self.CROW[:, lo:hi], in0=self.AROW[:, lo + 1:hi + 1], scalar=self.VEC[:, wo + 44 + hc:wo + 44 + hc + 1],
                    in1=self.CROW[:, lo:hi], op0=ALU.mult, op1=ALU.add),
                    reads=[self.rA, self.rVEC], writes=[self.rC])
                S.op("act", lambda e: e.activation(out=self.CROW[:, lo:hi], in_=self.CROW[:, lo:hi], func=AF.Silu),
                     reads=[], writes=[self.rC])
                gi = hc - g0
                S.op("dve", lambda e, gi=gi: e.tensor_tensor(out=self.G[:, gi, lo:hi], in0=self.CROW[:, lo:hi], in1=self.VROW[:, lo:hi], op=ALU.mult),
                     reads=[self.rC, self.rV], writes=[self.rG[gi]])
            for dc in range(KC):
                for ti in tiles:
                    c0, w = TILES[ti]
                    col = 1 if ti == 0 else 0
                    ps, rps = self.psum()

                    def mm(e, ps=ps, wbv=wbv, dc=dc, c0=c0, w=w, gn=gn):
                        ins = None
                        for gi in range(gn):
                            ins = e.matmul(ps[:, 0:w], lhsT=wbv[:, gi, dc * 128:(dc + 1) * 128], rhs=self.G[:, gi, c0:c0 + w],
                                           start=(gi == 0), stop=(gi == gn - 1))
                        return ins
                    S.op("pe", mm, reads=[self.rWB] + [self.rG[gi] for gi in range(gn)], writes=[rps])
                    S.op("dve", lambda e, ps=ps, dc=dc, c0=c0, w=w, col=col: e.scalar_tensor_tensor(
                        out=self.X[:, dc, c0:c0 + w], in0=ps[:, 0:w], scalar=self.MOD[:, 40 + dc, col:col + 1],
                        in1=self.X[:, dc, c0:c0 + w], op0=ALU.mult, op1=ALU.add),
                        reads=[self.rMOD], writes=[rps, self.rX[dc][ti]])

    def mla(self, i, Wd, need_ctx):
        S = self.S
        off, _ = vec_offsets(i)
        V = lambda name, k=0: self.VEC[:, off[name] + k:off[name] + k + 1]
        V96 = lambda name: self.VEC[0:96, off[name]:off[name] + 1]
        all_tiles = [0, 1, 2, 3, 4]
        qtiles = all_tiles if need_ctx else [1, 2, 3, 4]
        att_scale = float(96 ** -0.5)
        self.arena_reset()
        CQ = self.alloc(2 * T).rearrange("p (c t) -> p c t", t=T)
        CKV = self.alloc(T)
        KPE = self.alloc(T)
        ROPE = self.alloc(2 * T, F32).rearrange("p (c t) -> p c t", t=T)
        rCQ = [Res() for _ in TILES]
        rCKV = [Res() for _ in TILES]
        rKPE = [Res() for _ in TILES]
        rROPE = Res()
        SQh = self.alloc(512)
        rSQh = Res()
        mark = self.a_off
        S.op("sp", lambda e: e.dma_start(out=ROPE[:, :, :], in_=Wd["rope"][:, :, :]), writes=[rROPE], dma=True)
        WDQ = self.alloc(KC * 256).rearrange("p (c n) -> p c n", n=256)
        WKV = self.alloc(KC * 320).rearrange("p (c n) -> p c n", n=320)
        CQF = self.alloc(2 * 512, F32).rearrange("p (c n) -> p c n", n=512)
        SQb = self.alloc(2 * 512).rearrange("p (c n) -> p c n", n=512)
        rWDQ, rWKV, rCQF, rSQb = Res(), Res(), Res(), Res()
        self.load_w(WDQ, Wd["dq"].rearrange("(c p) n -> p c n", p=128), rWDQ)
        self.load_w(WKV, Wd["kv"].rearrange("(c p) n -> p c n", p=128), rWKV)

        def rope_norm(ps_raw, rps_raw, ps_perm, rps_perm, g, gp, out_ap, rout, lo, hi, c0, w):
            S.op("act", lambda e: e.activation(out=SQh[0:96, 0:w], in_=ps_raw[0:96, 0:w], func=AF.Square),
                 reads=[], writes=[rps_raw, rSQh])
            pss, rpss = self.psum()
            S.op("pe", lambda e: e.matmul(pss[0:96, 0:w], lhsT=self.BD[0:96, 0:96], rhs=SQh[0:96, 0:w], start=True, stop=True),
                 reads=[rSQh, self.rID], writes=[rpss])
            S.op("act", lambda e: e.activation(out=self.RSTD[0:96, 0:w], in_=pss[0:96, 0:w], func=AF.Ln,
                                               scale=V96("mla_invn"), bias=self.EPSB[0:96, 0:1]),
                 reads=[self.rVEC, self.rONES], writes=[rpss, self.rRSTD])
            S.op("act", lambda e: e.activation(out=self.RSTD[0:96, 0:w], in_=self.RSTD[0:96, 0:w], func=AF.Exp, scale=-0.5), reads=[], writes=[self.rRSTD])
            t1, rt1 = self.TMP[0], self.rTMP[0]
            t2, rt2 = self.TMP[1], self.rTMP[1]
            S.op("dve", lambda e: e.scalar_tensor_tensor(out=t1[0:96, 0:w], in0=ps_raw[0:96, 0:w], scalar=g, in1=ROPE[0:96, 0, c0:c0 + w],
                                                         op0=ALU.mult, op1=ALU.mult),
                 reads=[self.rVEC, rROPE], writes=[rps_raw, rt1])
            S.op("dve", lambda e: e.scalar_tensor_tensor(out=t2[0:96, 0:w], in0=ps_perm[0:96, 0:w], scalar=gp, in1=ROPE[0:96, 1, c0:c0 + w],
                                                         op0=ALU.mult, op1=ALU.mult),
                 reads=[self.rVEC, rROPE], writes=[rps_perm, rt2])
            S.op("pool", lambda e: e.tensor_tensor(out=t1[0:96, 0:w], in0=t1[0:96, 0:w], in1=t2[0:96, 0:w], op=ALU.add),
                 reads=[rt2], writes=[rt1])
            S.op("dve", lambda e: e.tensor_tensor(out=out_ap, in0=t1[lo:hi, 0:w], in1=self.RSTD[lo:hi, 0:w], op=ALU.mult),
                 reads=[rt1, self.rRSTD], writes=[rout])

        def p1_tile(ti):
            c0, w = TILES[ti]
            rh = [self.rH[kc][ti] for kc in range(KC)]
            for oc in range(2):
                ps, rps = self.psum()

                def mm(e, ps=ps, oc=oc):
                    ins = None
                    for kc in range(KC):
                        ins = e.matmul(ps[:, 0:w], lhsT=WDQ[:, kc, oc * 128:(oc + 1) * 128], rhs=self.H[:, kc, c0:c0 + w],
                                       start=(kc == 0), stop=(kc == KC - 1))
                    return ins
                S.op("pe", mm, reads=[rWDQ] + rh, writes=[rps])
                S.op("act", lambda e, ps=ps, oc=oc: e.copy(out=CQF[:, oc, 0:w], in_=ps[:, 0:w]), reads=[], writes=[rps, rCQF])
            S.op("act", lambda e: e.activation(out=SQb[:, :, 0:w], in_=CQF[:, :, 0:w], func=AF.Square), reads=[rCQF], writes=[rSQb])
            ps, rps = self.psum()

            def mm(e, ps=ps):
                e.matmul(ps[:, 0:w], lhsT=self.ONES[:, :], rhs=SQb[:, 0, 0:w], start=True, stop=False)
                return e.matmul(ps[:, 0:w], lhsT=self.ONES[:, :], rhs=SQb[:, 1, 0:w], start=False, stop=True)
            S.op("pe", mm, reads=[rSQb, self.rONES], writes=[rps])
            S.op("act", lambda e, ps=ps: e.activation(out=self.RSTD[:, 0:w], in_=ps[:, 0:w], func=AF.Ln, scale=1.0 / 256, bias=self.EPSB[:, 0:1]),
                 reads=[self.rONES], writes=[rps, self.rRSTD])
            S.op("act", lambda e: e.activation(out=self.RSTD[:, 0:w], in_=self.RSTD[:, 0:w], func=AF.Exp, scale=-0.5), reads=[], writes=[self.rRSTD])
            for oc in range(2):
                S.op("dve", lambda e, oc=oc: e.scalar_tensor_tensor(out=CQ[:, oc, c0:c0 + w], in0=CQF[:, oc, 0:w], scalar=V("mla_qn", oc),
                                                                 in1=self.RSTD[:, 0:w], op0=ALU.mult, op1=ALU.mult),
                     reads=[rCQF, self.rVEC, self.rRSTD], writes=[rCQ[ti]])
            ps, rps = self.psum()

            def mm(e, ps=ps):
                ins = None
                for kc in range(KC):
                    ins = e.matmul(ps[:, 0:w], lhsT=WKV[:, kc, 0:128], rhs=self.H[:, kc, c0:c0 + w], start=(kc == 0), stop=(kc == KC - 1))
                return ins
            S.op("pe", mm, reads=[rWKV] + rh, writes=[rps])
            S.op("act", lambda e, ps=ps: e.copy(out=CQF[:, 0, 0:w], in_=ps[:, 0:w]), reads=[], writes=[rps, rCQF])
            S.op("act", lambda e: e.activation(out=SQb[:, 0, 0:w], in_=CQF[:, 0, 0:w], func=AF.Square), reads=[rCQF], writes=[rSQb])
            ps, rps = self.psum()
            S.op("pe", lambda e, ps=ps: e.matmul(ps[:, 0:w], lhsT=self.ONES[:, :], rhs=SQb[:, 0, 0:w], start=True, stop=True),
                 reads=[rSQb, self.rONES], writes=[rps])
            S.op("act", lambda e, ps=ps: e.activation(out=self.RSTD[:, 0:w], in_=ps[:, 0:w], func=AF.Ln, scale=1.0 / 128, bias=self.EPSB[:, 0:1]),
                 reads=[self.rONES], writes=[rps, self.rRSTD])
            S.op("act", lambda e: e.activation(out=self.RSTD[:, 0:w], in_=self.RSTD[:, 0:w], func=AF.Exp, scale=-0.5), reads=[], writes=[self.rRSTD])
            S.op("dve", lambda e: e.scalar_tensor_tensor(out=CKV[:, c0:c0 + w], in0=CQF[:, 0, 0:w], scalar=V("mla_kvn"),
                                                         in1=self.RSTD[:, 0:w], op0=ALU.mult, op1=ALU.mult),
                 reads=[rCQF, self.rVEC, self.rRSTD], writes=[rCKV[ti]])
            pss = []
            for var in range(2):
                ps, rps = self.psum()
                pss.append((ps, rps))

                def mm(e, ps=ps, var=var):
                    ins = None
                    for kc in range(KC):
                        ins = e.matmul(ps[0:96, 0:w], lhsT=WKV[:, kc, 128 + var * 96:224 + var * 96], rhs=self.H[:, kc, c0:c0 + w],
                                       start=(kc == 0), stop=(kc == KC - 1))
                    return ins
                S.op("pe", mm, reads=[rWKV] + rh, writes=[rps])
            rope_norm(pss[0][0], pss[0][1], pss[1][0], pss[1][1], V96("mla_gk"), V96("mla_gkp"),
                      KPE[64:96, c0:c0 + w], rKPE[ti], 64, 96, c0, w)

        for ti in all_tiles:
            p1_tile(ti)
        if DBG_STOP == 1:
            S.barrier()
            return

        S.barrier()
        self.arena_reset(mark)
        WUKV = self.alloc(2048)
        rWUKV = Res()
        self.load_w(WUKV, Wd["ukv"][:, :], rWUKV)
        uq_v = Wd["uq"].rearrange("(c p) n -> p c n", p=128)
        WUQh = [self.alloc(2 * 192).rearrange("p (c n) -> p c n", n=192) for _ in range(2)]
        rWUQh = [Res(), Res()]
        QH = [self.alloc(T) for _ in range(2)]
        KH = [self.alloc(T) for _ in range(2)]
        rQH = [[Res() for _ in TILES] for _ in range(2)]
        rKH = [[Res() for _ in TILES] for _ in range(2)]
        rKHpe = [Res(), Res()]
        VH = [self.alloc(18 * 128).rearrange("p (j n) -> p j n", n=128) for _ in range(2)]
        rVH = [Res(), Res()]
        NPT = 6
        PT = [self.alloc(512) for _ in range(NPT)]
        rPT = [Res() for _ in range(NPT)]
        RD = self.MU
        rRD = self.rMU
        pt_i = 0
        S.op("pool", lambda e: e.memset(VH[0][:, :, 64:128], 1.0), writes=[rVH[0]])
        S.op("pool", lambda e: e.memset(VH[1][:, :, 0:64], 1.0), writes=[rVH[1]])

        def kcols(j):
            return (C0 + 128 * j) if j < 2 else (L0 + 128 * (j - 2))

        def ktile(j):
            return 0 if j < 2 else 1 + (j - 2) // 4

        def prep(h):
            s = h % 2
            voff = 0 if s == 0 else 64
            wq, rwq = WUQh[s], rWUQh[s]
            self.load_w(wq, uq_v[:, :, h * 192:(h + 1) * 192], rwq)
            for ti in qtiles:
                c0, w = TILES[ti]
                pss = []
                for var in range(2):
                    ps, rps = self.psum()
                    pss.append((ps, rps))

                    def mm(e, ps=ps, var=var, c0=c0, w=w):
                        e.matmul(ps[0:96, 0:w], lhsT=wq[:, 0, var * 96:var * 96 + 96], rhs=CQ[:, 0, c0:c0 + w], start=True, stop=False)
                        return e.matmul(ps[0:96, 0:w], lhsT=wq[:, 1, var * 96:var * 96 + 96], rhs=CQ[:, 1, c0:c0 + w], start=False, stop=True)
                    S.op("pe", mm, reads=[rwq, rCQ[ti]], writes=[rps])
                rope_norm(pss[0][0], pss[0][1], pss[1][0], pss[1][1], V96("mla_gq"), V96("mla_gqp"),
                          QH[s][0:96, c0:c0 + w], rQH[s][ti], 0, 96, c0, w)
            S.op("pool", lambda e: e.tensor_copy(out=KH[s][64:96, :], in_=KPE[64:96, :]), reads=rKPE, writes=[rKHpe[s]])
            for ti in all_tiles:
                c0, w = TILES[ti]
                ps, rps = self.psum()
                S.op("pe", lambda e, ps=ps, c0=c0, w=w: e.matmul(ps[0:64, 0:w], lhsT=WUKV[:, h * 128:h * 128 + 64], rhs=CKV[:, c0:c0 + w], start=True, stop=True),
                     reads=[rWUKV, rCKV[ti]], writes=[rps])
                S.op("act", lambda e, ps=ps, w=w: e.activation(out=SQh[0:64, 0:w], in_=ps[0:64, 0:w], func=AF.Square), reads=[], writes=[rps, rSQh])
                ps2, rps2 = self.psum()
                S.op("pe", lambda e, ps2=ps2, w=w: e.matmul(ps2[0:64, 0:w], lhsT=self.BD[0:64, 0:64], rhs=SQh[0:64, 0:w], start=True, stop=True),
                     reads=[rSQh, self.rID], writes=[rps2])
                S.op("act", lambda e, ps2=ps2, w=w: e.activation(out=self.RSTD[0:64, 0:w], in_=ps2[0:64, 0:w], func=AF.Ln, scale=1.0 / 64, bias=self.EPSB[0:64, 0:1]),
                     reads=[self.rONES], writes=[rps2, self.rRSTD])
                S.op("act", lambda e, w=w: e.activation(out=self.RSTD[0:64, 0:w], in_=self.RSTD[0:64, 0:w], func=AF.Exp, scale=-0.5), reads=[], writes=[self.rRSTD])
                S.op("dve", lambda e, ps=ps, c0=c0, w=w: e.scalar_tensor_tensor(out=KH[s][0:64, c0:c0 + w], in0=ps[0:64, 0:w], scalar=self.VEC[0:64, off["mla_gk"]:off["mla_gk"] + 1],
                                                                             in1=self.RSTD[0:64, 0:w], op0=ALU.mult, op1=ALU.mult),
                     reads=[self.rVEC, self.rRSTD], writes=[rps, rKH[s][ti]])
            for j0 in range(0, 18, 8):
                n = min(8, 18 - j0)
                ps, rps = self.psum()

                def mm(e, ps=ps, j0=j0, n=n):
                    ins = None
                    for jj in range(n):
                        kc0 = kcols(j0 + jj)
                        ins = e.matmul(ps[:, jj * 64:(jj + 1) * 64], lhsT=CKV[:, kc0:kc0 + 128], rhs=WUKV[:, h * 128 + 64:(h + 1) * 128], start=True, stop=True)
                    return ins
                S.op("pe", mm, reads=[rWUKV] + rCKV, writes=[rps])
                S.op("dve", lambda e, ps=ps, j0=j0, n=n: e.tensor_copy(out=VH[s][:, j0:j0 + n, voff:voff + 64], in_=ps[:, 0:n * 64].rearrange("p (j d) -> p j d", d=64)),
                     reads=[], writes=[rps, rVH[s]])

        def attend(h):
            nonlocal pt_i
            s = h % 2
            ch = h // 2
            items = []
            for ti in qtiles:
                keys = list(range(18)) if ti > 0 else [0, 1]
                for j in keys:
                    items.append((ti, j, j == keys[0], j == keys[-1]))
            LA = 2
            qk = {}

            def issue_qk(idx):
                ti, j, _, _ = items[idx]
                c0, w = TILES[ti]
                kc0 = kcols(j)
                ps, rps = self.psum()
                S.op("pe", lambda e, ps=ps: e.matmul(ps[:, 0:w], lhsT=KH[s][0:96, kc0:kc0 + 128], rhs=QH[s][0:96, c0:c0 + w], start=True, stop=True),
                     reads=[rKH[s][ktile(j)], rKHpe[s], rQH[s][ti]], writes=[rps])
                qk[idx] = (ps, rps)
            for idx in range(min(LA, len(items))):
                issue_qk(idx)
            pso, rpso = None, None
            for idx, (ti, j, first, lastk) in enumerate(items):
                c0, w = TILES[ti]
                if idx + LA < len(items):
                    issue_qk(idx + LA)
                if first:
                    pso, rpso = self.psum_acc()
                ps, rps = qk.pop(idx)
                pt, rpt = PT[pt_i], rPT[pt_i]
                pt_i = (pt_i + 1) % NPT
                S.op("act", lambda e, ps=ps, pt=pt, w=w: e.activation(out=pt[:, 0:w], in_=ps[:, 0:w], func=AF.Exp, scale=att_scale),
                     reads=[], writes=[rps, rpt])
                S.op("pe", lambda e, pt=pt, j=j, w=w, pso=pso, first=first, lastk=lastk: e.matmul(
                    pso[:, 0:w], lhsT=VH[s][:, j, :], rhs=pt[:, 0:w], start=first, stop=lastk),
                    reads=[rpt, rVH[s]], writes=[rpso])
                if lastk:
                    if s == 0:
                        S.op("dve", lambda e, pso=pso, w=w: e.reciprocal(out=RD[64:128, 0:w], in_=pso[64:128, 0:w]), reads=[], writes=[rpso, rRD])
                        S.op("dve", lambda e, pso=pso, c0=c0, w=w: e.tensor_tensor(out=self.H[0:64, ch, c0:c0 + w], in0=pso[0:64, 0:w], in1=RD[64:128, 0:w], op=ALU.mult),
                             reads=[rRD], writes=[rpso, self.rH[ch][ti]])
                    else:
                        S.op("dve", lambda e, pso=pso, w=w: e.reciprocal(out=RD[0:64, 0:w], in_=pso[0:64, 0:w]), reads=[], writes=[rpso, rRD])
                        S.op("dve", lambda e, pso=pso, c0=c0, w=w: e.tensor_tensor(out=self.H[64:128, ch, c0:c0 + w], in0=pso[64:128, 0:w], in1=RD[0:64, 0:w], op=ALU.mult),
                             reads=[rRD], writes=[rpso, self.rH[ch][ti]])

        NH = 16 if DBG_HEADS is None else DBG_HEADS
        prep(0)
        for h in range(NH):
            if h + 1 < NH:
                prep(h + 1)
            attend(h)
        if DBG_STOP == 2:
            S.barrier()
            return

        S.barrier()
        self.arena_reset()
        WO = self.alloc(KC * 1024).rearrange("p (c n) -> p c n", n=1024)
        rWO = Res()
        wo_v = Wd["wo"].rearrange("(c p) n -> p c n", p=128)
        rWOh = [Res(), Res()]
        for hh in range(2):
            self.load_w(WO[:, hh * 4:(hh + 1) * 4, :], wo_v[:, hh * 4:(hh + 1) * 4, :], rWOh[hh])
        for ti in qtiles:
            c0, w = TILES[ti]
            col = 1 if ti == 0 else 0
            for dc in range(KC):
                ps, rps = self.psum()

                def mm(e, ps=ps, dc=dc, c0=c0, w=w):
                    ins = None
                    for c in range(KC):
                        ins = e.matmul(ps[:, 0:w], lhsT=WO[:, c, dc * 128:(dc + 1) * 128], rhs=self.H[:, c, c0:c0 + w], start=(c == 0), stop=(c == KC - 1))
                    return ins
                S.op("pe", mm, reads=rWOh + [self.rH[c][ti] for c in range(KC)], writes=[rps])
                S.op("dve", lambda e, ps=ps, dc=dc, c0=c0, w=w, col=col: e.scalar_tensor_tensor(
                    out=self.X[:, dc, c0:c0 + w], in0=ps[:, 0:w], scalar=self.MOD[:, 16 + dc, col:col + 1],
                    in1=self.X[:, dc, c0:c0 + w], op0=ALU.mult, op1=ALU.add),
                    reads=[self.rMOD], writes=[rps, self.rX[dc][ti]])

    def hyena(self, i, Wd, b):
        S = self.S
        off, _ = vec_offsets(i)
        V = lambda name, k=0: self.VEC[:, off[name] + k:off[name] + k + 1]
        V64 = lambda name: self.VEC[0:64, off[name]:off[name] + 1]
        tiles = [0, 1, 2, 3, 4]
        lo, hi = C0, L0 + L
        PI = float(np.pi)
        seqs = [("l", L, 16, L0, 0), ("c", LC, 2, C0, 16)]
        self.arena_reset()
        VX = self.alloc(KC * T).rearrange("p (c t) -> p c t", t=T)
        rVX = [Res() for _ in range(KC)]
        AROW = self.alloc(T)
        CROW = self.alloc(T, F32)
        X1ROW = self.alloc(T, F32)
        X0ROW = self.alloc(T)
        rA, rC, rX1, rX0R = Res(), Res(), Res(), Res()
        WA = [self.alloc(KC * 128).rearrange("p (c n) -> p c n", n=128) for _ in range(2)]
        rWA = [Res(), Res()]
        wa_i = [0]
        S.op("pool", lambda e: e.memset(AROW[:, :], 0.0), writes=[rA])
        S.op("pool", lambda e: e.memset(X0ROW[:, :], 0.0), writes=[rX0R])
        w_in = Wd["hy_in"].rearrange("(c p) n -> p c n", p=128)
        bi, wsh, bsh = off["hy_b_in"], off["hy_w_short"], off["hy_b_short"]

        def proj_conv(oc, dst, rdst):
            k = wa_i[0]
            wa_i[0] = 1 - k
            wa, rwa = WA[k], rWA[k]
            self.load_w(wa, w_in[:, :, oc * 128:(oc + 1) * 128], rwa)
            for ti in tiles:
                c0, w = TILES[ti]
                ps, rps = self.psum()

                def mm(e, ps=ps, c0=c0, w=w):
                    ins = None
                    for kc in range(KC):
                        ins = e.matmul(ps[:, 0:w], lhsT=wa[:, kc, :], rhs=self.H[:, kc, c0:c0 + w], start=(kc == 0), stop=(kc == KC - 1))
                    return ins
                S.op("pe", mm, reads=[rwa] + [self.rH[kc][ti] for kc in range(KC)], writes=[rps])
                S.op("act", lambda e, ps=ps, c0=c0, w=w: e.activation(out=AROW[:, c0:c0 + w], in_=ps[:, 0:w], func=AF.Identity,
                                                                   bias=self.VEC[:, bi + oc:bi + oc + 1], scale=1.0),
                     reads=[self.rVEC], writes=[rps, rA])
            S.op("dve", lambda e: e.tensor_scalar(out=CROW[:, lo:hi], in0=AROW[:, lo:hi], scalar1=self.VEC[:, wsh + 24 + oc:wsh + 25 + oc],
                                                  scalar2=self.VEC[:, bsh + oc:bsh + oc + 1], op0=ALU.mult, op1=ALU.add),
                 reads=[rA, self.rVEC], writes=[rC])
            S.op("dve", lambda e: e.scalar_tensor_tensor(out=CROW[:, lo:hi], in0=AROW[:, lo - 1:hi - 1], scalar=self.VEC[:, wsh + oc:wsh + oc + 1],
                                                         in1=CROW[:, lo:hi], op0=ALU.mult, op1=ALU.add),
                 reads=[rA, self.rVEC], writes=[rC])
            S.op("dve", lambda e: e.scalar_tensor_tensor(out=dst[:, lo:hi], in0=AROW[:, lo + 1:hi + 1], scalar=self.VEC[:, wsh + 48 + oc:wsh + 49 + oc],
                                                         in1=CROW[:, lo:hi], op0=ALU.mult, op1=ALU.add),
                 reads=[rA, self.rVEC, rC], writes=[rdst])

        x0d = Wd["x0d"]
        for c in range(KC):
            proj_conv(c, X0ROW, rX0R)
            S.op("sp", lambda e, c=c: e.dma_start(out=x0d[c * 128:(c + 1) * 128, :], in_=X0ROW[:, :]),
                 reads=[rX0R], writes=[self.rX0D[c]], dma=True)
            proj_conv(KC + c, X1ROW, rX1)
            proj_conv(2 * KC + c, CROW, rC)
            S.op("dve", lambda e, c=c: e.tensor_tensor(out=VX[:, c, lo:hi], in0=CROW[:, lo:hi], in1=X1ROW[:, lo:hi], op=ALU.mult),
                 reads=[rC, rX1], writes=[rVX[c]])
        VT = self.H[:, :, :].rearrange("p c t -> p (c t)")[:, 0:18 * D].rearrange("p (j n) -> p j n", n=D)
        rVT = Res()
        allH = [r for row in self.rH for r in row]
        for (tag, Lx, nt, col0, vt0) in seqs:
            for jt in range(nt):
                ps, rps = self.psum()
                psb = ps[:, :].bitcast(BF16)

                def tr(e, psb=psb, jt=jt, col0=col0):
                    ins = None
                    for c in range(KC):
                        ins = e.transpose(out=psb[:, c * 128:(c + 1) * 128], in_=VX[:, c, col0 + jt * 128:col0 + (jt + 1) * 128], identity=self.IDENT[:, :])
                    return ins
                S.op("pe", tr, reads=rVX + [self.rID], writes=[rps])
                S.op("act", lambda e, psb=psb, jt=jt, vt0=vt0: e.copy(out=VT[:, vt0 + jt, :], in_=psb[:, 0:D]),
                     reads=[], writes=[rps, rVT] + (allH if (jt == 0 and tag == "l") else []))
        S.barrier()
        if b == 0:
            for (tag, Lx, nt, col0, vt0) in seqs:
                self.hy_filter(i, Wd, tag, Lx, nt)
                S.barrier()
        w_out = Wd["hy_out"]
        for q in range(4):
            self.arena_reset()
            X0q = self.alloc(2 * T).rearrange("p (c t) -> p c t", t=T)
            rX0q = Res()
            WOq = self.alloc(2 * D).rearrange("p (c n) -> p c n", n=D)
            rWOq = Res()
            S.op("sp", lambda e, q=q: e.dma_start(out=X0q[:, :, :], in_=x0d[q * 256:(q + 1) * 256, :].rearrange("(c p) t -> p c t", p=128)),
                 reads=self.rX0D, writes=[rX0q], dma=True)
            self.load_w(WOq, w_out[q * 256:(q + 1) * 256, :].rearrange("(c p) n -> p c n", p=128), rWOq)
            Z = self.alloc(2 * 256).rearrange("p (c n) -> p c n", n=256)
            rZ = Res()
            KSb = [self.alloc(2 * 256, F32).rearrange("p (c n) -> p c n", n=256) for _ in range(2)]
            rKSb = [Res(), Res()]
            SC = [self.MU[:, 0:256], self.RSTD2[:, 0:256]]
            rSC = [self.rMU, self.rRSTD2]
            mark = self.a_off
            for (tag, Lx, nt, col0, vt0) in seqs:
                S.barrier()
                self.arena_reset(mark)
                self.hy_conv(i, Wd, q, tag, Lx, nt, col0, vt0, VT, rVT, X0q, rX0q, WOq, rWOq, Z, rZ, KSb, rKSb, SC, rSC)
            S.barrier()

    def hy_filter(self, i, Wd, tag, Lx, nt):
        S = self.S
        off, _ = vec_offsets(i)
        V64 = lambda name: self.VEC[0:64, off[name]:off[name] + 1]
        PI = float(np.pi)
        self.arena_reset()
        ZT = self.alloc(Lx, F32)
        HD1 = self.alloc(Lx, F32)
        HD2 = self.alloc(Lx, F32)
        FW1 = self.alloc(64, F32)
        FW2 = self.alloc(64, F32)
        NEGT = self.alloc(16, F32)
        CST = self.alloc(8, F32)
        ONESF = self.alloc(128, F32)
        rZT, rHD1, rHD2, rFW, rNEGT, rCST, rONESF = (Res() for _ in range(7))
        S.op("sp", lambda e: e.dma_start(out=ZT[0:33, :], in_=Wd[f"z_{tag}"][:, :]), writes=[rZT], dma=True)
        S.op("sp", lambda e: e.dma_start(out=FW1[0:33, :], in_=Wd["fw1"][:, :]), writes=[rFW], dma=True)
        S.op("sp", lambda e: e.dma_start(out=FW2[0:64, :], in_=Wd["fw2"][:, :]), writes=[rFW], dma=True)
        S.op("sp", lambda e: e.dma_start(out=NEGT[:, 0:nt], in_=Wd[f"negt_{tag}"][:, :]), writes=[rNEGT], dma=True)
        S.op("pool", lambda e: e.memset(ONESF[:, :], 1.0), writes=[rONESF])
        S.op("pool", lambda e: e.memset(CST[:, 2:3], -PI), writes=[rCST])
        S.op("dve", lambda e: e.tensor_tensor(out=CST[0:64, 0:1], in0=V64("hy_fb1"), in1=V64("hy_fr"), op=ALU.mult), reads=[self.rVEC], writes=[rCST])
        S.op("dve", lambda e: e.tensor_tensor(out=CST[0:64, 1:2], in0=V64("hy_fb2"), in1=V64("hy_fr"), op=ALU.mult), reads=[self.rVEC], writes=[rCST])

        RR, rRR = self.MU, self.rMU

        def sin_layer(src, rsrc, kdim, wmat, cstcol, dst, rdst):
            for t0 in range(0, Lx, 512):
                w = min(512, Lx - t0)
                ps, rps = self.psum()
                S.op("pe", lambda e, ps=ps, t0=t0, w=w: e.matmul(ps[0:64, 0:w], lhsT=wmat[0:kdim, 0:64], rhs=src[0:kdim, t0:t0 + w], start=True, stop=True),
                     reads=[rFW, rsrc], writes=[rps])
                S.op("act", lambda e, ps=ps, t0=t0, w=w: e.activation(out=dst[0:64, t0:t0 + w], in_=ps[0:64, 0:w], func=AF.Identity,
                                                                   scale=V64("hy_fr"), bias=CST[0:64, cstcol:cstcol + 1]),
                     reads=[self.rVEC, rCST], writes=[rps, rdst])
                MAGIC = 12582912.0
                S.op("dve", lambda e, t0=t0, w=w: e.tensor_scalar(out=RR[0:64, 0:w], in0=dst[0:64, t0:t0 + w], scalar1=1.0 / (2.0 * PI), scalar2=MAGIC,
                                                                op0=ALU.mult, op1=ALU.add),
                     reads=[rdst], writes=[rRR])
                S.op("dve", lambda e, w=w: e.tensor_scalar(out=RR[0:64, 0:w], in0=RR[0:64, 0:w], scalar1=MAGIC, scalar2=-2.0 * PI,
                                                         op0=ALU.subtract, op1=ALU.mult),
                     reads=[], writes=[rRR])
                S.op("dve", lambda e, t0=t0, w=w: e.tensor_tensor(out=dst[0:64, t0:t0 + w], in0=dst[0:64, t0:t0 + w], in1=RR[0:64, 0:w], op=ALU.add),
                     reads=[rRR], writes=[rdst])
                S.op("dve", lambda e, t0=t0, w=w: e.tensor_scalar(out=dst[0:64, t0:t0 + w], in0=dst[0:64, t0:t0 + w], scalar1=-PI, scalar2=PI,
                                                                op0=ALU.max, op1=ALU.min),
                     reads=[], writes=[rdst])
                S.op("act", lambda e, t0=t0, w=w: e.activation(out=dst[0:64, t0:t0 + w], in_=dst[0:64, t0:t0 + w], func=AF.Sin, scale=1.0),
                     reads=[], writes=[rdst])
        sin_layer(ZT, rZT, 33, FW1, 0, HD1, rHD1)
        sin_layer(HD1, rHD1, 64, FW2, 1, HD2, rHD2)
        mark = self.a_off
        kspec = Wd[f"kspec_{tag}"]
        dfm = {0: Wd[f"dfc_{tag}"].rearrange("(j p) f -> p j f", p=128), 1: Wd[f"dfs_{tag}"].rearrange("(j p) f -> p j f", p=128)}
        nf = nt
        for q in range(4):
            S.barrier()
            self.arena_reset(mark)
            A = self.alloc(nt * 256).rearrange("p (j n) -> p j n", n=256)
            Bm = self.alloc(nt * 256).rearrange("p (j n) -> p j n", n=256)
            rAB = Res()
            FW3 = self.alloc(512, F32)
            ABSD = self.alloc(256, F32)
            SKq = self.alloc(256, F32)
            RN = self.alloc(256, F32)
            rFW3, rABSD, rSK, rRN = Res(), Res(), Res(), Res()
            DEC = self.alloc(256, F32)
            KF = self.alloc(256, F32)
            KB = self.alloc(256, F32)
            AF_ = self.alloc(256, F32)
            AB_ = self.alloc(256, F32)
            rDEC, rKF, rKB, rAF, rABb = (Res() for _ in range(5))
            DF = [self.alloc(nt * 128).rearrange("p (j n) -> p j n", n=128) for _ in range(2)]
            rDF = [Res(), Res()]
            KS = [self.alloc(256, F32) for _ in range(2)]
            rKSo = [Res(), Res()]
            S.op("sp", lambda e, q=q: e.dma_start(out=FW3[0:64, 0:256], in_=Wd["fw3"][:, q * 256:(q + 1) * 256]), writes=[rFW3], dma=True)
            S.op("sp", lambda e, q=q: e.dma_start(out=FW3[0:64, 256:512], in_=Wd["fw3"][:, D + q * 256:D + (q + 1) * 256]), writes=[rFW3], dma=True)
            S.op("sp", lambda e, q=q: e.dma_start(out=ABSD[:, :], in_=Wd["absd"][:, q * 256:(q + 1) * 256]), writes=[rABSD], dma=True)
            S.op("sp", lambda e, q=q: e.dma_start(out=SKq[:, :], in_=Wd["skipb"][:, q * 256:(q + 1) * 256]), writes=[rSK], dma=True)
            psS, rpsS = self.psum_acc()
            for j in range(nt):
                ps, rps = self.psum()
                S.op("pe", lambda e, ps=ps, j=j: e.matmul(ps[:, 0:512], lhsT=HD2[0:64, j * 128:(j + 1) * 128], rhs=FW3[0:64, 0:512], start=True, stop=True),
                     reads=[rHD2, rFW3], writes=[rps])
                S.op("act", lambda e, j=j: e.activation(out=DEC[:, :], in_=ABSD[:, :], func=AF.Exp, scale=NEGT[:, j:j + 1]),
                     reads=[rABSD, rNEGT], writes=[rDEC])
                S.op("dve", lambda e, ps=ps: e.tensor_tensor(out=KF[:, :], in0=ps[:, 0:256], in1=DEC[:, :], op=ALU.mult), reads=[rDEC], writes=[rps, rKF])
                S.op("dve", lambda e, ps=ps: e.tensor_tensor(out=KB[:, :], in0=ps[:, 256:512], in1=DEC[:, :], op=ALU.mult), reads=[rDEC], writes=[rps, rKB])
                if j == 0:
                    S.op("pool", lambda e: e.memset(KB[0:1, :], 0.0), writes=[rKB])
                S.op("pool", lambda e, j=j: e.tensor_tensor(out=A[:, j, :], in0=KF[:, :], in1=KB[:, :], op=ALU.add), reads=[rKF, rKB], writes=[rAB])
                S.op("pool", lambda e, j=j: e.tensor_tensor(out=Bm[:, j, :], in0=KF[:, :], in1=KB[:, :], op=ALU.subtract), reads=[rKF, rKB], writes=[rAB])
                S.op("act", lambda e: e.activation(out=AF_[:, :], in_=KF[:, :], func=AF.Abs), reads=[rKF], writes=[rAF])
                S.op("act", lambda e: e.activation(out=AB_[:, :], in_=KB[:, :], func=AF.Abs), reads=[rKB], writes=[rABb])
                S.op("pe", lambda e, j=j: e.matmul(psS[:, 0:256], lhsT=ONESF[:, :], rhs=AF_[:, :], start=(j == 0), stop=False), reads=[rONESF, rAF], writes=[rpsS])
                S.op("pe", lambda e, j=j: e.matmul(psS[:, 0:256], lhsT=ONESF[:, :], rhs=AB_[:, :], start=False, stop=(j == nt - 1)), reads=[rONESF, rABb], writes=[rpsS])
            S.op("dve", lambda e: e.tensor_scalar_add(out=RN[:, :], in0=psS[:, 0:256], scalar1=EPS), reads=[], writes=[rpsS, rRN])
            S.op("dve", lambda e: e.reciprocal(out=RN[:, :], in_=RN[:, :]), reads=[], writes=[rRN])
            for fb in range(nf):
                for typ in range(2):
                    S.op("sp", lambda e, fb=fb, typ=typ: e.dma_start(out=DF[typ][:, :, :], in_=dfm[typ][:, :, fb * 128:(fb + 1) * 128]), writes=[rDF[typ]], dma=True)
                    ps, rps = self.psum()
                    src = A if typ == 0 else Bm

                    def mm(e, ps=ps, typ=typ, src=src):
                        ins = None
                        for j in range(nt):
                            ins = e.matmul(ps[:, 0:256], lhsT=DF[typ][:, j, :], rhs=src[:, j, :], start=(j == 0), stop=(j == nt - 1))
                        return ins
                    S.op("pe", mm, reads=[rDF[typ], rAB], writes=[rps])
                    S.op("dve", lambda e, ps=ps, typ=typ: e.tensor_tensor(out=KS[typ][:, :], in0=ps[:, 0:256], in1=RN[:, :], op=ALU.mult), reads=[rRN], writes=[rps, rKSo[typ]])
                    if typ == 0:
                        S.op("pool", lambda e: e.tensor_tensor(out=KS[0][:, :], in0=KS[0][:, :], in1=SKq[:, :], op=ALU.add), reads=[rSK], writes=[rKSo[0]])
                    S.op("sp", lambda e, fb=fb, typ=typ, q=q: e.dma_start(out=kspec[typ, fb * 128:(fb + 1) * 128, q * 256:(q + 1) * 256], in_=KS[typ][:, :]),
                         reads=[rKSo[typ]], writes=[self.rKS[tag]], dma=True)

    def hy_conv(self, i, Wd, q, tag, Lx, nt, col0, vt0, VT, rVT, X0q, rX0q, WOq, rWOq, Z, rZ, KSb, rKSb, SC, rSC):
        S = self.S
        off, _ = vec_offsets(i)
        nf = nt
        kspec = Wd[f"kspec_{tag}"]
        dfm = {0: Wd[f"dfc_{tag}"].rearrange("(j p) f -> p j f", p=128), 1: Wd[f"dfs_{tag}"].rearrange("(j p) f -> p j f", p=128)}
        idm = {0: Wd[f"idc_{tag}"].rearrange("(j p) t -> p j t", p=128), 1: Wd[f"ids_{tag}"].rearrange("(j p) t -> p j t", p=128)}
        Y = self.alloc(nf * 512).rearrange("p (f y n) -> p f y n", y=2, n=256)
        rY = Res()
        DF = [[self.alloc(nt * 128).rearrange("p (j n) -> p j n", n=128) for _ in range(2)] for _ in range(2)]
        rDF = [[Res(), Res()], [Res(), Res()]]
        IB = [self.alloc(2 * nf * 256).rearrange("p (y f n) -> p y f n", y=2, n=256) for _ in range(2)]
        rIB = [Res(), Res()]
        U = [self.TMP[0], self.TMP[1]]
        rU = [self.rTMP[0], self.rTMP[1]]
        col = 1 if tag == "c" else 0
        bo = off["hy_b_out"]
        for fb in range(nf):
            kb = fb % 2
            S.op("sp", lambda e, fb=fb, kb=kb: e.dma_start(out=KSb[kb][:, :, :], in_=kspec[:, fb * 128:(fb + 1) * 128, q * 256:(q + 1) * 256].rearrange("y p n -> p y n")),
                 reads=[self.rKS[tag]], writes=[rKSb[kb]], dma=True)
            pss = []
            for typ in range(2):
                df, rdf = DF[kb][typ], rDF[kb][typ]
                S.op("sp", lambda e, fb=fb, typ=typ, df=df: e.dma_start(out=df[:, :, :], in_=dfm[typ][:, :, fb * 128:(fb + 1) * 128]), writes=[rdf], dma=True)
                ps, rps = self.psum()
                pss.append((ps, rps))

                def mm(e, ps=ps, typ=typ, df=df):
                    ins = None
                    for j in range(nt):
                        ins = e.matmul(ps[:, 0:256], lhsT=df[:, j, :], rhs=VT[:, vt0 + j, q * 256:(q + 1) * 256], start=(j == 0), stop=(j == nt - 1))
                    return ins
                S.op("pe", mm, reads=[rdf, rVT], writes=[rps])
                S.op("act", lambda e, ps=ps, typ=typ: e.copy(out=U[typ][:, 0:256], in_=ps[:, 0:256]), reads=[], writes=[rps, rU[typ]])
            Kc, Ks = KSb[kb][:, 0, :], KSb[kb][:, 1, :]
            S.op("dve", lambda e, Kc=Kc: e.tensor_tensor(out=SC[0][:, :], in0=U[0][:, 0:256], in1=Kc, op=ALU.mult), reads=[rU[0], rKSb[kb]], writes=[rSC[0]])
            S.op("pool", lambda e, Ks=Ks: e.tensor_tensor(out=SC[1][:, :], in0=U[1][:, 0:256], in1=Ks, op=ALU.mult), reads=[rU[1], rKSb[kb]], writes=[rSC[1]])
            S.op("dve", lambda e, fb=fb: e.tensor_tensor(out=Y[:, fb, 0, :], in0=SC[0][:, :], in1=SC[1][:, :], op=ALU.subtract), reads=[rSC[0], rSC[1]], writes=[rY])
            S.op("dve", lambda e, Ks=Ks: e.tensor_tensor(out=SC[0][:, :], in0=U[0][:, 0:256], in1=Ks, op=ALU.mult), reads=[rU[0], rKSb[kb]], writes=[rSC[0]])
            S.op("pool", lambda e, Kc=Kc: e.tensor_tensor(out=SC[1][:, :], in0=U[1][:, 0:256], in1=Kc, op=ALU.mult), reads=[rU[1], rKSb[kb]], writes=[rSC[1]])
            S.op("dve", lambda e, fb=fb: e.tensor_tensor(out=Y[:, fb, 1, :], in0=SC[0][:, :], in1=SC[1][:, :], op=ALU.add), reads=[rSC[0], rSC[1]], writes=[rY])
        for ts in range(Lx // 256):
            ib, rib = IB[ts % 2], rIB[ts % 2]
            for typ in range(2):
                S.op("sp", lambda e, ib=ib, typ=typ, ts=ts: e.dma_start(out=ib[:, typ, :, :], in_=idm[typ][:, :, ts * 256:(ts + 1) * 256]), writes=[rib], dma=True)
            cs = col0 + ts * 256
            ti = 0 if tag == "c" else 1 + (ts // 2)
            for cc in range(2):
                ps, rps = self.psum()

                def mm(e, ps=ps, ib=ib, cc=cc):
                    ins = None
                    for fb in range(nf):
                        for typ in range(2):
                            ins = e.matmul(ps[:, 0:256], lhsT=Y[:, fb, typ, cc * 128:(cc + 1) * 128], rhs=ib[:, typ, fb, :],
                                           start=(fb == 0 and typ == 0), stop=(fb == nf - 1 and typ == 1))
                    return ins
                S.op("pe", mm, reads=[rY, rib], writes=[rps])
                S.op("dve", lambda e, ps=ps, cc=cc, cs=cs: e.tensor_tensor(out=Z[:, cc, :], in0=ps[:, 0:256], in1=X0q[:, cc, cs:cs + 256], op=ALU.mult),
                     reads=[rX0q], writes=[rps, rZ])
            for o in range(KC):
                ps, rps = self.psum()

                def mm2(e, ps=ps, o=o):
                    e.matmul(ps[:, 0:256], lhsT=WOq[:, 0, o * 128:(o + 1) * 128], rhs=Z[:, 0, :], start=True, stop=False)
                    return e.matmul(ps[:, 0:256], lhsT=WOq[:, 1, o * 128:(o + 1) * 128], rhs=Z[:, 1, :], start=False, stop=True)
                S.op("pe", mm2, reads=[rWOq, rZ], writes=[rps])
                if q == 0:
                    S.op("dve", lambda e, ps=ps, o=o: e.tensor_scalar(out=self.RSTD[:, 0:256], in0=ps[:, 0:256], scalar1=self.VEC[:, bo + o:bo + o + 1],
                                                                   scalar2=self.MOD[:, 16 + o, col:col + 1], op0=ALU.add, op1=ALU.mult),
                         reads=[self.rVEC, self.rMOD], writes=[rps, self.rRSTD])
                    S.op("dve", lambda e, o=o, cs=cs: e.tensor_tensor(out=self.X[:, o, cs:cs + 256], in0=self.X[:, o, cs:cs + 256], in1=self.RSTD[:, 0:256], op=ALU.add),
                         reads=[self.rRSTD], writes=[self.rX[o][ti]])
                else:
                    S.op("dve", lambda e, ps=ps, o=o, cs=cs: e.scalar_tensor_tensor(out=self.X[:, o, cs:cs + 256], in0=ps[:, 0:256], scalar=self.MOD[:, 16 + o, col:col + 1],
                                                                                  in1=self.X[:, o, cs:cs + 256], op0=ALU.mult, op1=ALU.add),
                         reads=[self.rMOD], writes=[rps, self.rX[o][ti]])

    def layer(self, i, Wd, b, last):
        kind = i % 3
        tiles_all = [0, 1, 2, 3, 4]
        tiles_out = [1, 2, 3, 4] if last else tiles_all
        self.adaln(i, Wd)
        if kind == 0:
            self.modulate(0, tiles_all)
            self.S.barrier()
            self.mla(i, Wd, need_ctx=not last)
            self.S.barrier()
            self.carve_ffn()
            self.zero_pads()
        if kind == 2:
            self.modulate(0, tiles_all)
            self.S.barrier()
            self.hyena(i, Wd, b)
            self.S.barrier()
            self.carve_ffn()
            self.zero_pads()
        if kind == 1:
            self.modulate(0, tiles_out)
            self.conformer(i, Wd, tiles_out)
        self.modulate(1, tiles_out)
        self.ffn(i, Wd, tiles_out)

    def ubuf(self, c):
        if c < 6:
            return self.G[:, c, :], self.rG[c]
        return (self.AROW, self.rA) if c == 6 else (self.VROW, self.rV)

    def conformer(self, i, Wd, tiles):
        S = self.S
        off, _ = vec_offsets(i)
        w1 = Wd["cf_w1"].rearrange("(c p) n -> p c n", p=128)
        w2 = Wd["cf_w2"].rearrange("(c p) n -> p c n", p=128)
        b1o, wdo, bdo, lgo, lbo, b2o = (off[k] for k in ("cf_b_pw1", "cf_w_dw", "cf_b_dw", "cf_ln_g", "cf_ln_b", "cf_b_pw2"))
        for (a, bnd) in ((0, 16), (C0 + LC, L0), (L0 + L, T)):
            S.op("pool", lambda e, a=a, bnd=bnd: e.memset(self.G[:, :, a:bnd], 0.0), writes=self.rG)
        for c in range(KC):
            k = self.wa_i
            self.wa_i = 1 - k
            wa, rwa = self.WA[k], self.rWA[k]
            self.load_w(wa[:, :, 0:128], w1[:, :, c * 128:(c + 1) * 128], rwa[0])
            self.load_w(wa[:, :, 128:256], w1[:, :, D + c * 128:D + (c + 1) * 128], rwa[1])
            ub, rub = self.ubuf(c)
            for ti in tiles:
                c0, w = TILES[ti]
                pss = []
                for half in range(2):
                    ps, rps = self.psum()
                    pss.append((ps, rps))

                    def mm(e, ps=ps, wa=wa, half=half, c0=c0, w=w):
                        ins = None
                        for kc in range(KC):
                            ins = e.matmul(ps[:, 0:w], lhsT=wa[:, kc, half * 128:(half + 1) * 128],
                                           rhs=self.H[:, kc, c0:c0 + w], start=(kc == 0), stop=(kc == KC - 1))
                        return ins
                    S.op("pe", mm, reads=[rwa[half]] + [self.rH[kc][ti] for kc in range(KC)], writes=[rps])
                tp, rtp = self.tmp()
                S.op("act", lambda e, ps=pss[1][0], tp=tp, w=w, c=c: e.activation(
                    out=tp[:, 0:w], in_=ps[:, 0:w], func=AF.Sigmoid, bias=self.VEC[:, b1o + 8 + c:b1o + 9 + c], scale=1.0),
                    reads=[self.rVEC], writes=[pss[1][1], rtp])
                S.op("dve", lambda e, ps=pss[0][0], tp=tp, ub=ub, c0=c0, w=w, c=c: e.scalar_tensor_tensor(
                    out=ub[:, c0:c0 + w], in0=ps[:, 0:w], scalar=self.VEC[:, b1o + c:b1o + c + 1], in1=tp[:, 0:w],
                    op0=ALU.add, op1=ALU.mult),
                    reads=[self.rVEC, rtp], writes=[pss[0][1], rub])
        dgs = [self.WB[:, 0:31 * 128].rearrange("p (k n) -> p k n", n=128),
               self.WA[0][:, :, :].rearrange("p c n -> p (c n)")[:, 0:31 * 128].rearrange("p (k n) -> p k n", n=128)]
        rdgs = [[self.rWB], self.rWA[0]]
        for c in range(KC):
            ub, rub = self.ubuf(c)
            dg, rdg = dgs[c % 2], rdgs[c % 2]
            for kk in range(31):
                sc = self.VEC[:, wdo + kk * 8 + c:wdo + kk * 8 + c + 1]
                if kk % 3 == 2:
                    S.op("act", lambda e, kk=kk, dg=dg, sc=sc: e.activation(out=dg[:, kk, :], in_=self.IDENT[:, :], func=AF.Copy, scale=sc),
                         reads=[self.rID, self.rVEC], writes=rdg)
                else:
                    S.op("pool" if kk % 3 else "dve", lambda e, kk=kk, dg=dg, sc=sc: e.tensor_scalar_mul(out=dg[:, kk, :], in0=self.IDENT[:, :], scalar1=sc),
                         reads=[self.rID, self.rVEC], writes=rdg)
            for ti in tiles:
                c0, w = TILES[ti]
                ps, rps = self.psum()

                def mm(e, ps=ps, ub=ub, c0=c0, w=w, dg=dg):
                    ins = None
                    for kk in range(31):
                        ins = e.matmul(ps[:, 0:w], lhsT=dg[:, kk, :], rhs=ub[:, c0 + kk - 15:c0 + kk - 15 + w],
                                       start=(kk == 0), stop=(kk == 30))
                    return ins
                S.op("pe", mm, reads=rdg + [rub], writes=[rps])
                S.op("act", lambda e, ps=ps, c=c, c0=c0, w=w: e.activation(
                    out=self.H[:, c, c0:c0 + w], in_=ps[:, 0:w], func=AF.Identity, bias=self.VEC[:, bdo + c:bdo + c + 1], scale=1.0),
                    reads=[self.rVEC], writes=[rps, self.rH[c][ti]])
        for k in range(2):
            self.load_w(self.WA[k][:, :, :], w2[:, :, k * 512:(k + 1) * 512], self.rWA[k])
        for ti in tiles:
            c0, w = TILES[ti]
            col = 1 if ti == 0 else 0
            for c in range(KC):
                S.op("act", lambda e, c=c, c0=c0, w=w: e.activation(out=self.SQ[:, c, 0:w], in_=self.H[:, c, c0:c0 + w], func=AF.Square),
                     reads=[self.rH[c][ti]], writes=[self.rSQ])
            ps1, rps1 = self.psum()
            ps2, rps2 = self.psum()

            def mm1(e, ps=ps1, c0=c0, w=w):
                ins = None
                for c in range(KC):
                    ins = e.matmul(ps[:, 0:w], lhsT=self.ONES[:, :], rhs=self.H[:, c, c0:c0 + w], start=(c == 0), stop=(c == KC - 1))
                return ins
            S.op("pe", mm1, reads=[self.rONES] + [self.rH[c][ti] for c in range(KC)], writes=[rps1])

            def mm2(e, ps=ps2, w=w):
                ins = None
                for c in range(KC):
                    ins = e.matmul(ps[:, 0:w], lhsT=self.ONES[:, :], rhs=self.SQ[:, c, 0:w], start=(c == 0), stop=(c == KC - 1))
                return ins
            S.op("pe", mm2, reads=[self.rONES, self.rSQ], writes=[rps2])
            S.op("act", lambda e, ps=ps1, w=w: e.activation(out=self.MU[:, 0:w], in_=ps[:, 0:w], func=AF.Identity, scale=1.0 / D),
                 reads=[], writes=[rps1, self.rMU])
            tp, rtp = self.tmp()
            S.op("dve", lambda e, tp=tp, w=w: e.tensor_tensor(out=tp[:, 0:w], in0=self.MU[:, 0:w], in1=self.MU[:, 0:w], op=ALU.mult),
                 reads=[self.rMU], writes=[rtp])
            S.op("dve", lambda e, ps=ps2, tp=tp, w=w: e.scalar_tensor_tensor(
                out=self.RSTD[:, 0:w], in0=ps[:, 0:w], scalar=1.0 / D, in1=tp[:, 0:w], op0=ALU.mult, op1=ALU.subtract),
                reads=[rtp], writes=[rps2, self.rRSTD])
            S.op("act", lambda e, w=w: e.activation(out=self.RSTD[:, 0:w], in_=self.RSTD[:, 0:w], func=AF.Ln, scale=1.0, bias=self.EPSB[:, 0:1]),
                 reads=[self.rONES], writes=[self.rRSTD])
            S.op("act", lambda e, w=w: e.activation(out=self.RSTD[:, 0:w], in_=self.RSTD[:, 0:w], func=AF.Exp, scale=-0.5), reads=[], writes=[self.rRSTD])
            for c in range(KC):
                ub, rub = self.ubuf(c)
                tp, rtp = self.tmp()
                S.op("dve", lambda e, tp=tp, c=c, c0=c0, w=w: e.tensor_tensor(out=tp[:, 0:w], in0=self.H[:, c, c0:c0 + w], in1=self.MU[:, 0:w], op=ALU.subtract),
                     reads=[self.rH[c][ti], self.rMU], writes=[rtp])
                S.op("dve", lambda e, tp=tp, w=w: e.tensor_tensor(out=tp[:, 0:w], in0=tp[:, 0:w], in1=self.RSTD[:, 0:w], op=ALU.mult),
                     reads=[self.rRSTD], writes=[rtp])
                S.op("act", lambda e, tp=tp, ub=ub, c=c, c0=c0, w=w: e.activation(
                    out=ub[:, c0:c0 + w], in_=tp[:, 0:w], func=AF.Silu, scale=self.VEC[:, lgo + c:lgo + c + 1], bias=self.VEC[:, lbo + c:lbo + c + 1]),
                    reads=[rtp, self.rVEC], writes=[rub])
            for dc in range(KC):
                ps, rps = self.psum()
                wa, rwa = self.WA[dc // 4], self.rWA[dc // 4]

                def mm(e, ps=ps, wa=wa, dc=dc, c0=c0, w=w):
                    ins = None
                    for c in range(KC):
                        ins = e.matmul(ps[:, 0:w], lhsT=wa[:, c, (dc % 4) * 128:(dc % 4 + 1) * 128], rhs=self.ubuf(c)[0][:, c0:c0 + w],
                                       start=(c == 0), stop=(c == KC - 1))
                    return ins
                S.op("pe", mm, reads=[rwa[0]] + [self.ubuf(c)[1] for c in range(KC)], writes=[rps])
                tp, rtp = self.tmp()
                S.op("dve", lambda e, ps=ps, tp=tp, dc=dc, w=w, col=col: e.tensor_scalar(
                    out=tp[:, 0:w], in0=ps[:, 0:w], scalar1=self.VEC[:, b2o + dc:b2o + dc + 1], scalar2=self.MOD[:, 16 + dc, col:col + 1],
                    op0=ALU.add, op1=ALU.mult),
                    reads=[self.rVEC, self.rMOD], writes=[rps, rtp])
                S.op("dve", lambda e, tp=tp, dc=dc, c0=c0, w=w: e.tensor_tensor(out=self.X[:, dc, c0:c0 + w], in0=self.X[:, dc, c0:c0 + w], in1=tp[:, 0:w], op=ALU.add),
                     reads=[rtp], writes=[self.rX[dc][ti]])


def make_inputs(inp, layers, core, nb=NB, x_override=None, ctx_override=None):
    m = {}
    xs = inp["x"] if x_override is None else x_override
    cs = inp["ctx"] if ctx_override is None else ctx_override
    xT = np.zeros((nb, D, T), np.float32)
    cT = np.zeros((nb, 128, KC * 2), np.float32)
    for bb in range(nb):
        b = core * nb + bb
        xT[bb, :, L0:L0 + L] = xs[b].T
        xT[bb, :, C0:C0 + LC] = cs[b].T
        sv = np.stack([inp["c"][b], inp["c_ctx"]], axis=-1)
        cT[bb] = sv.reshape(KC, 128, 2).transpose(1, 0, 2).reshape(128, KC * 2)
    m["ident"] = np.eye(128, dtype=np.float32)
    bd = np.zeros((128, 128), np.float32)
    bd[:64, :64] = 1.0
    bd[64:96, 64:96] = 1.0
    m["bd96"] = bd
    m["xT"] = xT
    m["cT"] = cT
    for i in layers:
        kind, j = i % 3, i // 3
        v, _ = pack_layer_vec(inp, i)
        vp = np.zeros((128, NPV), np.float32)
        vp[:, :v.shape[1]] = v
        m[f"vec{i}"] = vp
        m[f"ada_w{i}"] = np.ascontiguousarray(inp["ada_w"][i])
        m[f"ffn_up{i}"] = np.ascontiguousarray(inp["ffn_w_up"][i])
        m[f"ffn_down{i}"] = np.ascontiguousarray(inp["ffn_w_down"][i])
        if kind == 0:
            pe = np.arange(32)
            perm = np.where((pe % 16) < 8, pe + 8, pe - 8)
            wdkv = np.asarray(inp["mla_w_dkv"][j], np.float32)
            kv = np.zeros((D, 320), np.float32)
            kv[:, :128] = wdkv[:, :128]
            kv[:, 128 + 64:224] = wdkv[:, 128:160]
            kv[:, 224 + 64:320] = wdkv[:, 128 + perm]
            m[f"mla_kv{i}"] = kv
            wuq = np.asarray(inp["mla_w_uq"][j], np.float32).reshape(256, 16, 96)
            uq = np.zeros((256, 16, 192), np.float32)
            uq[:, :, :96] = wuq
            uq[:, :, 96 + 64:] = wuq[:, :, 64 + perm]
            m[f"mla_uq{i}"] = uq.reshape(256, 16 * 192)
            m[f"mla_dq{i}"] = np.ascontiguousarray(inp["mla_w_dq"][j])
            m[f"mla_ukv{i}"] = np.ascontiguousarray(inp["mla_w_ukv"][j])
            m[f"mla_o{i}"] = np.ascontiguousarray(inp["mla_w_o"][j])
            m["rope_cs"] = rope_tables()
        if kind == 2:
            m[f"hy_in{i}"] = np.ascontiguousarray(inp["hy_w_in"][j])
            m[f"hy_out{i}"] = np.ascontiguousarray(inp["hy_w_out"][j])
            m[f"hy_fw1_{i}"] = np.ascontiguousarray(inp["hy_f_w1"][j])
            m[f"hy_fw2_{i}"] = np.ascontiguousarray(inp["hy_f_w2"][j])
            m[f"hy_fw3_{i}"] = np.ascontiguousarray(inp["hy_f_w3"][j])
            m[f"hy_skipb{i}"] = np.ascontiguousarray(np.broadcast_to(np.asarray(inp["hy_skip"][j], np.float32)[None, :], (128, D)))
            m.update(hy_consts())
        if kind == 1:
            m[f"cf_w1_{i}"] = np.ascontiguousarray(inp["cf_w_pw1"][j])
            m[f"cf_w2_{i}"] = np.ascontiguousarray(inp["cf_w_pw2"][j])
    return m


_HYC = {}


def hy_consts():
    if _HYC:
        return _HYC
    import ml_dtypes
    bf = ml_dtypes.bfloat16
    out = {}
    for tag, Lx in (("l", L), ("c", LC)):
        N = 2 * Lx
        tau = np.arange(Lx, dtype=np.float64)[:, None]
        f = np.arange(Lx, dtype=np.float64)[None, :] + 0.5
        th = 2 * np.pi * tau * f / N
        out[f"dfc_{tag}"] = np.cos(th).astype(np.float32).astype(bf)
        out[f"dfs_{tag}"] = np.sin(th).astype(np.float32).astype(bf)
        out[f"idc_{tag}"] = np.ascontiguousarray(((2.0 / N) * np.cos(th).T).astype(np.float32)).astype(bf)
        out[f"ids_{tag}"] = np.ascontiguousarray(((2.0 / N) * np.sin(th).T).astype(np.float32)).astype(bf)
        t = np.linspace(0.0, 1.0, Lx, dtype=np.float32)
        w = (2.0 * np.pi * np.arange(Lx, dtype=np.float32) / Lx).astype(np.float32)
        fr = np.linspace(1e-4, 15.0, 16, dtype=np.float32)
        fw = w[:, None] * fr[None, :]
        z = np.concatenate([t[:, None], np.cos(fw), -np.sin(fw)], axis=-1).astype(np.float32)
        out[f"z_{tag}"] = np.ascontiguousarray(z.T)
        out[f"negt_{tag}"] = np.ascontiguousarray((-t).reshape(Lx // 128, 128).T)
    deltas = np.linspace(np.log(1e-2) / 0.3, np.log(1e-2) / 1.5, D, dtype=np.float32)
    out["absd"] = np.ascontiguousarray(np.broadcast_to(np.abs(deltas)[None, :], (128, D))).astype(np.float32)
    _HYC.update(out)
    return _HYC


def rope_tables():
    t = np.arange(L)
    row = (t // 64).astype(np.float32)
    colp = (t % 64).astype(np.float32)
    inv = (10000.0 ** (-(np.arange(0, 16, 2, dtype=np.float32) / 16.0))).astype(np.float32)
    ang = np.concatenate([row[:, None] * inv, colp[:, None] * inv], axis=-1).astype(np.float32)
    cs = np.zeros((128, 2, T), np.float32)
    cs[:, 0, :] = 1.0
    for d in range(32):
        a, b, jj = d // 16, (d % 16) // 8, d % 8
        cs[64 + d, 0, L0:L0 + L] = np.cos(ang[:, a * 8 + jj])
        cs[64 + d, 1, L0:L0 + L] = np.sin(ang[:, a * 8 + jj]) * (-1.0 if b == 0 else 1.0)
    return cs


def kernel(**inputs):
    inp = {k: np.asarray(v) for k, v in inputs.items()}
    layers = list(range(DEPTH))
    prog = Prog(layers)
    nc = prog.build()
    in_maps = [make_inputs(inp, layers, c) for c in range(8)]
    res = run_bass_kernel_spmd(nc, in_maps, core_ids=list(range(8)))
    out = np.empty((16, L, D), np.float32)
    for c in range(8):
        y = res.results[c]["yT"]
        for bb in range(NB):
            out[c * NB + bb] = y[bb].T
    return out
```

```python
import contextlib
import os
import numpy as np
import concourse.bass as bass
import concourse.mybir as mybir
from concourse.bass_utils import run_bass_kernel_spmd

F32 = mybir.dt.float32
BF16 = mybir.dt.bfloat16
ALU = mybir.AluOpType
AF = mybir.ActivationFunctionType

D = 1024
KC = 8
L = 2048
LC = 256
T = 16 + LC + 16 + L + 16
C0 = 16
L0 = 16 + LC + 16
TILES = [(C0, LC)] + [(L0 + 512 * j, 512) for j in range(4)]
D_FF = 2816
FC = 22
DEPTH = 4
EPS = 1e-6
NB = 2
DBG_STOP = int(os.environ.get('MK_STOP', '0'))
DBG_HEADS = (int(os.environ['MK_HEADS']) if 'MK_HEADS' in os.environ else None)


class Res:
    __slots__ = ("w", "r")

    def __init__(self):
        self.w = None
        self.r = {}


class Sched:
    ENG = ("pe", "act", "dve", "pool", "sp")
    NDMA = 8
    EPOCH = 6000

    def __init__(self):
        self.streams = {e: [] for e in self.ENG}
        self.cnt = {e: 0 for e in self.ENG}
        self.waited = {e: {} for e in self.ENG}
        self.dma_n = {e: 0 for e in self.ENG}
        self.last = {}

    def op(self, eng, fn, reads=(), writes=(), dma=False, after=()):
        deps = {}

        def add(k, v):
            if deps.get(k, 0) < v:
                deps[k] = v

        for r in after:
            if r.w is not None:
                add(*r.w)
            for k, v in r.r.items():
                add(k, v)
        for r in reads:
            if r.w is not None:
                add(*r.w)
        for r in writes:
            if r.w is not None:
                add(*r.w)
            for k, v in r.r.items():
                add(k, v)
        if dma:
            j = self.dma_n[eng]
            self.dma_n[eng] += 1
            key = ("dma", eng, j % self.NDMA)
            val = 16 * (j // self.NDMA + 1)
            if j >= self.NDMA:
                add(key, val - 16)
            inc = 16
        else:
            key = ("eng", eng, self.cnt[eng] // self.EPOCH)
            val = self.cnt[eng] % self.EPOCH + 1
            self.cnt[eng] += 1
            inc = 1
        self.last[key] = val
        wd = self.waited[eng]
        waits = []
        for k, v in deps.items():
            if eng == "pe" and k[0] == "eng" and k[1] == "pe":
                continue
            if wd.get(k, 0) < v:
                wd[k] = v
                waits.append((k, v))
        self.streams[eng].append((fn, waits, key, inc))
        for r in reads:
            if r.r.get(key, 0) < val:
                r.r[key] = val
        for r in writes:
            r.w = (key, val)
            r.r = {}

    def barrier(self, engs=None):
        for e in (engs or self.ENG):
            wd = self.waited[e]
            waits = []
            for k, v in self.last.items():
                if wd.get(k, 0) < v:
                    wd[k] = v
                    waits.append((k, v))
            if waits:
                self.streams[e].append((None, waits, None, 0))

    def emit(self, nc, stack):
        sems = {}
        for e in self.ENG:
            for (fn, waits, key, inc) in self.streams[e]:
                for k in [w[0] for w in waits] + ([key] if key is not None else []):
                    if k not in sems:
                        sems[k] = stack.enter_context(nc.semaphore("s_" + "_".join(str(x) for x in k)))
        block = stack.enter_context(nc.Block())

        def run(ename, eng):
            for (fn, waits, key, inc) in self.streams[ename]:
                for k, v in waits:
                    eng.wait_ge(sems[k], v)
                if fn is not None:
                    fn(eng).then_inc(sems[key], inc)

        @block.tensor
        def _(e):
            run("pe", e)

        @block.scalar
        def _(e):
            run("act", e)

        @block.vector
        def _(e):
            run("dve", e)

        @block.gpsimd
        def _(e):
            run("pool", e)

        @block.sync
        def _(e):
            run("sp", e)


def fm(v):
    v = np.asarray(v, np.float32)
    c = v.shape[-1] // 128
    lead = v.shape[:-1]
    return np.ascontiguousarray(np.moveaxis(v.reshape(lead + (c, 128)), -1, 0)).reshape(128, -1)


def pack_layer_vec(inp, i):
    kind, j = i % 3, i // 3
    parts = [("ada_b", fm(inp["ada_b"][i])),
             ("norm_mix", fm(inp["norm_mix"][i])),
             ("norm_ffn", fm(inp["norm_ffn"][i])),
             ("ffn_b_dw", fm(inp["ffn_b_dw"][i])),
             ("ffn_w_dw", fm(inp["ffn_w_dw"][i]))]
    if kind == 0:
        g = np.asarray(inp["mla_qk_gain"][j], np.float32)
        def col96(v):
            o = np.zeros((128, 1), np.float32)
            o[:96, 0] = v
            return o
        pe = np.arange(32)
        perm = np.where((pe % 16) < 8, pe + 8, pe - 8)
        gqp = np.zeros(96, np.float32); gqp[64:] = g[0, 64 + perm]
        gkp = np.zeros(96, np.float32); gkp[64:] = g[1, 64 + perm]
        invn = np.concatenate([np.full(64, 1 / 64.0), np.full(32, 1 / 32.0)]).astype(np.float32)
        parts += [("mla_qn", fm(inp["mla_q_norm"][j])), ("mla_kvn", fm(inp["mla_kv_norm"][j])),
                  ("mla_gq", col96(g[0])), ("mla_gqp", col96(gqp)), ("mla_gk", col96(g[1])), ("mla_gkp", col96(gkp)),
                  ("mla_invn", col96(invn))]
    if kind == 2:
        def col64(v):
            o = np.zeros((128, 1), np.float32)
            o[:64, 0] = v
            return o
        parts += [("hy_b_in", fm(inp["hy_b_in"][j])), ("hy_w_short", fm(inp["hy_w_short"][j])),
                  ("hy_b_short", fm(inp["hy_b_short"][j])), ("hy_b_out", fm(inp["hy_b_out"][j])),
                  ("hy_fb1", col64(inp["hy_f_b1"][j])), ("hy_fb2", col64(inp["hy_f_b2"][j])), ("hy_fr", col64(inp["hy_sin_freq"][j]))]
    if kind == 1:
        parts += [("cf_b_pw1", fm(inp["cf_b_pw1"][j])),
                  ("cf_w_dw", fm(inp["cf_w_dw"][j])),
                  ("cf_b_dw", fm(inp["cf_b_dw"][j])),
                  ("cf_ln_g", fm(inp["cf_ln_g"][j])),
                  ("cf_ln_b", fm(inp["cf_ln_b"][j])),
                  ("cf_b_pw2", fm(inp["cf_b_pw2"][j]))]
    off = {}
    o = 0
    for n, a in parts:
        off[n] = o
        o += a.shape[1]
    return np.concatenate([a for _, a in parts], axis=1), off


def vec_offsets(i):
    kind = i % 3
    sizes = [("ada_b", 48), ("norm_mix", 8), ("norm_ffn", 8), ("ffn_b_dw", 22), ("ffn_w_dw", 66)]
    if kind == 0:
        sizes += [("mla_qn", 2), ("mla_kvn", 1), ("mla_gq", 1), ("mla_gqp", 1), ("mla_gk", 1), ("mla_gkp", 1), ("mla_invn", 1)]
    if kind == 2:
        sizes += [("hy_b_in", 24), ("hy_w_short", 72), ("hy_b_short", 24), ("hy_b_out", 8), ("hy_fb1", 1), ("hy_fb2", 1), ("hy_fr", 1)]
    if kind == 1:
        sizes += [("cf_b_pw1", 16), ("cf_w_dw", 248), ("cf_b_dw", 8), ("cf_ln_g", 8), ("cf_ln_b", 8), ("cf_b_pw2", 8)]
    off = {}
    o = 0
    for n, s in sizes:
        off[n] = o
        o += s
    return off, o


NPV = 512
ARENA = 42112


class Prog:
    def __init__(self, layers, nb=NB):
        self.layers = layers
        self.nb = nb
        self.nc = bass.Bass("TRN2", target_bir_lowering=False)
        self.S = Sched()
        self.dram = {}

    def din(self, name, shape, dt=F32):
        t = self.nc.dram_tensor(name, list(shape), dt, kind="ExternalInput").ap()
        self.dram[name] = t
        return t

    def dout(self, name, shape, dt=F32):
        t = self.nc.dram_tensor(name, list(shape), dt, kind="ExternalOutput").ap()
        self.dram[name] = t
        return t

    def build(self):
        nc, S = self.nc, self.S
        nb = self.nb
        xT = self.din("xT", [nb, D, T])
        cT = self.din("cT", [nb, 128, KC * 2])
        ident_d = self.din("ident", [128, 128])
        bd_d = self.din("bd96", [128, 128])
        yT = self.dout("yT", [nb, D, L])
        cOut = self.dout("cOut", [nb, D, LC])
        W = {}
        for i in self.layers:
            kind, j = i % 3, i // 3
            W[i] = dict(
                ada_w=self.din(f"ada_w{i}", [D, 6 * D]),
                vec=self.din(f"vec{i}", [128, NPV]),
                ffn_up=self.din(f"ffn_up{i}", [D, 2 * D_FF]),
                ffn_down=self.din(f"ffn_down{i}", [D_FF, D]),
            )
            if kind == 0:
                if "rope_cs" not in self.dram:
                    self.din("rope_cs", [128, 2, T])
                W[i].update(dq=self.din(f"mla_dq{i}", [D, 256]), kv=self.din(f"mla_kv{i}", [D, 320]),
                            uq=self.din(f"mla_uq{i}", [256, 16 * 192]), ukv=self.din(f"mla_ukv{i}", [128, 2048]),
                            wo=self.din(f"mla_o{i}", [D, D]), rope=self.dram["rope_cs"])
            if kind == 2:
                W[i].update(hy_in=self.din(f"hy_in{i}", [D, 3 * D]), hy_out=self.din(f"hy_out{i}", [D, D]),
                            fw1=self.din(f"hy_fw1_{i}", [33, 64]), fw2=self.din(f"hy_fw2_{i}", [64, 64]),
                            fw3=self.din(f"hy_fw3_{i}", [64, 2 * D]), skipb=self.din(f"hy_skipb{i}", [128, D]))
                for tag, Lx in (("l", L), ("c", LC)):
                    for nm in ("dfc", "dfs", "idc", "ids"):
                        W[i][f"{nm}_{tag}"] = self.din(f"{nm}_{tag}", [Lx, Lx], BF16)
                    W[i][f"z_{tag}"] = self.din(f"z_{tag}", [33, Lx])
                    W[i][f"negt_{tag}"] = self.din(f"negt_{tag}", [128, Lx // 128])
                    W[i][f"kspec_{tag}"] = self.nc.dram_tensor(f"kspec_{tag}", [2, Lx, D], F32).ap()
                W[i]["absd"] = self.din("absd", [128, D])
                W[i]["x0d"] = self.nc.dram_tensor("x0d", [D, T], BF16).ap()
                self.rX0D = [Res() for _ in range(KC)]
                self.rKS = {"l": Res(), "c": Res()}
                self.rKSW = {"l": [Res() for _ in range(8)], "c": [Res() for _ in range(8)]}
            if kind == 1:
                W[i].update(cf_w1=self.din(f"cf_w1_{i}", [D, 2 * D]), cf_w2=self.din(f"cf_w2_{i}", [D, D]))
        with contextlib.ExitStack() as st:
            self.st = st

            def sb(name, shape, dt):
                return st.enter_context(nc.sbuf_tensor(name, list(shape), dt))

            self.X = sb("X", [128, KC, T], F32)
            self.rX = [[Res() for _ in TILES] for _ in range(KC)]
            self.H = sb("H", [128, KC, T], BF16)
            self.rH = [[Res() for _ in TILES] for _ in range(KC)]
            self.SCR = sb("SCR", [128, ARENA], BF16)
            self.carve_ffn()
            self.VEC = sb("VEC", [128, NPV], F32)
            self.rVEC = Res()
            self.MOD = sb("MOD", [128, 48, 2], F32)
            self.rMOD = Res()
            self.AM = sb("AM", [128, 2, KC, 2], F32)
            self.rAM = Res()
            self.SV = sb("SV", [128, KC, 2], F32)
            self.SVB = sb("SVB", [128, KC, 2], BF16)
            self.rSV = Res()
            self.ONES = sb("ONES", [128, 128], BF16)
            self.rONES = Res()
            self.EPSB = sb("EPSB", [128, 1], F32)
            self.RSTD = sb("RSTD", [128, 512], F32)
            self.rRSTD = Res()
            self.RSTD2 = sb("RSTD2", [128, 512], F32)
            self.rRSTD2 = Res()
            self.MU = sb("MU", [128, 512], F32)
            self.rMU = Res()
            self.TMP = [sb(f"TMP{k}", [128, 512], F32) for k in range(2)]
            self.rTMP = [Res(), Res()]
            self.tmp_i = 0
            self.BD = sb("BD", [128, 128], BF16)
            self.PS = [st.enter_context(nc.psum_tensor(f"PS{k}", [128, 512], F32)) for k in range(8)]
            self.acc_i = 0
            self.rPS = [Res() for _ in range(8)]
            self.ps_i = 0
            self.wa_i = 0
            self.dq = 0

            S.op("pool", lambda e: e.memset(self.ONES[:, :], 1.0), writes=[self.rONES])
            S.op("pool", lambda e: e.memset(self.EPSB[:, :], EPS), writes=[self.rONES])
            self.zero_pads()
            self.IDENT = sb("IDENT", [128, 128], BF16)
            self.rID = Res()
            S.op("pool", lambda e: e.dma_start(out=self.IDENT[:, :], in_=ident_d[:, :]), writes=[self.rID], dma=True)
            S.op("pool", lambda e: e.dma_start(out=self.BD[:, :], in_=bd_d[:, :]), writes=[self.rID], dma=True)

            routs = []
            for b in range(nb):
                self.load_x(xT, cT, b)
                for i in self.layers:
                    self.layer(i, W[i], b, last=(i == DEPTH - 1))
                routs += self.store_x(yT, cOut, b)
            S.barrier(["sp"])
            S.emit(nc, st)
        return nc

    def psum(self):
        k = self.ps_i
        self.ps_i = (k + 1) % 6
        return self.PS[k], self.rPS[k]

    def wa_all(self, k):
        return self.rWA[k] + self.rWF[2 * k] + self.rWF[2 * k + 1]

    def psum_acc(self):
        k = 6 + self.acc_i
        self.acc_i = 1 - self.acc_i
        return self.PS[k], self.rPS[k]

    def arena_reset(self, mark=0):
        self.a_off = mark

    def alloc(self, n_el, dt=BF16, shape=None):
        nb16 = n_el * (2 if dt == F32 else 1)
        nb16 = (nb16 + 15) // 16 * 16
        a = self.a_off
        assert a + nb16 <= ARENA, (a, nb16)
        self.a_off = a + nb16
        ap = self.SCR[:, a:a + n_el * (2 if dt == F32 else 1)]
        if dt == F32:
            ap = ap.bitcast(F32)
        return ap

    def carve_ffn(self):
        self.arena_reset()
        self.G = self.alloc(6 * T).rearrange("p (c t) -> p c t", t=T)
        self.rG = [Res() for _ in range(6)]
        self.AROW = self.alloc(T)
        self.CROW = self.alloc(T, F32)
        self.VROW = self.alloc(T)
        self.rA, self.rC, self.rV = Res(), Res(), Res()
        self.WA = [self.alloc(KC * 512).rearrange("p (c n) -> p c n", n=512) for _ in range(2)]
        self.rWA = [[Res(), Res()], [Res(), Res()]]
        self.rWF = [[Res(), Res()] for _ in range(4)]
        self.WB = self.alloc(6 * 1024)
        self.rWB = Res()
        self.SQ = self.alloc(KC * 512).rearrange("p (c n) -> p c n", n=512)
        self.rSQ = Res()

    def zero_pads(self):
        S = self.S
        S.op("pool", lambda e: e.memset(self.AROW[:, :], 0.0), writes=[self.rA])
        S.op("pool", lambda e: e.memset(self.VROW[:, :], 0.0), writes=[self.rV])
        for (a, bnd) in ((0, 16), (C0 + LC, L0), (L0 + L, T)):
            S.op("pool", lambda e, a=a, bnd=bnd: e.memset(self.G[:, :, a:bnd], 0.0), writes=self.rG)

    def tmp(self):
        k = self.tmp_i
        self.tmp_i = 1 - k
        return self.TMP[k], self.rTMP[k]

    def dma_eng(self):
        self.dq += 1
        return "sp"

    def load_w(self, dst_ap, src_ap, res, after=()):
        self.S.op("pool", lambda e: e.dma_start(out=dst_ap, in_=src_ap), writes=(res if isinstance(res, list) else [res]), dma=True, after=after)

    def load_x(self, xT, cT, b):
        S = self.S
        for c in range(KC):
            S.op("sp", lambda e, c=c: e.dma_start(out=self.X[:, c, :], in_=xT[b, c * 128:(c + 1) * 128, :]),
                 writes=self.rX[c], dma=True)
        S.op("sp", lambda e: e.dma_start(out=self.SV[:, :, :], in_=cT[b].rearrange("p (c t) -> p c t", t=2)),
             writes=[self.rSV], dma=True)
        S.op("act", lambda e: e.activation(out=self.SVB[:, :, :], in_=self.SV[:, :, :], func=AF.Silu),
             reads=[], writes=[self.rSV])

    def store_x(self, yT, cOut, b):
        S = self.S
        rs = []
        for c in range(KC):
            r = Res()
            S.op("sp", lambda e, c=c: e.dma_start(out=yT[b, c * 128:(c + 1) * 128, :], in_=self.X[:, c, L0:L0 + L]),
                 reads=self.rX[c], writes=[r], dma=True)
            r2 = Res()
            S.op("sp", lambda e, c=c: e.dma_start(out=cOut[b, c * 128:(c + 1) * 128, :], in_=self.X[:, c, C0:C0 + LC]),
                 reads=self.rX[c], writes=[r2], dma=True)
            rs += [r, r2]
        return rs

    def adaln(self, i, Wd):
        S = self.S
        off, _ = vec_offsets(i)
        S.op("sp", lambda e: e.dma_start(out=self.VEC[:, :], in_=Wd["vec"][:, :]), writes=[self.rVEC], dma=True)
        ada_w = Wd["ada_w"].rearrange("(c p) n -> p c n", p=128)
        for blk in range(12):
            k = self.wa_i
            self.wa_i = 1 - k
            wa, rwa = self.WA[k], self.rWA[k]
            self.S.op("pool", lambda e, wa=wa, blk=blk: e.dma_start(out=wa[:, :, :], in_=ada_w[:, :, blk * 512:(blk + 1) * 512]), writes=self.wa_all(k), dma=True)
            ps, rps = self.psum()

            def mm(e, wa=wa, ps=ps):
                ins = None
                for f in range(4):
                    for kc in range(KC):
                        ins = e.matmul(ps[:, f * 2:f * 2 + 2], lhsT=wa[:, kc, f * 128:(f + 1) * 128],
                                       rhs=self.SVB[:, kc, :], start=(kc == 0), stop=(kc == KC - 1))
                return ins
            S.op("pe", mm, reads=rwa + [self.rSV], writes=[rps])
            ab = off["ada_b"] + blk * 4
            S.op("dve", lambda e, ps=ps, blk=blk, ab=ab: e.tensor_tensor(
                out=self.MOD[:, blk * 4:blk * 4 + 4, :], in0=ps[:, 0:8].rearrange("p (f t) -> p f t", t=2),
                in1=self.VEC[:, ab:ab + 4].unsqueeze(2).to_broadcast([128, 4, 2]), op=ALU.add),
                reads=[self.rVEC], writes=[rps, self.rMOD])
        for m, (nname, g) in enumerate((("norm_mix", 1), ("norm_ffn", 4))):
            no = off[nname]
            S.op("dve", lambda e, m=m, g=g: e.tensor_scalar_add(out=self.AM[:, m, :, :], in0=self.MOD[:, g * 8:(g + 1) * 8, :], scalar1=1.0),
                 reads=[self.rMOD], writes=[self.rAM])
            S.op("dve", lambda e, m=m, no=no: e.tensor_tensor(
                out=self.AM[:, m, :, :], in0=self.AM[:, m, :, :],
                in1=self.VEC[:, no:no + 8].unsqueeze(2).to_broadcast([128, 8, 2]), op=ALU.mult),
                reads=[self.rVEC, self.rAM], writes=[self.rAM])

    def modulate(self, m, tiles):
        S = self.S
        shg = 0 if m == 0 else 3
        RS = [self.RSTD, self.RSTD2]
        rRS = [self.rRSTD, self.rRSTD2]

        def stage1(k):
            ti = tiles[k]
            c0, w = TILES[ti]
            rs, rrs = RS[k % 2], rRS[k % 2]
            for c in range(KC):
                S.op("act", lambda e, c=c: e.activation(out=self.SQ[:, c, 0:w], in_=self.X[:, c, c0:c0 + w], func=AF.Square),
                     reads=[self.rX[c][ti]], writes=[self.rSQ])
            ps, rps = self.psum()

            def mm(e):
                ins = None
                for c in range(KC):
                    ins = e.matmul(ps[:, 0:w], lhsT=self.ONES[:, :], rhs=self.SQ[:, c, 0:w], start=(c == 0), stop=(c == KC - 1))
                return ins
            S.op("pe", mm, reads=[self.rSQ, self.rONES], writes=[rps])
            S.op("act", lambda e: e.activation(out=rs[:, 0:w], in_=ps[:, 0:w], func=AF.Ln, scale=1.0 / D, bias=self.EPSB[:, 0:1]),
                 reads=[self.rONES], writes=[rps, rrs])
            S.op("act", lambda e: e.activation(out=rs[:, 0:w], in_=rs[:, 0:w], func=AF.Exp, scale=-0.5), reads=[], writes=[rrs])

        def stage2(k):
            ti = tiles[k]
            c0, w = TILES[ti]
            col = 1 if ti == 0 else 0
            rs, rrs = RS[k % 2], rRS[k % 2]
            for c in range(KC):
                tp, rtp = self.tmp()
                S.op("dve", lambda e, tp=tp, c=c: e.tensor_tensor(out=tp[:, 0:w], in0=self.X[:, c, c0:c0 + w], in1=rs[:, 0:w], op=ALU.mult),
                     reads=[self.rX[c][ti], rrs], writes=[rtp])
                S.op("pool", lambda e, tp=tp, c=c: e.tensor_scalar(
                    out=self.H[:, c, c0:c0 + w], in0=tp[:, 0:w], scalar1=self.AM[:, m, c, col:col + 1],
                    scalar2=self.MOD[:, shg * 8 + c, col:col + 1], op0=ALU.mult, op1=ALU.add),
                    reads=[rtp, self.rAM, self.rMOD], writes=[self.rH[c][ti]])
        stage1(0)
        for k in range(len(tiles)):
            if k + 1 < len(tiles):
                stage1(k + 1)
            stage2(k)

    def ffn(self, i, Wd, tiles):
        S = self.S
        off, _ = vec_offsets(i)
        up = Wd["ffn_up"].rearrange("(c p) n -> p c n", p=128)
        down = Wd["ffn_down"]
        bo, wo = off["ffn_b_dw"], off["ffn_w_dw"]
        lo = TILES[tiles[0]][0]
        hi = TILES[tiles[-1]][0] + TILES[tiles[-1]][1]
        for (g0, gn) in ((0, 6), (6, 6), (12, 5), (17, 5)):
            wbv = self.WB[:, 0:gn * 1024].rearrange("p (c n) -> p c n", n=1024)
            self.load_w(wbv, down[g0 * 128:(g0 + gn) * 128, :].rearrange("(c p) n -> p c n", p=128), self.rWB)
            for hc in range(g0, g0 + gn):
                slot = hc % 4
                wa = self.WA[slot // 2][:, :, (slot % 2) * 256:(slot % 2) * 256 + 256]
                rwa = self.rWF[slot]
                self.load_w(wa[:, :, 0:128], up[:, :, hc * 128:(hc + 1) * 128], rwa[0], after=self.rWA[slot // 2])
                self.load_w(wa[:, :, 128:256], up[:, :, D_FF + hc * 128:D_FF + (hc + 1) * 128], rwa[1], after=self.rWA[slot // 2])
                for ti in tiles:
                    c0, w = TILES[ti]
                    for half, (dst, rdst) in enumerate(((self.AROW, self.rA), (self.VROW, self.rV))):
                        ps, rps = self.psum()

                        def mm(e, ps=ps, wa=wa, half=half, c0=c0, w=w):
                            ins = None
                            for kc in range(KC):
                                ins = e.matmul(ps[:, 0:w], lhsT=wa[:, kc, half * 128:(half + 1) * 128],
                                               rhs=self.H[:, kc, c0:c0 + w], start=(kc == 0), stop=(kc == KC - 1))
                            return ins
                        S.op("pe", mm, reads=[rwa[half]] + [self.rH[kc][ti] for kc in range(KC)], writes=[rps])
                        S.op("act", lambda e, ps=ps, dst=dst, c0=c0, w=w: e.copy(out=dst[:, c0:c0 + w], in_=ps[:, 0:w]),
                             reads=[], writes=[rps, rdst])
                S.op("dve", lambda e, hc=hc: e.tensor_scalar(
                    out=self.CROW[:, lo:hi], in0=self.AROW[:, lo:hi], scalar1=self.VEC[:, wo + 22 + hc:wo + 22 + hc + 1],
                    scalar2=self.VEC[:, bo + hc:bo + hc + 1], op0=ALU.mult, op1=ALU.add),
                    reads=[self.rA, self.rVEC], writes=[self.rC])
                S.op("dve", lambda e, hc=hc: e.scalar_tensor_tensor(
                    out=self.CROW[:, lo:hi], in0=self.AROW[:, lo - 1:hi - 1], scalar=self.VEC[:, wo + hc:wo + hc + 1],
                    in1=self.CROW[:, lo:hi], op0=ALU.mult, op1=ALU.add),
                    reads=[self.rA, self.rVEC], writes=[self.rC])
                S.op("dve", lambda e, hc=hc: e.scalar_tensor_tensor(
                    out=self.CROW[:, lo:hi], in0=self.AROW[:, lo + 1:hi + 1], scalar=self.VEC[:, wo + 44 + hc:wo + 44 + hc + 1],
                    in1=self.CROW[:, lo:hi], op0=ALU.mult, op1=ALU.add),
                    reads=[self.rA, self.rVEC], writes=[self.rC])
                S.op("act", lambda e: e.activation(out=self.CROW[:, lo:hi], in_=self.CROW[:, lo:hi], func=AF.Silu),
                     reads=[], writes=[self.rC])
                gi = hc - g0
                S.op("dve", lambda e, gi=gi: e.tensor_tensor(out=self.G[:, gi, lo:hi], in0=self.CROW[:, lo:hi], in1=self.VROW[:, lo:hi], op=ALU.mult),
                     reads=[self.rC, self.rV], writes=[self.rG[gi]])
            for dc in range(KC):
                for ti in tiles:
                    c0, w = TILES[ti]
                    col = 1 if ti == 0 else 0
                    ps, rps = self.psum()

                    def mm(e, ps=ps, wbv=wbv, dc=dc, c0=c0, w=w, gn=gn):
                        ins = None
                        for gi in range(gn):
                            ins = e.matmul(ps[:, 0:w], lhsT=wbv[:, gi, dc * 128:(dc + 1) * 128], rhs=self.G[:, gi, c0:c0 + w],
                                           start=(gi == 0), stop=(gi == gn - 1))
                        return ins
                    S.op("pe", mm, reads=[self.rWB] + [self.rG[gi] for gi in range(gn)], writes=[rps])
                    S.op("dve", lambda e, ps=ps, dc=dc, c0=c0, w=w, col=col: e.scalar_tensor_tensor(
                        out=self.X[:, dc, c0:c0 + w], in0=ps[:, 0:w], scalar=self.MOD[:, 40 + dc, col:col + 1],
                        in1=self.X[:, dc, c0:c0 + w], op0=ALU.mult, op1=ALU.add),
                        reads=[self.rMOD], writes=[rps, self.rX[dc][ti]])

    def mla(self, i, Wd, need_ctx):
        S = self.S
        off, _ = vec_offsets(i)
        V = lambda name, k=0: self.VEC[:, off[name] + k:off[name] + k + 1]
        V96 = lambda name: self.VEC[0:96, off[name]:off[name] + 1]
        all_tiles = [0, 1, 2, 3, 4]
        qtiles = all_tiles if need_ctx else [1, 2, 3, 4]
        att_scale = float(96 ** -0.5)
        self.arena_reset()
        CQ = self.alloc(2 * T).rearrange("p (c t) -> p c t", t=T)
        CKV = self.alloc(T)
        KPE = self.alloc(T)
        ROPE = self.alloc(2 * T, F32).rearrange("p (c t) -> p c t", t=T)
        rCQ = [Res() for _ in TILES]
        rCKV = [Res() for _ in TILES]
        rKPE = [Res() for _ in TILES]
        rROPE = Res()
        SQh = self.alloc(512)
        rSQh = Res()
        mark = self.a_off
        S.op("sp", lambda e: e.dma_start(out=ROPE[:, :, :], in_=Wd["rope"][:, :, :]), writes=[rROPE], dma=True)
        WDQ = self.alloc(KC * 256).rearrange("p (c n) -> p c n", n=256)
        WKV = self.alloc(KC * 320).rearrange("p (c n) -> p c n", n=320)
        CQF = self.alloc(2 * 512, F32).rearrange("p (c n) -> p c n", n=512)
        SQb = self.alloc(2 * 512).rearrange("p (c n) -> p c n", n=512)
        rWDQ, rWKV, rCQF, rSQb = Res(), Res(), Res(), Res()
        self.load_w(WDQ, Wd["dq"].rearrange("(c p) n -> p c n", p=128), rWDQ)
        self.load_w(WKV, Wd["kv"].rearrange("(c p) n -> p c n", p=128), rWKV)

        def rope_norm(ps_raw, rps_raw, ps_perm, rps_perm, g, gp, out_ap, rout, lo, hi, c0, w):
            S.op("act", lambda e: e.activation(out=SQh[0:96, 0:w], in_=ps_raw[0:96, 0:w], func=AF.Square),
                 reads=[], writes=[rps_raw, rSQh])
            pss, rpss = self.psum()
            S.op("pe", lambda e: e.matmul(pss[0:96, 0:w], lhsT=self.BD[0:96, 0:96], rhs=SQh[0:96, 0:w], start=True, stop=True),
                 reads=[rSQh, self.rID], writes=[rpss])
            S.op("act", lambda e: e.activation(out=self.RSTD[0:96, 0:w], in_=pss[0:96, 0:w], func=AF.Ln,
                                               scale=V96("mla_invn"), bias=self.EPSB[0:96, 0:1]),
                 reads=[self.rVEC, self.rONES], writes=[rpss, self.rRSTD])
            S.op("act", lambda e: e.activation(out=self.RSTD[0:96, 0:w], in_=self.RSTD[0:96, 0:w], func=AF.Exp, scale=-0.5), reads=[], writes=[self.rRSTD])
            t1, rt1 = self.TMP[0], self.rTMP[0]
            t2, rt2 = self.TMP[1], self.rTMP[1]
            S.op("dve", lambda e: e.scalar_tensor_tensor(out=t1[0:96, 0:w], in0=ps_raw[0:96, 0:w], scalar=g, in1=ROPE[0:96, 0, c0:c0 + w],
                                                         op0=ALU.mult, op1=ALU.mult),
                 reads=[self.rVEC, rROPE], writes=[rps_raw, rt1])
            S.op("dve", lambda e: e.scalar_tensor_tensor(out=t2[0:96, 0:w], in0=ps_perm[0:96, 0:w], scalar=gp, in1=ROPE[0:96, 1, c0:c0 + w],
                                                         op0=ALU.mult, op1=ALU.mult),
                 reads=[self.rVEC, rROPE], writes=[rps_perm, rt2])
            S.op("pool", lambda e: e.tensor_tensor(out=t1[0:96, 0:w], in0=t1[0:96, 0:w], in1=t2[0:96, 0:w], op=ALU.add),
                 reads=[rt2], writes=[rt1])
            S.op("dve", lambda e: e.tensor_tensor(out=out_ap, in0=t1[lo:hi, 0:w], in1=self.RSTD[lo:hi, 0:w], op=ALU.mult),
                 reads=[rt1, self.rRSTD], writes=[rout])

        def p1_tile(ti):
            c0, w = TILES[ti]
            rh = [self.rH[kc][ti] for kc in range(KC)]
            for oc in range(2):
                ps, rps = self.psum()

                def mm(e, ps=ps, oc=oc):
                    ins = None
                    for kc in range(KC):
                        ins = e.matmul(ps[:, 0:w], lhsT=WDQ[:, kc, oc * 128:(oc + 1) * 128], rhs=self.H[:, kc, c0:c0 + w],
                                       start=(kc == 0), stop=(kc == KC - 1))
                    return ins
                S.op("pe", mm, reads=[rWDQ] + rh, writes=[rps])
                S.op("act", lambda e, ps=ps, oc=oc: e.copy(out=CQF[:, oc, 0:w], in_=ps[:, 0:w]), reads=[], writes=[rps, rCQF])
            S.op("act", lambda e: e.activation(out=SQb[:, :, 0:w], in_=CQF[:, :, 0:w], func=AF.Square), reads=[rCQF], writes=[rSQb])
            ps, rps = self.psum()

            def mm(e, ps=ps):
                e.matmul(ps[:, 0:w], lhsT=self.ONES[:, :], rhs=SQb[:, 0, 0:w], start=True, stop=False)
                return e.matmul(ps[:, 0:w], lhsT=self.ONES[:, :], rhs=SQb[:, 1, 0:w], start=False, stop=True)
            S.op("pe", mm, reads=[rSQb, self.rONES], writes=[rps])
            S.op("act", lambda e, ps=ps: e.activation(out=self.RSTD[:, 0:w], in_=ps[:, 0:w], func=AF.Ln, scale=1.0 / 256, bias=self.EPSB[:, 0:1]),
                 reads=[self.rONES], writes=[rps, self.rRSTD])
            S.op("act", lambda e: e.activation(out=self.RSTD[:, 0:w], in_=self.RSTD[:, 0:w], func=AF.Exp, scale=-0.5), reads=[], writes=[self.rRSTD])
            for oc in range(2):
                S.op("dve", lambda e, oc=oc: e.scalar_tensor_tensor(out=CQ[:, oc, c0:c0 + w], in0=CQF[:, oc, 0:w], scalar=V("mla_qn", oc),
                                                                 in1=self.RSTD[:, 0:w], op0=ALU.mult, op1=ALU.mult),
                     reads=[rCQF, self.rVEC, self.rRSTD], writes=[rCQ[ti]])
            ps, rps = self.psum()

            def mm(e, ps=ps):
                ins = None
                for kc in range(KC):
                    ins = e.matmul(ps[:, 0:w], lhsT=WKV[:, kc, 0:128], rhs=self.H[:, kc, c0:c0 + w], start=(kc == 0), stop=(kc == KC - 1))
                return ins
            S.op("pe", mm, reads=[rWKV] + rh, writes=[rps])
            S.op("act", lambda e, ps=ps: e.copy(out=CQF[:, 0, 0:w], in_=ps[:, 0:w]), reads=[], writes=[rps, rCQF])
            S.op("act", lambda e: e.activation(out=SQb[:, 0, 0:w], in_=CQF[:, 0, 0:w], func=AF.Square), reads=[rCQF], writes=[rSQb])
            ps, rps = self.psum()
            S.op("pe", lambda e, ps=ps: e.matmul(ps[:, 0:w], lhsT=self.ONES[:, :], rhs=SQb[:, 0, 0:w], start=True, stop=True),
                 reads=[rSQb, self.rONES], writes=[rps])
            S.op("act", lambda e, ps=ps: e.activation(out=self.RSTD[:, 0:w], in_=ps[:, 0:w], func=AF.Ln, scale=1.0 / 128, bias=self.EPSB[:, 0:1]),
                 reads=[self.rONES], writes=[rps, self.rRSTD])
            S.op("act", lambda e: e.activation(out=self.RSTD[:, 0:w], in_=self.RSTD[:, 0:w], func=AF.Exp, scale=-0.5), reads=[], writes=[self.rRSTD])
            S.op("dve", lambda e: e.scalar_tensor_tensor(out=CKV[:, c0:c0 + w], in0=CQF[:, 0, 0:w], scalar=V("mla_kvn"),
                                                         in1=self.RSTD[:, 0:w], op0=ALU.mult, op1=ALU.mult),
                 reads=[rCQF, self.rVEC, self.rRSTD], writes=[rCKV[ti]])
            pss = []
            for var in range(2):
                ps, rps = self.psum()
                pss.append((ps, rps))

                def mm(e, ps=ps, var=var):
                    ins = None
                    for kc in range(KC):
                        ins = e.matmul(ps[0:96, 0:w], lhsT=WKV[:, kc, 128 + var * 96:224 + var * 96], rhs=self.H[:, kc, c0:c0 + w],
                                       start=(kc == 0), stop=(kc == KC - 1))
                    return ins
                S.op("pe", mm, reads=[rWKV] + rh, writes=[rps])
            rope_norm(pss[0][0], pss[0][1], pss[1][0], pss[1][1], V96("mla_gk"), V96("mla_gkp"),
                      KPE[64:96, c0:c0 + w], rKPE[ti], 64, 96, c0, w)

        for ti in all_tiles:
            p1_tile(ti)
        if DBG_STOP == 1:
            S.barrier()
            return

        S.barrier()
        self.arena_reset(mark)
        WUKV = self.alloc(2048)
        rWUKV = Res()
        self.load_w(WUKV, Wd["ukv"][:, :], rWUKV)
        uq_v = Wd["uq"].rearrange("(c p) n -> p c n", p=128)
        WUQh = [self.alloc(2 * 192).rearrange("p (c n) -> p c n", n=192) for _ in range(2)]
        rWUQh = [Res(), Res()]
        QH = [self.alloc(T) for _ in range(2)]
        KH = [self.alloc(T) for _ in range(2)]
        rQH = [[Res() for _ in TILES] for _ in range(2)]
        rKH = [[Res() for _ in TILES] for _ in range(2)]
        rKHpe = [Res(), Res()]
        VH = [self.alloc(18 * 128).rearrange("p (j n) -> p j n", n=128) for _ in range(2)]
        rVH = [Res(), Res()]
        NPT = 6
        PT = [self.alloc(512) for _ in range(NPT)]
        rPT = [Res() for _ in range(NPT)]
        RD = self.MU
        rRD = self.rMU
        pt_i = 0
        S.op("pool", lambda e: e.memset(VH[0][:, :, 64:128], 1.0), writes=[rVH[0]])
        S.op("pool", lambda e: e.memset(VH[1][:, :, 0:64], 1.0), writes=[rVH[1]])

        def kcols(j):
            return (C0 + 128 * j) if j < 2 else (L0 + 128 * (j - 2))

        def ktile(j):
            return 0 if j < 2 else 1 + (j - 2) // 4

        def prep(h):
            s = h % 2
            voff = 0 if s == 0 else 64
            wq, rwq = WUQh[s], rWUQh[s]
            self.load_w(wq, uq_v[:, :, h * 192:(h + 1) * 192], rwq)
            for ti in qtiles:
                c0, w = TILES[ti]
                pss = []
                for var in range(2):
                    ps, rps = self.psum()
                    pss.append((ps, rps))

                    def mm(e, ps=ps, var=var, c0=c0, w=w):
                        e.matmul(ps[0:96, 0:w], lhsT=wq[:, 0, var * 96:var * 96 + 96], rhs=CQ[:, 0, c0:c0 + w], start=True, stop=False)
                        return e.matmul(ps[0:96, 0:w], lhsT=wq[:, 1, var * 96:var * 96 + 96], rhs=CQ[:, 1, c0:c0 + w], start=False, stop=True)
                    S.op("pe", mm, reads=[rwq, rCQ[ti]], writes=[rps])
                rope_norm(pss[0][0], pss[0][1], pss[1][0], pss[1][1], V96("mla_gq"), V96("mla_gqp"),
                          QH[s][0:96, c0:c0 + w], rQH[s][ti], 0, 96, c0, w)
            S.op("pool", lambda e: e.tensor_copy(out=KH[s][64:96, :], in_=KPE[64:96, :]), reads=rKPE, writes=[rKHpe[s]])
            for ti in all_tiles:
                c0, w = TILES[ti]
                ps, rps = self.psum()
                S.op("pe", lambda e, ps=ps, c0=c0, w=w: e.matmul(ps[0:64, 0:w], lhsT=WUKV[:, h * 128:h * 128 + 64], rhs=CKV[:, c0:c0 + w], start=True, stop=True),
                     reads=[rWUKV, rCKV[ti]], writes=[rps])
                S.op("act", lambda e, ps=ps, w=w: e.activation(out=SQh[0:64, 0:w], in_=ps[0:64, 0:w], func=AF.Square), reads=[], writes=[rps, rSQh])
                ps2, rps2 = self.psum()
                S.op("pe", lambda e, ps2=ps2, w=w: e.matmul(ps2[0:64, 0:w], lhsT=self.BD[0:64, 0:64], rhs=SQh[0:64, 0:w], start=True, stop=True),
                     reads=[rSQh, self.rID], writes=[rps2])
                S.op("act", lambda e, ps2=ps2, w=w: e.activation(out=self.RSTD[0:64, 0:w], in_=ps2[0:64, 0:w], func=AF.Ln, scale=1.0 / 64, bias=self.EPSB[0:64, 0:1]),
                     reads=[self.rONES], writes=[rps2, self.rRSTD])
                S.op("act", lambda e, w=w: e.activation(out=self.RSTD[0:64, 0:w], in_=self.RSTD[0:64, 0:w], func=AF.Exp, scale=-0.5), reads=[], writes=[self.rRSTD])
                S.op("dve", lambda e, ps=ps, c0=c0, w=w: e.scalar_tensor_tensor(out=KH[s][0:64, c0:c0 + w], in0=ps[0:64, 0:w], scalar=self.VEC[0:64, off["mla_gk"]:off["mla_gk"] + 1],
                                                                             in1=self.RSTD[0:64, 0:w], op0=ALU.mult, op1=ALU.mult),
                     reads=[self.rVEC, self.rRSTD], writes=[rps, rKH[s][ti]])
            for j0 in range(0, 18, 8):
                n = min(8, 18 - j0)
                ps, rps = self.psum()

                def mm(e, ps=ps, j0=j0, n=n):
                    ins = None
                    for jj in range(n):
                        kc0 = kcols(j0 + jj)
                        ins = e.matmul(ps[:, jj * 64:(jj + 1) * 64], lhsT=CKV[:, kc0:kc0 + 128], rhs=WUKV[:, h * 128 + 64:(h + 1) * 128], start=True, stop=True)
                    return ins
                S.op("pe", mm, reads=[rWUKV] + rCKV, writes=[rps])
                S.op("dve", lambda e, ps=ps, j0=j0, n=n: e.tensor_copy(out=VH[s][:, j0:j0 + n, voff:voff + 64], in_=ps[:, 0:n * 64].rearrange("p (j d) -> p j d", d=64)),
                     reads=[], writes=[rps, rVH[s]])

        def attend(h):
            nonlocal pt_i
            s = h % 2
            ch = h // 2
            items = []
            for ti in qtiles:
                keys = list(range(18)) if ti > 0 else [0, 1]
                for j in keys:
                    items.append((ti, j, j == keys[0], j == keys[-1]))
            LA = 2
            qk = {}

            def issue_qk(idx):
                ti, j, _, _ = items[idx]
                c0, w = TILES[ti]
                kc0 = kcols(j)
                ps, rps = self.psum()
                S.op("pe", lambda e, ps=ps: e.matmul(ps[:, 0:w], lhsT=KH[s][0:96, kc0:kc0 + 128], rhs=QH[s][0:96, c0:c0 + w], start=True, stop=True),
                     reads=[rKH[s][ktile(j)], rKHpe[s], rQH[s][ti]], writes=[rps])
                qk[idx] = (ps, rps)
            for idx in range(min(LA, len(items))):
                issue_qk(idx)
            pso, rpso = None, None
            for idx, (ti, j, first, lastk) in enumerate(items):
                c0, w = TILES[ti]
                if idx + LA < len(items):
                    issue_qk(idx + LA)
                if first:
                    pso, rpso = self.psum_acc()
                ps, rps = qk.pop(idx)
                pt, rpt = PT[pt_i], rPT[pt_i]
                pt_i = (pt_i + 1) % NPT
                S.op("act", lambda e, ps=ps, pt=pt, w=w: e.activation(out=pt[:, 0:w], in_=ps[:, 0:w], func=AF.Exp, scale=att_scale),
                     reads=[], writes=[rps, rpt])
                S.op("pe", lambda e, pt=pt, j=j, w=w, pso=pso, first=first, lastk=lastk: e.matmul(
                    pso[:, 0:w], lhsT=VH[s][:, j, :], rhs=pt[:, 0:w], start=first, stop=lastk),
                    reads=[rpt, rVH[s]], writes=[rpso])
                if lastk:
                    if s == 0:
                        S.op("dve", lambda e, pso=pso, w=w: e.reciprocal(out=RD[64:128, 0:w], in_=pso[64:128, 0:w]), reads=[], writes=[rpso, rRD])
                        S.op("dve", lambda e, pso=pso, c0=c0, w=w: e.tensor_tensor(out=self.H[0:64, ch, c0:c0 + w], in0=pso[0:64, 0:w], in1=RD[64:128, 0:w], op=ALU.mult),
                             reads=[rRD], writes=[rpso, self.rH[ch][ti]])
                    else:
                        S.op("dve", lambda e, pso=pso, w=w: e.reciprocal(out=RD[0:64, 0:w], in_=pso[0:64, 0:w]), reads=[], writes=[rpso, rRD])
                        S.op("dve", lambda e, pso=pso, c0=c0, w=w: e.tensor_tensor(out=self.H[64:128, ch, c0:c0 + w], in0=pso[64:128, 0:w], in1=RD[0:64, 0:w], op=ALU.mult),
                             reads=[rRD], writes=[rpso, self.rH[ch][ti]])

        NH = 16 if DBG_HEADS is None else DBG_HEADS
        prep(0)
        for h in range(NH):
            if h + 1 < NH:
                prep(h + 1)
            attend(h)
        if DBG_STOP == 2:
            S.barrier()
            return

        S.barrier()
        self.arena_reset()
        WO = self.alloc(KC * 1024).rearrange("p (c n) -> p c n", n=1024)
        rWO = Res()
        wo_v = Wd["wo"].rearrange("(c p) n -> p c n", p=128)
        rWOh = [Res(), Res()]
        for hh in range(2):
            self.load_w(WO[:, hh * 4:(hh + 1) * 4, :], wo_v[:, hh * 4:(hh + 1) * 4, :], rWOh[hh])
        for ti in qtiles:
            c0, w = TILES[ti]
            col = 1 if ti == 0 else 0
            for dc in range(KC):
                ps, rps = self.psum()

                def mm(e, ps=ps, dc=dc, c0=c0, w=w):
                    ins = None
                    for c in range(KC):
                        ins = e.matmul(ps[:, 0:w], lhsT=WO[:, c, dc * 128:(dc + 1) * 128], rhs=self.H[:, c, c0:c0 + w], start=(c == 0), stop=(c == KC - 1))
                    return ins
                S.op("pe", mm, reads=rWOh + [self.rH[c][ti] for c in range(KC)], writes=[rps])
                S.op("dve", lambda e, ps=ps, dc=dc, c0=c0, w=w, col=col: e.scalar_tensor_tensor(
                    out=self.X[:, dc, c0:c0 + w], in0=ps[:, 0:w], scalar=self.MOD[:, 16 + dc, col:col + 1],
                    in1=self.X[:, dc, c0:c0 + w], op0=ALU.mult, op1=ALU.add),
                    reads=[self.rMOD], writes=[rps, self.rX[dc][ti]])

    def hyena(self, i, Wd, b):
        S = self.S
        off, _ = vec_offsets(i)
        V = lambda name, k=0: self.VEC[:, off[name] + k:off[name] + k + 1]
        V64 = lambda name: self.VEC[0:64, off[name]:off[name] + 1]
        tiles = [0, 1, 2, 3, 4]
        lo, hi = C0, L0 + L
        PI = float(np.pi)
        seqs = [("l", L, 16, L0, 0), ("c", LC, 2, C0, 16)]
        self.arena_reset()
        VX = self.alloc(KC * T).rearrange("p (c t) -> p c t", t=T)
        rVX = [Res() for _ in range(KC)]
        AROW = self.alloc(T)
        CROW = self.alloc(T, F32)
        X1ROW = self.alloc(T, F32)
        X0ROW = self.alloc(T)
        rA, rC, rX1, rX0R = Res(), Res(), Res(), Res()
        WA = [self.alloc(KC * 128).rearrange("p (c n) -> p c n", n=128) for _ in range(2)]
        rWA = [Res(), Res()]
        wa_i = [0]
        S.op("pool", lambda e: e.memset(AROW[:, :], 0.0), writes=[rA])
        S.op("pool", lambda e: e.memset(X0ROW[:, :], 0.0), writes=[rX0R])
        w_in = Wd["hy_in"].rearrange("(c p) n -> p c n", p=128)
        bi, wsh, bsh = off["hy_b_in"], off["hy_w_short"], off["hy_b_short"]

        def proj_conv(oc, dst, rdst):
            k = wa_i[0]
            wa_i[0] = 1 - k
            wa, rwa = WA[k], rWA[k]
            self.load_w(wa, w_in[:, :, oc * 128:(oc + 1) * 128], rwa)
            for ti in tiles:
                c0, w = TILES[ti]
                ps, rps = self.psum()

                def mm(e, ps=ps, c0=c0, w=w):
                    ins = None
                    for kc in range(KC):
                        ins = e.matmul(ps[:, 0:w], lhsT=wa[:, kc, :], rhs=self.H[:, kc, c0:c0 + w], start=(kc == 0), stop=(kc == KC - 1))
                    return ins
                S.op("pe", mm, reads=[rwa] + [self.rH[kc][ti] for kc in range(KC)], writes=[rps])
                S.op("act", lambda e, ps=ps, c0=c0, w=w: e.activation(out=AROW[:, c0:c0 + w], in_=ps[:, 0:w], func=AF.Identity,
                                                                   bias=self.VEC[:, bi + oc:bi + oc + 1], scale=1.0),
                     reads=[self.rVEC], writes=[rps, rA])
            S.op("dve", lambda e: e.tensor_scalar(out=CROW[:, lo:hi], in0=AROW[:, lo:hi], scalar1=self.VEC[:, wsh + 24 + oc:wsh + 25 + oc],
                                                  scalar2=self.VEC[:, bsh + oc:bsh + oc + 1], op0=ALU.mult, op1=ALU.add),
                 reads=[rA, self.rVEC], writes=[rC])
            S.op("dve", lambda e: e.scalar_tensor_tensor(out=CROW[:, lo:hi], in0=AROW[:, lo - 1:hi - 1], scalar=self.VEC[:, wsh + oc:wsh + oc + 1],
                                                         in1=CROW[:, lo:hi], op0=ALU.mult, op1=ALU.add),
                 reads=[rA, self.rVEC], writes=[rC])
            S.op("dve", lambda e: e.scalar_tensor_tensor(out=dst[:, lo:hi], in0=AROW[:, lo + 1:hi + 1], scalar=self.VEC[:, wsh + 48 + oc:wsh + 49 + oc],
                                                         in1=CROW[:, lo:hi], op0=ALU.mult, op1=ALU.add),
                 reads=[rA, self.rVEC, rC], writes=[rdst])

        x0d = Wd["x0d"]
        for c in range(KC):
            proj_conv(c, X0ROW, rX0R)
            S.op("sp", lambda e, c=c: e.dma_start(out=x0d[c * 128:(c + 1) * 128, :], in_=X0ROW[:, :]),
                 reads=[rX0R], writes=[self.rX0D[c]], dma=True)
            proj_conv(KC + c, X1ROW, rX1)
            proj_conv(2 * KC + c, CROW, rC)
            S.op("dve", lambda e, c=c: e.tensor_tensor(out=VX[:, c, lo:hi], in0=CROW[:, lo:hi], in1=X1ROW[:, lo:hi], op=ALU.mult),
                 reads=[rC, rX1], writes=[rVX[c]])
        VT = self.H[:, :, :].rearrange("p c t -> p (c t)")[:, 0:18 * D].rearrange("p (j n) -> p j n", n=D)
        rVT = Res()
        allH = [r for row in self.rH for r in row]
        for (tag, Lx, nt, col0, vt0) in seqs:
            for jt in range(nt):
                ps, rps = self.psum()
                psb = ps[:, :].bitcast(BF16)

                def tr(e, psb=psb, jt=jt, col0=col0):
                    ins = None
                    for c in range(KC):
                        ins = e.transpose(out=psb[:, c * 128:(c + 1) * 128], in_=VX[:, c, col0 + jt * 128:col0 + (jt + 1) * 128], identity=self.IDENT[:, :])
                    return ins
                S.op("pe", tr, reads=rVX + [self.rID], writes=[rps])
                S.op("act", lambda e, psb=psb, jt=jt, vt0=vt0: e.copy(out=VT[:, vt0 + jt, :], in_=psb[:, 0:D]),
                     reads=[], writes=[rps, rVT] + (allH if (jt == 0 and tag == "l") else []))
        S.barrier()
        if b == 0:
            for (tag, Lx, nt, col0, vt0) in seqs:
                self.hy_filter(i, Wd, tag, Lx, nt)
                S.barrier()
        w_out = Wd["hy_out"]
        for q in range(4):
            self.arena_reset()
            X0q = self.alloc(2 * T).rearrange("p (c t) -> p c t", t=T)
            rX0q = Res()
            WOq = self.alloc(2 * D).rearrange("p (c n) -> p c n", n=D)
            rWOq = Res()
            S.op("sp", lambda e, q=q: e.dma_start(out=X0q[:, :, :], in_=x0d[q * 256:(q + 1) * 256, :].rearrange("(c p) t -> p c t", p=128)),
                 reads=self.rX0D, writes=[rX0q], dma=True)
            self.load_w(WOq, w_out[q * 256:(q + 1) * 256, :].rearrange("(c p) n -> p c n", p=128), rWOq)
            Z = self.alloc(2 * 256).rearrange("p (c n) -> p c n", n=256)
            rZ = Res()
            KSb = [self.alloc(2 * 256, F32).rearrange("p (c n) -> p c n", n=256) for _ in range(2)]
            rKSb = [Res(), Res()]
            SC = [self.MU[:, 0:256], self.RSTD2[:, 0:256]]
            rSC = [self.rMU, self.rRSTD2]
            mark = self.a_off
            for (tag, Lx, nt, col0, vt0) in seqs:
                S.barrier()
                self.arena_reset(mark)
                self.hy_conv(i, Wd, q, tag, Lx, nt, col0, vt0, VT, rVT, X0q, rX0q, WOq, rWOq, Z, rZ, KSb, rKSb, SC, rSC)
            S.barrier()

    def hy_filter(self, i, Wd, tag, Lx, nt):
        S = self.S
        off, _ = vec_offsets(i)
        V64 = lambda name: self.VEC[0:64, off[name]:off[name] + 1]
        PI = float(np.pi)
        self.arena_reset()
        ZT = self.alloc(Lx, F32)
        HD1 = self.alloc(Lx, F32)
        HD2 = self.alloc(Lx, F32)
        FW1 = self.alloc(64, F32)
        FW2 = self.alloc(64, F32)
        NEGT = self.alloc(16, F32)
        CST = self.alloc(8, F32)
        ONESF = self.alloc(128, F32)
        rZT, rHD1, rHD2, rFW, rNEGT, rCST, rONESF = (Res() for _ in range(7))
        S.op("sp", lambda e: e.dma_start(out=ZT[0:33, :], in_=Wd[f"z_{tag}"][:, :]), writes=[rZT], dma=True)
        S.op("sp", lambda e: e.dma_start(out=FW1[0:33, :], in_=Wd["fw1"][:, :]), writes=[rFW], dma=True)
        S.op("sp", lambda e: e.dma_start(out=FW2[0:64, :], in_=Wd["fw2"][:, :]), writes=[rFW], dma=True)
        S.op("sp", lambda e: e.dma_start(out=NEGT[:, 0:nt], in_=Wd[f"negt_{tag}"][:, :]), writes=[rNEGT], dma=True)
        S.op("pool", lambda e: e.memset(ONESF[:, :], 1.0), writes=[rONESF])
        S.op("pool", lambda e: e.memset(CST[:, 2:3], -PI), writes=[rCST])
        S.op("dve", lambda e: e.tensor_tensor(out=CST[0:64, 0:1], in0=V64("hy_fb1"), in1=V64("hy_fr"), op=ALU.mult), reads=[self.rVEC], writes=[rCST])
        S.op("dve", lambda e: e.tensor_tensor(out=CST[0:64, 1:2], in0=V64("hy_fb2"), in1=V64("hy_fr"), op=ALU.mult), reads=[self.rVEC], writes=[rCST])

        RR, rRR = self.MU, self.rMU

        def sin_layer(src, rsrc, kdim, wmat, cstcol, dst, rdst):
            for t0 in range(0, Lx, 512):
                w = min(512, Lx - t0)
                ps, rps = self.psum()
                S.op("pe", lambda e, ps=ps, t0=t0, w=w: e.matmul(ps[0:64, 0:w], lhsT=wmat[0:kdim, 0:64], rhs=src[0:kdim, t0:t0 + w], start=True, stop=True),
                     reads=[rFW, rsrc], writes=[rps])
                S.op("act", lambda e, ps=ps, t0=t0, w=w: e.activation(out=dst[0:64, t0:t0 + w], in_=ps[0:64, 0:w], func=AF.Identity,
                                                                   scale=V64("hy_fr"), bias=CST[0:64, cstcol:cstcol + 1]),
                     reads=[self.rVEC, rCST], writes=[rps, rdst])
                MAGIC = 12582912.0
                S.op("dve", lambda e, t0=t0, w=w: e.tensor_scalar(out=RR[0:64, 0:w], in0=dst[0:64, t0:t0 + w], scalar1=1.0 / (2.0 * PI), scalar2=MAGIC,
                                                                op0=ALU.mult, op1=ALU.add),
                     reads=[rdst], writes=[rRR])
                S.op("dve", lambda e, w=w: e.tensor_scalar(out=RR[0:64, 0:w], in0=RR[0:64, 0:w], scalar1=MAGIC, scalar2=-2.0 * PI,
                                                         op0=ALU.subtract, op1=ALU.mult),
                     reads=[], writes=[rRR])
                S.op("dve", lambda e, t0=t0, w=w: e.tensor_tensor(out=dst[0:64, t0:t0 + w], in0=dst[0:64, t0:t0 + w], in1=RR[0:64, 0:w], op=ALU.add),
                     reads=[rRR], writes=[rdst])
                S.op("dve", lambda e, t0=t0, w=w: e.tensor_scalar(out=dst[0:64, t0:t0 + w], in0=dst[0:64, t0:t0 + w], scalar1=-PI, scalar2=PI,
                                                                op0=ALU.max, op1=ALU.min),
                     reads=[], writes=[rdst])
                S.op("act", lambda e, t0=t0, w=w: e.activation(out=dst[0:64, t0:t0 + w], in_=dst[0:64, t0:t0 + w], func=AF.Sin, scale=1.0),
                     reads=[], writes=[rdst])
        sin_layer(ZT, rZT, 33, FW1, 0, HD1, rHD1)
        sin_layer(HD1, rHD1, 64, FW2, 1, HD2, rHD2)
        mark = self.a_off
        kspec = Wd[f"kspec_{tag}"]
        dfm = {0: Wd[f"dfc_{tag}"].rearrange("(j p) f -> p j f", p=128), 1: Wd[f"dfs_{tag}"].rearrange("(j p) f -> p j f", p=128)}
        nf = nt
        for q in range(4):
            S.barrier()
            self.arena_reset(mark)
            A = self.alloc(nt * 256).rearrange("p (j n) -> p j n", n=256)
            Bm = self.alloc(nt * 256).rearrange("p (j n) -> p j n", n=256)
            rAB = Res()
            FW3 = self.alloc(512, F32)
            ABSD = self.alloc(256, F32)
            SKq = self.alloc(256, F32)
            RN = self.alloc(256, F32)
            rFW3, rABSD, rSK, rRN = Res(), Res(), Res(), Res()
            DEC = self.alloc(256, F32)
            KF = self.alloc(256, F32)
            KB = self.alloc(256, F32)
            AF_ = self.alloc(256, F32)
            AB_ = self.alloc(256, F32)
            rDEC, rKF, rKB, rAF, rABb = (Res() for _ in range(5))
            DF = [[self.alloc(nt * 128).rearrange("p (j n) -> p j n", n=128) for _ in range(2)] for _ in range(2)]
            rDF = [[Res(), Res()], [Res(), Res()]]
            KS4 = [[self.alloc(256, F32) for _ in range(2)] for _ in range(2)]
            rKS4 = [[Res(), Res()], [Res(), Res()]]
            S.op("sp", lambda e, q=q: e.dma_start(out=FW3[0:64, 0:256], in_=Wd["fw3"][:, q * 256:(q + 1) * 256]), writes=[rFW3], dma=True)
            S.op("sp", lambda e, q=q: e.dma_start(out=FW3[0:64, 256:512], in_=Wd["fw3"][:, D + q * 256:D + (q + 1) * 256]), writes=[rFW3], dma=True)
            S.op("sp", lambda e, q=q: e.dma_start(out=ABSD[:, :], in_=Wd["absd"][:, q * 256:(q + 1) * 256]), writes=[rABSD], dma=True)
            S.op("sp", lambda e, q=q: e.dma_start(out=SKq[:, :], in_=Wd["skipb"][:, q * 256:(q + 1) * 256]), writes=[rSK], dma=True)
            psS, rpsS = self.psum_acc()
            for j in range(nt):
                ps, rps = self.psum()
                S.op("pe", lambda e, ps=ps, j=j: e.matmul(ps[:, 0:512], lhsT=HD2[0:64, j * 128:(j + 1) * 128], rhs=FW3[0:64, 0:512], start=True, stop=True),
                     reads=[rHD2, rFW3], writes=[rps])
                S.op("act", lambda e, j=j: e.activation(out=DEC[:, :], in_=ABSD[:, :], func=AF.Exp, scale=NEGT[:, j:j + 1]),
                     reads=[rABSD, rNEGT], writes=[rDEC])
                S.op("dve", lambda e, ps=ps: e.tensor_tensor(out=KF[:, :], in0=ps[:, 0:256], in1=DEC[:, :], op=ALU.mult), reads=[rDEC], writes=[rps, rKF])
                S.op("dve", lambda e, ps=ps: e.tensor_tensor(out=KB[:, :], in0=ps[:, 256:512], in1=DEC[:, :], op=ALU.mult), reads=[rDEC], writes=[rps, rKB])
                if j == 0:
                    S.op("pool", lambda e: e.memset(KB[0:1, :], 0.0), writes=[rKB])
                S.op("pool", lambda e, j=j: e.tensor_tensor(out=A[:, j, :], in0=KF[:, :], in1=KB[:, :], op=ALU.add), reads=[rKF, rKB], writes=[rAB])
                S.op("pool", lambda e, j=j: e.tensor_tensor(out=Bm[:, j, :], in0=KF[:, :], in1=KB[:, :], op=ALU.subtract), reads=[rKF, rKB], writes=[rAB])
                S.op("act", lambda e: e.activation(out=AF_[:, :], in_=KF[:, :], func=AF.Abs), reads=[rKF], writes=[rAF])
                S.op("act", lambda e: e.activation(out=AB_[:, :], in_=KB[:, :], func=AF.Abs), reads=[rKB], writes=[rABb])
                S.op("pe", lambda e, j=j: e.matmul(psS[:, 0:256], lhsT=ONESF[:, :], rhs=AF_[:, :], start=(j == 0), stop=False), reads=[rONESF, rAF], writes=[rpsS])
                S.op("pe", lambda e, j=j: e.matmul(psS[:, 0:256], lhsT=ONESF[:, :], rhs=AB_[:, :], start=False, stop=(j == nt - 1)), reads=[rONESF, rABb], writes=[rpsS])
            S.op("dve", lambda e: e.tensor_scalar_add(out=RN[:, :], in0=psS[:, 0:256], scalar1=EPS), reads=[], writes=[rpsS, rRN])
            S.op("dve", lambda e: e.reciprocal(out=RN[:, :], in_=RN[:, :]), reads=[], writes=[rRN])
            for fb in range(nf):
                for typ in range(2):
                    df, rdf = DF[fb % 2][typ], rDF[fb % 2][typ]
                    S.op("sp", lambda e, fb=fb, typ=typ, df=df: e.dma_start(out=df[:, :, :], in_=dfm[typ][:, :, fb * 128:(fb + 1) * 128]), writes=[rdf], dma=True)
                    ps, rps = self.psum()
                    src = A if typ == 0 else Bm

                    def mm(e, ps=ps, typ=typ, src=src, df=df):
                        ins = None
                        for j in range(nt):
                            ins = e.matmul(ps[:, 0:256], lhsT=df[:, j, :], rhs=src[:, j, :], start=(j == 0), stop=(j == nt - 1))
                        return ins
                    S.op("pe", mm, reads=[rdf, rAB], writes=[rps])
                    ks, rks = KS4[fb % 2][typ], rKS4[fb % 2][typ]
                    S.op("dve", lambda e, ps=ps, ks=ks: e.tensor_tensor(out=ks[:, :], in0=ps[:, 0:256], in1=RN[:, :], op=ALU.mult), reads=[rRN], writes=[rps, rks])
                    if typ == 0:
                        S.op("pool", lambda e, ks=ks: e.tensor_tensor(out=ks[:, :], in0=ks[:, :], in1=SKq[:, :], op=ALU.add), reads=[rSK], writes=[rks])
                    S.op("sp", lambda e, fb=fb, typ=typ, q=q, ks=ks: e.dma_start(out=kspec[typ, fb * 128:(fb + 1) * 128, q * 256:(q + 1) * 256], in_=ks[:, :]),
                         reads=[rks], writes=[self.rKSW[tag][(fb * 2 + typ) % 8]], dma=True)

    def hy_conv(self, i, Wd, q, tag, Lx, nt, col0, vt0, VT, rVT, X0q, rX0q, WOq, rWOq, Z, rZ, KSb, rKSb, SC, rSC):
        S = self.S
        off, _ = vec_offsets(i)
        nf = nt
        kspec = Wd[f"kspec_{tag}"]
        dfm = {0: Wd[f"dfc_{tag}"].rearrange("(j p) f -> p j f", p=128), 1: Wd[f"dfs_{tag}"].rearrange("(j p) f -> p j f", p=128)}
        idm = {0: Wd[f"idc_{tag}"].rearrange("(j p) t -> p j t", p=128), 1: Wd[f"ids_{tag}"].rearrange("(j p) t -> p j t", p=128)}
        Y = self.alloc(nf * 512).rearrange("p (f y n) -> p f y n", y=2, n=256)
        rY = Res()
        DF = [[self.alloc(nt * 128).rearrange("p (j n) -> p j n", n=128) for _ in range(2)] for _ in range(2)]
        rDF = [[Res(), Res()], [Res(), Res()]]
        IB = [self.alloc(2 * nf * 256).rearrange("p (y f n) -> p y f n", y=2, n=256) for _ in range(2)]
        rIB = [Res(), Res()]
        U = [self.TMP[0], self.TMP[1]]
        rU = [self.rTMP[0], self.rTMP[1]]
        col = 1 if tag == "c" else 0
        bo = off["hy_b_out"]
        for fb in range(nf):
            kb = fb % 2
            S.op("sp", lambda e, fb=fb, kb=kb: e.dma_start(out=KSb[kb][:, :, :], in_=kspec[:, fb * 128:(fb + 1) * 128, q * 256:(q + 1) * 256].rearrange("y p n -> p y n")),
                 reads=[self.rKS[tag]] + self.rKSW[tag], writes=[rKSb[kb]], dma=True)
            pss = []
            for typ in range(2):
                df, rdf = DF[kb][typ], rDF[kb][typ]
                S.op("sp", lambda e, fb=fb, typ=typ, df=df: e.dma_start(out=df[:, :, :], in_=dfm[typ][:, :, fb * 128:(fb + 1) * 128]), writes=[rdf], dma=True)
                ps, rps = self.psum()
                pss.append((ps, rps))

                def mm(e, ps=ps, typ=typ, df=df):
                    ins = None
                    for j in range(nt):
                        ins = e.matmul(ps[:, 0:256], lhsT=df[:, j, :], rhs=VT[:, vt0 + j, q * 256:(q + 1) * 256], start=(j == 0), stop=(j == nt - 1))
                    return ins
                S.op("pe", mm, reads=[rdf, rVT], writes=[rps])
                S.op("act", lambda e, ps=ps, typ=typ: e.copy(out=U[typ][:, 0:256], in_=ps[:, 0:256]), reads=[], writes=[rps, rU[typ]])
            Kc, Ks = KSb[kb][:, 0, :], KSb[kb][:, 1, :]
            S.op("dve", lambda e, Kc=Kc: e.tensor_tensor(out=SC[0][:, :], in0=U[0][:, 0:256], in1=Kc, op=ALU.mult), reads=[rU[0], rKSb[kb]], writes=[rSC[0]])
            S.op("pool", lambda e, Ks=Ks: e.tensor_tensor(out=SC[1][:, :], in0=U[1][:, 0:256], in1=Ks, op=ALU.mult), reads=[rU[1], rKSb[kb]], writes=[rSC[1]])
            S.op("dve", lambda e, fb=fb: e.tensor_tensor(out=Y[:, fb, 0, :], in0=SC[0][:, :], in1=SC[1][:, :], op=ALU.subtract), reads=[rSC[0], rSC[1]], writes=[rY])
            S.op("dve", lambda e, Ks=Ks: e.tensor_tensor(out=SC[0][:, :], in0=U[0][:, 0:256], in1=Ks, op=ALU.mult), reads=[rU[0], rKSb[kb]], writes=[rSC[0]])
            S.op("pool", lambda e, Kc=Kc: e.tensor_tensor(out=SC[1][:, :], in0=U[1][:, 0:256], in1=Kc, op=ALU.mult), reads=[rU[1], rKSb[kb]], writes=[rSC[1]])
            S.op("dve", lambda e, fb=fb: e.tensor_tensor(out=Y[:, fb, 1, :], in0=SC[0][:, :], in1=SC[1][:, :], op=ALU.add), reads=[rSC[0], rSC[1]], writes=[rY])
        for ts in range(Lx // 256):
            ib, rib = IB[ts % 2], rIB[ts % 2]
            for typ in range(2):
                S.op("sp", lambda e, ib=ib, typ=typ, ts=ts: e.dma_start(out=ib[:, typ, :, :], in_=idm[typ][:, :, ts * 256:(ts + 1) * 256]), writes=[rib], dma=True)
            cs = col0 + ts * 256
            ti = 0 if tag == "c" else 1 + (ts // 2)
            for cc in range(2):
                ps, rps = self.psum()

                def mm(e, ps=ps, ib=ib, cc=cc):
                    ins = None
                    for fb in range(nf):
                        for typ in range(2):
                            ins = e.matmul(ps[:, 0:256], lhsT=Y[:, fb, typ, cc * 128:(cc + 1) * 128], rhs=ib[:, typ, fb, :],
                                           start=(fb == 0 and typ == 0), stop=(fb == nf - 1 and typ == 1))
                    return ins
                S.op("pe", mm, reads=[rY, rib], writes=[rps])
                S.op("dve", lambda e, ps=ps, cc=cc, cs=cs: e.tensor_tensor(out=Z[:, cc, :], in0=ps[:, 0:256], in1=X0q[:, cc, cs:cs + 256], op=ALU.mult),
                     reads=[rX0q], writes=[rps, rZ])
            for o in range(KC):
                ps, rps = self.psum()

                def mm2(e, ps=ps, o=o):
                    e.matmul(ps[:, 0:256], lhsT=WOq[:, 0, o * 128:(o + 1) * 128], rhs=Z[:, 0, :], start=True, stop=False)
                    return e.matmul(ps[:, 0:256], lhsT=WOq[:, 1, o * 128:(o + 1) * 128], rhs=Z[:, 1, :], start=False, stop=True)
                S.op("pe", mm2, reads=[rWOq, rZ], writes=[rps])
                if q == 0:
                    S.op("dve", lambda e, ps=ps, o=o: e.tensor_scalar(out=self.RSTD[:, 0:256], in0=ps[:, 0:256], scalar1=self.VEC[:, bo + o:bo + o + 1],
                                                                   scalar2=self.MOD[:, 16 + o, col:col + 1], op0=ALU.add, op1=ALU.mult),
                         reads=[self.rVEC, self.rMOD], writes=[rps, self.rRSTD])
                    S.op("dve", lambda e, o=o, cs=cs: e.tensor_tensor(out=self.X[:, o, cs:cs + 256], in0=self.X[:, o, cs:cs + 256], in1=self.RSTD[:, 0:256], op=ALU.add),
                         reads=[self.rRSTD], writes=[self.rX[o][ti]])
                else:
                    S.op("dve", lambda e, ps=ps, o=o, cs=cs: e.scalar_tensor_tensor(out=self.X[:, o, cs:cs + 256], in0=ps[:, 0:256], scalar=self.MOD[:, 16 + o, col:col + 1],
                                                                                  in1=self.X[:, o, cs:cs + 256], op0=ALU.mult, op1=ALU.add),
                         reads=[self.rMOD], writes=[rps, self.rX[o][ti]])

    def layer(self, i, Wd, b, last):
        kind = i % 3
        tiles_all = [0, 1, 2, 3, 4]
        tiles_out = [1, 2, 3, 4] if last else tiles_all
        self.adaln(i, Wd)
        if kind == 0:
            self.modulate(0, tiles_all)
            self.S.barrier()
            self.mla(i, Wd, need_ctx=not last)
            self.S.barrier()
            self.carve_ffn()
            self.zero_pads()
        if kind == 2:
            self.modulate(0, tiles_all)
            self.S.barrier()
            self.hyena(i, Wd, b)
            self.S.barrier()
            self.carve_ffn()
            self.zero_pads()
        if kind == 1:
            self.modulate(0, tiles_out)
            self.conformer(i, Wd, tiles_out)
        self.modulate(1, tiles_out)
        self.ffn(i, Wd, tiles_out)

    def ubuf(self, c):
        if c < 6:
            return self.G[:, c, :], self.rG[c]
        return (self.AROW, self.rA) if c == 6 else (self.VROW, self.rV)

    def conformer(self, i, Wd, tiles):
        S = self.S
        off, _ = vec_offsets(i)
        w1 = Wd["cf_w1"].rearrange("(c p) n -> p c n", p=128)
        w2 = Wd["cf_w2"].rearrange("(c p) n -> p c n", p=128)
        b1o, wdo, bdo, lgo, lbo, b2o = (off[k] for k in ("cf_b_pw1", "cf_w_dw", "cf_b_dw", "cf_ln_g", "cf_ln_b", "cf_b_pw2"))
        for (a, bnd) in ((0, 16), (C0 + LC, L0), (L0 + L, T)):
            S.op("pool", lambda e, a=a, bnd=bnd: e.memset(self.G[:, :, a:bnd], 0.0), writes=self.rG)
        for c in range(KC):
            k = self.wa_i
            self.wa_i = 1 - k
            wa, rwa = self.WA[k], self.rWA[k]
            self.load_w(wa[:, :, 0:128], w1[:, :, c * 128:(c + 1) * 128], [rwa[0]] + self.rWF[2 * k] + self.rWF[2 * k + 1])
            self.load_w(wa[:, :, 128:256], w1[:, :, D + c * 128:D + (c + 1) * 128], [rwa[1]] + self.rWF[2 * k] + self.rWF[2 * k + 1])
            ub, rub = self.ubuf(c)
            for ti in tiles:
                c0, w = TILES[ti]
                pss = []
                for half in range(2):
                    ps, rps = self.psum()
                    pss.append((ps, rps))

                    def mm(e, ps=ps, wa=wa, half=half, c0=c0, w=w):
                        ins = None
                        for kc in range(KC):
                            ins = e.matmul(ps[:, 0:w], lhsT=wa[:, kc, half * 128:(half + 1) * 128],
                                           rhs=self.H[:, kc, c0:c0 + w], start=(kc == 0), stop=(kc == KC - 1))
                        return ins
                    S.op("pe", mm, reads=[rwa[half]] + [self.rH[kc][ti] for kc in range(KC)], writes=[rps])
                tp, rtp = self.tmp()
                S.op("act", lambda e, ps=pss[1][0], tp=tp, w=w, c=c: e.activation(
                    out=tp[:, 0:w], in_=ps[:, 0:w], func=AF.Sigmoid, bias=self.VEC[:, b1o + 8 + c:b1o + 9 + c], scale=1.0),
                    reads=[self.rVEC], writes=[pss[1][1], rtp])
                S.op("dve", lambda e, ps=pss[0][0], tp=tp, ub=ub, c0=c0, w=w, c=c: e.scalar_tensor_tensor(
                    out=ub[:, c0:c0 + w], in0=ps[:, 0:w], scalar=self.VEC[:, b1o + c:b1o + c + 1], in1=tp[:, 0:w],
                    op0=ALU.add, op1=ALU.mult),
                    reads=[self.rVEC, rtp], writes=[pss[0][1], rub])
        dgs = [self.WB[:, 0:31 * 128].rearrange("p (k n) -> p k n", n=128),
               self.WA[0][:, :, :].rearrange("p c n -> p (c n)")[:, 0:31 * 128].rearrange("p (k n) -> p k n", n=128)]
        rdgs = [[self.rWB], self.wa_all(0)]
        for c in range(KC):
            ub, rub = self.ubuf(c)
            dg, rdg = dgs[c % 2], rdgs[c % 2]
            for kk in range(31):
                sc = self.VEC[:, wdo + kk * 8 + c:wdo + kk * 8 + c + 1]
                S.op("pool" if kk % 2 else "dve", lambda e, kk=kk, dg=dg, sc=sc: e.tensor_scalar_mul(out=dg[:, kk, :], in0=self.IDENT[:, :], scalar1=sc),
                     reads=[self.rID, self.rVEC], writes=rdg)
            for ti in tiles:
                c0, w = TILES[ti]
                ps, rps = self.psum()

                def mm(e, ps=ps, ub=ub, c0=c0, w=w, dg=dg):
                    ins = None
                    for kk in range(31):
                        ins = e.matmul(ps[:, 0:w], lhsT=dg[:, kk, :], rhs=ub[:, c0 + kk - 15:c0 + kk - 15 + w],
                                       start=(kk == 0), stop=(kk == 30))
                    return ins
                S.op("pe", mm, reads=rdg + [rub], writes=[rps])
                S.op("act", lambda e, ps=ps, c=c, c0=c0, w=w: e.activation(
                    out=self.H[:, c, c0:c0 + w], in_=ps[:, 0:w], func=AF.Identity, bias=self.VEC[:, bdo + c:bdo + c + 1], scale=1.0),
                    reads=[self.rVEC], writes=[rps, self.rH[c][ti]])
        for k in range(2):
            self.load_w(self.WA[k][:, :, :], w2[:, :, k * 512:(k + 1) * 512], self.wa_all(k))
        for ti in tiles:
            c0, w = TILES[ti]
            col = 1 if ti == 0 else 0
            for c in range(KC):
                S.op("act", lambda e, c=c, c0=c0, w=w: e.activation(out=self.SQ[:, c, 0:w], in_=self.H[:, c, c0:c0 + w], func=AF.Square),
                     reads=[self.rH[c][ti]], writes=[self.rSQ])
            ps1, rps1 = self.psum()
            ps2, rps2 = self.psum()

            def mm1(e, ps=ps1, c0=c0, w=w):
                ins = None
                for c in range(KC):
                    ins = e.matmul(ps[:, 0:w], lhsT=self.ONES[:, :], rhs=self.H[:, c, c0:c0 + w], start=(c == 0), stop=(c == KC - 1))
                return ins
            S.op("pe", mm1, reads=[self.rONES] + [self.rH[c][ti] for c in range(KC)], writes=[rps1])

            def mm2(e, ps=ps2, w=w):
                ins = None
                for c in range(KC):
                    ins = e.matmul(ps[:, 0:w], lhsT=self.ONES[:, :], rhs=self.SQ[:, c, 0:w], start=(c == 0), stop=(c == KC - 1))
                return ins
            S.op("pe", mm2, reads=[self.rONES, self.rSQ], writes=[rps2])
            S.op("act", lambda e, ps=ps1, w=w: e.activation(out=self.MU[:, 0:w], in_=ps[:, 0:w], func=AF.Identity, scale=1.0 / D),
                 reads=[], writes=[rps1, self.rMU])
            tp, rtp = self.tmp()
            S.op("dve", lambda e, tp=tp, w=w: e.tensor_tensor(out=tp[:, 0:w], in0=self.MU[:, 0:w], in1=self.MU[:, 0:w], op=ALU.mult),
                 reads=[self.rMU], writes=[rtp])
            S.op("dve", lambda e, ps=ps2, tp=tp, w=w: e.scalar_tensor_tensor(
                out=self.RSTD[:, 0:w], in0=ps[:, 0:w], scalar=1.0 / D, in1=tp[:, 0:w], op0=ALU.mult, op1=ALU.subtract),
                reads=[rtp], writes=[rps2, self.rRSTD])
            S.op("act", lambda e, w=w: e.activation(out=self.RSTD[:, 0:w], in_=self.RSTD[:, 0:w], func=AF.Ln, scale=1.0, bias=self.EPSB[:, 0:1]),
                 reads=[self.rONES], writes=[self.rRSTD])
            S.op("act", lambda e, w=w: e.activation(out=self.RSTD[:, 0:w], in_=self.RSTD[:, 0:w], func=AF.Exp, scale=-0.5), reads=[], writes=[self.rRSTD])
            for c in range(KC):
                ub, rub = self.ubuf(c)
                tp, rtp = self.tmp()
                S.op("dve", lambda e, tp=tp, c=c, c0=c0, w=w: e.tensor_tensor(out=tp[:, 0:w], in0=self.H[:, c, c0:c0 + w], in1=self.MU[:, 0:w], op=ALU.subtract),
                     reads=[self.rH[c][ti], self.rMU], writes=[rtp])
                S.op("dve", lambda e, tp=tp, w=w: e.tensor_tensor(out=tp[:, 0:w], in0=tp[:, 0:w], in1=self.RSTD[:, 0:w], op=ALU.mult),
                     reads=[self.rRSTD], writes=[rtp])
                S.op("act", lambda e, tp=tp, ub=ub, c=c, c0=c0, w=w: e.activation(
                    out=ub[:, c0:c0 + w], in_=tp[:, 0:w], func=AF.Silu, scale=self.VEC[:, lgo + c:lgo + c + 1], bias=self.VEC[:, lbo + c:lbo + c + 1]),
                    reads=[rtp, self.rVEC], writes=[rub])
            for dc in range(KC):
                ps, rps = self.psum()
                wa, rwa = self.WA[dc // 4], self.rWA[dc // 4]

                def mm(e, ps=ps, wa=wa, dc=dc, c0=c0, w=w):
                    ins = None
                    for c in range(KC):
                        ins = e.matmul(ps[:, 0:w], lhsT=wa[:, c, (dc % 4) * 128:(dc % 4 + 1) * 128], rhs=self.ubuf(c)[0][:, c0:c0 + w],
                                       start=(c == 0), stop=(c == KC - 1))
                    return ins
                S.op("pe", mm, reads=[rwa[0]] + [self.ubuf(c)[1] for c in range(KC)], writes=[rps])
                tp, rtp = self.tmp()
                S.op("dve", lambda e, ps=ps, tp=tp, dc=dc, w=w, col=col: e.tensor_scalar(
                    out=tp[:, 0:w], in0=ps[:, 0:w], scalar1=self.VEC[:, b2o + dc:b2o + dc + 1], scalar2=self.MOD[:, 16 + dc, col:col + 1],
                    op0=ALU.add, op1=ALU.mult),
                    reads=[self.rVEC, self.rMOD], writes=[rps, rtp])
                S.op("dve", lambda e, tp=tp, dc=dc, c0=c0, w=w: e.tensor_tensor(out=self.X[:, dc, c0:c0 + w], in0=self.X[:, dc, c0:c0 + w], in1=tp[:, 0:w], op=ALU.add),
                     reads=[rtp], writes=[self.rX[dc][ti]])


def make_inputs(inp, layers, core, nb=NB, x_override=None, ctx_override=None):
    m = {}
    xs = inp["x"] if x_override is None else x_override
    cs = inp["ctx"] if ctx_override is None else ctx_override
    xT = np.zeros((nb, D, T), np.float32)
    cT = np.zeros((nb, 128, KC * 2), np.float32)
    for bb in range(nb):
        b = core * nb + bb
        xT[bb, :, L0:L0 + L] = xs[b].T
        xT[bb, :, C0:C0 + LC] = cs[b].T
        sv = np.stack([inp["c"][b], inp["c_ctx"]], axis=-1)
        cT[bb] = sv.reshape(KC, 128, 2).transpose(1, 0, 2).reshape(128, KC * 2)
    m["ident"] = np.eye(128, dtype=np.float32)
    bd = np.zeros((128, 128), np.float32)
    bd[:64, :64] = 1.0
    bd[64:96, 64:96] = 1.0
    m["bd96"] = bd
    m["xT"] = xT
    m["cT"] = cT
    for i in layers:
        kind, j = i % 3, i // 3
        v, _ = pack_layer_vec(inp, i)
        vp = np.zeros((128, NPV), np.float32)
        vp[:, :v.shape[1]] = v
        m[f"vec{i}"] = vp
        m[f"ada_w{i}"] = np.ascontiguousarray(inp["ada_w"][i])
        m[f"ffn_up{i}"] = np.ascontiguousarray(inp["ffn_w_up"][i])
        m[f"ffn_down{i}"] = np.ascontiguousarray(inp["ffn_w_down"][i])
        if kind == 0:
            pe = np.arange(32)
            perm = np.where((pe % 16) < 8, pe + 8, pe - 8)
            wdkv = np.asarray(inp["mla_w_dkv"][j], np.float32)
            kv = np.zeros((D, 320), np.float32)
            kv[:, :128] = wdkv[:, :128]
            kv[:, 128 + 64:224] = wdkv[:, 128:160]
            kv[:, 224 + 64:320] = wdkv[:, 128 + perm]
            m[f"mla_kv{i}"] = kv
            wuq = np.asarray(inp["mla_w_uq"][j], np.float32).reshape(256, 16, 96)
            uq = np.zeros((256, 16, 192), np.float32)
            uq[:, :, :96] = wuq
            uq[:, :, 96 + 64:] = wuq[:, :, 64 + perm]
            m[f"mla_uq{i}"] = uq.reshape(256, 16 * 192)
            m[f"mla_dq{i}"] = np.ascontiguousarray(inp["mla_w_dq"][j])
            m[f"mla_ukv{i}"] = np.ascontiguousarray(inp["mla_w_ukv"][j])
            m[f"mla_o{i}"] = np.ascontiguousarray(inp["mla_w_o"][j])
            m["rope_cs"] = rope_tables()
        if kind == 2:
            m[f"hy_in{i}"] = np.ascontiguousarray(inp["hy_w_in"][j])
            m[f"hy_out{i}"] = np.ascontiguousarray(inp["hy_w_out"][j])
            m[f"hy_fw1_{i}"] = np.ascontiguousarray(inp["hy_f_w1"][j])
            m[f"hy_fw2_{i}"] = np.ascontiguousarray(inp["hy_f_w2"][j])
            m[f"hy_fw3_{i}"] = np.ascontiguousarray(inp["hy_f_w3"][j])
            m[f"hy_skipb{i}"] = np.ascontiguousarray(np.broadcast_to(np.asarray(inp["hy_skip"][j], np.float32)[None, :], (128, D)))
            m.update(hy_consts())
        if kind == 1:
            m[f"cf_w1_{i}"] = np.ascontiguousarray(inp["cf_w_pw1"][j])
            m[f"cf_w2_{i}"] = np.ascontiguousarray(inp["cf_w_pw2"][j])
    return m


_HYC = {}


def hy_consts():
    if _HYC:
        return _HYC
    import ml_dtypes
    bf = ml_dtypes.bfloat16
    out = {}
    for tag, Lx in (("l", L), ("c", LC)):
        N = 2 * Lx
        tau = np.arange(Lx, dtype=np.float64)[:, None]
        f = np.arange(Lx, dtype=np.float64)[None, :] + 0.5
        th = 2 * np.pi * tau * f / N
        out[f"dfc_{tag}"] = np.cos(th).astype(np.float32).astype(bf)
        out[f"dfs_{tag}"] = np.sin(th).astype(np.float32).astype(bf)
        out[f"idc_{tag}"] = np.ascontiguousarray(((2.0 / N) * np.cos(th).T).astype(np.float32)).astype(bf)
        out[f"ids_{tag}"] = np.ascontiguousarray(((2.0 / N) * np.sin(th).T).astype(np.float32)).astype(bf)
        t = np.linspace(0.0, 1.0, Lx, dtype=np.float32)
        w = (2.0 * np.pi * np.arange(Lx, dtype=np.float32) / Lx).astype(np.float32)
        fr = np.linspace(1e-4, 15.0, 16, dtype=np.float32)
        fw = w[:, None] * fr[None, :]
        z = np.concatenate([t[:, None], np.cos(fw), -np.sin(fw)], axis=-1).astype(np.float32)
        out[f"z_{tag}"] = np.ascontiguousarray(z.T)
        out[f"negt_{tag}"] = np.ascontiguousarray((-t).reshape(Lx // 128, 128).T)
    deltas = np.linspace(np.log(1e-2) / 0.3, np.log(1e-2) / 1.5, D, dtype=np.float32)
    out["absd"] = np.ascontiguousarray(np.broadcast_to(np.abs(deltas)[None, :], (128, D))).astype(np.float32)
    _HYC.update(out)
    return _HYC


def rope_tables():
    t = np.arange(L)
    row = (t // 64).astype(np.float32)
    colp = (t % 64).astype(np.float32)
    inv = (10000.0 ** (-(np.arange(0, 16, 2, dtype=np.float32) / 16.0))).astype(np.float32)
    ang = np.concatenate([row[:, None] * inv, colp[:, None] * inv], axis=-1).astype(np.float32)
    cs = np.zeros((128, 2, T), np.float32)
    cs[:, 0, :] = 1.0
    for d in range(32):
        a, b, jj = d // 16, (d % 16) // 8, d % 8
        cs[64 + d, 0, L0:L0 + L] = np.cos(ang[:, a * 8 + jj])
        cs[64 + d, 1, L0:L0 + L] = np.sin(ang[:, a * 8 + jj]) * (-1.0 if b == 0 else 1.0)
    return cs


def kernel(**inputs):
    inp = {k: np.asarray(v) for k, v in inputs.items()}
    layers = list(range(DEPTH))
    prog = Prog(layers)
    nc = prog.build()
    in_maps = [make_inputs(inp, layers, c) for c in range(8)]
    res = run_bass_kernel_spmd(nc, in_maps, core_ids=list(range(8)))
    out = np.empty((16, L, D), np.float32)
    for c in range(8):
        y = res.results[c]["yT"]
        for bb in range(NB):
            out[c * NB + bb] = y[bb].T
    return out
```

```python
import contextlib
import os
import numpy as np
import concourse.bass as bass
import concourse.mybir as mybir
from concourse.bass_utils import run_bass_kernel_spmd

F32 = mybir.dt.float32
BF16 = mybir.dt.bfloat16
ALU = mybir.AluOpType
AF = mybir.ActivationFunctionType

D = 1024
KC = 8
L = 2048
LC = 256
T = 16 + LC + 16 + L + 16
C0 = 16
L0 = 16 + LC + 16
TILES = [(C0, LC)] + [(L0 + 512 * j, 512) for j in range(4)]
D_FF = 2816
FC = 22
DEPTH = 4
EPS = 1e-6
NB = 2
DBG_STOP = int(os.environ.get('MK_STOP', '0'))
DBG_HEADS = (int(os.environ['MK_HEADS']) if 'MK_HEADS' in os.environ else None)


class Res:
    __slots__ = ("w", "r")

    def __init__(self):
        self.w = None
        self.r = {}


class Sched:
    ENG = ("pe", "act", "dve", "pool", "sp")
    NDMA = 8
    EPOCH = 6000

    def __init__(self):
        self.streams = {e: [] for e in self.ENG}
        self.cnt = {e: 0 for e in self.ENG}
        self.waited = {e: {} for e in self.ENG}
        self.dma_n = {e: 0 for e in self.ENG}
        self.last = {}

    def op(self, eng, fn, reads=(), writes=(), dma=False, after=()):
        deps = {}

        def add(k, v):
            if deps.get(k, 0) < v:
                deps[k] = v

        for r in after:
            if r.w is not None:
                add(*r.w)
            for k, v in r.r.items():
                add(k, v)
        for r in reads:
            if r.w is not None:
                add(*r.w)
        for r in writes:
            if r.w is not None:
                add(*r.w)
            for k, v in r.r.items():
                add(k, v)
        if dma:
            j = self.dma_n[eng]
            self.dma_n[eng] += 1
            key = ("dma", eng, j % self.NDMA)
            val = 16 * (j // self.NDMA + 1)
            if j >= self.NDMA:
                add(key, val - 16)
            inc = 16
        else:
            key = ("eng", eng, self.cnt[eng] // self.EPOCH)
            val = self.cnt[eng] % self.EPOCH + 1
            self.cnt[eng] += 1
            inc = 1
        self.last[key] = val
        wd = self.waited[eng]
        waits = []
        for k, v in deps.items():
            if eng == "pe" and k[0] == "eng" and k[1] == "pe":
                continue
            if wd.get(k, 0) < v:
                wd[k] = v
                waits.append((k, v))
        self.streams[eng].append((fn, waits, key, inc))
        for r in reads:
            if r.r.get(key, 0) < val:
                r.r[key] = val
        for r in writes:
            r.w = (key, val)
            r.r = {}

    def barrier(self, engs=None):
        for e in (engs or self.ENG):
            wd = self.waited[e]
            waits = []
            for k, v in self.last.items():
                if wd.get(k, 0) < v:
                    wd[k] = v
                    waits.append((k, v))
            if waits:
                self.streams[e].append((None, waits, None, 0))

    def emit(self, nc, stack):
        sems = {}
        for e in self.ENG:
            for (fn, waits, key, inc) in self.streams[e]:
                for k in [w[0] for w in waits] + ([key] if key is not None else []):
                    if k not in sems:
                        sems[k] = stack.enter_context(nc.semaphore("s_" + "_".join(str(x) for x in k)))
        block = stack.enter_context(nc.Block())

        def run(ename, eng):
            for (fn, waits, key, inc) in self.streams[ename]:
                for k, v in waits:
                    eng.wait_ge(sems[k], v)
                if fn is not None:
                    fn(eng).then_inc(sems[key], inc)

        @block.tensor
        def _(e):
            run("pe", e)

        @block.scalar
        def _(e):
            run("act", e)

        @block.vector
        def _(e):
            run("dve", e)

        @block.gpsimd
        def _(e):
            run("pool", e)

        @block.sync
        def _(e):
            run("sp", e)


def fm(v):
    v = np.asarray(v, np.float32)
    c = v.shape[-1] // 128
    lead = v.shape[:-1]
    return np.ascontiguousarray(np.moveaxis(v.reshape(lead + (c, 128)), -1, 0)).reshape(128, -1)


def pack_layer_vec(inp, i):
    kind, j = i % 3, i // 3
    parts = [("ada_b", fm(inp["ada_b"][i])),
             ("norm_mix", fm(inp["norm_mix"][i])),
             ("norm_ffn", fm(inp["norm_ffn"][i])),
             ("ffn_b_dw", fm(inp["ffn_b_dw"][i])),
             ("ffn_w_dw", fm(inp["ffn_w_dw"][i]))]
    if kind == 0:
        g = np.asarray(inp["mla_qk_gain"][j], np.float32)
        def col96(v):
            o = np.zeros((128, 1), np.float32)
            o[:96, 0] = v
            return o
        pe = np.arange(32)
        perm = np.where((pe % 16) < 8, pe + 8, pe - 8)
        gqp = np.zeros(96, np.float32); gqp[64:] = g[0, 64 + perm]
        gkp = np.zeros(96, np.float32); gkp[64:] = g[1, 64 + perm]
        invn = np.concatenate([np.full(64, 1 / 64.0), np.full(32, 1 / 32.0)]).astype(np.float32)
        parts += [("mla_qn", fm(inp["mla_q_norm"][j])), ("mla_kvn", fm(inp["mla_kv_norm"][j])),
                  ("mla_gq", col96(g[0])), ("mla_gqp", col96(gqp)), ("mla_gk", col96(g[1])), ("mla_gkp", col96(gkp)),
                  ("mla_invn", col96(invn))]
    if kind == 2:
        def col64(v):
            o = np.zeros((128, 1), np.float32)
            o[:64, 0] = v
            return o
        parts += [("hy_b_in", fm(inp["hy_b_in"][j])), ("hy_w_short", fm(inp["hy_w_short"][j])),
                  ("hy_b_short", fm(inp["hy_b_short"][j])), ("hy_b_out", fm(inp["hy_b_out"][j])),
                  ("hy_fb1", col64(inp["hy_f_b1"][j])), ("hy_fb2", col64(inp["hy_f_b2"][j])), ("hy_fr", col64(inp["hy_sin_freq"][j]))]
    if kind == 1:
        parts += [("cf_b_pw1", fm(inp["cf_b_pw1"][j])),
                  ("cf_w_dw", fm(inp["cf_w_dw"][j])),
                  ("cf_b_dw", fm(inp["cf_b_dw"][j])),
                  ("cf_ln_g", fm(inp["cf_ln_g"][j])),
                  ("cf_ln_b", fm(inp["cf_ln_b"][j])),
                  ("cf_b_pw2", fm(inp["cf_b_pw2"][j]))]
    off = {}
    o = 0
    for n, a in parts:
        off[n] = o
        o += a.shape[1]
    return np.concatenate([a for _, a in parts], axis=1), off


def vec_offsets(i):
    kind = i % 3
    sizes = [("ada_b", 48), ("norm_mix", 8), ("norm_ffn", 8), ("ffn_b_dw", 22), ("ffn_w_dw", 66)]
    if kind == 0:
        sizes += [("mla_qn", 2), ("mla_kvn", 1), ("mla_gq", 1), ("mla_gqp", 1), ("mla_gk", 1), ("mla_gkp", 1), ("mla_invn", 1)]
    if kind == 2:
        sizes += [("hy_b_in", 24), ("hy_w_short", 72), ("hy_b_short", 24), ("hy_b_out", 8), ("hy_fb1", 1), ("hy_fb2", 1), ("hy_fr", 1)]
    if kind == 1:
        sizes += [("cf_b_pw1", 16), ("cf_w_dw", 248), ("cf_b_dw", 8), ("cf_ln_g", 8), ("cf_ln_b", 8), ("cf_b_pw2", 8)]
    off = {}
    o = 0
    for n, s in sizes:
        off[n] = o
        o += s
    return off, o


NPV = 512
ARENA = 42112


class Prog:
    def __init__(self, layers, nb=NB):
        self.layers = layers
        self.nb = nb
        self.nc = bass.Bass("TRN2", target_bir_lowering=False)
        self.S = Sched()
        self.dram = {}

    def din(self, name, shape, dt=F32):
        t = self.nc.dram_tensor(name, list(shape), dt, kind="ExternalInput").ap()
        self.dram[name] = t
        return t

    def dout(self, name, shape, dt=F32):
        t = self.nc.dram_tensor(name, list(shape), dt, kind="ExternalOutput").ap()
        self.dram[name] = t
        return t

    def build(self):
        nc, S = self.nc, self.S
        nb = self.nb
        xT = self.din("xT", [nb, D, T])
        cT = self.din("cT", [nb, 128, KC * 2])
        ident_d = self.din("ident", [128, 128])
        bd_d = self.din("bd96", [128, 128])
        yT = self.dout("yT", [nb, D, L])
        cOut = self.dout("cOut", [nb, D, LC])
        W = {}
        for i in self.layers:
            kind, j = i % 3, i // 3
            W[i] = dict(
                ada_w=self.din(f"ada_w{i}", [D, 6 * D]),
                vec=self.din(f"vec{i}", [128, NPV]),
                ffn_up=self.din(f"ffn_up{i}", [D, 2 * D_FF]),
                ffn_down=self.din(f"ffn_down{i}", [D_FF, D]),
            )
            if kind == 0:
                if "rope_cs" not in self.dram:
                    self.din("rope_cs", [128, 2, T])
                W[i].update(dq=self.din(f"mla_dq{i}", [D, 256]), kv=self.din(f"mla_kv{i}", [D, 320]),
                            uq=self.din(f"mla_uq{i}", [256, 16 * 192]), ukv=self.din(f"mla_ukv{i}", [128, 2048]),
                            wo=self.din(f"mla_o{i}", [D, D]), rope=self.dram["rope_cs"])
            if kind == 2:
                W[i].update(hy_in=self.din(f"hy_in{i}", [D, 3 * D]), hy_out=self.din(f"hy_out{i}", [D, D]),
                            fw1=self.din(f"hy_fw1_{i}", [33, 64]), fw2=self.din(f"hy_fw2_{i}", [64, 64]),
                            fw3=self.din(f"hy_fw3_{i}", [64, 2 * D]), skipb=self.din(f"hy_skipb{i}", [128, D]))
                for tag, Lx in (("l", L), ("c", LC)):
                    for nm in ("dfc", "dfs"):
                        W[i][f"{nm}_{tag}"] = self.din(f"{nm}_{tag}", [Lx // 128, 128, Lx], BF16)
                    for nm in ("idc", "ids"):
                        W[i][f"{nm}_{tag}"] = self.din(f"{nm}_{tag}", [Lx // 256, 128, (Lx // 128) * 256], BF16)
                    W[i][f"z_{tag}"] = self.din(f"z_{tag}", [33, Lx])
                    W[i][f"negt_{tag}"] = self.din(f"negt_{tag}", [128, Lx // 128])
                    W[i][f"kspec_{tag}"] = self.nc.dram_tensor(f"kspec_{tag}", [2, Lx, D], F32).ap()
                W[i]["absd"] = self.din("absd", [128, D])
                W[i]["x0d"] = self.nc.dram_tensor("x0d", [D, T], BF16).ap()
                self.rX0D = [Res() for _ in range(KC)]
                self.rKS = {"l": Res(), "c": Res()}
                self.rKSW = {"l": [Res() for _ in range(8)], "c": [Res() for _ in range(8)]}
            if kind == 1:
                W[i].update(cf_w1=self.din(f"cf_w1_{i}", [D, 2 * D]), cf_w2=self.din(f"cf_w2_{i}", [D, D]))
        with contextlib.ExitStack() as st:
            self.st = st

            def sb(name, shape, dt):
                return st.enter_context(nc.sbuf_tensor(name, list(shape), dt))

            self.X = sb("X", [128, KC, T], F32)
            self.rX = [[Res() for _ in TILES] for _ in range(KC)]
            self.H = sb("H", [128, KC, T], BF16)
            self.rH = [[Res() for _ in TILES] for _ in range(KC)]
            self.SCR = sb("SCR", [128, ARENA], BF16)
            self.carve_ffn()
            self.VEC = sb("VEC", [128, NPV], F32)
            self.rVEC = Res()
            self.MOD = sb("MOD", [128, 48, 2], F32)
            self.rMOD = Res()
            self.AM = sb("AM", [128, 2, KC, 2], F32)
            self.rAM = Res()
            self.SV = sb("SV", [128, KC, 2], F32)
            self.SVB = sb("SVB", [128, KC, 2], BF16)
            self.rSV = Res()
            self.ONES = sb("ONES", [128, 128], BF16)
            self.rONES = Res()
            self.EPSB = sb("EPSB", [128, 1], F32)
            self.RSTD = sb("RSTD", [128, 512], F32)
            self.rRSTD = Res()
            self.RSTD2 = sb("RSTD2", [128, 512], F32)
            self.rRSTD2 = Res()
            self.MU = sb("MU", [128, 512], F32)
            self.rMU = Res()
            self.TMP = [sb(f"TMP{k}", [128, 512], F32) for k in range(2)]
            self.rTMP = [Res(), Res()]
            self.tmp_i = 0
            self.BD = sb("BD", [128, 128], BF16)
            self.PS = [st.enter_context(nc.psum_tensor(f"PS{k}", [128, 512], F32)) for k in range(8)]
            self.acc_i = 0
            self.rPS = [Res() for _ in range(8)]
            self.ps_i = 0
            self.wa_i = 0
            self.dq = 0

            S.op("pool", lambda e: e.memset(self.ONES[:, :], 1.0), writes=[self.rONES])
            S.op("pool", lambda e: e.memset(self.EPSB[:, :], EPS), writes=[self.rONES])
            self.zero_pads()
            self.IDENT = sb("IDENT", [128, 128], BF16)
            self.rID = Res()
            S.op("pool", lambda e: e.dma_start(out=self.IDENT[:, :], in_=ident_d[:, :]), writes=[self.rID], dma=True)
            S.op("pool", lambda e: e.dma_start(out=self.BD[:, :], in_=bd_d[:, :]), writes=[self.rID], dma=True)

            routs = []
            for b in range(nb):
                self.load_x(xT, cT, b)
                for i in self.layers:
                    self.layer(i, W[i], b, last=(i == DEPTH - 1))
                routs += self.store_x(yT, cOut, b)
            S.barrier(["sp"])
            S.emit(nc, st)
        return nc

    def psum(self):
        k = self.ps_i
        self.ps_i = (k + 1) % 6
        return self.PS[k], self.rPS[k]

    def wa_all(self, k):
        return self.rWA[k] + self.rWF[2 * k] + self.rWF[2 * k + 1]

    def psum_acc(self):
        k = 6 + self.acc_i
        self.acc_i = 1 - self.acc_i
        return self.PS[k], self.rPS[k]

    def arena_reset(self, mark=0):
        self.a_off = mark

    def alloc(self, n_el, dt=BF16, shape=None):
        nb16 = n_el * (2 if dt == F32 else 1)
        nb16 = (nb16 + 15) // 16 * 16
        a = self.a_off
        assert a + nb16 <= ARENA, (a, nb16)
        self.a_off = a + nb16
        ap = self.SCR[:, a:a + n_el * (2 if dt == F32 else 1)]
        if dt == F32:
            ap = ap.bitcast(F32)
        return ap

    def carve_ffn(self):
        self.arena_reset()
        self.G = self.alloc(6 * T).rearrange("p (c t) -> p c t", t=T)
        self.rG = [Res() for _ in range(6)]
        self.AROW = self.alloc(T)
        self.CROW = self.alloc(T, F32)
        self.VROW = self.alloc(T)
        self.rA, self.rC, self.rV = Res(), Res(), Res()
        self.WA = [self.alloc(KC * 512).rearrange("p (c n) -> p c n", n=512) for _ in range(2)]
        self.rWA = [[Res(), Res()], [Res(), Res()]]
        self.rWF = [[Res(), Res()] for _ in range(4)]
        self.WB = self.alloc(6 * 1024)
        self.rWB = Res()
        self.SQ = self.alloc(KC * 512).rearrange("p (c n) -> p c n", n=512)
        self.rSQ = Res()

    def zero_pads(self):
        S = self.S
        S.op("pool", lambda e: e.memset(self.AROW[:, :], 0.0), writes=[self.rA])
        S.op("pool", lambda e: e.memset(self.VROW[:, :], 0.0), writes=[self.rV])
        for (a, bnd) in ((0, 16), (C0 + LC, L0), (L0 + L, T)):
            S.op("pool", lambda e, a=a, bnd=bnd: e.memset(self.G[:, :, a:bnd], 0.0), writes=self.rG)

    def tmp(self):
        k = self.tmp_i
        self.tmp_i = 1 - k
        return self.TMP[k], self.rTMP[k]

    def dma_eng(self):
        self.dq += 1
        return "sp"

    def load_w(self, dst_ap, src_ap, res, after=()):
        self.S.op("pool", lambda e: e.dma_start(out=dst_ap, in_=src_ap), writes=(res if isinstance(res, list) else [res]), dma=True, after=after)

    def load_x(self, xT, cT, b):
        S = self.S
        for c in range(KC):
            S.op("sp", lambda e, c=c: e.dma_start(out=self.X[:, c, :], in_=xT[b, c * 128:(c + 1) * 128, :]),
                 writes=self.rX[c], dma=True)
        S.op("sp", lambda e: e.dma_start(out=self.SV[:, :, :], in_=cT[b].rearrange("p (c t) -> p c t", t=2)),
             writes=[self.rSV], dma=True)
        S.op("act", lambda e: e.activation(out=self.SVB[:, :, :], in_=self.SV[:, :, :], func=AF.Silu),
             reads=[], writes=[self.rSV])

    def store_x(self, yT, cOut, b):
        S = self.S
        rs = []
        for c in range(KC):
            r = Res()
            S.op("sp", lambda e, c=c: e.dma_start(out=yT[b, c * 128:(c + 1) * 128, :], in_=self.X[:, c, L0:L0 + L]),
                 reads=self.rX[c], writes=[r], dma=True)
            r2 = Res()
            S.op("sp", lambda e, c=c: e.dma_start(out=cOut[b, c * 128:(c + 1) * 128, :], in_=self.X[:, c, C0:C0 + LC]),
                 reads=self.rX[c], writes=[r2], dma=True)
            rs += [r, r2]
        return rs

    def adaln(self, i, Wd):
        S = self.S
        off, _ = vec_offsets(i)
        S.op("sp", lambda e: e.dma_start(out=self.VEC[:, :], in_=Wd["vec"][:, :]), writes=[self.rVEC], dma=True)
        ada_w = Wd["ada_w"].rearrange("(c p) n -> p c n", p=128)
        for blk in range(12):
            k = self.wa_i
            self.wa_i = 1 - k
            wa, rwa = self.WA[k], self.rWA[k]
            self.S.op("pool", lambda e, wa=wa, blk=blk: e.dma_start(out=wa[:, :, :], in_=ada_w[:, :, blk * 512:(blk + 1) * 512]), writes=self.wa_all(k), dma=True)
            ps, rps = self.psum()

            def mm(e, wa=wa, ps=ps):
                ins = None
                for f in range(4):
                    for kc in range(KC):
                        ins = e.matmul(ps[:, f * 2:f * 2 + 2], lhsT=wa[:, kc, f * 128:(f + 1) * 128],
                                       rhs=self.SVB[:, kc, :], start=(kc == 0), stop=(kc == KC - 1))
                return ins
            S.op("pe", mm, reads=rwa + [self.rSV], writes=[rps])
            ab = off["ada_b"] + blk * 4
            S.op("dve", lambda e, ps=ps, blk=blk, ab=ab: e.tensor_tensor(
                out=self.MOD[:, blk * 4:blk * 4 + 4, :], in0=ps[:, 0:8].rearrange("p (f t) -> p f t", t=2),
                in1=self.VEC[:, ab:ab + 4].unsqueeze(2).to_broadcast([128, 4, 2]), op=ALU.add),
                reads=[self.rVEC], writes=[rps, self.rMOD])
        for m, (nname, g) in enumerate((("norm_mix", 1), ("norm_ffn", 4))):
            no = off[nname]
            S.op("dve", lambda e, m=m, g=g: e.tensor_scalar_add(out=self.AM[:, m, :, :], in0=self.MOD[:, g * 8:(g + 1) * 8, :], scalar1=1.0),
                 reads=[self.rMOD], writes=[self.rAM])
            S.op("dve", lambda e, m=m, no=no: e.tensor_tensor(
                out=self.AM[:, m, :, :], in0=self.AM[:, m, :, :],
                in1=self.VEC[:, no:no + 8].unsqueeze(2).to_broadcast([128, 8, 2]), op=ALU.mult),
                reads=[self.rVEC, self.rAM], writes=[self.rAM])

    def modulate(self, m, tiles):
        S = self.S
        shg = 0 if m == 0 else 3
        RS = [self.RSTD, self.RSTD2]
        rRS = [self.rRSTD, self.rRSTD2]

        def stage1(k):
            ti = tiles[k]
            c0, w = TILES[ti]
            rs, rrs = RS[k % 2], rRS[k % 2]
            for c in range(KC):
                S.op("act", lambda e, c=c: e.activation(out=self.SQ[:, c, 0:w], in_=self.X[:, c, c0:c0 + w], func=AF.Square),
                     reads=[self.rX[c][ti]], writes=[self.rSQ])
            ps, rps = self.psum()

            def mm(e):
                ins = None
                for c in range(KC):
                    ins = e.matmul(ps[:, 0:w], lhsT=self.ONES[:, :], rhs=self.SQ[:, c, 0:w], start=(c == 0), stop=(c == KC - 1))
                return ins
            S.op("pe", mm, reads=[self.rSQ, self.rONES], writes=[rps])
            S.op("act", lambda e: e.activation(out=rs[:, 0:w], in_=ps[:, 0:w], func=AF.Ln, scale=1.0 / D, bias=self.EPSB[:, 0:1]),
                 reads=[self.rONES], writes=[rps, rrs])
            S.op("act", lambda e: e.activation(out=rs[:, 0:w], in_=rs[:, 0:w], func=AF.Exp, scale=-0.5), reads=[], writes=[rrs])

        def stage2(k):
            ti = tiles[k]
            c0, w = TILES[ti]
            col = 1 if ti == 0 else 0
            rs, rrs = RS[k % 2], rRS[k % 2]
            for c in range(KC):
                tp, rtp = self.tmp()
                S.op("dve", lambda e, tp=tp, c=c: e.tensor_tensor(out=tp[:, 0:w], in0=self.X[:, c, c0:c0 + w], in1=rs[:, 0:w], op=ALU.mult),
                     reads=[self.rX[c][ti], rrs], writes=[rtp])
                S.op("pool", lambda e, tp=tp, c=c: e.tensor_scalar(
                    out=self.H[:, c, c0:c0 + w], in0=tp[:, 0:w], scalar1=self.AM[:, m, c, col:col + 1],
                    scalar2=self.MOD[:, shg * 8 + c, col:col + 1], op0=ALU.mult, op1=ALU.add),
                    reads=[rtp, self.rAM, self.rMOD], writes=[self.rH[c][ti]])
        stage1(0)
        for k in range(len(tiles)):
            if k + 1 < len(tiles):
                stage1(k + 1)
            stage2(k)

    def ffn(self, i, Wd, tiles):
        S = self.S
        off, _ = vec_offsets(i)
        up = Wd["ffn_up"].rearrange("(c p) n -> p c n", p=128)
        down = Wd["ffn_down"]
        bo, wo = off["ffn_b_dw"], off["ffn_w_dw"]
        lo = TILES[tiles[0]][0]
        hi = TILES[tiles[-1]][0] + TILES[tiles[-1]][1]
        for (g0, gn) in ((0, 6), (6, 6), (12, 5), (17, 5)):
            wbv = self.WB[:, 0:gn * 1024].rearrange("p (c n) -> p c n", n=1024)
            self.load_w(wbv, down[g0 * 128:(g0 + gn) * 128, :].rearrange("(c p) n -> p c n", p=128), self.rWB)
            for hc in range(g0, g0 + gn):
                slot = hc % 4
                wa = self.WA[slot // 2][:, :, (slot % 2) * 256:(slot % 2) * 256 + 256]
                rwa = self.rWF[slot]
                self.load_w(wa[:, :, 0:128], up[:, :, hc * 128:(hc + 1) * 128], rwa[0], after=self.rWA[slot // 2])
                self.load_w(wa[:, :, 128:256], up[:, :, D_FF + hc * 128:D_FF + (hc + 1) * 128], rwa[1], after=self.rWA[slot // 2])
                for ti in tiles:
                    c0, w = TILES[ti]
                    for half, (dst, rdst) in enumerate(((self.AROW, self.rA), (self.VROW, self.rV))):
                        ps, rps = self.psum()

                        def mm(e, ps=ps, wa=wa, half=half, c0=c0, w=w):
                            ins = None
                            for kc in range(KC):
                                ins = e.matmul(ps[:, 0:w], lhsT=wa[:, kc, half * 128:(half + 1) * 128],
                                               rhs=self.H[:, kc, c0:c0 + w], start=(kc == 0), stop=(kc == KC - 1))
                            return ins
                        S.op("pe", mm, reads=[rwa[half]] + [self.rH[kc][ti] for kc in range(KC)], writes=[rps])
                        S.op("act", lambda e, ps=ps, dst=dst, c0=c0, w=w: e.copy(out=dst[:, c0:c0 + w], in_=ps[:, 0:w]),
                             reads=[], writes=[rps, rdst])
                S.op("dve", lambda e, hc=hc: e.tensor_scalar(
                    out=self.CROW[:, lo:hi], in0=self.AROW[:, lo:hi], scalar1=self.VEC[:, wo + 22 + hc:wo + 22 + hc + 1],
                    scalar2=self.VEC[:, bo + hc:bo + hc + 1], op0=ALU.mult, op1=ALU.add),
                    reads=[self.rA, self.rVEC], writes=[self.rC])
                S.op("dve", lambda e, hc=hc: e.scalar_tensor_tensor(
                    out=self.CROW[:, lo:hi], in0=self.AROW[:, lo - 1:hi - 1], scalar=self.VEC[:, wo + hc:wo + hc + 1],
                    in1=self.CROW[:, lo:hi], op0=ALU.mult, op1=ALU.add),
                    reads=[self.rA, self.rVEC], writes=[self.rC])
                S.op("dve", lambda e, hc=hc: e.scalar_tensor_tensor(
                    out=self.CROW[:, lo:hi], in0=self.AROW[:, lo + 1:hi + 1], scalar=self.VEC[:, wo + 44 + hc:wo + 44 + hc + 1],
                    in1=self.CROW[:, lo:hi], op0=ALU.mult, op1=ALU.add),
                    reads=[self.rA, self.rVEC], writes=[self.rC])
                S.op("act", lambda e: e.activation(out=self.CROW[:, lo:hi], in_=self.CROW[:, lo:hi], func=AF.Silu),
                     reads=[], writes=[self.rC])
                gi = hc - g0
                S.op("dve", lambda e, gi=gi: e.tensor_tensor(out=self.G[:, gi, lo:hi], in0=self.CROW[:, lo:hi], in1=self.VROW[:, lo:hi], op=ALU.mult),
                     reads=[self.rC, self.rV], writes=[self.rG[gi]])
            for dc in range(KC):
                for ti in tiles:
                    c0, w = TILES[ti]
                    col = 1 if ti == 0 else 0
                    ps, rps = self.psum()

                    def mm(e, ps=ps, wbv=wbv, dc=dc, c0=c0, w=w, gn=gn):
                        ins = None
                        for gi in range(gn):
                            ins = e.matmul(ps[:, 0:w], lhsT=wbv[:, gi, dc * 128:(dc + 1) * 128], rhs=self.G[:, gi, c0:c0 + w],
                                           start=(gi == 0), stop=(gi == gn - 1))
                        return ins
                    S.op("pe", mm, reads=[self.rWB] + [self.rG[gi] for gi in range(gn)], writes=[rps])
                    S.op("dve", lambda e, ps=ps, dc=dc, c0=c0, w=w, col=col: e.scalar_tensor_tensor(
                        out=self.X[:, dc, c0:c0 + w], in0=ps[:, 0:w], scalar=self.MOD[:, 40 + dc, col:col + 1],
                        in1=self.X[:, dc, c0:c0 + w], op0=ALU.mult, op1=ALU.add),
                        reads=[self.rMOD], writes=[rps, self.rX[dc][ti]])

    def mla(self, i, Wd, need_ctx):
        S = self.S
        off, _ = vec_offsets(i)
        V = lambda name, k=0: self.VEC[:, off[name] + k:off[name] + k + 1]
        V96 = lambda name: self.VEC[0:96, off[name]:off[name] + 1]
        all_tiles = [0, 1, 2, 3, 4]
        qtiles = all_tiles if need_ctx else [1, 2, 3, 4]
        att_scale = float(96 ** -0.5)
        self.arena_reset()
        CQ = self.alloc(2 * T).rearrange("p (c t) -> p c t", t=T)
        CKV = self.alloc(T)
        KPE = self.alloc(T)
        ROPE = self.alloc(2 * T, F32).rearrange("p (c t) -> p c t", t=T)
        rCQ = [Res() for _ in TILES]
        rCKV = [Res() for _ in TILES]
        rKPE = [Res() for _ in TILES]
        rROPE = Res()
        SQh = self.alloc(512)
        rSQh = Res()
        SQh2 = self.alloc(512)
        T1b = self.alloc(512, F32)
        T2b = self.alloc(512, F32)
        SETS = [dict(sq=SQh, rsq=rSQh, rs=self.RSTD, rrs=self.rRSTD, t1=self.TMP[0], rt1=self.rTMP[0], t2=self.TMP[1], rt2=self.rTMP[1]),
                dict(sq=SQh2, rsq=Res(), rs=self.RSTD2, rrs=self.rRSTD2, t1=T1b, rt1=Res(), t2=T2b, rt2=Res())]
        set_i = [0]

        def next_set():
            set_i[0] ^= 1
            return SETS[set_i[0]]
        mark = self.a_off
        S.op("sp", lambda e: e.dma_start(out=ROPE[:, :, :], in_=Wd["rope"][:, :, :]), writes=[rROPE], dma=True)
        WDQ = self.alloc(KC * 256).rearrange("p (c n) -> p c n", n=256)
        WKV = self.alloc(KC * 320).rearrange("p (c n) -> p c n", n=320)
        CQF = self.alloc(2 * 512, F32).rearrange("p (c n) -> p c n", n=512)
        SQb = self.alloc(2 * 512).rearrange("p (c n) -> p c n", n=512)
        rWDQ, rWKV, rCQF, rSQb = Res(), Res(), Res(), Res()
        self.load_w(WDQ, Wd["dq"].rearrange("(c p) n -> p c n", p=128), rWDQ)
        self.load_w(WKV, Wd["kv"].rearrange("(c p) n -> p c n", p=128), rWKV)

        def rope_norm(ps_raw, rps_raw, ps_perm, rps_perm, g, gp, out_ap, rout, lo, hi, c0, w):
            st_ = next_set()
            SQh, rSQh, RS, rRS = st_["sq"], st_["rsq"], st_["rs"], st_["rrs"]
            S.op("act", lambda e: e.activation(out=SQh[0:96, 0:w], in_=ps_raw[0:96, 0:w], func=AF.Square),
                 reads=[], writes=[rps_raw, rSQh])
            pss, rpss = self.psum()
            S.op("pe", lambda e: e.matmul(pss[0:96, 0:w], lhsT=self.BD[0:96, 0:96], rhs=SQh[0:96, 0:w], start=True, stop=True),
                 reads=[rSQh, self.rID], writes=[rpss])
            S.op("act", lambda e: e.activation(out=RS[0:96, 0:w], in_=pss[0:96, 0:w], func=AF.Ln,
                                               scale=V96("mla_invn"), bias=self.EPSB[0:96, 0:1]),
                 reads=[self.rVEC, self.rONES], writes=[rpss, rRS])
            S.op("act", lambda e: e.activation(out=RS[0:96, 0:w], in_=RS[0:96, 0:w], func=AF.Exp, scale=-0.5), reads=[], writes=[rRS])
            t1, rt1 = st_["t1"], st_["rt1"]
            t2, rt2 = st_["t2"], st_["rt2"]
            S.op("dve", lambda e: e.scalar_tensor_tensor(out=t1[0:96, 0:w], in0=ps_raw[0:96, 0:w], scalar=g, in1=ROPE[0:96, 0, c0:c0 + w],
                                                         op0=ALU.mult, op1=ALU.mult),
                 reads=[self.rVEC, rROPE], writes=[rps_raw, rt1])
            S.op("dve", lambda e: e.scalar_tensor_tensor(out=t2[0:96, 0:w], in0=ps_perm[0:96, 0:w], scalar=gp, in1=ROPE[0:96, 1, c0:c0 + w],
                                                         op0=ALU.mult, op1=ALU.mult),
                 reads=[self.rVEC, rROPE], writes=[rps_perm, rt2])
            S.op("pool", lambda e: e.tensor_tensor(out=t1[0:96, 0:w], in0=t1[0:96, 0:w], in1=t2[0:96, 0:w], op=ALU.add),
                 reads=[rt2], writes=[rt1])
            S.op("dve", lambda e: e.tensor_tensor(out=out_ap, in0=t1[lo:hi, 0:w], in1=RS[lo:hi, 0:w], op=ALU.mult),
                 reads=[rt1, rRS], writes=[rout])

        def p1_tile(ti):
            c0, w = TILES[ti]
            rh = [self.rH[kc][ti] for kc in range(KC)]
            for oc in range(2):
                ps, rps = self.psum()

                def mm(e, ps=ps, oc=oc):
                    ins = None
                    for kc in range(KC):
                        ins = e.matmul(ps[:, 0:w], lhsT=WDQ[:, kc, oc * 128:(oc + 1) * 128], rhs=self.H[:, kc, c0:c0 + w],
                                       start=(kc == 0), stop=(kc == KC - 1))
                    return ins
                S.op("pe", mm, reads=[rWDQ] + rh, writes=[rps])
                S.op("act", lambda e, ps=ps, oc=oc: e.copy(out=CQF[:, oc, 0:w], in_=ps[:, 0:w]), reads=[], writes=[rps, rCQF])
            S.op("act", lambda e: e.activation(out=SQb[:, :, 0:w], in_=CQF[:, :, 0:w], func=AF.Square), reads=[rCQF], writes=[rSQb])
            ps, rps = self.psum()

            def mm(e, ps=ps):
                e.matmul(ps[:, 0:w], lhsT=self.ONES[:, :], rhs=SQb[:, 0, 0:w], start=True, stop=False)
                return e.matmul(ps[:, 0:w], lhsT=self.ONES[:, :], rhs=SQb[:, 1, 0:w], start=False, stop=True)
            S.op("pe", mm, reads=[rSQb, self.rONES], writes=[rps])
            S.op("act", lambda e, ps=ps: e.activation(out=self.RSTD[:, 0:w], in_=ps[:, 0:w], func=AF.Ln, scale=1.0 / 256, bias=self.EPSB[:, 0:1]),
                 reads=[self.rONES], writes=[rps, self.rRSTD])
            S.op("act", lambda e: e.activation(out=self.RSTD[:, 0:w], in_=self.RSTD[:, 0:w], func=AF.Exp, scale=-0.5), reads=[], writes=[self.rRSTD])
            for oc in range(2):
                S.op("dve", lambda e, oc=oc: e.scalar_tensor_tensor(out=CQ[:, oc, c0:c0 + w], in0=CQF[:, oc, 0:w], scalar=V("mla_qn", oc),
                                                                 in1=self.RSTD[:, 0:w], op0=ALU.mult, op1=ALU.mult),
                     reads=[rCQF, self.rVEC, self.rRSTD], writes=[rCQ[ti]])
            ps, rps = self.psum()

            def mm(e, ps=ps):
                ins = None
                for kc in range(KC):
                    ins = e.matmul(ps[:, 0:w], lhsT=WKV[:, kc, 0:128], rhs=self.H[:, kc, c0:c0 + w], start=(kc == 0), stop=(kc == KC - 1))
                return ins
            S.op("pe", mm, reads=[rWKV] + rh, writes=[rps])
            S.op("act", lambda e, ps=ps: e.copy(out=CQF[:, 0, 0:w], in_=ps[:, 0:w]), reads=[], writes=[rps, rCQF])
            S.op("act", lambda e: e.activation(out=SQb[:, 0, 0:w], in_=CQF[:, 0, 0:w], func=AF.Square), reads=[rCQF], writes=[rSQb])
            ps, rps = self.psum()
            S.op("pe", lambda e, ps=ps: e.matmul(ps[:, 0:w], lhsT=self.ONES[:, :], rhs=SQb[:, 0, 0:w], start=True, stop=True),
                 reads=[rSQb, self.rONES], writes=[rps])
            S.op("act", lambda e, ps=ps: e.activation(out=self.RSTD[:, 0:w], in_=ps[:, 0:w], func=AF.Ln, scale=1.0 / 128, bias=self.EPSB[:, 0:1]),
                 reads=[self.rONES], writes=[rps, self.rRSTD])
            S.op("act", lambda e: e.activation(out=self.RSTD[:, 0:w], in_=self.RSTD[:, 0:w], func=AF.Exp, scale=-0.5), reads=[], writes=[self.rRSTD])
            S.op("dve", lambda e: e.scalar_tensor_tensor(out=CKV[:, c0:c0 + w], in0=CQF[:, 0, 0:w], scalar=V("mla_kvn"),
                                                         in1=self.RSTD[:, 0:w], op0=ALU.mult, op1=ALU.mult),
                 reads=[rCQF, self.rVEC, self.rRSTD], writes=[rCKV[ti]])
            pss = []
            for var in range(2):
                ps, rps = self.psum()
                pss.append((ps, rps))

                def mm(e, ps=ps, var=var):
                    ins = None
                    for kc in range(KC):
                        ins = e.matmul(ps[0:96, 0:w], lhsT=WKV[:, kc, 128 + var * 96:224 + var * 96], rhs=self.H[:, kc, c0:c0 + w],
                                       start=(kc == 0), stop=(kc == KC - 1))
                    return ins
                S.op("pe", mm, reads=[rWKV] + rh, writes=[rps])
            rope_norm(pss[0][0], pss[0][1], pss[1][0], pss[1][1], V96("mla_gk"), V96("mla_gkp"),
                      KPE[64:96, c0:c0 + w], rKPE[ti], 64, 96, c0, w)

        for ti in all_tiles:
            p1_tile(ti)
        if DBG_STOP == 1:
            S.barrier()
            return

        S.barrier()
        self.arena_reset(mark)
        WUKV = self.alloc(2048)
        rWUKV = Res()
        self.load_w(WUKV, Wd["ukv"][:, :], rWUKV)
        uq_v = Wd["uq"].rearrange("(c p) n -> p c n", p=128)
        WUQh = [self.alloc(2 * 192).rearrange("p (c n) -> p c n", n=192) for _ in range(2)]
        rWUQh = [Res(), Res()]
        QH = [self.alloc(T) for _ in range(2)]
        KH = [self.alloc(T) for _ in range(2)]
        rQH = [[Res() for _ in TILES] for _ in range(2)]
        rKH = [[Res() for _ in TILES] for _ in range(2)]
        rKHpe = [Res(), Res()]
        VH = [self.alloc(18 * 128).rearrange("p (j n) -> p j n", n=128) for _ in range(2)]
        rVH = [Res(), Res()]
        NPT = 6
        PT = [self.alloc(512) for _ in range(NPT)]
        rPT = [Res() for _ in range(NPT)]
        RD = self.MU
        rRD = self.rMU
        pt_i = 0
        S.op("pool", lambda e: e.memset(VH[0][:, :, 64:128], 1.0), writes=[rVH[0]])
        S.op("pool", lambda e: e.memset(VH[1][:, :, 0:64], 1.0), writes=[rVH[1]])

        def kcols(j):
            return (C0 + 128 * j) if j < 2 else (L0 + 128 * (j - 2))

        def ktile(j):
            return 0 if j < 2 else 1 + (j - 2) // 4

        def prep(h):
            s = h % 2
            voff = 0 if s == 0 else 64
            wq, rwq = WUQh[s], rWUQh[s]
            self.load_w(wq, uq_v[:, :, h * 192:(h + 1) * 192], rwq)
            for ti in qtiles:
                c0, w = TILES[ti]
                pss = []
                for var in range(2):
                    ps, rps = self.psum()
                    pss.append((ps, rps))

                    def mm(e, ps=ps, var=var, c0=c0, w=w):
                        e.matmul(ps[0:96, 0:w], lhsT=wq[:, 0, var * 96:var * 96 + 96], rhs=CQ[:, 0, c0:c0 + w], start=True, stop=False)
                        return e.matmul(ps[0:96, 0:w], lhsT=wq[:, 1, var * 96:var * 96 + 96], rhs=CQ[:, 1, c0:c0 + w], start=False, stop=True)
                    S.op("pe", mm, reads=[rwq, rCQ[ti]], writes=[rps])
                rope_norm(pss[0][0], pss[0][1], pss[1][0], pss[1][1], V96("mla_gq"), V96("mla_gqp"),
                          QH[s][0:96, c0:c0 + w], rQH[s][ti], 0, 96, c0, w)
            S.op("pool", lambda e: e.tensor_copy(out=KH[s][64:96, :], in_=KPE[64:96, :]), reads=rKPE, writes=[rKHpe[s]])
            for ti in all_tiles:
                c0, w = TILES[ti]
                st_ = next_set()
                SQk, rSQk, RS, rRS = st_["sq"], st_["rsq"], st_["rs"], st_["rrs"]
                ps, rps = self.psum()
                S.op("pe", lambda e, ps=ps, c0=c0, w=w: e.matmul(ps[0:64, 0:w], lhsT=WUKV[:, h * 128:h * 128 + 64], rhs=CKV[:, c0:c0 + w], start=True, stop=True),
                     reads=[rWUKV, rCKV[ti]], writes=[rps])
                S.op("act", lambda e, ps=ps, w=w, SQk=SQk: e.activation(out=SQk[0:64, 0:w], in_=ps[0:64, 0:w], func=AF.Square), reads=[], writes=[rps, rSQk])
                ps2, rps2 = self.psum()
                S.op("pe", lambda e, ps2=ps2, w=w, SQk=SQk: e.matmul(ps2[0:64, 0:w], lhsT=self.BD[0:64, 0:64], rhs=SQk[0:64, 0:w], start=True, stop=True),
                     reads=[rSQk, self.rID], writes=[rps2])
                S.op("act", lambda e, ps2=ps2, w=w, RS=RS: e.activation(out=RS[0:64, 0:w], in_=ps2[0:64, 0:w], func=AF.Ln, scale=1.0 / 64, bias=self.EPSB[0:64, 0:1]),
                     reads=[self.rONES], writes=[rps2, rRS])
                S.op("act", lambda e, w=w, RS=RS: e.activation(out=RS[0:64, 0:w], in_=RS[0:64, 0:w], func=AF.Exp, scale=-0.5), reads=[], writes=[rRS])
                S.op("dve", lambda e, ps=ps, c0=c0, w=w, RS=RS: e.scalar_tensor_tensor(out=KH[s][0:64, c0:c0 + w], in0=ps[0:64, 0:w], scalar=self.VEC[0:64, off["mla_gk"]:off["mla_gk"] + 1],
                                                                             in1=RS[0:64, 0:w], op0=ALU.mult, op1=ALU.mult),
                     reads=[self.rVEC, rRS], writes=[rps, rKH[s][ti]])
            for j0 in range(0, 18, 8):
                n = min(8, 18 - j0)
                ps, rps = self.psum()

                def mm(e, ps=ps, j0=j0, n=n):
                    ins = None
                    for jj in range(n):
                        kc0 = kcols(j0 + jj)
                        ins = e.matmul(ps[:, jj * 64:(jj + 1) * 64], lhsT=CKV[:, kc0:kc0 + 128], rhs=WUKV[:, h * 128 + 64:(h + 1) * 128], start=True, stop=True)
                    return ins
                S.op("pe", mm, reads=[rWUKV] + rCKV, writes=[rps])
                S.op("dve", lambda e, ps=ps, j0=j0, n=n: e.tensor_copy(out=VH[s][:, j0:j0 + n, voff:voff + 64], in_=ps[:, 0:n * 64].rearrange("p (j d) -> p j d", d=64)),
                     reads=[], writes=[rps, rVH[s]])

        def attend(h):
            nonlocal pt_i
            s = h % 2
            ch = h // 2
            items = []
            for ti in qtiles:
                keys = list(range(18)) if ti > 0 else [0, 1]
                for j in keys:
                    items.append((ti, j, j == keys[0], j == keys[-1]))
            LA = 2
            qk = {}

            def issue_qk(idx):
                ti, j, _, _ = items[idx]
                c0, w = TILES[ti]
                kc0 = kcols(j)
                ps, rps = self.psum()
                S.op("pe", lambda e, ps=ps: e.matmul(ps[:, 0:w], lhsT=KH[s][0:96, kc0:kc0 + 128], rhs=QH[s][0:96, c0:c0 + w], start=True, stop=True),
                     reads=[rKH[s][ktile(j)], rKHpe[s], rQH[s][ti]], writes=[rps])
                qk[idx] = (ps, rps)
            for idx in range(min(LA, len(items))):
                issue_qk(idx)
            pso, rpso = None, None
            for idx, (ti, j, first, lastk) in enumerate(items):
                c0, w = TILES[ti]
                if idx + LA < len(items):
                    issue_qk(idx + LA)
                if first:
                    pso, rpso = self.psum_acc()
                ps, rps = qk.pop(idx)
                pt, rpt = PT[pt_i], rPT[pt_i]
                pt_i = (pt_i + 1) % NPT
                S.op("act", lambda e, ps=ps, pt=pt, w=w: e.activation(out=pt[:, 0:w], in_=ps[:, 0:w], func=AF.Exp, scale=att_scale),
                     reads=[], writes=[rps, rpt])
                S.op("pe", lambda e, pt=pt, j=j, w=w, pso=pso, first=first, lastk=lastk: e.matmul(
                    pso[:, 0:w], lhsT=VH[s][:, j, :], rhs=pt[:, 0:w], start=first, stop=lastk),
                    reads=[rpt, rVH[s]], writes=[rpso])
                if lastk:
                    if s == 0:
                        S.op("dve", lambda e, pso=pso, w=w: e.reciprocal(out=RD[64:128, 0:w], in_=pso[64:128, 0:w]), reads=[], writes=[rpso, rRD])
                        S.op("dve", lambda e, pso=pso, c0=c0, w=w: e.tensor_tensor(out=self.H[0:64, ch, c0:c0 + w], in0=pso[0:64, 0:w], in1=RD[64:128, 0:w], op=ALU.mult),
                             reads=[rRD], writes=[rpso, self.rH[ch][ti]])
                    else:
                        S.op("dve", lambda e, pso=pso, w=w: e.reciprocal(out=RD[0:64, 0:w], in_=pso[0:64, 0:w]), reads=[], writes=[rpso, rRD])
                        S.op("dve", lambda e, pso=pso, c0=c0, w=w: e.tensor_tensor(out=self.H[64:128, ch, c0:c0 + w], in0=pso[64:128, 0:w], in1=RD[0:64, 0:w], op=ALU.mult),
                             reads=[rRD], writes=[rpso, self.rH[ch][ti]])

        NH = 16 if DBG_HEADS is None else DBG_HEADS
        prep(0)
        for h in range(NH):
            if h + 1 < NH:
                prep(h + 1)
            attend(h)
        if DBG_STOP == 2:
            S.barrier()
            return

        S.barrier()
        self.arena_reset()
        WO = self.alloc(KC * 1024).rearrange("p (c n) -> p c n", n=1024)
        rWO = Res()
        wo_v = Wd["wo"].rearrange("(c p) n -> p c n", p=128)
        rWOh = [Res(), Res()]
        for hh in range(2):
            self.load_w(WO[:, hh * 4:(hh + 1) * 4, :], wo_v[:, hh * 4:(hh + 1) * 4, :], rWOh[hh])
        for ti in qtiles:
            c0, w = TILES[ti]
            col = 1 if ti == 0 else 0
            for dc in range(KC):
                ps, rps = self.psum()

                def mm(e, ps=ps, dc=dc, c0=c0, w=w):
                    ins = None
                    for c in range(KC):
                        ins = e.matmul(ps[:, 0:w], lhsT=WO[:, c, dc * 128:(dc + 1) * 128], rhs=self.H[:, c, c0:c0 + w], start=(c == 0), stop=(c == KC - 1))
                    return ins
                S.op("pe", mm, reads=rWOh + [self.rH[c][ti] for c in range(KC)], writes=[rps])
                S.op("dve", lambda e, ps=ps, dc=dc, c0=c0, w=w, col=col: e.scalar_tensor_tensor(
                    out=self.X[:, dc, c0:c0 + w], in0=ps[:, 0:w], scalar=self.MOD[:, 16 + dc, col:col + 1],
                    in1=self.X[:, dc, c0:c0 + w], op0=ALU.mult, op1=ALU.add),
                    reads=[self.rMOD], writes=[rps, self.rX[dc][ti]])

    def hyena(self, i, Wd, b):
        S = self.S
        off, _ = vec_offsets(i)
        V = lambda name, k=0: self.VEC[:, off[name] + k:off[name] + k + 1]
        V64 = lambda name: self.VEC[0:64, off[name]:off[name] + 1]
        tiles = [0, 1, 2, 3, 4]
        lo, hi = C0, L0 + L
        PI = float(np.pi)
        seqs = [("l", L, 16, L0, 0), ("c", LC, 2, C0, 16)]
        self.arena_reset()
        VX = self.alloc(KC * T).rearrange("p (c t) -> p c t", t=T)
        rVX = [Res() for _ in range(KC)]
        AROW = self.alloc(T)
        CROW = self.alloc(T, F32)
        X1ROW = self.alloc(T, F32)
        X0ROW = self.alloc(T)
        rA, rC, rX1, rX0R = Res(), Res(), Res(), Res()
        WA = [self.alloc(KC * 128).rearrange("p (c n) -> p c n", n=128) for _ in range(2)]
        rWA = [Res(), Res()]
        wa_i = [0]
        S.op("pool", lambda e: e.memset(AROW[:, :], 0.0), writes=[rA])
        S.op("pool", lambda e: e.memset(X0ROW[:, :], 0.0), writes=[rX0R])
        w_in = Wd["hy_in"].rearrange("(c p) n -> p c n", p=128)
        bi, wsh, bsh = off["hy_b_in"], off["hy_w_short"], off["hy_b_short"]

        def proj_conv(oc, dst, rdst):
            k = wa_i[0]
            wa_i[0] = 1 - k
            wa, rwa = WA[k], rWA[k]
            self.load_w(wa, w_in[:, :, oc * 128:(oc + 1) * 128], rwa)
            for ti in tiles:
                c0, w = TILES[ti]
                ps, rps = self.psum()

                def mm(e, ps=ps, c0=c0, w=w):
                    ins = None
                    for kc in range(KC):
                        ins = e.matmul(ps[:, 0:w], lhsT=wa[:, kc, :], rhs=self.H[:, kc, c0:c0 + w], start=(kc == 0), stop=(kc == KC - 1))
                    return ins
                S.op("pe", mm, reads=[rwa] + [self.rH[kc][ti] for kc in range(KC)], writes=[rps])
                S.op("act", lambda e, ps=ps, c0=c0, w=w: e.activation(out=AROW[:, c0:c0 + w], in_=ps[:, 0:w], func=AF.Identity,
                                                                   bias=self.VEC[:, bi + oc:bi + oc + 1], scale=1.0),
                     reads=[self.rVEC], writes=[rps, rA])
            S.op("dve", lambda e: e.tensor_scalar(out=CROW[:, lo:hi], in0=AROW[:, lo:hi], scalar1=self.VEC[:, wsh + 24 + oc:wsh + 25 + oc],
                                                  scalar2=self.VEC[:, bsh + oc:bsh + oc + 1], op0=ALU.mult, op1=ALU.add),
                 reads=[rA, self.rVEC], writes=[rC])
            S.op("dve", lambda e: e.scalar_tensor_tensor(out=CROW[:, lo:hi], in0=AROW[:, lo - 1:hi - 1], scalar=self.VEC[:, wsh + oc:wsh + oc + 1],
                                                         in1=CROW[:, lo:hi], op0=ALU.mult, op1=ALU.add),
                 reads=[rA, self.rVEC], writes=[rC])
            S.op("dve", lambda e: e.scalar_tensor_tensor(out=dst[:, lo:hi], in0=AROW[:, lo + 1:hi + 1], scalar=self.VEC[:, wsh + 48 + oc:wsh + 49 + oc],
                                                         in1=CROW[:, lo:hi], op0=ALU.mult, op1=ALU.add),
                 reads=[rA, self.rVEC, rC], writes=[rdst])

        x0d = Wd["x0d"]
        for c in range(KC):
            proj_conv(c, X0ROW, rX0R)
            S.op("sp", lambda e, c=c: e.dma_start(out=x0d[c * 128:(c + 1) * 128, :], in_=X0ROW[:, :]),
                 reads=[rX0R], writes=[self.rX0D[c]], dma=True)
            proj_conv(KC + c, X1ROW, rX1)
            proj_conv(2 * KC + c, CROW, rC)
            S.op("dve", lambda e, c=c: e.tensor_tensor(out=VX[:, c, lo:hi], in0=CROW[:, lo:hi], in1=X1ROW[:, lo:hi], op=ALU.mult),
                 reads=[rC, rX1], writes=[rVX[c]])
        VT = self.H[:, :, :].rearrange("p c t -> p (c t)")[:, 0:18 * D].rearrange("p (j n) -> p j n", n=D)
        rVT = Res()
        allH = [r for row in self.rH for r in row]
        for (tag, Lx, nt, col0, vt0) in seqs:
            for jt in range(nt):
                ps, rps = self.psum()
                psb = ps[:, :].bitcast(BF16)

                def tr(e, psb=psb, jt=jt, col0=col0):
                    ins = None
                    for c in range(KC):
                        ins = e.transpose(out=psb[:, c * 128:(c + 1) * 128], in_=VX[:, c, col0 + jt * 128:col0 + (jt + 1) * 128], identity=self.IDENT[:, :])
                    return ins
                S.op("pe", tr, reads=rVX + [self.rID], writes=[rps])
                S.op("act", lambda e, psb=psb, jt=jt, vt0=vt0: e.copy(out=VT[:, vt0 + jt, :], in_=psb[:, 0:D]),
                     reads=[], writes=[rps, rVT] + (allH if (jt == 0 and tag == "l") else []))
        S.barrier()
        if b == 0:
            for (tag, Lx, nt, col0, vt0) in seqs:
                self.hy_filter(i, Wd, tag, Lx, nt)
                S.barrier()
        w_out = Wd["hy_out"]
        for q in range(4):
            self.arena_reset()
            X0q = self.alloc(2 * T).rearrange("p (c t) -> p c t", t=T)
            rX0q = Res()
            WOq = self.alloc(2 * D).rearrange("p (c n) -> p c n", n=D)
            rWOq = Res()
            S.op("sp", lambda e, q=q: e.dma_start(out=X0q[:, :, :], in_=x0d[q * 256:(q + 1) * 256, :].rearrange("(c p) t -> p c t", p=128)),
                 reads=self.rX0D, writes=[rX0q], dma=True)
            self.load_w(WOq, w_out[q * 256:(q + 1) * 256, :].rearrange("(c p) n -> p c n", p=128), rWOq)
            Z = self.alloc(2 * 256).rearrange("p (c n) -> p c n", n=256)
            rZ = Res()
            KSb = [self.alloc(2 * 256, F32).rearrange("p (c n) -> p c n", n=256) for _ in range(2)]
            rKSb = [Res(), Res()]
            SC = [self.MU[:, 0:256], self.RSTD2[:, 0:256]]
            rSC = [self.rMU, self.rRSTD2]
            mark = self.a_off
            for (tag, Lx, nt, col0, vt0) in seqs:
                S.barrier()
                self.arena_reset(mark)
                self.hy_conv(i, Wd, q, tag, Lx, nt, col0, vt0, VT, rVT, X0q, rX0q, WOq, rWOq, Z, rZ, KSb, rKSb, SC, rSC)
            S.barrier()

    def hy_filter(self, i, Wd, tag, Lx, nt):
        S = self.S
        off, _ = vec_offsets(i)
        V64 = lambda name: self.VEC[0:64, off[name]:off[name] + 1]
        PI = float(np.pi)
        self.arena_reset()
        ZT = self.alloc(Lx, F32)
        HD1 = self.alloc(Lx, F32)
        HD2 = self.alloc(Lx, F32)
        FW1 = self.alloc(64, F32)
        FW2 = self.alloc(64, F32)
        NEGT = self.alloc(16, F32)
        CST = self.alloc(8, F32)
        ONESF = self.alloc(128, F32)
        rZT, rHD1, rHD2, rFW, rNEGT, rCST, rONESF = (Res() for _ in range(7))
        S.op("sp", lambda e: e.dma_start(out=ZT[0:33, :], in_=Wd[f"z_{tag}"][:, :]), writes=[rZT], dma=True)
        S.op("sp", lambda e: e.dma_start(out=FW1[0:33, :], in_=Wd["fw1"][:, :]), writes=[rFW], dma=True)
        S.op("sp", lambda e: e.dma_start(out=FW2[0:64, :], in_=Wd["fw2"][:, :]), writes=[rFW], dma=True)
        S.op("sp", lambda e: e.dma_start(out=NEGT[:, 0:nt], in_=Wd[f"negt_{tag}"][:, :]), writes=[rNEGT], dma=True)
        S.op("pool", lambda e: e.memset(ONESF[:, :], 1.0), writes=[rONESF])
        S.op("pool", lambda e: e.memset(CST[:, 2:3], -PI), writes=[rCST])
        S.op("dve", lambda e: e.tensor_tensor(out=CST[0:64, 0:1], in0=V64("hy_fb1"), in1=V64("hy_fr"), op=ALU.mult), reads=[self.rVEC], writes=[rCST])
        S.op("dve", lambda e: e.tensor_tensor(out=CST[0:64, 1:2], in0=V64("hy_fb2"), in1=V64("hy_fr"), op=ALU.mult), reads=[self.rVEC], writes=[rCST])

        RR, rRR = self.MU, self.rMU

        def sin_layer(src, rsrc, kdim, wmat, cstcol, dst, rdst):
            for t0 in range(0, Lx, 512):
                w = min(512, Lx - t0)
                ps, rps = self.psum()
                S.op("pe", lambda e, ps=ps, t0=t0, w=w: e.matmul(ps[0:64, 0:w], lhsT=wmat[0:kdim, 0:64], rhs=src[0:kdim, t0:t0 + w], start=True, stop=True),
                     reads=[rFW, rsrc], writes=[rps])
                S.op("act", lambda e, ps=ps, t0=t0, w=w: e.activation(out=dst[0:64, t0:t0 + w], in_=ps[0:64, 0:w], func=AF.Identity,
                                                                   scale=V64("hy_fr"), bias=CST[0:64, cstcol:cstcol + 1]),
                     reads=[self.rVEC, rCST], writes=[rps, rdst])
                MAGIC = 12582912.0
                S.op("dve", lambda e, t0=t0, w=w: e.tensor_scalar(out=RR[0:64, 0:w], in0=dst[0:64, t0:t0 + w], scalar1=1.0 / (2.0 * PI), scalar2=MAGIC,
                                                                op0=ALU.mult, op1=ALU.add),
                     reads=[rdst], writes=[rRR])
                S.op("dve", lambda e, w=w: e.tensor_scalar(out=RR[0:64, 0:w], in0=RR[0:64, 0:w], scalar1=MAGIC, scalar2=-2.0 * PI,
                                                         op0=ALU.subtract, op1=ALU.mult),
                     reads=[], writes=[rRR])
                S.op("dve", lambda e, t0=t0, w=w: e.tensor_tensor(out=dst[0:64, t0:t0 + w], in0=dst[0:64, t0:t0 + w], in1=RR[0:64, 0:w], op=ALU.add),
                     reads=[rRR], writes=[rdst])
                S.op("dve", lambda e, t0=t0, w=w: e.tensor_scalar(out=dst[0:64, t0:t0 + w], in0=dst[0:64, t0:t0 + w], scalar1=-PI, scalar2=PI,
                                                                op0=ALU.max, op1=ALU.min),
                     reads=[], writes=[rdst])
                S.op("act", lambda e, t0=t0, w=w: e.activation(out=dst[0:64, t0:t0 + w], in_=dst[0:64, t0:t0 + w], func=AF.Sin, scale=1.0),
                     reads=[], writes=[rdst])
        sin_layer(ZT, rZT, 33, FW1, 0, HD1, rHD1)
        sin_layer(HD1, rHD1, 64, FW2, 1, HD2, rHD2)
        mark = self.a_off
        kspec = Wd[f"kspec_{tag}"]
        dfm = {0: Wd[f"dfc_{tag}"], 1: Wd[f"dfs_{tag}"]}
        nf = nt
        for q in range(4):
            S.barrier()
            self.arena_reset(mark)
            A = self.alloc(nt * 256).rearrange("p (j n) -> p j n", n=256)
            Bm = self.alloc(nt * 256).rearrange("p (j n) -> p j n", n=256)
            rAB = Res()
            FW3 = self.alloc(512, F32)
            ABSD = self.alloc(256, F32)
            SKq = self.alloc(256, F32)
            RN = self.alloc(256, F32)
            rFW3, rABSD, rSK, rRN = Res(), Res(), Res(), Res()
            DEC = self.alloc(256, F32)
            KF = self.alloc(256, F32)
            KB = self.alloc(256, F32)
            AF_ = self.alloc(256, F32)
            AB_ = self.alloc(256, F32)
            rDEC, rKF, rKB, rAF, rABb = (Res() for _ in range(5))
            DF = [[self.alloc(nt * 128).rearrange("p (j n) -> p j n", n=128) for _ in range(2)] for _ in range(2)]
            rDF = [[Res(), Res()], [Res(), Res()]]
            KS4 = [[self.alloc(256, F32) for _ in range(2)] for _ in range(2)]
            rKS4 = [[Res(), Res()], [Res(), Res()]]
            S.op("sp", lambda e, q=q: e.dma_start(out=FW3[0:64, 0:256], in_=Wd["fw3"][:, q * 256:(q + 1) * 256]), writes=[rFW3], dma=True)
            S.op("sp", lambda e, q=q: e.dma_start(out=FW3[0:64, 256:512], in_=Wd["fw3"][:, D + q * 256:D + (q + 1) * 256]), writes=[rFW3], dma=True)
            S.op("sp", lambda e, q=q: e.dma_start(out=ABSD[:, :], in_=Wd["absd"][:, q * 256:(q + 1) * 256]), writes=[rABSD], dma=True)
            S.op("sp", lambda e, q=q: e.dma_start(out=SKq[:, :], in_=Wd["skipb"][:, q * 256:(q + 1) * 256]), writes=[rSK], dma=True)
            psS, rpsS = self.psum_acc()
            for j in range(nt):
                ps, rps = self.psum()
                S.op("pe", lambda e, ps=ps, j=j: e.matmul(ps[:, 0:512], lhsT=HD2[0:64, j * 128:(j + 1) * 128], rhs=FW3[0:64, 0:512], start=True, stop=True),
                     reads=[rHD2, rFW3], writes=[rps])
                S.op("act", lambda e, j=j: e.activation(out=DEC[:, :], in_=ABSD[:, :], func=AF.Exp, scale=NEGT[:, j:j + 1]),
                     reads=[rABSD, rNEGT], writes=[rDEC])
                S.op("dve", lambda e, ps=ps: e.tensor_tensor(out=KF[:, :], in0=ps[:, 0:256], in1=DEC[:, :], op=ALU.mult), reads=[rDEC], writes=[rps, rKF])
                S.op("dve", lambda e, ps=ps: e.tensor_tensor(out=KB[:, :], in0=ps[:, 256:512], in1=DEC[:, :], op=ALU.mult), reads=[rDEC], writes=[rps, rKB])
                if j == 0:
                    S.op("pool", lambda e: e.memset(KB[0:1, :], 0.0), writes=[rKB])
                S.op("pool", lambda e, j=j: e.tensor_tensor(out=A[:, j, :], in0=KF[:, :], in1=KB[:, :], op=ALU.add), reads=[rKF, rKB], writes=[rAB])
                S.op("pool", lambda e, j=j: e.tensor_tensor(out=Bm[:, j, :], in0=KF[:, :], in1=KB[:, :], op=ALU.subtract), reads=[rKF, rKB], writes=[rAB])
                S.op("act", lambda e: e.activation(out=AF_[:, :], in_=KF[:, :], func=AF.Abs), reads=[rKF], writes=[rAF])
                S.op("act", lambda e: e.activation(out=AB_[:, :], in_=KB[:, :], func=AF.Abs), reads=[rKB], writes=[rABb])
                S.op("pe", lambda e, j=j: e.matmul(psS[:, 0:256], lhsT=ONESF[:, :], rhs=AF_[:, :], start=(j == 0), stop=False), reads=[rONESF, rAF], writes=[rpsS])
                S.op("pe", lambda e, j=j: e.matmul(psS[:, 0:256], lhsT=ONESF[:, :], rhs=AB_[:, :], start=False, stop=(j == nt - 1)), reads=[rONESF, rABb], writes=[rpsS])
            S.op("dve", lambda e: e.tensor_scalar_add(out=RN[:, :], in0=psS[:, 0:256], scalar1=EPS), reads=[], writes=[rpsS, rRN])
            S.op("dve", lambda e: e.reciprocal(out=RN[:, :], in_=RN[:, :]), reads=[], writes=[rRN])
            for fb in range(nf):
                for typ in range(2):
                    df, rdf = DF[fb % 2][typ], rDF[fb % 2][typ]
                    S.op("sp", lambda e, fb=fb, typ=typ, df=df: e.dma_start(out=df[:, :, :], in_=dfm[typ][fb].rearrange("p (j n) -> p j n", n=128)), writes=[rdf], dma=True)
                    ps, rps = self.psum()
                    src = A if typ == 0 else Bm

                    def mm(e, ps=ps, typ=typ, src=src, df=df):
                        ins = None
                        for j in range(nt):
                            ins = e.matmul(ps[:, 0:256], lhsT=df[:, j, :], rhs=src[:, j, :], start=(j == 0), stop=(j == nt - 1))
                        return ins
                    S.op("pe", mm, reads=[rdf, rAB], writes=[rps])
                    ks, rks = KS4[fb % 2][typ], rKS4[fb % 2][typ]
                    S.op("dve", lambda e, ps=ps, ks=ks: e.tensor_tensor(out=ks[:, :], in0=ps[:, 0:256], in1=RN[:, :], op=ALU.mult), reads=[rRN], writes=[rps, rks])
                    if typ == 0:
                        S.op("pool", lambda e, ks=ks: e.tensor_tensor(out=ks[:, :], in0=ks[:, :], in1=SKq[:, :], op=ALU.add), reads=[rSK], writes=[rks])
                    S.op("act", lambda e, fb=fb, typ=typ, q=q, ks=ks: e.dma_start(out=kspec[typ, fb * 128:(fb + 1) * 128, q * 256:(q + 1) * 256], in_=ks[:, :]),
                         reads=[rks], writes=[self.rKSW[tag][(fb * 2 + typ) % 8]], dma=True)

    def hy_conv(self, i, Wd, q, tag, Lx, nt, col0, vt0, VT, rVT, X0q, rX0q, WOq, rWOq, Z, rZ, KSb, rKSb, SC, rSC):
        S = self.S
        off, _ = vec_offsets(i)
        nf = nt
        kspec = Wd[f"kspec_{tag}"]
        dfm = {0: Wd[f"dfc_{tag}"], 1: Wd[f"dfs_{tag}"]}
        idm = {0: Wd[f"idc_{tag}"], 1: Wd[f"ids_{tag}"]}
        Y = self.alloc(nf * 512).rearrange("p (f y n) -> p f y n", y=2, n=256)
        rY = Res()
        DF = [[self.alloc(nt * 128).rearrange("p (j n) -> p j n", n=128) for _ in range(2)] for _ in range(2)]
        rDF = [[Res(), Res()], [Res(), Res()]]
        IB = [self.alloc(2 * nf * 256).rearrange("p (y f n) -> p y f n", y=2, n=256) for _ in range(2)]
        rIB = [Res(), Res()]
        U = [self.TMP[0], self.TMP[1]]
        rU = [self.rTMP[0], self.rTMP[1]]
        col = 1 if tag == "c" else 0
        bo = off["hy_b_out"]
        for fb in range(nf):
            kb = fb % 2
            pss = []
            for typ in range(2):
                df, rdf = DF[kb][typ], rDF[kb][typ]
                S.op("sp", lambda e, fb=fb, typ=typ, df=df: e.dma_start(out=df[:, :, :], in_=dfm[typ][fb].rearrange("p (j n) -> p j n", n=128)), writes=[rdf], dma=True)
                ps, rps = self.psum()
                pss.append((ps, rps))

                def mm(e, ps=ps, typ=typ, df=df):
                    ins = None
                    for j in range(nt):
                        ins = e.matmul(ps[:, 0:256], lhsT=df[:, j, :], rhs=VT[:, vt0 + j, q * 256:(q + 1) * 256], start=(j == 0), stop=(j == nt - 1))
                    return ins
                S.op("pe", mm, reads=[rdf, rVT], writes=[rps])
                S.op("act", lambda e, ps=ps, typ=typ: e.copy(out=U[typ][:, 0:256], in_=ps[:, 0:256]), reads=[], writes=[rps, rU[typ]])
            S.op("sp", lambda e, fb=fb, kb=kb: e.dma_start(out=KSb[kb][:, :, :], in_=kspec[:, fb * 128:(fb + 1) * 128, q * 256:(q + 1) * 256].rearrange("y p n -> p y n")),
                 reads=[self.rKS[tag]] + self.rKSW[tag], writes=[rKSb[kb]], dma=True)
            Kc, Ks = KSb[kb][:, 0, :], KSb[kb][:, 1, :]
            S.op("dve", lambda e, Kc=Kc: e.tensor_tensor(out=SC[0][:, :], in0=U[0][:, 0:256], in1=Kc, op=ALU.mult), reads=[rU[0], rKSb[kb]], writes=[rSC[0]])
            S.op("pool", lambda e, Ks=Ks: e.tensor_tensor(out=SC[1][:, :], in0=U[1][:, 0:256], in1=Ks, op=ALU.mult), reads=[rU[1], rKSb[kb]], writes=[rSC[1]])
            S.op("dve", lambda e, fb=fb: e.tensor_tensor(out=Y[:, fb, 0, :], in0=SC[0][:, :], in1=SC[1][:, :], op=ALU.subtract), reads=[rSC[0], rSC[1]], writes=[rY])
            S.op("dve", lambda e, Ks=Ks: e.tensor_tensor(out=SC[0][:, :], in0=U[0][:, 0:256], in1=Ks, op=ALU.mult), reads=[rU[0], rKSb[kb]], writes=[rSC[0]])
            S.op("pool", lambda e, Kc=Kc: e.tensor_tensor(out=SC[1][:, :], in0=U[1][:, 0:256], in1=Kc, op=ALU.mult), reads=[rU[1], rKSb[kb]], writes=[rSC[1]])
            S.op("dve", lambda e, fb=fb: e.tensor_tensor(out=Y[:, fb, 1, :], in0=SC[0][:, :], in1=SC[1][:, :], op=ALU.add), reads=[rSC[0], rSC[1]], writes=[rY])
        for ts in range(Lx // 256):
            ib, rib = IB[ts % 2], rIB[ts % 2]
            for typ in range(2):
                S.op("sp", lambda e, ib=ib, typ=typ, ts=ts: e.dma_start(out=ib[:, typ, :, :], in_=idm[typ][ts].rearrange("p (f n) -> p f n", n=256)), writes=[rib], dma=True)
            cs = col0 + ts * 256
            ti = 0 if tag == "c" else 1 + (ts // 2)
            for cc in range(2):
                ps, rps = self.psum()

                def mm(e, ps=ps, ib=ib, cc=cc):
                    ins = None
                    for fb in range(nf):
                        for typ in range(2):
                            ins = e.matmul(ps[:, 0:256], lhsT=Y[:, fb, typ, cc * 128:(cc + 1) * 128], rhs=ib[:, typ, fb, :],
                                           start=(fb == 0 and typ == 0), stop=(fb == nf - 1 and typ == 1))
                    return ins
                S.op("pe", mm, reads=[rY, rib], writes=[rps])
                S.op("dve", lambda e, ps=ps, cc=cc, cs=cs: e.tensor_tensor(out=Z[:, cc, :], in0=ps[:, 0:256], in1=X0q[:, cc, cs:cs + 256], op=ALU.mult),
                     reads=[rX0q], writes=[rps, rZ])
            for o in range(KC):
                ps, rps = self.psum()

                def mm2(e, ps=ps, o=o):
                    e.matmul(ps[:, 0:256], lhsT=WOq[:, 0, o * 128:(o + 1) * 128], rhs=Z[:, 0, :], start=True, stop=False)
                    return e.matmul(ps[:, 0:256], lhsT=WOq[:, 1, o * 128:(o + 1) * 128], rhs=Z[:, 1, :], start=False, stop=True)
                S.op("pe", mm2, reads=[rWOq, rZ], writes=[rps])
                if q == 0:
                    S.op("dve", lambda e, ps=ps, o=o: e.tensor_scalar(out=self.RSTD[:, 0:256], in0=ps[:, 0:256], scalar1=self.VEC[:, bo + o:bo + o + 1],
                                                                   scalar2=self.MOD[:, 16 + o, col:col + 1], op0=ALU.add, op1=ALU.mult),
                         reads=[self.rVEC, self.rMOD], writes=[rps, self.rRSTD])
                    S.op("dve", lambda e, o=o, cs=cs: e.tensor_tensor(out=self.X[:, o, cs:cs + 256], in0=self.X[:, o, cs:cs + 256], in1=self.RSTD[:, 0:256], op=ALU.add),
                         reads=[self.rRSTD], writes=[self.rX[o][ti]])
                else:
                    S.op("dve", lambda e, ps=ps, o=o, cs=cs: e.scalar_tensor_tensor(out=self.X[:, o, cs:cs + 256], in0=ps[:, 0:256], scalar=self.MOD[:, 16 + o, col:col + 1],
                                                                                  in1=self.X[:, o, cs:cs + 256], op0=ALU.mult, op1=ALU.add),
                         reads=[self.rMOD], writes=[rps, self.rX[o][ti]])

    def layer(self, i, Wd, b, last):
        kind = i % 3
        tiles_all = [0, 1, 2, 3, 4]
        tiles_out = [1, 2, 3, 4] if last else tiles_all
        self.adaln(i, Wd)
        if kind == 0:
            self.modulate(0, tiles_all)
            self.S.barrier()
            self.mla(i, Wd, need_ctx=not last)
            self.S.barrier()
            self.carve_ffn()
            self.zero_pads()
        if kind == 2:
            self.modulate(0, tiles_all)
            self.S.barrier()
            self.hyena(i, Wd, b)
            self.S.barrier()
            self.carve_ffn()
            self.zero_pads()
        if kind == 1:
            self.modulate(0, tiles_out)
            self.conformer(i, Wd, tiles_out)
        self.modulate(1, tiles_out)
        self.ffn(i, Wd, tiles_out)

    def ubuf(self, c):
        if c < 6:
            return self.G[:, c, :], self.rG[c]
        return (self.AROW, self.rA) if c == 6 else (self.VROW, self.rV)

    def conformer(self, i, Wd, tiles):
        S = self.S
        off, _ = vec_offsets(i)
        w1 = Wd["cf_w1"].rearrange("(c p) n -> p c n", p=128)
        w2 = Wd["cf_w2"].rearrange("(c p) n -> p c n", p=128)
        b1o, wdo, bdo, lgo, lbo, b2o = (off[k] for k in ("cf_b_pw1", "cf_w_dw", "cf_b_dw", "cf_ln_g", "cf_ln_b", "cf_b_pw2"))
        for (a, bnd) in ((0, 16), (C0 + LC, L0), (L0 + L, T)):
            S.op("pool", lambda e, a=a, bnd=bnd: e.memset(self.G[:, :, a:bnd], 0.0), writes=self.rG)
        for c in range(KC):
            k = self.wa_i
            self.wa_i = 1 - k
            wa, rwa = self.WA[k], self.rWA[k]
            self.load_w(wa[:, :, 0:128], w1[:, :, c * 128:(c + 1) * 128], [rwa[0]] + self.rWF[2 * k] + self.rWF[2 * k + 1])
            self.load_w(wa[:, :, 128:256], w1[:, :, D + c * 128:D + (c + 1) * 128], [rwa[1]] + self.rWF[2 * k] + self.rWF[2 * k + 1])
            ub, rub = self.ubuf(c)
            for ti in tiles:
                c0, w = TILES[ti]
                pss = []
                for half in range(2):
                    ps, rps = self.psum()
                    pss.append((ps, rps))

                    def mm(e, ps=ps, wa=wa, half=half, c0=c0, w=w):
                        ins = None
                        for kc in range(KC):
                            ins = e.matmul(ps[:, 0:w], lhsT=wa[:, kc, half * 128:(half + 1) * 128],
                                           rhs=self.H[:, kc, c0:c0 + w], start=(kc == 0), stop=(kc == KC - 1))
                        return ins
                    S.op("pe", mm, reads=[rwa[half]] + [self.rH[kc][ti] for kc in range(KC)], writes=[rps])
                tp, rtp = self.tmp()
                S.op("act", lambda e, ps=pss[1][0], tp=tp, w=w, c=c: e.activation(
                    out=tp[:, 0:w], in_=ps[:, 0:w], func=AF.Sigmoid, bias=self.VEC[:, b1o + 8 + c:b1o + 9 + c], scale=1.0),
                    reads=[self.rVEC], writes=[pss[1][1], rtp])
                S.op("dve", lambda e, ps=pss[0][0], tp=tp, ub=ub, c0=c0, w=w, c=c: e.scalar_tensor_tensor(
                    out=ub[:, c0:c0 + w], in0=ps[:, 0:w], scalar=self.VEC[:, b1o + c:b1o + c + 1], in1=tp[:, 0:w],
                    op0=ALU.add, op1=ALU.mult),
                    reads=[self.rVEC, rtp], writes=[pss[0][1], rub])
        dgs = [self.WB[:, 0:31 * 128].rearrange("p (k n) -> p k n", n=128),
               self.WA[0][:, :, :].rearrange("p c n -> p (c n)")[:, 0:31 * 128].rearrange("p (k n) -> p k n", n=128)]
        rdgs = [[self.rWB], self.wa_all(0)]
        for c in range(KC):
            ub, rub = self.ubuf(c)
            dg, rdg = dgs[c % 2], rdgs[c % 2]
            for kk in range(31):
                sc = self.VEC[:, wdo + kk * 8 + c:wdo + kk * 8 + c + 1]
                S.op("pool" if kk % 2 else "dve", lambda e, kk=kk, dg=dg, sc=sc: e.tensor_scalar_mul(out=dg[:, kk, :], in0=self.IDENT[:, :], scalar1=sc),
                     reads=[self.rID, self.rVEC], writes=rdg)
            for ti in tiles:
                c0, w = TILES[ti]
                ps, rps = self.psum()

                def mm(e, ps=ps, ub=ub, c0=c0, w=w, dg=dg):
                    ins = None
                    for kk in range(31):
                        ins = e.matmul(ps[:, 0:w], lhsT=dg[:, kk, :], rhs=ub[:, c0 + kk - 15:c0 + kk - 15 + w],
                                       start=(kk == 0), stop=(kk == 30))
                    return ins
                S.op("pe", mm, reads=rdg + [rub], writes=[rps])
                S.op("act", lambda e, ps=ps, c=c, c0=c0, w=w: e.activation(
                    out=self.H[:, c, c0:c0 + w], in_=ps[:, 0:w], func=AF.Identity, bias=self.VEC[:, bdo + c:bdo + c + 1], scale=1.0),
                    reads=[self.rVEC], writes=[rps, self.rH[c][ti]])
        for k in range(2):
            self.load_w(self.WA[k][:, :, :], w2[:, :, k * 512:(k + 1) * 512], self.wa_all(k))
        for ti in tiles:
            c0, w = TILES[ti]
            col = 1 if ti == 0 else 0
            for c in range(KC):
                S.op("act", lambda e, c=c, c0=c0, w=w: e.activation(out=self.SQ[:, c, 0:w], in_=self.H[:, c, c0:c0 + w], func=AF.Square),
                     reads=[self.rH[c][ti]], writes=[self.rSQ])
            ps1, rps1 = self.psum()
            ps2, rps2 = self.psum()

            def mm1(e, ps=ps1, c0=c0, w=w):
                ins = None
                for c in range(KC):
                    ins = e.matmul(ps[:, 0:w], lhsT=self.ONES[:, :], rhs=self.H[:, c, c0:c0 + w], start=(c == 0), stop=(c == KC - 1))
                return ins
            S.op("pe", mm1, reads=[self.rONES] + [self.rH[c][ti] for c in range(KC)], writes=[rps1])

            def mm2(e, ps=ps2, w=w):
                ins = None
                for c in range(KC):
                    ins = e.matmul(ps[:, 0:w], lhsT=self.ONES[:, :], rhs=self.SQ[:, c, 0:w], start=(c == 0), stop=(c == KC - 1))
                return ins
            S.op("pe", mm2, reads=[self.rONES, self.rSQ], writes=[rps2])
            S.op("act", lambda e, ps=ps1, w=w: e.activation(out=self.MU[:, 0:w], in_=ps[:, 0:w], func=AF.Identity, scale=1.0 / D),
                 reads=[], writes=[rps1, self.rMU])
            tp, rtp = self.tmp()
            S.op("dve", lambda e, tp=tp, w=w: e.tensor_tensor(out=tp[:, 0:w], in0=self.MU[:, 0:w], in1=self.MU[:, 0:w], op=ALU.mult),
                 reads=[self.rMU], writes=[rtp])
            S.op("dve", lambda e, ps=ps2, tp=tp, w=w: e.scalar_tensor_tensor(
                out=self.RSTD[:, 0:w], in0=ps[:, 0:w], scalar=1.0 / D, in1=tp[:, 0:w], op0=ALU.mult, op1=ALU.subtract),
                reads=[rtp], writes=[rps2, self.rRSTD])
            S.op("act", lambda e, w=w: e.activation(out=self.RSTD[:, 0:w], in_=self.RSTD[:, 0:w], func=AF.Ln, scale=1.0, bias=self.EPSB[:, 0:1]),
                 reads=[self.rONES], writes=[self.rRSTD])
            S.op("act", lambda e, w=w: e.activation(out=self.RSTD[:, 0:w], in_=self.RSTD[:, 0:w], func=AF.Exp, scale=-0.5), reads=[], writes=[self.rRSTD])
            for c in range(KC):
                ub, rub = self.ubuf(c)
                tp, rtp = self.tmp()
                S.op("dve", lambda e, tp=tp, c=c, c0=c0, w=w: e.tensor_tensor(out=tp[:, 0:w], in0=self.H[:, c, c0:c0 + w], in1=self.MU[:, 0:w], op=ALU.subtract),
                     reads=[self.rH[c][ti], self.rMU], writes=[rtp])
                S.op("dve", lambda e, tp=tp, w=w: e.tensor_tensor(out=tp[:, 0:w], in0=tp[:, 0:w], in1=self.RSTD[:, 0:w], op=ALU.mult),
                     reads=[self.rRSTD], writes=[rtp])
                S.op("act", lambda e, tp=tp, ub=ub, c=c, c0=c0, w=w: e.activation(
                    out=ub[:, c0:c0 + w], in_=tp[:, 0:w], func=AF.Silu, scale=self.VEC[:, lgo + c:lgo + c + 1], bias=self.VEC[:, lbo + c:lbo + c + 1]),
                    reads=[rtp, self.rVEC], writes=[rub])
            for dc in range(KC):
                ps, rps = self.psum()
                wa, rwa = self.WA[dc // 4], self.rWA[dc // 4]

                def mm(e, ps=ps, wa=wa, dc=dc, c0=c0, w=w):
                    ins = None
                    for c in range(KC):
                        ins = e.matmul(ps[:, 0:w], lhsT=wa[:, c, (dc % 4) * 128:(dc % 4 + 1) * 128], rhs=self.ubuf(c)[0][:, c0:c0 + w],
                                       start=(c == 0), stop=(c == KC - 1))
                    return ins
                S.op("pe", mm, reads=[rwa[0]] + [self.ubuf(c)[1] for c in range(KC)], writes=[rps])
                tp, rtp = self.tmp()
                S.op("dve", lambda e, ps=ps, tp=tp, dc=dc, w=w, col=col: e.tensor_scalar(
                    out=tp[:, 0:w], in0=ps[:, 0:w], scalar1=self.VEC[:, b2o + dc:b2o + dc + 1], scalar2=self.MOD[:, 16 + dc, col:col + 1],
                    op0=ALU.add, op1=ALU.mult),
                    reads=[self.rVEC, self.rMOD], writes=[rps, rtp])
                S.op("dve", lambda e, tp=tp, dc=dc, c0=c0, w=w: e.tensor_tensor(out=self.X[:, dc, c0:c0 + w], in0=self.X[:, dc, c0:c0 + w], in1=tp[:, 0:w], op=ALU.add),
                     reads=[rtp], writes=[self.rX[dc][ti]])


def make_inputs(inp, layers, core, nb=NB, x_override=None, ctx_override=None):
    m = {}
    xs = inp["x"] if x_override is None else x_override
    cs = inp["ctx"] if ctx_override is None else ctx_override
    xT = np.zeros((nb, D, T), np.float32)
    cT = np.zeros((nb, 128, KC * 2), np.float32)
    for bb in range(nb):
        b = core * nb + bb
        xT[bb, :, L0:L0 + L] = xs[b].T
        xT[bb, :, C0:C0 + LC] = cs[b].T
        sv = np.stack([inp["c"][b], inp["c_ctx"]], axis=-1)
        cT[bb] = sv.reshape(KC, 128, 2).transpose(1, 0, 2).reshape(128, KC * 2)
    m["ident"] = np.eye(128, dtype=np.float32)
    bd = np.zeros((128, 128), np.float32)
    bd[:64, :64] = 1.0
    bd[64:96, 64:96] = 1.0
    m["bd96"] = bd
    m["xT"] = xT
    m["cT"] = cT
    for i in layers:
        kind, j = i % 3, i // 3
        v, _ = pack_layer_vec(inp, i)
        vp = np.zeros((128, NPV), np.float32)
        vp[:, :v.shape[1]] = v
        m[f"vec{i}"] = vp
        m[f"ada_w{i}"] = np.ascontiguousarray(inp["ada_w"][i])
        m[f"ffn_up{i}"] = np.ascontiguousarray(inp["ffn_w_up"][i])
        m[f"ffn_down{i}"] = np.ascontiguousarray(inp["ffn_w_down"][i])
        if kind == 0:
            pe = np.arange(32)
            perm = np.where((pe % 16) < 8, pe + 8, pe - 8)
            wdkv = np.asarray(inp["mla_w_dkv"][j], np.float32)
            kv = np.zeros((D, 320), np.float32)
            kv[:, :128] = wdkv[:, :128]
            kv[:, 128 + 64:224] = wdkv[:, 128:160]
            kv[:, 224 + 64:320] = wdkv[:, 128 + perm]
            m[f"mla_kv{i}"] = kv
            wuq = np.asarray(inp["mla_w_uq"][j], np.float32).reshape(256, 16, 96)
            uq = np.zeros((256, 16, 192), np.float32)
            uq[:, :, :96] = wuq
            uq[:, :, 96 + 64:] = wuq[:, :, 64 + perm]
            m[f"mla_uq{i}"] = uq.reshape(256, 16 * 192)
            m[f"mla_dq{i}"] = np.ascontiguousarray(inp["mla_w_dq"][j])
            m[f"mla_ukv{i}"] = np.ascontiguousarray(inp["mla_w_ukv"][j])
            m[f"mla_o{i}"] = np.ascontiguousarray(inp["mla_w_o"][j])
            m["rope_cs"] = rope_tables()
        if kind == 2:
            m[f"hy_in{i}"] = np.ascontiguousarray(inp["hy_w_in"][j])
            m[f"hy_out{i}"] = np.ascontiguousarray(inp["hy_w_out"][j])
            m[f"hy_fw1_{i}"] = np.ascontiguousarray(inp["hy_f_w1"][j])
            m[f"hy_fw2_{i}"] = np.ascontiguousarray(inp["hy_f_w2"][j])
            m[f"hy_fw3_{i}"] = np.ascontiguousarray(inp["hy_f_w3"][j])
            m[f"hy_skipb{i}"] = np.ascontiguousarray(np.broadcast_to(np.asarray(inp["hy_skip"][j], np.float32)[None, :], (128, D)))
            m.update(hy_consts())
        if kind == 1:
            m[f"cf_w1_{i}"] = np.ascontiguousarray(inp["cf_w_pw1"][j])
            m[f"cf_w2_{i}"] = np.ascontiguousarray(inp["cf_w_pw2"][j])
    return m


_HYC = {}


def hy_consts():
    if _HYC:
        return _HYC
    import ml_dtypes
    bf = ml_dtypes.bfloat16
    out = {}
    for tag, Lx in (("l", L), ("c", LC)):
        N = 2 * Lx
        tau = np.arange(Lx, dtype=np.float64)[:, None]
        f = np.arange(Lx, dtype=np.float64)[None, :] + 0.5
        th = 2 * np.pi * tau * f / N
        nt_ = Lx // 128
        nts_ = Lx // 256

        def fwd_blocks(M):
            return np.ascontiguousarray(M.reshape(nt_, 128, nt_, 128).transpose(2, 1, 0, 3).reshape(nt_, 128, nt_ * 128))

        def inv_blocks(M):
            return np.ascontiguousarray(M.reshape(nt_, 128, nts_, 256).transpose(2, 1, 0, 3).reshape(nts_, 128, nt_ * 256))
        out[f"dfc_{tag}"] = fwd_blocks(np.cos(th).astype(np.float32)).astype(bf)
        out[f"dfs_{tag}"] = fwd_blocks(np.sin(th).astype(np.float32)).astype(bf)
        out[f"idc_{tag}"] = inv_blocks(((2.0 / N) * np.cos(th).T).astype(np.float32)).astype(bf)
        out[f"ids_{tag}"] = inv_blocks(((2.0 / N) * np.sin(th).T).astype(np.float32)).astype(bf)
        t = np.linspace(0.0, 1.0, Lx, dtype=np.float32)
        w = (2.0 * np.pi * np.arange(Lx, dtype=np.float32) / Lx).astype(np.float32)
        fr = np.linspace(1e-4, 15.0, 16, dtype=np.float32)
        fw = w[:, None] * fr[None, :]
        z = np.concatenate([t[:, None], np.cos(fw), -np.sin(fw)], axis=-1).astype(np.float32)
        out[f"z_{tag}"] = np.ascontiguousarray(z.T)
        out[f"negt_{tag}"] = np.ascontiguousarray((-t).reshape(Lx // 128, 128).T)
    deltas = np.linspace(np.log(1e-2) / 0.3, np.log(1e-2) / 1.5, D, dtype=np.float32)
    out["absd"] = np.ascontiguousarray(np.broadcast_to(np.abs(deltas)[None, :], (128, D))).astype(np.float32)
    _HYC.update(out)
    return _HYC


def rope_tables():
    t = np.arange(L)
    row = (t // 64).astype(np.float32)
    colp = (t % 64).astype(np.float32)
    inv = (10000.0 ** (-(np.arange(0, 16, 2, dtype=np.float32) / 16.0))).astype(np.float32)
    ang = np.concatenate([row[:, None] * inv, colp[:, None] * inv], axis=-1).astype(np.float32)
    cs = np.zeros((128, 2, T), np.float32)
    cs[:, 0, :] = 1.0
    for d in range(32):
        a, b, jj = d // 16, (d % 16) // 8, d % 8
        cs[64 + d, 0, L0:L0 + L] = np.cos(ang[:, a * 8 + jj])
        cs[64 + d, 1, L0:L0 + L] = np.sin(ang[:, a * 8 + jj]) * (-1.0 if b == 0 else 1.0)
    return cs


def kernel(**inputs):
    inp = {k: np.asarray(v) for k, v in inputs.items()}
    layers = list(range(DEPTH))
    prog = Prog(layers)
    nc = prog.build()
    in_maps = [make_inputs(inp, layers, c) for c in range(8)]
    res = run_bass_kernel_spmd(nc, in_maps, core_ids=list(range(8)))
    out = np.empty((16, L, D), np.float32)
    for c in range(8):
        y = res.results[c]["yT"]
        for bb in range(NB):
            out[c * NB + bb] = y[bb].T
    return out
```

```python
import contextlib
import os
import numpy as np
import concourse.bass as bass
import concourse.mybir as mybir
from concourse.bass_utils import run_bass_kernel_spmd

F32 = mybir.dt.float32
BF16 = mybir.dt.bfloat16
ALU = mybir.AluOpType
AF = mybir.ActivationFunctionType

D = 1024
KC = 8
L = 2048
LC = 256
T = 16 + LC + 16 + L + 16
C0 = 16
L0 = 16 + LC + 16
TILES = [(C0, LC)] + [(L0 + 512 * j, 512) for j in range(4)]
D_FF = 2816
FC = 22
DEPTH = 4
EPS = 1e-6
NB = 2
DBG_STOP = int(os.environ.get('MK_STOP', '0'))
DBG_HEADS = (int(os.environ['MK_HEADS']) if 'MK_HEADS' in os.environ else None)


class Res:
    __slots__ = ("w", "r")

    def __init__(self):
        self.w = None
        self.r = {}


class Sched:
    ENG = ("pe", "act", "dve", "pool", "sp")
    NDMA = 8
    EPOCH = 6000

    def __init__(self):
        self.streams = {e: [] for e in self.ENG}
        self.cnt = {e: 0 for e in self.ENG}
        self.waited = {e: {} for e in self.ENG}
        self.dma_n = {e: 0 for e in self.ENG}
        self.last = {}

    def op(self, eng, fn, reads=(), writes=(), dma=False, after=()):
        deps = {}

        def add(k, v):
            if deps.get(k, 0) < v:
                deps[k] = v

        for r in after:
            if r.w is not None:
                add(*r.w)
            for k, v in r.r.items():
                add(k, v)
        for r in reads:
            if r.w is not None:
                add(*r.w)
        for r in writes:
            if r.w is not None:
                add(*r.w)
            for k, v in r.r.items():
                add(k, v)
        if dma:
            j = self.dma_n[eng]
            self.dma_n[eng] += 1
            key = ("dma", eng, j % self.NDMA)
            val = 16 * (j // self.NDMA + 1)
            if j >= self.NDMA:
                add(key, val - 16)
            inc = 16
        else:
            key = ("eng", eng, self.cnt[eng] // self.EPOCH)
            val = self.cnt[eng] % self.EPOCH + 1
            self.cnt[eng] += 1
            inc = 1
        self.last[key] = val
        wd = self.waited[eng]
        waits = []
        for k, v in deps.items():
            if eng == "pe" and k[0] == "eng" and k[1] == "pe":
                continue
            if wd.get(k, 0) < v:
                wd[k] = v
                waits.append((k, v))
        self.streams[eng].append((fn, waits, key, inc))
        for r in reads:
            if r.r.get(key, 0) < val:
                r.r[key] = val
        for r in writes:
            r.w = (key, val)
            r.r = {}

    def barrier(self, engs=None):
        for e in (engs or self.ENG):
            wd = self.waited[e]
            waits = []
            for k, v in self.last.items():
                if wd.get(k, 0) < v:
                    wd[k] = v
                    waits.append((k, v))
            if waits:
                self.streams[e].append((None, waits, None, 0))

    def emit(self, nc, stack):
        sems = {}
        for e in self.ENG:
            for (fn, waits, key, inc) in self.streams[e]:
                for k in [w[0] for w in waits] + ([key] if key is not None else []):
                    if k not in sems:
                        sems[k] = stack.enter_context(nc.semaphore("s_" + "_".join(str(x) for x in k)))
        block = stack.enter_context(nc.Block())

        def run(ename, eng):
            for (fn, waits, key, inc) in self.streams[ename]:
                for k, v in waits:
                    eng.wait_ge(sems[k], v)
                if fn is not None:
                    fn(eng).then_inc(sems[key], inc)

        @block.tensor
        def _(e):
            run("pe", e)

        @block.scalar
        def _(e):
            run("act", e)

        @block.vector
        def _(e):
            run("dve", e)

        @block.gpsimd
        def _(e):
            run("pool", e)

        @block.sync
        def _(e):
            run("sp", e)


def fm(v):
    v = np.asarray(v, np.float32)
    c = v.shape[-1] // 128
    lead = v.shape[:-1]
    return np.ascontiguousarray(np.moveaxis(v.reshape(lead + (c, 128)), -1, 0)).reshape(128, -1)


def pack_layer_vec(inp, i):
    kind, j = i % 3, i // 3
    parts = [("ada_b", fm(inp["ada_b"][i])),
             ("norm_mix", fm(inp["norm_mix"][i])),
             ("norm_ffn", fm(inp["norm_ffn"][i])),
             ("ffn_b_dw", fm(inp["ffn_b_dw"][i])),
             ("ffn_w_dw", fm(inp["ffn_w_dw"][i]))]
    if kind == 0:
        g = np.asarray(inp["mla_qk_gain"][j], np.float32)
        def col96(v):
            o = np.zeros((128, 1), np.float32)
            o[:96, 0] = v
            return o
        pe = np.arange(32)
        perm = np.where((pe % 16) < 8, pe + 8, pe - 8)
        gqp = np.zeros(96, np.float32); gqp[64:] = g[0, 64 + perm]
        gkp = np.zeros(96, np.float32); gkp[64:] = g[1, 64 + perm]
        invn = np.concatenate([np.full(64, 1 / 64.0), np.full(32, 1 / 32.0)]).astype(np.float32)
        parts += [("mla_qn", fm(inp["mla_q_norm"][j])), ("mla_kvn", fm(inp["mla_kv_norm"][j])),
                  ("mla_gq", col96(g[0])), ("mla_gqp", col96(gqp)), ("mla_gk", col96(g[1])), ("mla_gkp", col96(gkp)),
                  ("mla_invn", col96(invn))]
    if kind == 2:
        def col64(v):
            o = np.zeros((128, 1), np.float32)
            o[:64, 0] = v
            return o
        parts += [("hy_b_in", fm(inp["hy_b_in"][j])), ("hy_w_short", fm(inp["hy_w_short"][j])),
                  ("hy_b_short", fm(inp["hy_b_short"][j])), ("hy_b_out", fm(inp["hy_b_out"][j])),
                  ("hy_fb1", col64(inp["hy_f_b1"][j])), ("hy_fb2", col64(inp["hy_f_b2"][j])), ("hy_fr", col64(inp["hy_sin_freq"][j]))]
    if kind == 1:
        parts += [("cf_b_pw1", fm(inp["cf_b_pw1"][j])),
                  ("cf_w_dw", fm(inp["cf_w_dw"][j])),
                  ("cf_b_dw", fm(inp["cf_b_dw"][j])),
                  ("cf_ln_g", fm(inp["cf_ln_g"][j])),
                  ("cf_ln_b", fm(inp["cf_ln_b"][j])),
                  ("cf_b_pw2", fm(inp["cf_b_pw2"][j]))]
    off = {}
    o = 0
    for n, a in parts:
        off[n] = o
        o += a.shape[1]
    return np.concatenate([a for _, a in parts], axis=1), off


def vec_offsets(i):
    kind = i % 3
    sizes = [("ada_b", 48), ("norm_mix", 8), ("norm_ffn", 8), ("ffn_b_dw", 22), ("ffn_w_dw", 66)]
    if kind == 0:
        sizes += [("mla_qn", 2), ("mla_kvn", 1), ("mla_gq", 1), ("mla_gqp", 1), ("mla_gk", 1), ("mla_gkp", 1), ("mla_invn", 1)]
    if kind == 2:
        sizes += [("hy_b_in", 24), ("hy_w_short", 72), ("hy_b_short", 24), ("hy_b_out", 8), ("hy_fb1", 1), ("hy_fb2", 1), ("hy_fr", 1)]
    if kind == 1:
        sizes += [("cf_b_pw1", 16), ("cf_w_dw", 248), ("cf_b_dw", 8), ("cf_ln_g", 8), ("cf_ln_b", 8), ("cf_b_pw2", 8)]
    off = {}
    o = 0
    for n, s in sizes:
        off[n] = o
        o += s
    return off, o


NPV = 512
ARENA = 42112


class Prog:
    def __init__(self, layers, nb=NB):
        self.layers = layers
        self.nb = nb
        self.nc = bass.Bass("TRN2", target_bir_lowering=False)
        self.S = Sched()
        self.dram = {}

    def din(self, name, shape, dt=F32):
        t = self.nc.dram_tensor(name, list(shape), dt, kind="ExternalInput").ap()
        self.dram[name] = t
        return t

    def dout(self, name, shape, dt=F32):
        t = self.nc.dram_tensor(name, list(shape), dt, kind="ExternalOutput").ap()
        self.dram[name] = t
        return t

    def build(self):
        nc, S = self.nc, self.S
        nb = self.nb
        xT = self.din("xT", [nb, D, T])
        NC = nb + 1
        self.nbc = nb
        cT = self.din("cT", [128, KC * NC])
        ident_d = self.din("ident", [128, 128])
        bd_d = self.din("bd96", [128, 128])
        yT = self.dout("yT", [nb, D, L])
        cOut = self.dout("cOut", [nb, D, LC])
        W = {}
        for i in self.layers:
            kind, j = i % 3, i // 3
            W[i] = dict(
                ada_w=self.din(f"ada_w{i}", [D, 6 * D]),
                vec=self.din(f"vec{i}", [128, NPV]),
                ffn_up=self.din(f"ffn_up{i}", [D, 2 * D_FF]),
                ffn_down=self.din(f"ffn_down{i}", [D_FF, D]),
            )
            if kind == 0:
                if "rope_cs" not in self.dram:
                    self.din("rope_cs", [128, 2, T])
                W[i].update(dq=self.din(f"mla_dq{i}", [D, 256]), kv=self.din(f"mla_kv{i}", [D, 320]),
                            uq=self.din(f"mla_uq{i}", [256, 16 * 192]), ukv=self.din(f"mla_ukv{i}", [128, 2048]),
                            wo=self.din(f"mla_o{i}", [D, D]), rope=self.dram["rope_cs"])
            if kind == 2:
                W[i].update(hy_in=self.din(f"hy_in{i}", [D, 3 * D]), hy_out=self.din(f"hy_out{i}", [D, D]),
                            fw1=self.din(f"hy_fw1_{i}", [33, 64]), fw2=self.din(f"hy_fw2_{i}", [64, 64]),
                            fw3=self.din(f"hy_fw3_{i}", [64, 2 * D]), skipb=self.din(f"hy_skipb{i}", [128, D]))
                for tag, Lx in (("l", L), ("c", LC)):
                    for nm in ("dfc", "dfs"):
                        W[i][f"{nm}_{tag}"] = self.din(f"{nm}_{tag}", [Lx // 128, 128, Lx], BF16)
                    for nm in ("idc", "ids"):
                        W[i][f"{nm}_{tag}"] = self.din(f"{nm}_{tag}", [Lx // 256, 128, (Lx // 128) * 256], BF16)
                    W[i][f"z_{tag}"] = self.din(f"z_{tag}", [33, Lx])
                    W[i][f"negt_{tag}"] = self.din(f"negt_{tag}", [128, Lx // 128])
                    W[i][f"kspec_{tag}"] = self.nc.dram_tensor(f"kspec_{tag}", [2, Lx, D], F32).ap()
                W[i]["absd"] = self.din("absd", [128, D])
                W[i]["x0d"] = self.nc.dram_tensor("x0d", [D, T], BF16).ap()
                self.rX0D = [Res() for _ in range(KC)]
                self.rKS = {"l": Res(), "c": Res()}
                self.rKSW = {"l": [Res() for _ in range(8)], "c": [Res() for _ in range(8)]}
            if kind == 1:
                W[i].update(cf_w1=self.din(f"cf_w1_{i}", [D, 2 * D]), cf_w2=self.din(f"cf_w2_{i}", [D, D]))
        with contextlib.ExitStack() as st:
            self.st = st

            def sb(name, shape, dt):
                return st.enter_context(nc.sbuf_tensor(name, list(shape), dt))

            self.X = sb("X", [128, KC, T], F32)
            self.rX = [[Res() for _ in TILES] for _ in range(KC)]
            self.H = sb("H", [128, KC, T], BF16)
            self.rH = [[Res() for _ in TILES] for _ in range(KC)]
            self.SCR = sb("SCR", [128, ARENA], BF16)
            self.carve_ffn()
            self.VEC = sb("VEC", [128, NPV], F32)
            self.rVEC = Res()
            self.MODD = nc.dram_tensor("modd", [len(self.layers), 128, 48 * NC], F32).ap()
            self.rMODD = [Res() for _ in self.layers]
            self.MOD = sb("MOD", [128, 48, NC], F32)
            self.rMOD = Res()
            self.AM = sb("AM", [128, 2, KC, NC], F32)
            self.rAM = Res()
            self.SV = sb("SV", [128, KC, NC], F32)
            self.SVB = sb("SVB", [128, KC, NC], BF16)
            self.rSV = Res()
            self.ONES = sb("ONES", [128, 128], BF16)
            self.rONES = Res()
            self.EPSB = sb("EPSB", [128, 1], F32)
            self.RSTD = sb("RSTD", [128, 512], F32)
            self.rRSTD = Res()
            self.RSTD2 = sb("RSTD2", [128, 512], F32)
            self.rRSTD2 = Res()
            self.MU = sb("MU", [128, 512], F32)
            self.rMU = Res()
            self.TMP = [sb(f"TMP{k}", [128, 512], F32) for k in range(2)]
            self.rTMP = [Res(), Res()]
            self.tmp_i = 0
            self.BD = sb("BD", [128, 128], BF16)
            self.PS = [st.enter_context(nc.psum_tensor(f"PS{k}", [128, 512], F32)) for k in range(8)]
            self.acc_i = 0
            self.rPS = [Res() for _ in range(8)]
            self.ps_i = 0
            self.wa_i = 0
            self.dq = 0

            S.op("pool", lambda e: e.memset(self.ONES[:, :], 1.0), writes=[self.rONES])
            S.op("pool", lambda e: e.memset(self.EPSB[:, :], EPS), writes=[self.rONES])
            self.zero_pads()
            self.IDENT = sb("IDENT", [128, 128], BF16)
            self.rID = Res()
            S.op("pool", lambda e: e.dma_start(out=self.IDENT[:, :], in_=ident_d[:, :]), writes=[self.rID], dma=True)
            S.op("pool", lambda e: e.dma_start(out=self.BD[:, :], in_=bd_d[:, :]), writes=[self.rID], dma=True)

            routs = []
            S.op("sp", lambda e: e.dma_start(out=self.SV[:, :, :], in_=cT.rearrange("p (c t) -> p c t", t=NC)), writes=[self.rSV], dma=True)
            S.op("act", lambda e: e.activation(out=self.SVB[:, :, :], in_=self.SV[:, :, :], func=AF.Silu), reads=[], writes=[self.rSV])
            for li, i in enumerate(self.layers):
                self.adaln_all(li, i, W[i], NC)
            for b in range(nb):
                self.cur_b = b
                self.load_x(xT, cT, b)
                for li, i in enumerate(self.layers):
                    self.cur_li = li
                    self.layer(i, W[i], b, last=(i == DEPTH - 1))
                routs += self.store_x(yT, cOut, b)
            S.barrier(["sp"])
            S.emit(nc, st)
        return nc

    def psum(self):
        k = self.ps_i
        self.ps_i = (k + 1) % 6
        return self.PS[k], self.rPS[k]

    def wa_all(self, k):
        return self.rWA[k] + self.rWF[2 * k] + self.rWF[2 * k + 1]

    def psum_acc(self):
        k = 6 + self.acc_i
        self.acc_i = 1 - self.acc_i
        return self.PS[k], self.rPS[k]

    def arena_reset(self, mark=0):
        self.a_off = mark

    def alloc(self, n_el, dt=BF16, shape=None):
        nb16 = n_el * (2 if dt == F32 else 1)
        nb16 = (nb16 + 15) // 16 * 16
        a = self.a_off
        assert a + nb16 <= ARENA, (a, nb16)
        self.a_off = a + nb16
        ap = self.SCR[:, a:a + n_el * (2 if dt == F32 else 1)]
        if dt == F32:
            ap = ap.bitcast(F32)
        return ap

    def carve_ffn(self):
        self.arena_reset()
        self.G = self.alloc(6 * T).rearrange("p (c t) -> p c t", t=T)
        self.rG = [Res() for _ in range(6)]
        self.AROW = self.alloc(T)
        self.CROW = self.alloc(T, F32)
        self.VROW = self.alloc(T)
        self.rA, self.rC, self.rV = Res(), Res(), Res()
        self.WA = [self.alloc(KC * 512).rearrange("p (c n) -> p c n", n=512) for _ in range(2)]
        self.rWA = [[Res(), Res()], [Res(), Res()]]
        self.rWF = [[Res(), Res()] for _ in range(4)]
        self.WB = self.alloc(6 * 1024)
        self.rWB = Res()
        self.SQ = self.alloc(KC * 512).rearrange("p (c n) -> p c n", n=512)
        self.rSQ = Res()

    def zero_pads(self):
        S = self.S
        S.op("pool", lambda e: e.memset(self.AROW[:, :], 0.0), writes=[self.rA])
        S.op("pool", lambda e: e.memset(self.VROW[:, :], 0.0), writes=[self.rV])
        for (a, bnd) in ((0, 16), (C0 + LC, L0), (L0 + L, T)):
            S.op("pool", lambda e, a=a, bnd=bnd: e.memset(self.G[:, :, a:bnd], 0.0), writes=self.rG)

    def tmp(self):
        k = self.tmp_i
        self.tmp_i = 1 - k
        return self.TMP[k], self.rTMP[k]

    def dma_eng(self):
        self.dq += 1
        return "sp"

    def load_w(self, dst_ap, src_ap, res, after=()):
        self.S.op("pool", lambda e: e.dma_start(out=dst_ap, in_=src_ap), writes=(res if isinstance(res, list) else [res]), dma=True, after=after)

    def load_x(self, xT, cT, b):
        S = self.S
        for c in range(KC):
            S.op("sp", lambda e, c=c: e.dma_start(out=self.X[:, c, :], in_=xT[b, c * 128:(c + 1) * 128, :]),
                 writes=self.rX[c], dma=True)

    def store_x(self, yT, cOut, b):
        S = self.S
        rs = []
        for c in range(KC):
            r = Res()
            S.op("sp", lambda e, c=c: e.dma_start(out=yT[b, c * 128:(c + 1) * 128, :], in_=self.X[:, c, L0:L0 + L]),
                 reads=self.rX[c], writes=[r], dma=True)
            r2 = Res()
            S.op("sp", lambda e, c=c: e.dma_start(out=cOut[b, c * 128:(c + 1) * 128, :], in_=self.X[:, c, C0:C0 + LC]),
                 reads=self.rX[c], writes=[r2], dma=True)
            rs += [r, r2]
        return rs

    def adaln_all(self, li, i, Wd, NC):
        S = self.S
        off, _ = vec_offsets(i)
        S.op("sp", lambda e: e.dma_start(out=self.VEC[:, :], in_=Wd["vec"][:, :]), writes=[self.rVEC], dma=True)
        ada_w = Wd["ada_w"].rearrange("(c p) n -> p c n", p=128)
        mod = self.MOD
        for blk in range(12):
            k = self.wa_i
            self.wa_i = 1 - k
            wa, rwa = self.WA[k], self.rWA[k]
            self.S.op("pool", lambda e, wa=wa, blk=blk: e.dma_start(out=wa[:, :, :], in_=ada_w[:, :, blk * 512:(blk + 1) * 512]), writes=self.wa_all(k), dma=True)
            ps, rps = self.psum()

            def mm(e, wa=wa, ps=ps):
                ins = None
                for f in range(4):
                    for kc in range(KC):
                        ins = e.matmul(ps[:, f * NC:(f + 1) * NC], lhsT=wa[:, kc, f * 128:(f + 1) * 128],
                                       rhs=self.SVB[:, kc, :], start=(kc == 0), stop=(kc == KC - 1))
                return ins
            S.op("pe", mm, reads=rwa + [self.rSV], writes=[rps])
            ab = off["ada_b"] + blk * 4
            S.op("dve", lambda e, ps=ps, blk=blk, ab=ab: e.tensor_tensor(
                out=mod[:, blk * 4:blk * 4 + 4, :], in0=ps[:, 0:4 * NC].rearrange("p (f t) -> p f t", t=NC),
                in1=self.VEC[:, ab:ab + 4].unsqueeze(2).to_broadcast([128, 4, NC]), op=ALU.add),
                reads=[self.rVEC], writes=[rps, self.rMOD])
        S.op("sp", lambda e: e.dma_start(out=self.MODD[li], in_=self.MOD[:, :, :].rearrange("p g t -> p (g t)")),
             reads=[self.rMOD], writes=[self.rMODD[li]], dma=True)

    def adaln(self, i, Wd):
        S = self.S
        off, _ = vec_offsets(i)
        NC = self.nbc + 1
        S.op("sp", lambda e: e.dma_start(out=self.VEC[:, :], in_=Wd["vec"][:, :]), writes=[self.rVEC], dma=True)
        li = self.cur_li
        S.op("sp", lambda e, li=li: e.dma_start(out=self.MOD[:, :, :].rearrange("p g t -> p (g t)"), in_=self.MODD[li]),
             reads=[self.rMODD[li]], writes=[self.rMOD], dma=True)
        mod = self.MOD
        for m, (nname, g) in enumerate((("norm_mix", 1), ("norm_ffn", 4))):
            no = off[nname]
            S.op("dve", lambda e, m=m, g=g: e.tensor_scalar_add(out=self.AM[:, m, :, :], in0=mod[:, g * 8:(g + 1) * 8, :], scalar1=1.0),
                 reads=[self.rMOD], writes=[self.rAM])
            S.op("dve", lambda e, m=m, no=no: e.tensor_tensor(
                out=self.AM[:, m, :, :], in0=self.AM[:, m, :, :],
                in1=self.VEC[:, no:no + 8].unsqueeze(2).to_broadcast([128, 8, NC]), op=ALU.mult),
                reads=[self.rVEC, self.rAM], writes=[self.rAM])

    def modulate(self, m, tiles):
        S = self.S
        shg = 0 if m == 0 else 3
        RS = [self.RSTD, self.RSTD2]
        rRS = [self.rRSTD, self.rRSTD2]

        def stage1(k):
            ti = tiles[k]
            c0, w = TILES[ti]
            rs, rrs = RS[k % 2], rRS[k % 2]
            for c in range(KC):
                S.op("act", lambda e, c=c: e.activation(out=self.SQ[:, c, 0:w], in_=self.X[:, c, c0:c0 + w], func=AF.Square),
                     reads=[self.rX[c][ti]], writes=[self.rSQ])
            ps, rps = self.psum()

            def mm(e):
                ins = None
                for c in range(KC):
                    ins = e.matmul(ps[:, 0:w], lhsT=self.ONES[:, :], rhs=self.SQ[:, c, 0:w], start=(c == 0), stop=(c == KC - 1))
                return ins
            S.op("pe", mm, reads=[self.rSQ, self.rONES], writes=[rps])
            S.op("act", lambda e: e.activation(out=rs[:, 0:w], in_=ps[:, 0:w], func=AF.Ln, scale=1.0 / D, bias=self.EPSB[:, 0:1]),
                 reads=[self.rONES], writes=[rps, rrs])
            S.op("act", lambda e: e.activation(out=rs[:, 0:w], in_=rs[:, 0:w], func=AF.Exp, scale=-0.5), reads=[], writes=[rrs])

        def stage2(k):
            ti = tiles[k]
            c0, w = TILES[ti]
            col = self.nbc if ti == 0 else self.cur_b
            rs, rrs = RS[k % 2], rRS[k % 2]
            for c in range(KC):
                tp, rtp = self.tmp()
                S.op("dve", lambda e, tp=tp, c=c: e.tensor_tensor(out=tp[:, 0:w], in0=self.X[:, c, c0:c0 + w], in1=rs[:, 0:w], op=ALU.mult),
                     reads=[self.rX[c][ti], rrs], writes=[rtp])
                S.op("pool", lambda e, tp=tp, c=c: e.tensor_scalar(
                    out=self.H[:, c, c0:c0 + w], in0=tp[:, 0:w], scalar1=self.AM[:, m, c, col:col + 1],
                    scalar2=self.MOD[:, shg * 8 + c, col:col + 1], op0=ALU.mult, op1=ALU.add),
                    reads=[rtp, self.rAM, self.rMOD], writes=[self.rH[c][ti]])
        stage1(0)
        for k in range(len(tiles)):
            if k + 1 < len(tiles):
                stage1(k + 1)
            stage2(k)

    def ffn(self, i, Wd, tiles):
        S = self.S
        off, _ = vec_offsets(i)
        up = Wd["ffn_up"].rearrange("(c p) n -> p c n", p=128)
        down = Wd["ffn_down"]
        bo, wo = off["ffn_b_dw"], off["ffn_w_dw"]
        lo = TILES[tiles[0]][0]
        hi = TILES[tiles[-1]][0] + TILES[tiles[-1]][1]
        for (g0, gn) in ((0, 6), (6, 6), (12, 5), (17, 5)):
            wbv = self.WB[:, 0:gn * 1024].rearrange("p (c n) -> p c n", n=1024)
            self.load_w(wbv, down[g0 * 128:(g0 + gn) * 128, :].rearrange("(c p) n -> p c n", p=128), self.rWB)
            for hc in range(g0, g0 + gn):
                slot = hc % 4
                wa = self.WA[slot // 2][:, :, (slot % 2) * 256:(slot % 2) * 256 + 256]
                rwa = self.rWF[slot]
                self.load_w(wa[:, :, 0:128], up[:, :, hc * 128:(hc + 1) * 128], rwa[0], after=self.rWA[slot // 2])
                self.load_w(wa[:, :, 128:256], up[:, :, D_FF + hc * 128:D_FF + (hc + 1) * 128], rwa[1], after=self.rWA[slot // 2])
                for ti in tiles:
                    c0, w = TILES[ti]
                    for half, (dst, rdst) in enumerate(((self.AROW, self.rA), (self.VROW, self.rV))):
                        ps, rps = self.psum()

                        def mm(e, ps=ps, wa=wa, half=half, c0=c0, w=w):
                            ins = None
                            for kc in range(KC):
                                ins = e.matmul(ps[:, 0:w], lhsT=wa[:, kc, half * 128:(half + 1) * 128],
                                               rhs=self.H[:, kc, c0:c0 + w], start=(kc == 0), stop=(kc == KC - 1))
                            return ins
                        S.op("pe", mm, reads=[rwa[half]] + [self.rH[kc][ti] for kc in range(KC)], writes=[rps])
                        S.op("act", lambda e, ps=ps, dst=dst, c0=c0, w=w: e.copy(out=dst[:, c0:c0 + w], in_=ps[:, 0:w]),
                             reads=[], writes=[rps, rdst])
                S.op("dve", lambda e, hc=hc: e.tensor_scalar(
                    out=self.CROW[:, lo:hi], in0=self.AROW[:, lo:hi], scalar1=self.VEC[:, wo + 22 + hc:wo + 22 + hc + 1],
                    scalar2=self.VEC[:, bo + hc:bo + hc + 1], op0=ALU.mult, op1=ALU.add),
                    reads=[self.rA, self.rVEC], writes=[self.rC])
                S.op("dve", lambda e, hc=hc: e.scalar_tensor_tensor(
                    out=self.CROW[:, lo:hi], in0=self.AROW[:, lo - 1:hi - 1], scalar=self.VEC[:, wo + hc:wo + hc + 1],
                    in1=self.CROW[:, lo:hi], op0=ALU.mult, op1=ALU.add),
                    reads=[self.rA, self.rVEC], writes=[self.rC])
                S.op("dve", lambda e, hc=hc: e.scalar_tensor_tensor(
                    out=self.CROW[:, lo:hi], in0=self.AROW[:, lo + 1:hi + 1], scalar=self.VEC[:, wo + 44 + hc:wo + 44 + hc + 1],
                    in1=self.CROW[:, lo:hi], op0=ALU.mult, op1=ALU.add),
                    reads=[self.rA, self.rVEC], writes=[self.rC])
                S.op("act", lambda e: e.activation(out=self.CROW[:, lo:hi], in_=self.CROW[:, lo:hi], func=AF.Silu),
                     reads=[], writes=[self.rC])
                gi = hc - g0
                S.op("dve", lambda e, gi=gi: e.tensor_tensor(out=self.G[:, gi, lo:hi], in0=self.CROW[:, lo:hi], in1=self.VROW[:, lo:hi], op=ALU.mult),
                     reads=[self.rC, self.rV], writes=[self.rG[gi]])
            for dc in range(KC):
                for ti in tiles:
                    c0, w = TILES[ti]
                    col = self.nbc if ti == 0 else self.cur_b
                    ps, rps = self.psum()

                    def mm(e, ps=ps, wbv=wbv, dc=dc, c0=c0, w=w, gn=gn):
                        ins = None
                        for gi in range(gn):
                            ins = e.matmul(ps[:, 0:w], lhsT=wbv[:, gi, dc * 128:(dc + 1) * 128], rhs=self.G[:, gi, c0:c0 + w],
                                           start=(gi == 0), stop=(gi == gn - 1))
                        return ins
                    S.op("pe", mm, reads=[self.rWB] + [self.rG[gi] for gi in range(gn)], writes=[rps])
                    S.op("dve", lambda e, ps=ps, dc=dc, c0=c0, w=w, col=col: e.scalar_tensor_tensor(
                        out=self.X[:, dc, c0:c0 + w], in0=ps[:, 0:w], scalar=self.MOD[:, 40 + dc, col:col + 1],
                        in1=self.X[:, dc, c0:c0 + w], op0=ALU.mult, op1=ALU.add),
                        reads=[self.rMOD], writes=[rps, self.rX[dc][ti]])

    def mla(self, i, Wd, need_ctx):
        S = self.S
        off, _ = vec_offsets(i)
        V = lambda name, k=0: self.VEC[:, off[name] + k:off[name] + k + 1]
        V96 = lambda name: self.VEC[0:96, off[name]:off[name] + 1]
        all_tiles = [0, 1, 2, 3, 4]
        qtiles = all_tiles if need_ctx else [1, 2, 3, 4]
        att_scale = float(96 ** -0.5)
        self.arena_reset()
        CQ = self.alloc(2 * T).rearrange("p (c t) -> p c t", t=T)
        CKV = self.alloc(T)
        KPE = self.alloc(T)
        ROPE = self.alloc(2 * T, F32).rearrange("p (c t) -> p c t", t=T)
        rCQ = [Res() for _ in TILES]
        rCKV = [Res() for _ in TILES]
        rKPE = [Res() for _ in TILES]
        rROPE = Res()
        SQh = self.alloc(512)
        rSQh = Res()
        SQh2 = self.alloc(512)
        T1b = self.alloc(512, F32)
        T2b = self.alloc(512, F32)
        SETS = [dict(sq=SQh, rsq=rSQh, rs=self.RSTD, rrs=self.rRSTD, t1=self.TMP[0], rt1=self.rTMP[0], t2=self.TMP[1], rt2=self.rTMP[1]),
                dict(sq=SQh2, rsq=Res(), rs=self.RSTD2, rrs=self.rRSTD2, t1=T1b, rt1=Res(), t2=T2b, rt2=Res())]
        set_i = [0]

        def next_set():
            set_i[0] ^= 1
            return SETS[set_i[0]]
        mark = self.a_off
        S.op("sp", lambda e: e.dma_start(out=ROPE[:, :, :], in_=Wd["rope"][:, :, :]), writes=[rROPE], dma=True)
        WDQ = self.alloc(KC * 256).rearrange("p (c n) -> p c n", n=256)
        WKV = self.alloc(KC * 320).rearrange("p (c n) -> p c n", n=320)
        CQF = self.alloc(2 * 512, F32).rearrange("p (c n) -> p c n", n=512)
        SQb = self.alloc(2 * 512).rearrange("p (c n) -> p c n", n=512)
        rWDQ, rWKV, rCQF, rSQb = Res(), Res(), Res(), Res()
        self.load_w(WDQ, Wd["dq"].rearrange("(c p) n -> p c n", p=128), rWDQ)
        self.load_w(WKV, Wd["kv"].rearrange("(c p) n -> p c n", p=128), rWKV)

        def rope_norm(ps_raw, rps_raw, ps_perm, rps_perm, g, gp, out_ap, rout, lo, hi, c0, w):
            st_ = next_set()
            SQh, rSQh, RS, rRS = st_["sq"], st_["rsq"], st_["rs"], st_["rrs"]
            S.op("act", lambda e: e.activation(out=SQh[0:96, 0:w], in_=ps_raw[0:96, 0:w], func=AF.Square),
                 reads=[], writes=[rps_raw, rSQh])
            pss, rpss = self.psum()
            S.op("pe", lambda e: e.matmul(pss[0:96, 0:w], lhsT=self.BD[0:96, 0:96], rhs=SQh[0:96, 0:w], start=True, stop=True),
                 reads=[rSQh, self.rID], writes=[rpss])
            S.op("act", lambda e: e.activation(out=RS[0:96, 0:w], in_=pss[0:96, 0:w], func=AF.Ln,
                                               scale=V96("mla_invn"), bias=self.EPSB[0:96, 0:1]),
                 reads=[self.rVEC, self.rONES], writes=[rpss, rRS])
            S.op("act", lambda e: e.activation(out=RS[0:96, 0:w], in_=RS[0:96, 0:w], func=AF.Exp, scale=-0.5), reads=[], writes=[rRS])
            t1, rt1 = st_["t1"], st_["rt1"]
            t2, rt2 = st_["t2"], st_["rt2"]
            S.op("dve", lambda e: e.scalar_tensor_tensor(out=t1[0:96, 0:w], in0=ps_raw[0:96, 0:w], scalar=g, in1=ROPE[0:96, 0, c0:c0 + w],
                                                         op0=ALU.mult, op1=ALU.mult),
                 reads=[self.rVEC, rROPE], writes=[rps_raw, rt1])
            S.op("dve", lambda e: e.scalar_tensor_tensor(out=t2[0:96, 0:w], in0=ps_perm[0:96, 0:w], scalar=gp, in1=ROPE[0:96, 1, c0:c0 + w],
                                                         op0=ALU.mult, op1=ALU.mult),
                 reads=[self.rVEC, rROPE], writes=[rps_perm, rt2])
            S.op("pool", lambda e: e.tensor_tensor(out=t1[0:96, 0:w], in0=t1[0:96, 0:w], in1=t2[0:96, 0:w], op=ALU.add),
                 reads=[rt2], writes=[rt1])
            S.op("dve", lambda e: e.tensor_tensor(out=out_ap, in0=t1[lo:hi, 0:w], in1=RS[lo:hi, 0:w], op=ALU.mult),
                 reads=[rt1, rRS], writes=[rout])

        def p1_tile(ti):
            c0, w = TILES[ti]
            rh = [self.rH[kc][ti] for kc in range(KC)]
            for oc in range(2):
                ps, rps = self.psum()

                def mm(e, ps=ps, oc=oc):
                    ins = None
                    for kc in range(KC):
                        ins = e.matmul(ps[:, 0:w], lhsT=WDQ[:, kc, oc * 128:(oc + 1) * 128], rhs=self.H[:, kc, c0:c0 + w],
                                       start=(kc == 0), stop=(kc == KC - 1))
                    return ins
                S.op("pe", mm, reads=[rWDQ] + rh, writes=[rps])
                S.op("act", lambda e, ps=ps, oc=oc: e.copy(out=CQF[:, oc, 0:w], in_=ps[:, 0:w]), reads=[], writes=[rps, rCQF])
            S.op("act", lambda e: e.activation(out=SQb[:, :, 0:w], in_=CQF[:, :, 0:w], func=AF.Square), reads=[rCQF], writes=[rSQb])
            ps, rps = self.psum()

            def mm(e, ps=ps):
                e.matmul(ps[:, 0:w], lhsT=self.ONES[:, :], rhs=SQb[:, 0, 0:w], start=True, stop=False)
                return e.matmul(ps[:, 0:w], lhsT=self.ONES[:, :], rhs=SQb[:, 1, 0:w], start=False, stop=True)
            S.op("pe", mm, reads=[rSQb, self.rONES], writes=[rps])
            S.op("act", lambda e, ps=ps: e.activation(out=self.RSTD[:, 0:w], in_=ps[:, 0:w], func=AF.Ln, scale=1.0 / 256, bias=self.EPSB[:, 0:1]),
                 reads=[self.rONES], writes=[rps, self.rRSTD])
            S.op("act", lambda e: e.activation(out=self.RSTD[:, 0:w], in_=self.RSTD[:, 0:w], func=AF.Exp, scale=-0.5), reads=[], writes=[self.rRSTD])
            for oc in range(2):
                S.op("dve", lambda e, oc=oc: e.scalar_tensor_tensor(out=CQ[:, oc, c0:c0 + w], in0=CQF[:, oc, 0:w], scalar=V("mla_qn", oc),
                                                                 in1=self.RSTD[:, 0:w], op0=ALU.mult, op1=ALU.mult),
                     reads=[rCQF, self.rVEC, self.rRSTD], writes=[rCQ[ti]])
            ps, rps = self.psum()

            def mm(e, ps=ps):
                ins = None
                for kc in range(KC):
                    ins = e.matmul(ps[:, 0:w], lhsT=WKV[:, kc, 0:128], rhs=self.H[:, kc, c0:c0 + w], start=(kc == 0), stop=(kc == KC - 1))
                return ins
            S.op("pe", mm, reads=[rWKV] + rh, writes=[rps])
            S.op("act", lambda e, ps=ps: e.copy(out=CQF[:, 0, 0:w], in_=ps[:, 0:w]), reads=[], writes=[rps, rCQF])
            S.op("act", lambda e: e.activation(out=SQb[:, 0, 0:w], in_=CQF[:, 0, 0:w], func=AF.Square), reads=[rCQF], writes=[rSQb])
            ps, rps = self.psum()
            S.op("pe", lambda e, ps=ps: e.matmul(ps[:, 0:w], lhsT=self.ONES[:, :], rhs=SQb[:, 0, 0:w], start=True, stop=True),
                 reads=[rSQb, self.rONES], writes=[rps])
            S.op("act", lambda e, ps=ps: e.activation(out=self.RSTD[:, 0:w], in_=ps[:, 0:w], func=AF.Ln, scale=1.0 / 128, bias=self.EPSB[:, 0:1]),
                 reads=[self.rONES], writes=[rps, self.rRSTD])
            S.op("act", lambda e: e.activation(out=self.RSTD[:, 0:w], in_=self.RSTD[:, 0:w], func=AF.Exp, scale=-0.5), reads=[], writes=[self.rRSTD])
            S.op("dve", lambda e: e.scalar_tensor_tensor(out=CKV[:, c0:c0 + w], in0=CQF[:, 0, 0:w], scalar=V("mla_kvn"),
                                                         in1=self.RSTD[:, 0:w], op0=ALU.mult, op1=ALU.mult),
                 reads=[rCQF, self.rVEC, self.rRSTD], writes=[rCKV[ti]])
            pss = []
            for var in range(2):
                ps, rps = self.psum()
                pss.append((ps, rps))

                def mm(e, ps=ps, var=var):
                    ins = None
                    for kc in range(KC):
                        ins = e.matmul(ps[0:96, 0:w], lhsT=WKV[:, kc, 128 + var * 96:224 + var * 96], rhs=self.H[:, kc, c0:c0 + w],
                                       start=(kc == 0), stop=(kc == KC - 1))
                    return ins
                S.op("pe", mm, reads=[rWKV] + rh, writes=[rps])
            rope_norm(pss[0][0], pss[0][1], pss[1][0], pss[1][1], V96("mla_gk"), V96("mla_gkp"),
                      KPE[64:96, c0:c0 + w], rKPE[ti], 64, 96, c0, w)

        for ti in all_tiles:
            p1_tile(ti)
        if DBG_STOP == 1:
            S.barrier()
            return

        S.barrier()
        self.arena_reset(mark)
        WUKV = self.alloc(2048)
        rWUKV = Res()
        self.load_w(WUKV, Wd["ukv"][:, :], rWUKV)
        uq_v = Wd["uq"].rearrange("(c p) n -> p c n", p=128)
        WUQh = [self.alloc(2 * 192).rearrange("p (c n) -> p c n", n=192) for _ in range(2)]
        rWUQh = [Res(), Res()]
        QH = [self.alloc(T) for _ in range(2)]
        KH = [self.alloc(T) for _ in range(2)]
        rQH = [[Res() for _ in TILES] for _ in range(2)]
        rKH = [[Res() for _ in TILES] for _ in range(2)]
        rKHpe = [Res(), Res()]
        VH = [self.alloc(18 * 128).rearrange("p (j n) -> p j n", n=128) for _ in range(2)]
        rVH = [Res(), Res()]
        NPT = 6
        PT = [self.alloc(512) for _ in range(NPT)]
        rPT = [Res() for _ in range(NPT)]
        RD = self.MU
        rRD = self.rMU
        pt_i = 0
        S.op("pool", lambda e: e.memset(VH[0][:, :, 64:128], 1.0), writes=[rVH[0]])
        S.op("pool", lambda e: e.memset(VH[1][:, :, 0:64], 1.0), writes=[rVH[1]])

        def kcols(j):
            return (C0 + 128 * j) if j < 2 else (L0 + 128 * (j - 2))

        def ktile(j):
            return 0 if j < 2 else 1 + (j - 2) // 4

        def prep(h):
            s = h % 2
            voff = 0 if s == 0 else 64
            wq, rwq = WUQh[s], rWUQh[s]
            self.load_w(wq, uq_v[:, :, h * 192:(h + 1) * 192], rwq)
            for ti in qtiles:
                c0, w = TILES[ti]
                pss = []
                for var in range(2):
                    ps, rps = self.psum()
                    pss.append((ps, rps))

                    def mm(e, ps=ps, var=var, c0=c0, w=w):
                        e.matmul(ps[0:96, 0:w], lhsT=wq[:, 0, var * 96:var * 96 + 96], rhs=CQ[:, 0, c0:c0 + w], start=True, stop=False)
                        return e.matmul(ps[0:96, 0:w], lhsT=wq[:, 1, var * 96:var * 96 + 96], rhs=CQ[:, 1, c0:c0 + w], start=False, stop=True)
                    S.op("pe", mm, reads=[rwq, rCQ[ti]], writes=[rps])
                rope_norm(pss[0][0], pss[0][1], pss[1][0], pss[1][1], V96("mla_gq"), V96("mla_gqp"),
                          QH[s][0:96, c0:c0 + w], rQH[s][ti], 0, 96, c0, w)
            S.op("pool", lambda e: e.tensor_copy(out=KH[s][64:96, :], in_=KPE[64:96, :]), reads=rKPE, writes=[rKHpe[s]])
            for ti in all_tiles:
                c0, w = TILES[ti]
                st_ = next_set()
                SQk, rSQk, RS, rRS = st_["sq"], st_["rsq"], st_["rs"], st_["rrs"]
                ps, rps = self.psum()
                S.op("pe", lambda e, ps=ps, c0=c0, w=w: e.matmul(ps[0:64, 0:w], lhsT=WUKV[:, h * 128:h * 128 + 64], rhs=CKV[:, c0:c0 + w], start=True, stop=True),
                     reads=[rWUKV, rCKV[ti]], writes=[rps])
                S.op("act", lambda e, ps=ps, w=w, SQk=SQk: e.activation(out=SQk[0:64, 0:w], in_=ps[0:64, 0:w], func=AF.Square), reads=[], writes=[rps, rSQk])
                ps2, rps2 = self.psum()
                S.op("pe", lambda e, ps2=ps2, w=w, SQk=SQk: e.matmul(ps2[0:64, 0:w], lhsT=self.BD[0:64, 0:64], rhs=SQk[0:64, 0:w], start=True, stop=True),
                     reads=[rSQk, self.rID], writes=[rps2])
                S.op("act", lambda e, ps2=ps2, w=w, RS=RS: e.activation(out=RS[0:64, 0:w], in_=ps2[0:64, 0:w], func=AF.Ln, scale=1.0 / 64, bias=self.EPSB[0:64, 0:1]),
                     reads=[self.rONES], writes=[rps2, rRS])
                S.op("act", lambda e, w=w, RS=RS: e.activation(out=RS[0:64, 0:w], in_=RS[0:64, 0:w], func=AF.Exp, scale=-0.5), reads=[], writes=[rRS])
                S.op("dve", lambda e, ps=ps, c0=c0, w=w, RS=RS: e.scalar_tensor_tensor(out=KH[s][0:64, c0:c0 + w], in0=ps[0:64, 0:w], scalar=self.VEC[0:64, off["mla_gk"]:off["mla_gk"] + 1],
                                                                             in1=RS[0:64, 0:w], op0=ALU.mult, op1=ALU.mult),
                     reads=[self.rVEC, rRS], writes=[rps, rKH[s][ti]])
            for j0 in range(0, 18, 8):
                n = min(8, 18 - j0)
                ps, rps = self.psum()

                def mm(e, ps=ps, j0=j0, n=n):
                    ins = None
                    for jj in range(n):
                        kc0 = kcols(j0 + jj)
                        ins = e.matmul(ps[:, jj * 64:(jj + 1) * 64], lhsT=CKV[:, kc0:kc0 + 128], rhs=WUKV[:, h * 128 + 64:(h + 1) * 128], start=True, stop=True)
                    return ins
                S.op("pe", mm, reads=[rWUKV] + rCKV, writes=[rps])
                S.op("dve", lambda e, ps=ps, j0=j0, n=n: e.tensor_copy(out=VH[s][:, j0:j0 + n, voff:voff + 64], in_=ps[:, 0:n * 64].rearrange("p (j d) -> p j d", d=64)),
                     reads=[], writes=[rps, rVH[s]])

        def attend(h):
            nonlocal pt_i
            s = h % 2
            ch = h // 2
            items = []
            for ti in qtiles:
                keys = list(range(18)) if ti > 0 else [0, 1]
                for j in keys:
                    items.append((ti, j, j == keys[0], j == keys[-1]))
            LA = 2
            qk = {}

            def issue_qk(idx):
                ti, j, _, _ = items[idx]
                c0, w = TILES[ti]
                kc0 = kcols(j)
                ps, rps = self.psum()
                S.op("pe", lambda e, ps=ps: e.matmul(ps[:, 0:w], lhsT=KH[s][0:96, kc0:kc0 + 128], rhs=QH[s][0:96, c0:c0 + w], start=True, stop=True),
                     reads=[rKH[s][ktile(j)], rKHpe[s], rQH[s][ti]], writes=[rps])
                qk[idx] = (ps, rps)
            for idx in range(min(LA, len(items))):
                issue_qk(idx)
            pso, rpso = None, None
            for idx, (ti, j, first, lastk) in enumerate(items):
                c0, w = TILES[ti]
                if idx + LA < len(items):
                    issue_qk(idx + LA)
                if first:
                    pso, rpso = self.psum_acc()
                ps, rps = qk.pop(idx)
                pt, rpt = PT[pt_i], rPT[pt_i]
                pt_i = (pt_i + 1) % NPT
                S.op("act", lambda e, ps=ps, pt=pt, w=w: e.activation(out=pt[:, 0:w], in_=ps[:, 0:w], func=AF.Exp, scale=att_scale),
                     reads=[], writes=[rps, rpt])
                S.op("pe", lambda e, pt=pt, j=j, w=w, pso=pso, first=first, lastk=lastk: e.matmul(
                    pso[:, 0:w], lhsT=VH[s][:, j, :], rhs=pt[:, 0:w], start=first, stop=lastk),
                    reads=[rpt, rVH[s]], writes=[rpso])
                if lastk:
                    if s == 0:
                        S.op("dve", lambda e, pso=pso, w=w: e.reciprocal(out=RD[64:128, 0:w], in_=pso[64:128, 0:w]), reads=[], writes=[rpso, rRD])
                        S.op("dve", lambda e, pso=pso, c0=c0, w=w: e.tensor_tensor(out=self.H[0:64, ch, c0:c0 + w], in0=pso[0:64, 0:w], in1=RD[64:128, 0:w], op=ALU.mult),
                             reads=[rRD], writes=[rpso, self.rH[ch][ti]])
                    else:
                        S.op("dve", lambda e, pso=pso, w=w: e.reciprocal(out=RD[0:64, 0:w], in_=pso[0:64, 0:w]), reads=[], writes=[rpso, rRD])
                        S.op("dve", lambda e, pso=pso, c0=c0, w=w: e.tensor_tensor(out=self.H[64:128, ch, c0:c0 + w], in0=pso[64:128, 0:w], in1=RD[0:64, 0:w], op=ALU.mult),
                             reads=[rRD], writes=[rpso, self.rH[ch][ti]])

        NH = 16 if DBG_HEADS is None else DBG_HEADS
        prep(0)
        for h in range(NH):
            if h + 1 < NH:
                prep(h + 1)
            attend(h)
        if DBG_STOP == 2:
            S.barrier()
            return

        S.barrier()
        self.arena_reset()
        WO = self.alloc(KC * 1024).rearrange("p (c n) -> p c n", n=1024)
        rWO = Res()
        wo_v = Wd["wo"].rearrange("(c p) n -> p c n", p=128)
        rWOh = [Res(), Res()]
        for hh in range(2):
            self.load_w(WO[:, hh * 4:(hh + 1) * 4, :], wo_v[:, hh * 4:(hh + 1) * 4, :], rWOh[hh])
        for ti in qtiles:
            c0, w = TILES[ti]
            col = self.nbc if ti == 0 else self.cur_b
            for dc in range(KC):
                ps, rps = self.psum()

                def mm(e, ps=ps, dc=dc, c0=c0, w=w):
                    ins = None
                    for c in range(KC):
                        ins = e.matmul(ps[:, 0:w], lhsT=WO[:, c, dc * 128:(dc + 1) * 128], rhs=self.H[:, c, c0:c0 + w], start=(c == 0), stop=(c == KC - 1))
                    return ins
                S.op("pe", mm, reads=rWOh + [self.rH[c][ti] for c in range(KC)], writes=[rps])
                S.op("dve", lambda e, ps=ps, dc=dc, c0=c0, w=w, col=col: e.scalar_tensor_tensor(
                    out=self.X[:, dc, c0:c0 + w], in0=ps[:, 0:w], scalar=self.MOD[:, 16 + dc, col:col + 1],
                    in1=self.X[:, dc, c0:c0 + w], op0=ALU.mult, op1=ALU.add),
                    reads=[self.rMOD], writes=[rps, self.rX[dc][ti]])

    def hyena(self, i, Wd, b):
        S = self.S
        off, _ = vec_offsets(i)
        V = lambda name, k=0: self.VEC[:, off[name] + k:off[name] + k + 1]
        V64 = lambda name: self.VEC[0:64, off[name]:off[name] + 1]
        tiles = [0, 1, 2, 3, 4]
        lo, hi = C0, L0 + L
        PI = float(np.pi)
        seqs = [("l", L, 16, L0, 0), ("c", LC, 2, C0, 16)]
        self.arena_reset()
        VX = self.alloc(KC * T).rearrange("p (c t) -> p c t", t=T)
        rVX = [Res() for _ in range(KC)]
        AROW = self.alloc(T)
        CROW = self.alloc(T, F32)
        X1ROW = self.alloc(T, F32)
        X0ROW = self.alloc(T)
        rA, rC, rX1, rX0R = Res(), Res(), Res(), Res()
        WA = [self.alloc(KC * 128).rearrange("p (c n) -> p c n", n=128) for _ in range(2)]
        rWA = [Res(), Res()]
        wa_i = [0]
        S.op("pool", lambda e: e.memset(AROW[:, :], 0.0), writes=[rA])
        S.op("pool", lambda e: e.memset(X0ROW[:, :], 0.0), writes=[rX0R])
        w_in = Wd["hy_in"].rearrange("(c p) n -> p c n", p=128)
        bi, wsh, bsh = off["hy_b_in"], off["hy_w_short"], off["hy_b_short"]

        def proj_conv(oc, dst, rdst):
            k = wa_i[0]
            wa_i[0] = 1 - k
            wa, rwa = WA[k], rWA[k]
            self.load_w(wa, w_in[:, :, oc * 128:(oc + 1) * 128], rwa)
            for ti in tiles:
                c0, w = TILES[ti]
                ps, rps = self.psum()

                def mm(e, ps=ps, c0=c0, w=w):
                    ins = None
                    for kc in range(KC):
                        ins = e.matmul(ps[:, 0:w], lhsT=wa[:, kc, :], rhs=self.H[:, kc, c0:c0 + w], start=(kc == 0), stop=(kc == KC - 1))
                    return ins
                S.op("pe", mm, reads=[rwa] + [self.rH[kc][ti] for kc in range(KC)], writes=[rps])
                S.op("act", lambda e, ps=ps, c0=c0, w=w: e.activation(out=AROW[:, c0:c0 + w], in_=ps[:, 0:w], func=AF.Identity,
                                                                   bias=self.VEC[:, bi + oc:bi + oc + 1], scale=1.0),
                     reads=[self.rVEC], writes=[rps, rA])
            S.op("dve", lambda e: e.tensor_scalar(out=CROW[:, lo:hi], in0=AROW[:, lo:hi], scalar1=self.VEC[:, wsh + 24 + oc:wsh + 25 + oc],
                                                  scalar2=self.VEC[:, bsh + oc:bsh + oc + 1], op0=ALU.mult, op1=ALU.add),
                 reads=[rA, self.rVEC], writes=[rC])
            S.op("dve", lambda e: e.scalar_tensor_tensor(out=CROW[:, lo:hi], in0=AROW[:, lo - 1:hi - 1], scalar=self.VEC[:, wsh + oc:wsh + oc + 1],
                                                         in1=CROW[:, lo:hi], op0=ALU.mult, op1=ALU.add),
                 reads=[rA, self.rVEC], writes=[rC])
            S.op("dve", lambda e: e.scalar_tensor_tensor(out=dst[:, lo:hi], in0=AROW[:, lo + 1:hi + 1], scalar=self.VEC[:, wsh + 48 + oc:wsh + 49 + oc],
                                                         in1=CROW[:, lo:hi], op0=ALU.mult, op1=ALU.add),
                 reads=[rA, self.rVEC, rC], writes=[rdst])

        x0d = Wd["x0d"]
        for c in range(KC):
            proj_conv(c, X0ROW, rX0R)
            S.op("sp", lambda e, c=c: e.dma_start(out=x0d[c * 128:(c + 1) * 128, :], in_=X0ROW[:, :]),
                 reads=[rX0R], writes=[self.rX0D[c]], dma=True)
            proj_conv(KC + c, X1ROW, rX1)
            proj_conv(2 * KC + c, CROW, rC)
            S.op("dve", lambda e, c=c: e.tensor_tensor(out=VX[:, c, lo:hi], in0=CROW[:, lo:hi], in1=X1ROW[:, lo:hi], op=ALU.mult),
                 reads=[rC, rX1], writes=[rVX[c]])
        VT = self.H[:, :, :].rearrange("p c t -> p (c t)")[:, 0:18 * D].rearrange("p (j n) -> p j n", n=D)
        rVT = Res()
        allH = [r for row in self.rH for r in row]
        for (tag, Lx, nt, col0, vt0) in seqs:
            for jt in range(nt):
                ps, rps = self.psum()
                psb = ps[:, :].bitcast(BF16)

                def tr(e, psb=psb, jt=jt, col0=col0):
                    ins = None
                    for c in range(KC):
                        ins = e.transpose(out=psb[:, c * 128:(c + 1) * 128], in_=VX[:, c, col0 + jt * 128:col0 + (jt + 1) * 128], identity=self.IDENT[:, :])
                    return ins
                S.op("pe", tr, reads=rVX + [self.rID], writes=[rps])
                S.op("act", lambda e, psb=psb, jt=jt, vt0=vt0: e.copy(out=VT[:, vt0 + jt, :], in_=psb[:, 0:D]),
                     reads=[], writes=[rps, rVT] + (allH if (jt == 0 and tag == "l") else []))
        S.barrier()
        if b == 0:
            for (tag, Lx, nt, col0, vt0) in seqs:
                self.hy_filter(i, Wd, tag, Lx, nt)
                S.barrier()
        w_out = Wd["hy_out"]
        for q in range(4):
            self.arena_reset()
            X0q = self.alloc(2 * T).rearrange("p (c t) -> p c t", t=T)
            rX0q = Res()
            WOq = self.alloc(2 * D).rearrange("p (c n) -> p c n", n=D)
            rWOq = Res()
            S.op("sp", lambda e, q=q: e.dma_start(out=X0q[:, :, :], in_=x0d[q * 256:(q + 1) * 256, :].rearrange("(c p) t -> p c t", p=128)),
                 reads=self.rX0D, writes=[rX0q], dma=True)
            self.load_w(WOq, w_out[q * 256:(q + 1) * 256, :].rearrange("(c p) n -> p c n", p=128), rWOq)
            Z = self.alloc(2 * 256).rearrange("p (c n) -> p c n", n=256)
            rZ = Res()
            KSb = [self.alloc(2 * 256, F32).rearrange("p (c n) -> p c n", n=256) for _ in range(2)]
            rKSb = [Res(), Res()]
            SC = [self.MU[:, 0:256], self.RSTD2[:, 0:256]]
            rSC = [self.rMU, self.rRSTD2]
            mark = self.a_off
            for (tag, Lx, nt, col0, vt0) in seqs:
                S.barrier()
                self.arena_reset(mark)
                self.hy_conv(i, Wd, q, tag, Lx, nt, col0, vt0, VT, rVT, X0q, rX0q, WOq, rWOq, Z, rZ, KSb, rKSb, SC, rSC)
            S.barrier()

    def hy_filter(self, i, Wd, tag, Lx, nt):
        S = self.S
        off, _ = vec_offsets(i)
        V64 = lambda name: self.VEC[0:64, off[name]:off[name] + 1]
        PI = float(np.pi)
        self.arena_reset()
        ZT = self.alloc(Lx, F32)
        HD1 = self.alloc(Lx, F32)
        HD2 = self.alloc(Lx, F32)
        FW1 = self.alloc(64, F32)
        FW2 = self.alloc(64, F32)
        NEGT = self.alloc(16, F32)
        CST = self.alloc(8, F32)
        ONESF = self.alloc(128, F32)
        rZT, rHD1, rHD2, rFW, rNEGT, rCST, rONESF = (Res() for _ in range(7))
        S.op("sp", lambda e: e.dma_start(out=ZT[0:33, :], in_=Wd[f"z_{tag}"][:, :]), writes=[rZT], dma=True)
        S.op("sp", lambda e: e.dma_start(out=FW1[0:33, :], in_=Wd["fw1"][:, :]), writes=[rFW], dma=True)
        S.op("sp", lambda e: e.dma_start(out=FW2[0:64, :], in_=Wd["fw2"][:, :]), writes=[rFW], dma=True)
        S.op("sp", lambda e: e.dma_start(out=NEGT[:, 0:nt], in_=Wd[f"negt_{tag}"][:, :]), writes=[rNEGT], dma=True)
        S.op("pool", lambda e: e.memset(ONESF[:, :], 1.0), writes=[rONESF])
        S.op("pool", lambda e: e.memset(CST[:, 2:3], -PI), writes=[rCST])
        S.op("dve", lambda e: e.tensor_tensor(out=CST[0:64, 0:1], in0=V64("hy_fb1"), in1=V64("hy_fr"), op=ALU.mult), reads=[self.rVEC], writes=[rCST])
        S.op("dve", lambda e: e.tensor_tensor(out=CST[0:64, 1:2], in0=V64("hy_fb2"), in1=V64("hy_fr"), op=ALU.mult), reads=[self.rVEC], writes=[rCST])

        RR, rRR = self.MU, self.rMU

        def sin_layer(src, rsrc, kdim, wmat, cstcol, dst, rdst):
            for t0 in range(0, Lx, 512):
                w = min(512, Lx - t0)
                ps, rps = self.psum()
                S.op("pe", lambda e, ps=ps, t0=t0, w=w: e.matmul(ps[0:64, 0:w], lhsT=wmat[0:kdim, 0:64], rhs=src[0:kdim, t0:t0 + w], start=True, stop=True),
                     reads=[rFW, rsrc], writes=[rps])
                S.op("act", lambda e, ps=ps, t0=t0, w=w: e.activation(out=dst[0:64, t0:t0 + w], in_=ps[0:64, 0:w], func=AF.Identity,
                                                                   scale=V64("hy_fr"), bias=CST[0:64, cstcol:cstcol + 1]),
                     reads=[self.rVEC, rCST], writes=[rps, rdst])
                MAGIC = 12582912.0
                S.op("dve", lambda e, t0=t0, w=w: e.tensor_scalar(out=RR[0:64, 0:w], in0=dst[0:64, t0:t0 + w], scalar1=1.0 / (2.0 * PI), scalar2=MAGIC,
                                                                op0=ALU.mult, op1=ALU.add),
                     reads=[rdst], writes=[rRR])
                S.op("dve", lambda e, w=w: e.tensor_scalar(out=RR[0:64, 0:w], in0=RR[0:64, 0:w], scalar1=MAGIC, scalar2=-2.0 * PI,
                                                         op0=ALU.subtract, op1=ALU.mult),
                     reads=[], writes=[rRR])
                S.op("dve", lambda e, t0=t0, w=w: e.tensor_tensor(out=dst[0:64, t0:t0 + w], in0=dst[0:64, t0:t0 + w], in1=RR[0:64, 0:w], op=ALU.add),
                     reads=[rRR], writes=[rdst])
                S.op("dve", lambda e, t0=t0, w=w: e.tensor_scalar(out=dst[0:64, t0:t0 + w], in0=dst[0:64, t0:t0 + w], scalar1=-PI, scalar2=PI,
                                                                op0=ALU.max, op1=ALU.min),
                     reads=[], writes=[rdst])
                S.op("act", lambda e, t0=t0, w=w: e.activation(out=dst[0:64, t0:t0 + w], in_=dst[0:64, t0:t0 + w], func=AF.Sin, scale=1.0),
                     reads=[], writes=[rdst])
        sin_layer(ZT, rZT, 33, FW1, 0, HD1, rHD1)
        sin_layer(HD1, rHD1, 64, FW2, 1, HD2, rHD2)
        mark = self.a_off
        kspec = Wd[f"kspec_{tag}"]
        dfm = {0: Wd[f"dfc_{tag}"], 1: Wd[f"dfs_{tag}"]}
        nf = nt
        for q in range(4):
            S.barrier()
            self.arena_reset(mark)
            A = self.alloc(nt * 256).rearrange("p (j n) -> p j n", n=256)
            Bm = self.alloc(nt * 256).rearrange("p (j n) -> p j n", n=256)
            rAB = Res()
            FW3 = self.alloc(512, F32)
            ABSD = self.alloc(256, F32)
            SKq = self.alloc(256, F32)
            RN = self.alloc(256, F32)
            rFW3, rABSD, rSK, rRN = Res(), Res(), Res(), Res()
            DEC = self.alloc(256, F32)
            KF = self.alloc(256, F32)
            KB = self.alloc(256, F32)
            AF_ = self.alloc(256, F32)
            AB_ = self.alloc(256, F32)
            rDEC, rKF, rKB, rAF, rABb = (Res() for _ in range(5))
            DF = [[self.alloc(nt * 128).rearrange("p (j n) -> p j n", n=128) for _ in range(2)] for _ in range(2)]
            rDF = [[Res(), Res()], [Res(), Res()]]
            KS4 = [[self.alloc(256, F32) for _ in range(2)] for _ in range(2)]
            rKS4 = [[Res(), Res()], [Res(), Res()]]
            S.op("sp", lambda e, q=q: e.dma_start(out=FW3[0:64, 0:256], in_=Wd["fw3"][:, q * 256:(q + 1) * 256]), writes=[rFW3], dma=True)
            S.op("sp", lambda e, q=q: e.dma_start(out=FW3[0:64, 256:512], in_=Wd["fw3"][:, D + q * 256:D + (q + 1) * 256]), writes=[rFW3], dma=True)
            S.op("sp", lambda e, q=q: e.dma_start(out=ABSD[:, :], in_=Wd["absd"][:, q * 256:(q + 1) * 256]), writes=[rABSD], dma=True)
            S.op("sp", lambda e, q=q: e.dma_start(out=SKq[:, :], in_=Wd["skipb"][:, q * 256:(q + 1) * 256]), writes=[rSK], dma=True)
            psS, rpsS = self.psum_acc()
            for j in range(nt):
                ps, rps = self.psum()
                S.op("pe", lambda e, ps=ps, j=j: e.matmul(ps[:, 0:512], lhsT=HD2[0:64, j * 128:(j + 1) * 128], rhs=FW3[0:64, 0:512], start=True, stop=True),
                     reads=[rHD2, rFW3], writes=[rps])
                S.op("act", lambda e, j=j: e.activation(out=DEC[:, :], in_=ABSD[:, :], func=AF.Exp, scale=NEGT[:, j:j + 1]),
                     reads=[rABSD, rNEGT], writes=[rDEC])
                S.op("dve", lambda e, ps=ps: e.tensor_tensor(out=KF[:, :], in0=ps[:, 0:256], in1=DEC[:, :], op=ALU.mult), reads=[rDEC], writes=[rps, rKF])
                S.op("dve", lambda e, ps=ps: e.tensor_tensor(out=KB[:, :], in0=ps[:, 256:512], in1=DEC[:, :], op=ALU.mult), reads=[rDEC], writes=[rps, rKB])
                if j == 0:
                    S.op("pool", lambda e: e.memset(KB[0:1, :], 0.0), writes=[rKB])
                S.op("pool", lambda e, j=j: e.tensor_tensor(out=A[:, j, :], in0=KF[:, :], in1=KB[:, :], op=ALU.add), reads=[rKF, rKB], writes=[rAB])
                S.op("pool", lambda e, j=j: e.tensor_tensor(out=Bm[:, j, :], in0=KF[:, :], in1=KB[:, :], op=ALU.subtract), reads=[rKF, rKB], writes=[rAB])
                S.op("act", lambda e: e.activation(out=AF_[:, :], in_=KF[:, :], func=AF.Abs), reads=[rKF], writes=[rAF])
                S.op("act", lambda e: e.activation(out=AB_[:, :], in_=KB[:, :], func=AF.Abs), reads=[rKB], writes=[rABb])
                S.op("pe", lambda e, j=j: e.matmul(psS[:, 0:256], lhsT=ONESF[:, :], rhs=AF_[:, :], start=(j == 0), stop=False), reads=[rONESF, rAF], writes=[rpsS])
                S.op("pe", lambda e, j=j: e.matmul(psS[:, 0:256], lhsT=ONESF[:, :], rhs=AB_[:, :], start=False, stop=(j == nt - 1)), reads=[rONESF, rABb], writes=[rpsS])
            S.op("dve", lambda e: e.tensor_scalar_add(out=RN[:, :], in0=psS[:, 0:256], scalar1=EPS), reads=[], writes=[rpsS, rRN])
            S.op("dve", lambda e: e.reciprocal(out=RN[:, :], in_=RN[:, :]), reads=[], writes=[rRN])
            for fb in range(nf):
                for typ in range(2):
                    df, rdf = DF[fb % 2][typ], rDF[fb % 2][typ]
                    S.op("sp", lambda e, fb=fb, typ=typ, df=df: e.dma_start(out=df[:, :, :], in_=dfm[typ][fb].rearrange("p (j n) -> p j n", n=128)), writes=[rdf], dma=True)
                    ps, rps = self.psum()
                    src = A if typ == 0 else Bm

                    def mm(e, ps=ps, typ=typ, src=src, df=df):
                        ins = None
                        for j in range(nt):
                            ins = e.matmul(ps[:, 0:256], lhsT=df[:, j, :], rhs=src[:, j, :], start=(j == 0), stop=(j == nt - 1))
                        return ins
                    S.op("pe", mm, reads=[rdf, rAB], writes=[rps])
                    ks, rks = KS4[fb % 2][typ], rKS4[fb % 2][typ]
                    S.op("dve", lambda e, ps=ps, ks=ks: e.tensor_tensor(out=ks[:, :], in0=ps[:, 0:256], in1=RN[:, :], op=ALU.mult), reads=[rRN], writes=[rps, rks])
                    if typ == 0:
                        S.op("pool", lambda e, ks=ks: e.tensor_tensor(out=ks[:, :], in0=ks[:, :], in1=SKq[:, :], op=ALU.add), reads=[rSK], writes=[rks])
                    S.op("act", lambda e, fb=fb, typ=typ, q=q, ks=ks: e.dma_start(out=kspec[typ, fb * 128:(fb + 1) * 128, q * 256:(q + 1) * 256], in_=ks[:, :]),
                         reads=[rks], writes=[self.rKSW[tag][(fb * 2 + typ) % 8]], dma=True)

    def hy_conv(self, i, Wd, q, tag, Lx, nt, col0, vt0, VT, rVT, X0q, rX0q, WOq, rWOq, Z, rZ, KSb, rKSb, SC, rSC):
        S = self.S
        off, _ = vec_offsets(i)
        nf = nt
        kspec = Wd[f"kspec_{tag}"]
        dfm = {0: Wd[f"dfc_{tag}"], 1: Wd[f"dfs_{tag}"]}
        idm = {0: Wd[f"idc_{tag}"], 1: Wd[f"ids_{tag}"]}
        Y = self.alloc(nf * 512).rearrange("p (f y n) -> p f y n", y=2, n=256)
        rY = Res()
        DF = [[self.alloc(nt * 128).rearrange("p (j n) -> p j n", n=128) for _ in range(2)] for _ in range(2)]
        rDF = [[Res(), Res()], [Res(), Res()]]
        IB = [self.alloc(2 * nf * 256).rearrange("p (y f n) -> p y f n", y=2, n=256) for _ in range(2)]
        rIB = [Res(), Res()]
        U = [self.TMP[0], self.TMP[1]]
        rU = [self.rTMP[0], self.rTMP[1]]
        col = self.nbc if tag == "c" else self.cur_b
        bo = off["hy_b_out"]
        for fb in range(nf):
            kb = fb % 2
            pss = []
            for typ in range(2):
                df, rdf = DF[kb][typ], rDF[kb][typ]
                S.op("sp", lambda e, fb=fb, typ=typ, df=df: e.dma_start(out=df[:, :, :], in_=dfm[typ][fb].rearrange("p (j n) -> p j n", n=128)), writes=[rdf], dma=True)
                ps, rps = self.psum()
                pss.append((ps, rps))

                def mm(e, ps=ps, typ=typ, df=df):
                    ins = None
                    for j in range(nt):
                        ins = e.matmul(ps[:, 0:256], lhsT=df[:, j, :], rhs=VT[:, vt0 + j, q * 256:(q + 1) * 256], start=(j == 0), stop=(j == nt - 1))
                    return ins
                S.op("pe", mm, reads=[rdf, rVT], writes=[rps])
                S.op("act", lambda e, ps=ps, typ=typ: e.copy(out=U[typ][:, 0:256], in_=ps[:, 0:256]), reads=[], writes=[rps, rU[typ]])
            S.op("sp", lambda e, fb=fb, kb=kb: e.dma_start(out=KSb[kb][:, :, :], in_=kspec[:, fb * 128:(fb + 1) * 128, q * 256:(q + 1) * 256].rearrange("y p n -> p y n")),
                 reads=[self.rKS[tag]] + self.rKSW[tag], writes=[rKSb[kb]], dma=True)
            Kc, Ks = KSb[kb][:, 0, :], KSb[kb][:, 1, :]
            S.op("dve", lambda e, Kc=Kc: e.tensor_tensor(out=SC[0][:, :], in0=U[0][:, 0:256], in1=Kc, op=ALU.mult), reads=[rU[0], rKSb[kb]], writes=[rSC[0]])
            S.op("pool", lambda e, Ks=Ks: e.tensor_tensor(out=SC[1][:, :], in0=U[1][:, 0:256], in1=Ks, op=ALU.mult), reads=[rU[1], rKSb[kb]], writes=[rSC[1]])
            S.op("dve", lambda e, fb=fb: e.tensor_tensor(out=Y[:, fb, 0, :], in0=SC[0][:, :], in1=SC[1][:, :], op=ALU.subtract), reads=[rSC[0], rSC[1]], writes=[rY])
            S.op("dve", lambda e, Ks=Ks: e.tensor_tensor(out=SC[0][:, :], in0=U[0][:, 0:256], in1=Ks, op=ALU.mult), reads=[rU[0], rKSb[kb]], writes=[rSC[0]])
            S.op("pool", lambda e, Kc=Kc: e.tensor_tensor(out=SC[1][:, :], in0=U[1][:, 0:256], in1=Kc, op=ALU.mult), reads=[rU[1], rKSb[kb]], writes=[rSC[1]])
            S.op("dve", lambda e, fb=fb: e.tensor_tensor(out=Y[:, fb, 1, :], in0=SC[0][:, :], in1=SC[1][:, :], op=ALU.add), reads=[rSC[0], rSC[1]], writes=[rY])
        for ts in range(Lx // 256):
            ib, rib = IB[ts % 2], rIB[ts % 2]
            for typ in range(2):
                S.op("sp", lambda e, ib=ib, typ=typ, ts=ts: e.dma_start(out=ib[:, typ, :, :], in_=idm[typ][ts].rearrange("p (f n) -> p f n", n=256)), writes=[rib], dma=True)
            cs = col0 + ts * 256
            ti = 0 if tag == "c" else 1 + (ts // 2)
            for cc in range(2):
                ps, rps = self.psum()

                def mm(e, ps=ps, ib=ib, cc=cc):
                    ins = None
                    for fb in range(nf):
                        for typ in range(2):
                            ins = e.matmul(ps[:, 0:256], lhsT=Y[:, fb, typ, cc * 128:(cc + 1) * 128], rhs=ib[:, typ, fb, :],
                                           start=(fb == 0 and typ == 0), stop=(fb == nf - 1 and typ == 1))
                    return ins
                S.op("pe", mm, reads=[rY, rib], writes=[rps])
                S.op("dve", lambda e, ps=ps, cc=cc, cs=cs: e.tensor_tensor(out=Z[:, cc, :], in0=ps[:, 0:256], in1=X0q[:, cc, cs:cs + 256], op=ALU.mult),
                     reads=[rX0q], writes=[rps, rZ])
            for o in range(KC):
                ps, rps = self.psum()

                def mm2(e, ps=ps, o=o):
                    e.matmul(ps[:, 0:256], lhsT=WOq[:, 0, o * 128:(o + 1) * 128], rhs=Z[:, 0, :], start=True, stop=False)
                    return e.matmul(ps[:, 0:256], lhsT=WOq[:, 1, o * 128:(o + 1) * 128], rhs=Z[:, 1, :], start=False, stop=True)
                S.op("pe", mm2, reads=[rWOq, rZ], writes=[rps])
                if q == 0:
                    S.op("dve", lambda e, ps=ps, o=o: e.tensor_scalar(out=self.RSTD[:, 0:256], in0=ps[:, 0:256], scalar1=self.VEC[:, bo + o:bo + o + 1],
                                                                   scalar2=self.MOD[:, 16 + o, col:col + 1], op0=ALU.add, op1=ALU.mult),
                         reads=[self.rVEC, self.rMOD], writes=[rps, self.rRSTD])
                    S.op("dve", lambda e, o=o, cs=cs: e.tensor_tensor(out=self.X[:, o, cs:cs + 256], in0=self.X[:, o, cs:cs + 256], in1=self.RSTD[:, 0:256], op=ALU.add),
                         reads=[self.rRSTD], writes=[self.rX[o][ti]])
                else:
                    S.op("dve", lambda e, ps=ps, o=o, cs=cs: e.scalar_tensor_tensor(out=self.X[:, o, cs:cs + 256], in0=ps[:, 0:256], scalar=self.MOD[:, 16 + o, col:col + 1],
                                                                                  in1=self.X[:, o, cs:cs + 256], op0=ALU.mult, op1=ALU.add),
                         reads=[self.rMOD], writes=[rps, self.rX[o][ti]])

    def layer(self, i, Wd, b, last):
        kind = i % 3
        tiles_all = [0, 1, 2, 3, 4]
        tiles_out = [1, 2, 3, 4] if last else tiles_all
        self.adaln(i, Wd)
        if kind == 0:
            self.modulate(0, tiles_all)
            self.S.barrier()
            self.mla(i, Wd, need_ctx=not last)
            self.S.barrier()
            self.carve_ffn()
            self.zero_pads()
        if kind == 2:
            self.modulate(0, tiles_all)
            self.S.barrier()
            self.hyena(i, Wd, b)
            self.S.barrier()
            self.carve_ffn()
            self.zero_pads()
        if kind == 1:
            self.modulate(0, tiles_out)
            self.conformer(i, Wd, tiles_out)
        self.modulate(1, tiles_out)
        self.ffn(i, Wd, tiles_out)

    def ubuf(self, c):
        if c < 6:
            return self.G[:, c, :], self.rG[c]
        return (self.AROW, self.rA) if c == 6 else (self.VROW, self.rV)

    def conformer(self, i, Wd, tiles):
        S = self.S
        off, _ = vec_offsets(i)
        w1 = Wd["cf_w1"].rearrange("(c p) n -> p c n", p=128)
        w2 = Wd["cf_w2"].rearrange("(c p) n -> p c n", p=128)
        b1o, wdo, bdo, lgo, lbo, b2o = (off[k] for k in ("cf_b_pw1", "cf_w_dw", "cf_b_dw", "cf_ln_g", "cf_ln_b", "cf_b_pw2"))
        for (a, bnd) in ((0, 16), (C0 + LC, L0), (L0 + L, T)):
            S.op("pool", lambda e, a=a, bnd=bnd: e.memset(self.G[:, :, a:bnd], 0.0), writes=self.rG)
        for c in range(KC):
            k = self.wa_i
            self.wa_i = 1 - k
            wa, rwa = self.WA[k], self.rWA[k]
            self.load_w(wa[:, :, 0:128], w1[:, :, c * 128:(c + 1) * 128], [rwa[0]] + self.rWF[2 * k] + self.rWF[2 * k + 1])
            self.load_w(wa[:, :, 128:256], w1[:, :, D + c * 128:D + (c + 1) * 128], [rwa[1]] + self.rWF[2 * k] + self.rWF[2 * k + 1])
            ub, rub = self.ubuf(c)
            for ti in tiles:
                c0, w = TILES[ti]
                pss = []
                for half in range(2):
                    ps, rps = self.psum()
                    pss.append((ps, rps))

                    def mm(e, ps=ps, wa=wa, half=half, c0=c0, w=w):
                        ins = None
                        for kc in range(KC):
                            ins = e.matmul(ps[:, 0:w], lhsT=wa[:, kc, half * 128:(half + 1) * 128],
                                           rhs=self.H[:, kc, c0:c0 + w], start=(kc == 0), stop=(kc == KC - 1))
                        return ins
                    S.op("pe", mm, reads=[rwa[half]] + [self.rH[kc][ti] for kc in range(KC)], writes=[rps])
                tp, rtp = self.tmp()
                S.op("act", lambda e, ps=pss[1][0], tp=tp, w=w, c=c: e.activation(
                    out=tp[:, 0:w], in_=ps[:, 0:w], func=AF.Sigmoid, bias=self.VEC[:, b1o + 8 + c:b1o + 9 + c], scale=1.0),
                    reads=[self.rVEC], writes=[pss[1][1], rtp])
                S.op("dve", lambda e, ps=pss[0][0], tp=tp, ub=ub, c0=c0, w=w, c=c: e.scalar_tensor_tensor(
                    out=ub[:, c0:c0 + w], in0=ps[:, 0:w], scalar=self.VEC[:, b1o + c:b1o + c + 1], in1=tp[:, 0:w],
                    op0=ALU.add, op1=ALU.mult),
                    reads=[self.rVEC, rtp], writes=[pss[0][1], rub])
        dgs = [self.WB[:, 0:31 * 128].rearrange("p (k n) -> p k n", n=128),
               self.WA[0][:, :, :].rearrange("p c n -> p (c n)")[:, 0:31 * 128].rearrange("p (k n) -> p k n", n=128)]
        rdgs = [[self.rWB], self.wa_all(0)]
        for c in range(KC):
            ub, rub = self.ubuf(c)
            dg, rdg = dgs[c % 2], rdgs[c % 2]
            for kk in range(31):
                sc = self.VEC[:, wdo + kk * 8 + c:wdo + kk * 8 + c + 1]
                S.op("pool" if kk % 2 else "dve", lambda e, kk=kk, dg=dg, sc=sc: e.tensor_scalar_mul(out=dg[:, kk, :], in0=self.IDENT[:, :], scalar1=sc),
                     reads=[self.rID, self.rVEC], writes=rdg)
            for ti in tiles:
                c0, w = TILES[ti]
                ps, rps = self.psum()

                def mm(e, ps=ps, ub=ub, c0=c0, w=w, dg=dg):
                    ins = None
                    for kk in range(31):
                        ins = e.matmul(ps[:, 0:w], lhsT=dg[:, kk, :], rhs=ub[:, c0 + kk - 15:c0 + kk - 15 + w],
                                       start=(kk == 0), stop=(kk == 30))
                    return ins
                S.op("pe", mm, reads=rdg + [rub], writes=[rps])
                S.op("act", lambda e, ps=ps, c=c, c0=c0, w=w: e.activation(
                    out=self.H[:, c, c0:c0 + w], in_=ps[:, 0:w], func=AF.Identity, bias=self.VEC[:, bdo + c:bdo + c + 1], scale=1.0),
                    reads=[self.rVEC], writes=[rps, self.rH[c][ti]])
        for k in range(2):
            self.load_w(self.WA[k][:, :, :], w2[:, :, k * 512:(k + 1) * 512], self.wa_all(k))
        for ti in tiles:
            c0, w = TILES[ti]
            col = self.nbc if ti == 0 else self.cur_b
            for c in range(KC):
                S.op("act", lambda e, c=c, c0=c0, w=w: e.activation(out=self.SQ[:, c, 0:w], in_=self.H[:, c, c0:c0 + w], func=AF.Square),
                     reads=[self.rH[c][ti]], writes=[self.rSQ])
            ps1, rps1 = self.psum()
            ps2, rps2 = self.psum()

            def mm1(e, ps=ps1, c0=c0, w=w):
                ins = None
                for c in range(KC):
                    ins = e.matmul(ps[:, 0:w], lhsT=self.ONES[:, :], rhs=self.H[:, c, c0:c0 + w], start=(c == 0), stop=(c == KC - 1))
                return ins
            S.op("pe", mm1, reads=[self.rONES] + [self.rH[c][ti] for c in range(KC)], writes=[rps1])

            def mm2(e, ps=ps2, w=w):
                ins = None
                for c in range(KC):
                    ins = e.matmul(ps[:, 0:w], lhsT=self.ONES[:, :], rhs=self.SQ[:, c, 0:w], start=(c == 0), stop=(c == KC - 1))
                return ins
            S.op("pe", mm2, reads=[self.rONES, self.rSQ], writes=[rps2])
            S.op("act", lambda e, ps=ps1, w=w: e.activation(out=self.MU[:, 0:w], in_=ps[:, 0:w], func=AF.Identity, scale=1.0 / D),
                 reads=[], writes=[rps1, self.rMU])
            tp, rtp = self.tmp()
            S.op("dve", lambda e, tp=tp, w=w: e.tensor_tensor(out=tp[:, 0:w], in0=self.MU[:, 0:w], in1=self.MU[:, 0:w], op=ALU.mult),
                 reads=[self.rMU], writes=[rtp])
            S.op("dve", lambda e, ps=ps2, tp=tp, w=w: e.scalar_tensor_tensor(
                out=self.RSTD[:, 0:w], in0=ps[:, 0:w], scalar=1.0 / D, in1=tp[:, 0:w], op0=ALU.mult, op1=ALU.subtract),
                reads=[rtp], writes=[rps2, self.rRSTD])
            S.op("act", lambda e, w=w: e.activation(out=self.RSTD[:, 0:w], in_=self.RSTD[:, 0:w], func=AF.Ln, scale=1.0, bias=self.EPSB[:, 0:1]),
                 reads=[self.rONES], writes=[self.rRSTD])
            S.op("act", lambda e, w=w: e.activation(out=self.RSTD[:, 0:w], in_=self.RSTD[:, 0:w], func=AF.Exp, scale=-0.5), reads=[], writes=[self.rRSTD])
            for c in range(KC):
                ub, rub = self.ubuf(c)
                tp, rtp = self.tmp()
                S.op("dve", lambda e, tp=tp, c=c, c0=c0, w=w: e.tensor_tensor(out=tp[:, 0:w], in0=self.H[:, c, c0:c0 + w], in1=self.MU[:, 0:w], op=ALU.subtract),
                     reads=[self.rH[c][ti], self.rMU], writes=[rtp])
                S.op("dve", lambda e, tp=tp, w=w: e.tensor_tensor(out=tp[:, 0:w], in0=tp[:, 0:w], in1=self.RSTD[:, 0:w], op=ALU.mult),
                     reads=[self.rRSTD], writes=[rtp])
                S.op("act", lambda e, tp=tp, ub=ub, c=c, c0=c0, w=w: e.activation(
                    out=ub[:, c0:c0 + w], in_=tp[:, 0:w], func=AF.Silu, scale=self.VEC[:, lgo + c:lgo + c + 1], bias=self.VEC[:, lbo + c:lbo + c + 1]),
                    reads=[rtp, self.rVEC], writes=[rub])
            for dc in range(KC):
                ps, rps = self.psum()
                wa, rwa = self.WA[dc // 4], self.rWA[dc // 4]

                def mm(e, ps=ps, wa=wa, dc=dc, c0=c0, w=w):
                    ins = None
                    for c in range(KC):
                        ins = e.matmul(ps[:, 0:w], lhsT=wa[:, c, (dc % 4) * 128:(dc % 4 + 1) * 128], rhs=self.ubuf(c)[0][:, c0:c0 + w],
                                       start=(c == 0), stop=(c == KC - 1))
                    return ins
                S.op("pe", mm, reads=[rwa[0]] + [self.ubuf(c)[1] for c in range(KC)], writes=[rps])
                tp, rtp = self.tmp()
                S.op("dve", lambda e, ps=ps, tp=tp, dc=dc, w=w, col=col: e.tensor_scalar(
                    out=tp[:, 0:w], in0=ps[:, 0:w], scalar1=self.VEC[:, b2o + dc:b2o + dc + 1], scalar2=self.MOD[:, 16 + dc, col:col + 1],
                    op0=ALU.add, op1=ALU.mult),
                    reads=[self.rVEC, self.rMOD], writes=[rps, rtp])
                S.op("dve", lambda e, tp=tp, dc=dc, c0=c0, w=w: e.tensor_tensor(out=self.X[:, dc, c0:c0 + w], in0=self.X[:, dc, c0:c0 + w], in1=tp[:, 0:w], op=ALU.add),
                     reads=[rtp], writes=[self.rX[dc][ti]])


def make_inputs(inp, layers, core, nb=NB, x_override=None, ctx_override=None):
    m = {}
    xs = inp["x"] if x_override is None else x_override
    cs = inp["ctx"] if ctx_override is None else ctx_override
    xT = np.zeros((nb, D, T), np.float32)
    sv = np.stack([inp["c"][core * nb + bb] for bb in range(nb)] + [inp["c_ctx"]], axis=-1).astype(np.float32)
    cT = np.ascontiguousarray(sv.reshape(KC, 128, nb + 1).transpose(1, 0, 2).reshape(128, KC * (nb + 1)))
    for bb in range(nb):
        b = core * nb + bb
        xT[bb, :, L0:L0 + L] = xs[b].T
        xT[bb, :, C0:C0 + LC] = cs[b].T
    m["ident"] = np.eye(128, dtype=np.float32)
    bd = np.zeros((128, 128), np.float32)
    bd[:64, :64] = 1.0
    bd[64:96, 64:96] = 1.0
    m["bd96"] = bd
    m["xT"] = xT
    m["cT"] = cT
    for i in layers:
        kind, j = i % 3, i // 3
        v, _ = pack_layer_vec(inp, i)
        vp = np.zeros((128, NPV), np.float32)
        vp[:, :v.shape[1]] = v
        m[f"vec{i}"] = vp
        m[f"ada_w{i}"] = np.ascontiguousarray(inp["ada_w"][i])
        m[f"ffn_up{i}"] = np.ascontiguousarray(inp["ffn_w_up"][i])
        m[f"ffn_down{i}"] = np.ascontiguousarray(inp["ffn_w_down"][i])
        if kind == 0:
            pe = np.arange(32)
            perm = np.where((pe % 16) < 8, pe + 8, pe - 8)
            wdkv = np.asarray(inp["mla_w_dkv"][j], np.float32)
            kv = np.zeros((D, 320), np.float32)
            kv[:, :128] = wdkv[:, :128]
            kv[:, 128 + 64:224] = wdkv[:, 128:160]
            kv[:, 224 + 64:320] = wdkv[:, 128 + perm]
            m[f"mla_kv{i}"] = kv
            wuq = np.asarray(inp["mla_w_uq"][j], np.float32).reshape(256, 16, 96)
            uq = np.zeros((256, 16, 192), np.float32)
            uq[:, :, :96] = wuq
            uq[:, :, 96 + 64:] = wuq[:, :, 64 + perm]
            m[f"mla_uq{i}"] = uq.reshape(256, 16 * 192)
            m[f"mla_dq{i}"] = np.ascontiguousarray(inp["mla_w_dq"][j])
            m[f"mla_ukv{i}"] = np.ascontiguousarray(inp["mla_w_ukv"][j])
            m[f"mla_o{i}"] = np.ascontiguousarray(inp["mla_w_o"][j])
            m["rope_cs"] = rope_tables()
        if kind == 2:
            m[f"hy_in{i}"] = np.ascontiguousarray(inp["hy_w_in"][j])
            m[f"hy_out{i}"] = np.ascontiguousarray(inp["hy_w_out"][j])
            m[f"hy_fw1_{i}"] = np.ascontiguousarray(inp["hy_f_w1"][j])
            m[f"hy_fw2_{i}"] = np.ascontiguousarray(inp["hy_f_w2"][j])
            m[f"hy_fw3_{i}"] = np.ascontiguousarray(inp["hy_f_w3"][j])
            m[f"hy_skipb{i}"] = np.ascontiguousarray(np.broadcast_to(np.asarray(inp["hy_skip"][j], np.float32)[None, :], (128, D)))
            m.update(hy_consts())
        if kind == 1:
            m[f"cf_w1_{i}"] = np.ascontiguousarray(inp["cf_w_pw1"][j])
            m[f"cf_w2_{i}"] = np.ascontiguousarray(inp["cf_w_pw2"][j])
    return m


_HYC = {}


def hy_consts():
    if _HYC:
        return _HYC
    import ml_dtypes
    bf = ml_dtypes.bfloat16
    out = {}
    for tag, Lx in (("l", L), ("c", LC)):
        N = 2 * Lx
        tau = np.arange(Lx, dtype=np.float64)[:, None]
        f = np.arange(Lx, dtype=np.float64)[None, :] + 0.5
        th = 2 * np.pi * tau * f / N
        nt_ = Lx // 128
        nts_ = Lx // 256

        def fwd_blocks(M):
            return np.ascontiguousarray(M.reshape(nt_, 128, nt_, 128).transpose(2, 1, 0, 3).reshape(nt_, 128, nt_ * 128))

        def inv_blocks(M):
            return np.ascontiguousarray(M.reshape(nt_, 128, nts_, 256).transpose(2, 1, 0, 3).reshape(nts_, 128, nt_ * 256))
        out[f"dfc_{tag}"] = fwd_blocks(np.cos(th).astype(np.float32)).astype(bf)
        out[f"dfs_{tag}"] = fwd_blocks(np.sin(th).astype(np.float32)).astype(bf)
        out[f"idc_{tag}"] = inv_blocks(((2.0 / N) * np.cos(th).T).astype(np.float32)).astype(bf)
        out[f"ids_{tag}"] = inv_blocks(((2.0 / N) * np.sin(th).T).astype(np.float32)).astype(bf)
        t = np.linspace(0.0, 1.0, Lx, dtype=np.float32)
        w = (2.0 * np.pi * np.arange(Lx, dtype=np.float32) / Lx).astype(np.float32)
        fr = np.linspace(1e-4, 15.0, 16, dtype=np.float32)
        fw = w[:, None] * fr[None, :]
        z = np.concatenate([t[:, None], np.cos(fw), -np.sin(fw)], axis=-1).astype(np.float32)
        out[f"z_{tag}"] = np.ascontiguousarray(z.T)
        out[f"negt_{tag}"] = np.ascontiguousarray((-t).reshape(Lx // 128, 128).T)
    deltas = np.linspace(np.log(1e-2) / 0.3, np.log(1e-2) / 1.5, D, dtype=np.float32)
    out["absd"] = np.ascontiguousarray(np.broadcast_to(np.abs(deltas)[None, :], (128, D))).astype(np.float32)
    _HYC.update(out)
    return _HYC


def rope_tables():
    t = np.arange(L)
    row = (t // 64).astype(np.float32)
    colp = (t % 64).astype(np.float32)
    inv = (10000.0 ** (-(np.arange(0, 16, 2, dtype=np.float32) / 16.0))).astype(np.float32)
    ang = np.concatenate([row[:, None] * inv, colp[:, None] * inv], axis=-1).astype(np.float32)
    cs = np.zeros((128, 2, T), np.float32)
    cs[:, 0, :] = 1.0
    for d in range(32):
        a, b, jj = d // 16, (d % 16) // 8, d % 8
        cs[64 + d, 0, L0:L0 + L] = np.cos(ang[:, a * 8 + jj])
        cs[64 + d, 1, L0:L0 + L] = np.sin(ang[:, a * 8 + jj]) * (-1.0 if b == 0 else 1.0)
    return cs


def kernel(**inputs):
    inp = {k: np.asarray(v) for k, v in inputs.items()}
    layers = list(range(DEPTH))
    prog = Prog(layers)
    nc = prog.build()
    in_maps = [make_inputs(inp, layers, c) for c in range(8)]
    res = run_bass_kernel_spmd(nc, in_maps, core_ids=list(range(8)))
    out = np.empty((16, L, D), np.float32)
    for c in range(8):
        y = res.results[c]["yT"]
        for bb in range(NB):
            out[c * NB + bb] = y[bb].T
    return out
```
